# Optimizing a Trainium2 kernel written in Bass

```python
import math
import jax, jax.numpy as jnp
from jax import lax
import numpy as np

D_MODEL = 2048
BATCH = 4
SEQ = 2048
DEPTH = 2
DEC_BATCH = 128
DEC_SEQ = 8
PAST_LEN = 16384
PAGE_SIZE = 128

GLA_HEADS = 4
GLA_KEY_DIM = D_MODEL // 2
GLA_VAL_DIM = D_MODEL
GLA_HEAD_K = GLA_KEY_DIM // GLA_HEADS
GLA_HEAD_V = GLA_VAL_DIM // GLA_HEADS
GLA_GATE_RANK = 16
GLA_GATE_TAU = 16.0
GLA_CHUNK = 64
GLA_IN_WIDTH = 2 * GLA_KEY_DIM + 2 * GLA_VAL_DIM + GLA_GATE_RANK
SG_WIDTH = D_MODEL
SG_GROUPS = 8
SG_GROUP_DIM = SG_WIDTH // SG_GROUPS
SG_CHUNK = 128
D_FF = 4 * D_MODEL
N_MIXERS = 2
N_GLA = (DEPTH + 1) // 2
N_SG = DEPTH // 2
DEEPNORM_ALPHA = (2.0 * DEPTH) ** 0.25
DEEPNORM_BETA = (8.0 * DEPTH) ** -0.25
LN_EPS = 1e-5
HEAD_NORM_EPS = 1e-6

kernel_name = "gla_gmlp_hybrid_deepnorm_step"


def layer_norm(x, g, b):
    xf = x.astype(jnp.float32)
    mu = jnp.mean(xf, axis=-1, keepdims=True)
    var = jnp.mean(jnp.square(xf - mu), axis=-1, keepdims=True)
    return ((xf - mu) * lax.rsqrt(var + LN_EPS) * g + b).astype(x.dtype)


def gla_chunked(q, k, v, log_a, s0):
    bsz, t, h, _ = q.shape
    dv = v.shape[-1]
    c = min(GLA_CHUNK, t)
    n = -(-t // c)
    pad = n * c - t

    def prep(a):
        a = jnp.pad(a.astype(jnp.float32), ((0, 0), (0, pad), (0, 0), (0, 0)))
        return a.reshape(bsz, n, c, h, a.shape[-1]).transpose(1, 0, 3, 2, 4)

    qc, kc, vc, gc = prep(q), prep(k), prep(v), prep(log_a)
    causal = jnp.tril(jnp.ones((c, c), dtype=bool))

    def step(s, inp):
        qi, ki, vi, gi = inp
        b = jnp.cumsum(gi, axis=-2)
        b_last = b[..., -1:, :]
        q_dec = qi * jnp.exp(b)
        k_inv = ki * jnp.exp(-b)
        k_to_end = ki * jnp.exp(b_last - b)
        scores = jnp.where(causal, jnp.einsum('bhtk,bhsk->bhts', q_dec, k_inv), 0.0)
        o = (jnp.einsum('bhtk,bhkv->bhtv', q_dec, s)
             + jnp.einsum('bhts,bhsv->bhtv', scores, vi))
        s_new = (jnp.exp(b_last[..., 0, :])[..., None] * s
                 + jnp.einsum('bhsk,bhsv->bhkv', k_to_end, vi))
        return s_new, o

    s_fin, o = lax.scan(step, s0.astype(jnp.float32), (qc, kc, vc, gc))
    o = o.transpose(1, 0, 3, 2, 4).reshape(bsz, n * c, h, dv)[:, :t]
    return o, s_fin


def gla_mixer(x, w_in, w_gate, b_gate, norm_g, w_out, s0):
    bsz, t, _ = x.shape
    proj = x @ w_in
    q, k, v, r, g_low = jnp.split(
        proj, [GLA_KEY_DIM, 2 * GLA_KEY_DIM, 2 * GLA_KEY_DIM + GLA_VAL_DIM,
               2 * GLA_KEY_DIM + 2 * GLA_VAL_DIM], axis=-1)
    q = q.reshape(bsz, t, GLA_HEADS, GLA_HEAD_K) * (GLA_HEAD_K ** -0.5)
    k = k.reshape(bsz, t, GLA_HEADS, GLA_HEAD_K)
    v = v.reshape(bsz, t, GLA_HEADS, GLA_HEAD_V)
    gate_logit = (g_low @ w_gate + b_gate).astype(jnp.float32)
    log_a = (jax.nn.log_sigmoid(gate_logit) / GLA_GATE_TAU).reshape(bsz, t, GLA_HEADS, GLA_HEAD_K)
    o, s_fin = gla_chunked(q, k, v, log_a, s0)
    o = o * lax.rsqrt(jnp.mean(jnp.square(o), axis=-1, keepdims=True) + HEAD_NORM_EPS) * norm_g
    o = o.reshape(bsz, t, GLA_VAL_DIM).astype(x.dtype) * jax.nn.silu(r)
    return o @ w_out, s_fin.astype(s0.dtype)


def sg_mixer(x, w_in, b_in, v_norm_g, v_norm_b, w_spatial, b_spatial, w_out):
    bsz, t, _ = x.shape
    z = jax.nn.gelu(x @ w_in + b_in, approximate=False)
    u, v = jnp.split(z, 2, axis=-1)
    v = layer_norm(v, v_norm_g, v_norm_b)
    n = -(-t // SG_CHUNK)
    pad = n * SG_CHUNK - t
    vp = jnp.pad(v, ((0, 0), (0, pad), (0, 0))).reshape(bsz, n, SG_CHUNK, SG_GROUPS, SG_GROUP_DIM)
    causal = jnp.tril(jnp.ones((SG_CHUNK, SG_CHUNK), dtype=bool))
    w_causal = jnp.where(causal, w_spatial, 0.0).astype(v.dtype)
    mixed = jnp.einsum('gts,bnsgd->bntgd', w_causal, vp) + b_spatial.T[:, :, None]
    mixed = mixed.reshape(bsz, n * SG_CHUNK, SG_WIDTH)[:, :t]
    return (u * mixed) @ w_out, v


def sq_relu_mlp(x, w1, w2):
    return jnp.square(jax.nn.relu(x @ w1)) @ w2


def setup_inputs(seed: int = 0) -> dict:
    key = jax.random.key(seed)
    ks = jax.random.split(key, 24)
    nrm = lambda k, shape, s: jax.random.normal(k, shape, jnp.float32) * s
    return {
        "x_prompt": nrm(ks[0], (BATCH, SEQ, D_MODEL), 1.0),
        "x_sample": nrm(ks[1], (DEC_BATCH, DEC_SEQ, D_MODEL), 1.0),
        "state_gla": nrm(ks[2], (N_GLA, DEC_BATCH, GLA_HEADS, GLA_HEAD_K, GLA_HEAD_V), 1.0),
        "gla_w_in": nrm(ks[3], (N_GLA, D_MODEL, GLA_IN_WIDTH), D_MODEL ** -0.5),
        "gla_w_gate": nrm(ks[4], (N_GLA, GLA_GATE_RANK, GLA_KEY_DIM), GLA_GATE_RANK ** -0.5),
        "gla_b_gate": nrm(ks[5], (N_GLA, GLA_KEY_DIM), 0.1),
        "gla_norm_g": 1.0 + nrm(ks[6], (N_GLA, GLA_HEADS, GLA_HEAD_V), 0.02),
        "gla_w_out": nrm(ks[7], (N_GLA, GLA_VAL_DIM, D_MODEL), GLA_VAL_DIM ** -0.5 * DEEPNORM_BETA),
        "sg_w_in": nrm(ks[8], (N_SG, D_MODEL, 2 * SG_WIDTH), D_MODEL ** -0.5),
        "sg_b_in": nrm(ks[9], (N_SG, 2 * SG_WIDTH), 0.02),
        "sg_v_norm_g": 1.0 + nrm(ks[10], (N_SG, SG_WIDTH), 0.02),
        "sg_v_norm_b": nrm(ks[11], (N_SG, SG_WIDTH), 0.02),
        "sg_w_spatial": nrm(ks[12], (N_SG, SG_GROUPS, SG_CHUNK, SG_CHUNK), SG_CHUNK ** -0.5),
        "sg_b_spatial": 1.0 + nrm(ks[13], (N_SG, SG_GROUPS, SG_CHUNK), 0.02),
        "sg_w_out": nrm(ks[14], (N_SG, SG_WIDTH, D_MODEL), SG_WIDTH ** -0.5 * DEEPNORM_BETA),
        "mlp_w1": nrm(ks[15], (DEPTH, D_MODEL, D_FF), D_MODEL ** -0.5),
        "mlp_w2": nrm(ks[16], (DEPTH, D_FF, D_MODEL), D_FF ** -0.5 * DEEPNORM_BETA),
        "ln1_g": 1.0 + nrm(ks[17], (DEPTH, D_MODEL), 0.02),
        "ln1_b": nrm(ks[18], (DEPTH, D_MODEL), 0.02),
        "ln2_g": 1.0 + nrm(ks[19], (DEPTH, D_MODEL), 0.02),
        "ln2_b": nrm(ks[20], (DEPTH, D_MODEL), 0.02),
    }


def reference(x_prompt, x_sample, state_gla, gla_w_in, gla_w_gate, gla_b_gate, gla_norm_g,
              gla_w_out, sg_w_in, sg_b_in, sg_v_norm_g, sg_v_norm_b, sg_w_spatial,
              sg_b_spatial, sg_w_out, mlp_w1, mlp_w2, ln1_g, ln1_b, ln2_g, ln2_b):
    def run(x, gla_state0):
        new_gla, new_v = [], []
        for i in range(DEPTH):
            j = i // N_MIXERS
            if i % N_MIXERS == 0:
                h, s = gla_mixer(x, gla_w_in[j], gla_w_gate[j], gla_b_gate[j], gla_norm_g[j],
                                 gla_w_out[j], gla_state0[j])
                new_gla.append(s)
            else:
                h, v_rows = sg_mixer(x, sg_w_in[j], sg_b_in[j], sg_v_norm_g[j], sg_v_norm_b[j],
                                     sg_w_spatial[j], sg_b_spatial[j], sg_w_out[j])
                new_v.append(v_rows)
            x = layer_norm(DEEPNORM_ALPHA * x + h, ln1_g[i], ln1_b[i])
            x = layer_norm(DEEPNORM_ALPHA * x + sq_relu_mlp(x, mlp_w1[i], mlp_w2[i]), ln2_g[i], ln2_b[i])
        return x, jnp.stack(new_gla), jnp.stack(new_v)

    zero_state = jnp.zeros((N_GLA, x_prompt.shape[0], GLA_HEADS, GLA_HEAD_K, GLA_HEAD_V), state_gla.dtype)
    y_prompt, new_state_gla_prompt, _ = run(x_prompt, zero_state)
    y_sample, new_state_gla_sample, new_sg_v_sample = run(x_sample, state_gla)
    return (y_prompt, y_sample, new_state_gla_prompt, new_state_gla_sample, new_sg_v_sample)
```

```python
import numpy as np
from contextlib import ExitStack
import concourse.bass as bass
import concourse.mybir as mybir
from concourse.bass_utils import run_bass_kernel_spmd

F32 = mybir.dt.float32
BF16 = mybir.dt.bfloat16
AF = mybir.ActivationFunctionType
ALU = mybir.AluOpType

D = 2048
KD = 16
NT = 9
NPF = 8
T = NT * 128
TT = 384
DFF = 8192
ALPHA = float((2.0 * 2) ** 0.25)
LN_EPS = 1e-5
HN_EPS = 1e-6
import os as _os2
DBG_NOUNIT = bool(_os2.environ.get('DBG_NOUNIT'))
import os as _os
NO_SAME_ENG_SYNC = bool(_os.environ.get('NO_SAME_ENG_SYNC'))


class Sched:
    def __init__(self):
        self.ops = []
        self.res = {}
        self.ghost = set()

    def add(self, eng, fn, r=(), w=(), dma=None):
        i = len(self.ops)
        deps = set()
        for name in r:
            st = self.res.get(name)
            if st is None:
                st = self.res[name] = [None, list(self.ghost)]
            if st[0] is not None:
                deps.add(st[0])
        for name in w:
            st = self.res.get(name)
            if st is None:
                st = self.res[name] = [None, list(self.ghost)]
            if st[0] is not None:
                deps.add(st[0])
            deps.update(st[1])
        for name in r:
            self.res[name][1].append(i)
        for name in w:
            self.res[name] = [i, []]
        deps.discard(i)
        self.ops.append(dict(i=i, eng=eng, fn=fn, deps=deps, dma=dma, signal=False))
        return i

    def retire(self, names):
        for n in names:
            st = self.res.pop(n, None)
            if st is None:
                continue
            if st[0] is not None:
                self.ghost.add(st[0])
            self.ghost.update(st[1])
        best = {}
        for d in self.ghost:
            p = self.ops[d]
            key = ('d', p['dma']) if p['dma'] else ('c', p['eng'])
            if key not in best or best[key] < d:
                best[key] = d
        self.ghost = set(best.values())

    def retire_prefix(self, *prefixes):
        self.retire([n for n in list(self.res) if any(n.startswith(p) for p in prefixes)])

    def emit(self, nc, es, final_eng='sp'):
        ops = self.ops
        last_dma = {}
        for op in ops:
            if op['dma']:
                last_dma[op['dma']] = op['i']
        fin = dict(i=len(ops), eng=final_eng, fn=None, deps=set(last_dma.values()), dma=None, signal=False)
        ops.append(fin)
        for op in ops:
            best = {}
            for d in op['deps']:
                p = ops[d]
                key = ('d', p['dma']) if p['dma'] else ('c', p['eng'])
                if key not in best or best[key] < d:
                    best[key] = d
            rd = []
            for key, d in best.items():
                p = ops[d]
                if p['dma'] is None and p['eng'] == 'pe' and op['eng'] == 'pe' and op['dma'] is None:
                    continue
                if NO_SAME_ENG_SYNC and p['dma'] is None and op['dma'] is None and p['eng'] == op['eng']:
                    continue
                if p['dma'] is None:
                    p['signal'] = True
                rd.append(d)
            op['rdeps'] = rd
        cnt = {}
        dcnt = {}
        for op in ops:
            if op['dma']:
                dcnt[op['dma']] = dcnt.get(op['dma'], 0) + 16
                op['sval'] = dcnt[op['dma']]
            elif op['signal']:
                cnt[op['eng']] = cnt.get(op['eng'], 0) + 1
                op['sval'] = cnt[op['eng']]
        engs = ['pe', 'act', 'dve', 'pool', 'sp']
        sems = {e: es.enter_context(nc.semaphore("s_" + e)) for e in engs}
        dsems = {k: es.enter_context(nc.semaphore("d_%d" % n)) for n, k in enumerate(sorted(dcnt))}
        self.nsem = len(sems) + len(dsems)
        self.maxcnt = dict(cnt)
        block = es.enter_context(nc.Block())
        per = {e: [op for op in ops if op['eng'] == e] for e in engs}

        def run(e, h):
            waited = {}
            for op in per[e]:
                for d in op['rdeps']:
                    p = ops[d]
                    if p['dma']:
                        s, v, k = dsems[p['dma']], p['sval'], ('d', p['dma'])
                    else:
                        s, v, k = sems[p['eng']], p['sval'], ('c', p['eng'])
                    if waited.get(k, 0) >= v:
                        continue
                    waited[k] = v
                    h.wait_ge(s, v)
                if op['fn'] is None:
                    continue
                ins = op['fn'](h)
                if op['dma']:
                    ins.then_inc(dsems[op['dma']], 16)
                elif op['signal']:
                    ins.then_inc(sems[e], 1)

        @block.tensor
        def _(h):
            run('pe', h)

        @block.scalar
        def _(h):
            run('act', h)

        @block.vector
        def _(h):
            run('dve', h)

        @block.gpsimd
        def _(h):
            run('pool', h)

        @block.sync
        def _(h):
            run('sp', h)


class Arena:
    def __init__(self, ap_f32):
        self.ap = ap_f32
        self.n = ap_f32.shape[1]
        self.free = [(0, self.n)]
        self.live = {}
        self.peak = 0

    def alloc(self, shape, dtype, name=None):
        n = int(np.prod(shape))
        words = n if dtype == F32 else (n + 1) // 2
        words = (words + 1) // 2 * 2
        for idx, (o, sz) in enumerate(self.free):
            if sz >= words:
                break
        else:
            raise AssertionError(("arena overflow", name, words, self.free))
        if sz == words:
            self.free.pop(idx)
        else:
            self.free[idx] = (o + words, sz - words)
        self.peak = max(self.peak, o + words)
        v = self.ap[:, o:o + words]
        if dtype != F32:
            v = v.bitcast(dtype)
        v = v[:, 0:n]
        if len(shape) == 2:
            v = v.rearrange("p (a b) -> p a b", a=shape[0])
        elif len(shape) == 3:
            v = v.rearrange("p (a b c) -> p a b c", a=shape[0], b=shape[1])
        self.live[id(v)] = (o, words, v)
        return v

    def release(self, *views):
        for v in views:
            o, words, _ = self.live.pop(id(v))
            self.free.append((o, words))
        self.free.sort()
        merged = []
        for o, sz in self.free:
            if merged and merged[-1][0] + merged[-1][1] == o:
                merged[-1] = (merged[-1][0], merged[-1][1] + sz)
            else:
                merged.append((o, sz))
        self.free = merged


ARENA_WORDS = 53000

C_TRI_S, C_TRIU_S, C_BD_S, C_BDU_S, C_TRI01, C_BD01 = range(6)


def make_consts():
    p = np.arange(128)
    s, t = p[:, None], p[None, :]
    same = (s // 8) == (t // 8)
    tri = (s <= t)
    triu = (s > t)
    m = np.zeros((128, 6 * 128 + 16 + 2), np.float32)
    m[:, 0:128] = tri * (-1.0 / 16)
    m[:, 128:256] = triu * (-1.0 / 16)
    m[:, 256:384] = (tri & same) * (-1.0 / 16)
    m[:, 384:512] = (triu & same) * (-1.0 / 16)
    m[:, 512:640] = tri
    m[:, 640:768] = tri & same
    m[:, 768:784] = (p[:, None] // 8) == np.arange(16)[None, :]
    m[:, 784] = -1.0 / 16
    m[:, 785] = 1.0
    return m


def build(phases=("gla", "mlp0", "sg", "mlp1"), dbg=False):
    nc = bass.Bass("TRN2", target_bir_lowering=False)

    def din(name, shape):
        return nc.dram_tensor(name, list(shape), F32, kind="ExternalInput").ap()

    def dout(name, shape):
        return nc.dram_tensor(name, list(shape), F32, kind="ExternalOutput").ap()

    xm = din("xm", [T, D])
    xp = din("xp", [NPF * 128, D])
    st_in = din("st", [16, 4, 256, 512])
    gla_w_in = din("gla_w_in", [D, 6160])
    wg_aug = din("wg_aug", [32, 1024])
    gla_ng = din("gla_ng", [1, D])
    gla_w_out = din("gla_w_out", [D, D])
    sg_w_in = din("sg_w_in", [D, 2 * D])
    sg_binT = din("sg_binT", [128, 16])
    sg_bv = din("sg_bv", [1, D])
    sg_vg = din("sg_vg", [1, D])
    sg_vb = din("sg_vb", [1, D])
    sg_wsT = din("sg_wsT", [128, 8, 128])
    sg_wsTs = din("sg_wsTs", [128, 8, 128])
    sg_bsp = din("sg_bsp", [2, 8, 128])
    sg_w_out = din("sg_w_out", [D, D])
    w1 = din("mlp_w1", [2, D, DFF])
    w2 = din("mlp_w2", [2, DFF, D])
    ln1g = din("ln1_g", [2, D])
    ln1b = din("ln1_b", [2, D])
    ln2g = din("ln2_g", [2, D])
    ln2b = din("ln2_b", [2, D])
    ident_d = din("ident", [128, 128])
    lnT_d = din("lnT", [4, 128, 2, 16])
    cm_d = din("cmask", [128, 786])
    y = dout("y", [T, D])
    gst_p = dout("gst_p", [4, 256, 512])
    gst_s = dout("gst_s", [16, 4, 256, 512])
    sgv = dout("sgv", [128, D])
    xspill = nc.dram_tensor("xspill", [T, D], F32, kind="Internal").ap()
    sscr = nc.dram_tensor("sscr", [4, 256, 512], F32, kind="Internal").ap()

    S = Sched()
    es = ExitStack()
    arena_t = es.enter_context(nc.sbuf_tensor("arena", [128, ARENA_WORDS], F32))
    A = Arena(arena_t[:])
    ps = [es.enter_context(nc.psum_tensor("ps%d" % i, [128, 512], F32)) for i in range(8)]
    psb = [p_[:].bitcast(BF16) for p_ in ps]
    bank_ctr = [0]

    pinned = set()

    def bank():
        while True:
            b = bank_ctr[0] % 8
            bank_ctr[0] += 1
            if b not in pinned:
                return b

    def PS(b):
        return 'ps%d' % b

    ident = A.alloc([128], F32)
    identb = A.alloc([128], BF16)
    NWS = 4
    wbuf = [A.alloc([16, 512], BF16) for _ in range(NWS)]
    xT = A.alloc([KD, T], BF16)
    free_w = list(range(NWS))

    def take_w():
        return free_w.pop(0)

    def give_w(s_):
        free_w.append(s_)

    def XT(tc):
        return 'xT%d' % tc

    XT_ALL = [XT(tc) for tc in range(NT)]

    S.add('sp', lambda h: h.dma_start(out=ident, in_=ident_d), w=['ident'], dma='c_ident')
    S.add('pool', lambda h: h.dma_start(out=identb, in_=ident_d), w=['identb'], dma='c_identb')

    def load_w(slot, src_ap, dst=None):
        if dst is None:
            a, b = src_ap.shape[1], src_ap.shape[2]
            dst = wbuf[slot]
            if (a, b) != (16, 512):
                dst = dst.rearrange("p a b -> p (a b)").rearrange("p (a b) -> p a b", a=a)
        S.add('pool', lambda h: h.dma_start(out=dst, in_=src_ap), w=['w%d' % slot], dma='w%d' % slot)
        return dst

    cp_ctr = [0]

    def evac_copy(out, in_, r, w):
        cp_ctr[0] += 1
        if cp_ctr[0] % 2:
            S.add('act', lambda h: h.copy(out=out, in_=in_), r=r, w=w)
        else:
            S.add('dve', lambda h: h.tensor_copy(out=out, in_=in_), r=r, w=w)

    def transpose_f32_chunk(src, src_res, dstT, dst_res, tc):
        for g in range(4):
            b = bank()
            for i in range(4):
                kc = 4 * g + i
                S.add('pe', lambda h, kc=kc, i=i, b=b: h.transpose(
                    out=ps[b][:, i * 128:(i + 1) * 128], in_=src[:, kc * 128:(kc + 1) * 128], identity=ident),
                    r=[src_res[g] if isinstance(src_res, list) else src_res, 'ident'], w=[PS(b)])
            evac_copy(dstT[:, 4 * g:4 * g + 4, tc * 128:(tc + 1) * 128],
                      ps[b][:].rearrange("p (a t) -> p a t", a=4), r=[PS(b)], w=[dst_res])

    def transpose_bf16_chunk(src, src_res, dstT, dst_res, tc):
        for g in range(4):
            b = bank()
            for i in range(4):
                kc = 4 * g + i
                S.add('pe', lambda h, kc=kc, i=i, b=b: h.transpose(
                    out=psb[b][:, i * 128:(i + 1) * 128], in_=src[:, kc * 128:(kc + 1) * 128], identity=identb),
                    r=[src_res, 'identb'], w=[PS(b)])
            evac_copy(dstT[:, 4 * g:4 * g + 4, tc * 128:(tc + 1) * 128],
                      psb[b][:, 0:512].rearrange("p (a t) -> p a t", a=4), r=[PS(b)], w=[dst_res])

    def load_xT(src_dram, nchunks, dstT, dst_res_fn):
        stg = [A.alloc([D], F32) for _ in range(2)]
        for tc in range(nchunks):
            i = tc % 2
            S.add('sp', lambda h, tc=tc, i=i: h.dma_start(out=stg[i], in_=src_dram[tc * 128:(tc + 1) * 128, :]),
                  w=['xstg%d' % i], dma='xstg%d' % i)
            transpose_f32_chunk(stg[i], 'xstg%d' % i, dstT, dst_res_fn(tc), tc)
        S.retire_prefix('xstg')
        A.release(*stg)

    xres_box = [None]

    def XR(tc, c):
        return 'xres%d_%d' % (tc, c)

    def XRA(tc):
        return ['xres%d_%d' % (tc, c) for c in range(4)]

    def alloc_xres(src_dram):
        xres = A.alloc([NT, D], F32)
        xres_box[0] = xres
        for tc in range(NT):
            S.add('sp', lambda h, tc=tc: h.dma_start(out=xres[:, tc, :], in_=src_dram[tc * 128:(tc + 1) * 128, :]),
                  w=XRA(tc), dma='xres%d' % (tc % 3))

    def free_xres():
        S.retire([n for tc in range(NT) for n in XRA(tc)])
        A.release(xres_box[0])
        xres_box[0] = None

    def out_proj(w_dram):
        xres = xres_box[0]
        wv = w_dram.rearrange("(k p) d -> p k d", p=128)
        slots = {}

        def issue(cb):
            s_ = take_w()
            load_w(s_, wv[:, :, cb * 512:(cb + 1) * 512])
            slots[cb] = s_

        nxt = 0
        while nxt < 4 and free_w:
            issue(nxt)
            nxt += 1
        for cb in range(4):
            if cb not in slots:
                issue(cb)
                nxt = cb + 1
            s_ = slots[cb]
            for tc in range(NT):
                b = bank()
                for k in range(KD):
                    S.add('pe', lambda h, k=k, tc=tc, b=b, s_=s_: h.matmul(
                        ps[b][:, :], lhsT=xT[:, k, tc * 128:(tc + 1) * 128], rhs=wbuf[s_][:, k, :],
                        start=(k == 0), stop=(k == KD - 1)),
                        r=[XT(tc), 'w%d' % s_], w=[PS(b)])
                dst = xres[:, tc, cb * 512:(cb + 1) * 512]
                S.add('dve', lambda h, dst=dst, b=b: h.scalar_tensor_tensor(
                    out=dst, in0=dst, scalar=ALPHA, op0=ALU.mult, in1=ps[b][:, :], op1=ALU.add),
                    r=[PS(b), XR(tc, cb)], w=[XR(tc, cb)])
            give_w(s_)
            if nxt < 4 and free_w:
                issue(nxt)
                nxt += 1

    def mlp(layer):
        xres = xres_box[0]
        hT = [A.alloc([4, T], BF16) for _ in range(2)]
        rtmp = [A.alloc([TT], BF16) for _ in range(2)]
        NFB = DFF // 512
        w1v = w1[layer].rearrange("(k p) f -> p k f", p=128)
        w2v = w2[layer].rearrange("(fb c p) d -> fb p c d", p=128, c=4)
        slots = {}

        def issue1(fb):
            s1 = take_w()
            load_w(s1, w1v[:, :, fb * 512:(fb + 1) * 512])
            slots[fb] = [s1, None, None]

        def issue2(fb):
            s2 = take_w()
            d2 = load_w(s2, w2v[fb])
            slots[fb][1] = s2
            slots[fb][2] = d2

        def stage_a(fb):
            s1 = slots[fb][0]
            hs = fb % 2
            for fc in range(4):
                bs = [bank() for _ in range(3)]
                for k in range(KD):
                    for tt in range(3):
                        S.add('pe', lambda h, k=k, tt=tt, fc=fc, b=bs[tt]: h.matmul(
                            ps[b][:, 0:TT], lhsT=wbuf[s1][:, k, fc * 128:(fc + 1) * 128],
                            rhs=xT[:, k, tt * TT:(tt + 1) * TT], start=(k == 0), stop=(k == KD - 1)),
                            r=['w%d' % s1] + XT_ALL[3 * tt:3 * tt + 3], w=[PS(bs[tt])])
                for tt in range(3):
                    rt = (fc * 3 + tt) % 2
                    S.add('act', lambda h, tt=tt, b=bs[tt], rt=rt: h.activation(
                        out=rtmp[rt], in_=ps[b][:, 0:TT], func=AF.Relu),
                        r=[PS(bs[tt])], w=['rtmp%d' % rt])
                    eng = 'pool' if tt == 1 else 'dve'
                    S.add(eng, lambda h, tt=tt, fc=fc, rt=rt: h.tensor_tensor(
                        out=hT[hs][:, fc, tt * TT:(tt + 1) * TT], in0=rtmp[rt], in1=rtmp[rt], op=ALU.mult),
                        r=['rtmp%d' % rt], w=['hT%d' % hs])

        def stage_b(fb):
            s2, d2 = slots[fb][1], slots[fb][2]
            hs = fb % 2
            for tc in range(NT):
                for cb in range(4):
                    b = bank()
                    for fc in range(4):
                        S.add('pe', lambda h, fc=fc, tc=tc, cb=cb, b=b: h.matmul(
                            ps[b][:, :], lhsT=hT[hs][:, fc, tc * 128:(tc + 1) * 128],
                            rhs=d2[:, fc, cb * 512:(cb + 1) * 512], start=(fc == 0), stop=(fc == 3)),
                            r=['hT%d' % hs, 'w%d' % s2], w=[PS(b)])
                    dst = xres[:, tc, cb * 512:(cb + 1) * 512]
                    if fb == 0:
                        S.add('dve', lambda h, dst=dst, b=b: h.scalar_tensor_tensor(
                            out=dst, in0=dst, scalar=ALPHA, op0=ALU.mult, in1=ps[b][:, :], op1=ALU.add),
                            r=[PS(b), XR(tc, cb)], w=[XR(tc, cb)])
                    else:
                        S.add('dve', lambda h, dst=dst, b=b: h.tensor_tensor(
                            out=dst, in0=dst, in1=ps[b][:, :], op=ALU.add),
                            r=[PS(b), XR(tc, cb)], w=[XR(tc, cb)])

        issue1(0)
        issue2(0)
        for fb in range(NFB + 1):
            if fb < NFB:
                stage_a(fb)
                give_w(slots[fb][0])
                if fb + 1 < NFB:
                    issue1(fb + 1)
            if fb >= 1:
                stage_b(fb - 1)
                give_w(slots[fb - 1][1])
            if fb + 1 < NFB:
                issue2(fb + 1)
        S.retire_prefix('hT', 'rtmp')
        A.release(*hT, *rtmp)

    def ln_stats(z, zr, st, sm, tag, eps):
        for c in range(4):
            S.add('dve', lambda h, c=c: h.bn_stats(out=st[:, c, :], in_=z[:, c * 512:(c + 1) * 512]),
                  r=[zr[c]], w=[tag + 'st'])
        S.add('dve', lambda h: h.bn_aggr(out=sm[:, 0:2], in_=st), r=[tag + 'st'], w=[tag + 'sm'])
        S.add('act', lambda h: h.activation(out=sm[:, 2:3], in_=sm[:, 1:2], func=AF.Ln, bias=eps_t[:, 0:1], scale=1.0),
              r=[tag + 'sm', 'eps'], w=[tag + 'sm'])
        S.add('act', lambda h: h.activation(out=sm[:, 3:4], in_=sm[:, 2:3], func=AF.Exp, scale=-0.5),
              r=[tag + 'sm'], w=[tag + 'sm'])
        S.add('dve', lambda h: h.tensor_scalar(out=sm[:, 4:5], in0=sm[:, 0:1], scalar1=sm[:, 3:4],
                                               scalar2=-1.0, op0=ALU.mult, op1=ALU.mult),
              r=[tag + 'sm'], w=[tag + 'sm'])

    def layer_norm(idx, g_dram, b_dram, final):
        xres = xres_box[0]
        gt = A.alloc([D], F32)
        bt = A.alloc([D], F32)
        gbc = A.alloc([2, 16], F32)
        st = [A.alloc([4, 6], F32) for _ in range(3)]
        sm = [A.alloc([8], F32) for _ in range(3)]
        S.add('sp', lambda h: h.dma_start(out=gt, in_=g_dram.partition_broadcast(128)), w=['ln_g'], dma='ln_g')
        S.add('sp', lambda h: h.dma_start(out=bt, in_=b_dram.partition_broadcast(128)), w=['ln_b'], dma='ln_b')
        S.add('sp', lambda h: h.dma_start(out=gbc, in_=lnT_d[idx]), w=['ln_c'], dma='ln_c')

        def st_stage(tc):
            i = tc % 3
            ln_stats(xres[:, tc, :], XRA(tc), st[i], sm[i], 'ln%d' % i, LN_EPS)

        def nrm_stage(tc):
            i = tc % 3
            z = xres[:, tc, :]
            for c in range(4):
                zr = XR(tc, c)
                zc = z[:, c * 512:(c + 1) * 512]
                S.add('act', lambda h, i=i, zc=zc: h.activation(out=zc, in_=zc, func=AF.Identity,
                                                             scale=sm[i][:, 3:4], bias=sm[i][:, 4:5]),
                      r=[zr, 'ln%dsm' % i], w=[zr])

        def t_stage(tc):
            z = xres[:, tc, :]
            for g in range(4):
                zr = XR(tc, g)
                b = bank()
                for q_ in range(4):
                    kc = 4 * g + q_
                    S.add('pe', lambda h, kc=kc, q_=q_, b=b: h.transpose(
                        out=ps[b][:, q_ * 128:(q_ + 1) * 128], in_=z[:, kc * 128:(kc + 1) * 128], identity=ident),
                        r=[zr, 'ident'], w=[PS(b)])
                for q_ in range(4):
                    kc = 4 * g + q_
                    dst = xT[:, kc, tc * 128:(tc + 1) * 128]
                    src = ps[b][:, q_ * 128:(q_ + 1) * 128]
                    if q_ % 2 == 0:
                        S.add('act', lambda h, kc=kc, dst=dst, src=src: h.activation(
                            out=dst, in_=src, func=AF.Identity, scale=gbc[:, 0, kc:kc + 1], bias=gbc[:, 1, kc:kc + 1]),
                            r=[PS(b), 'ln_c'], w=[XT(tc)])
                    else:
                        S.add('dve', lambda h, kc=kc, dst=dst, src=src: h.tensor_scalar(
                            out=dst, in0=src, scalar1=gbc[:, 0, kc:kc + 1], scalar2=gbc[:, 1, kc:kc + 1],
                            op0=ALU.mult, op1=ALU.add), r=[PS(b), 'ln_c'], w=[XT(tc)])

        def gb_stage(tc):
            z = xres[:, tc, :]
            for c in range(4):
                zr = XR(tc, c)
                zc = z[:, c * 512:(c + 1) * 512]
                S.add('pool', lambda h, c=c, zc=zc: h.tensor_tensor(out=zc, in0=zc, in1=gt[:, c * 512:(c + 1) * 512], op=ALU.mult),
                      r=[zr, 'ln_g'], w=[zr])
                S.add('dve', lambda h, c=c, zc=zc: h.tensor_tensor(out=zc, in0=zc, in1=bt[:, c * 512:(c + 1) * 512], op=ALU.add),
                      r=[zr, 'ln_b'], w=[zr])
            if final:
                S.add('sp', lambda h, tc=tc, z=z: h.dma_start(out=y[tc * 128:(tc + 1) * 128, :], in_=z),
                      r=XRA(tc), dma='yout%d' % (tc % 3))

        st_stage(0)
        st_stage(1)
        nrm_stage(0)
        for tc in range(NT):
            if tc + 2 < NT:
                st_stage(tc + 2)
            if tc + 1 < NT:
                nrm_stage(tc + 1)
            if not final:
                t_stage(tc)
            gb_stage(tc)
        S.retire_prefix('ln')
        A.release(gt, bt, gbc, *st, *sm)

    cm = A.alloc([786], F32)
    S.add('sp', lambda h: h.dma_start(out=cm, in_=cm_d), w=['cm'], dma='c_cm')
    eps_t = A.alloc([2], F32)
    S.add('dve', lambda h: h.memset(eps_t[:, 0:1], LN_EPS), w=['eps'])
    S.add('dve', lambda h: h.memset(eps_t[:, 1:2], HN_EPS), w=['eps'])

    def CM(i):
        return cm[:, i * 128:(i + 1) * 128]

    def gla_layer():
        winv = gla_w_in.rearrange("(k p) f -> p k f", p=128)
        wg = A.alloc([1024], F32)
        S.add('sp', lambda h: h.dma_start(out=wg[0:32, :], in_=wg_aug), w=['wg'], dma='c_wg')
        wg16 = A.alloc([16, 16], BF16)
        S.add('pool', lambda h: h.dma_start(out=wg16, in_=winv[:, :, 6144:6160]), w=['wg16'], dma='c_wg16')
        identb_r = ['identb']

        def gate_T(srcT, src_res_list, ntok, name):
            gTa = A.alloc([ntok], F32)
            S.add('dve', lambda h: h.memset(gTa[0:32, :], 1.0), w=[name])
            ntt = ntok // TT if ntok % TT == 0 else None
            tiles = [(i * TT, TT) for i in range(ntok // TT)] if ntt else [(i * 512, 512) for i in range(ntok // 512)]
            for (o, n) in tiles:
                b = bank()
                rr = sorted(set(src_res_list[(o // 128):((o + n - 1) // 128) + 1]))
                for k in range(KD):
                    S.add('pe', lambda h, k=k, b=b, o=o, n=n: h.matmul(
                        ps[b][0:16, 0:n], lhsT=wg16[:, k, :], rhs=srcT[:, k, o:o + n],
                        start=(k == 0), stop=(k == KD - 1)), r=['wg16'] + rr, w=[PS(b)])
                S.add('act', lambda h, b=b, o=o, n=n: h.copy(out=gTa[0:16, o:o + n], in_=ps[b][0:16, 0:n]),
                      r=[PS(b)], w=[name])
            return gTa

        class WS:
            pass

        sgS = A.alloc([512], F32)
        t1S = A.alloc([512], BF16)
        junkS = A.alloc([512], BF16)
        v3 = [A.alloc([512], BF16) for _ in range(4)]

        def make_ws():
            w_ = WS()
            w_.qk = A.alloc([512], BF16)
            w_.sg = sgS
            w_.t1 = t1S
            w_.srg = A.alloc([512], BF16)
            w_.nl = A.alloc([256], F32)
            w_.eend = A.alloc([256], F32)
            w_.kte = A.alloc([256], BF16)
            w_.epos = A.alloc([2, 128], F32)
            w_.eneg = A.alloc([2, 128], F32)
            w_.qdT = A.alloc([2, 128], BF16)
            w_.kiT = A.alloc([2, 128], BF16)
            w_.scm = A.alloc([128], BF16)
            w_.junk = junkS
            w_.sm = A.alloc([8], F32)
            w_.all = [w_.qk, w_.srg, w_.nl, w_.eend, w_.kte, w_.epos, w_.eneg,
                      w_.qdT, w_.kiT, w_.scm, w_.sm]
            return w_

        xpT = A.alloc([KD, NPF * 128], BF16)
        load_xT(xp, NPF, xpT, lambda tc: 'xpT%d' % tc)
        XPT = ['xpT%d' % tc for tc in range(NPF)]
        gTp = gate_T(xpT, XPT, NPF * 128, 'gTp')
        Sf = A.alloc([2, 512], F32)
        wsets = [make_ws() for _ in range(3)]
        dec = [A.alloc([4], F32) for _ in range(2)]

        def gate_common(W, i, gsrc, gres, c, h_, tri_u):
            R = 'g%d' % i
            bg = bank()
            S.add('pe', lambda h, bg=bg: h.matmul(ps[bg][:, 0:256], lhsT=gsrc[0:32, c * 128:(c + 1) * 128],
                                                   rhs=wg[0:32, h_ * 256:(h_ + 1) * 256], start=True, stop=True),
                  r=['wg', gres], w=[PS(bg)])
            S.add('act', lambda h, bg=bg: h.activation(out=W.nl, in_=ps[bg][:, 0:256], func=AF.Exp, scale=-1.0),
                  r=[PS(bg)], w=[R + 'nl'])
            S.add('act', lambda h: h.activation(out=W.nl, in_=W.nl, func=AF.Ln, bias=cm[:, 785:786], scale=1.0),
                  r=[R + 'nl', 'cm'], w=[R + 'nl'])
            brc = bank()
            S.add('pe', lambda h, brc=brc: h.matmul(ps[brc][:, 0:256], lhsT=tri_u, rhs=W.nl, start=True, stop=True),
                  r=['cm', R + 'nl'], w=[PS(brc)])
            S.add('act', lambda h, brc=brc: h.activation(out=W.eend, in_=ps[brc][:, 0:256], func=AF.Exp),
                  r=[PS(brc)], w=[R + 'eend'])
            S.add('pool', lambda h: h.tensor_tensor(out=W.kte, in0=W.qk[:, 256:512], in1=W.eend, op=ALU.mult),
                  r=[R + 'qk', R + 'eend'], w=[R + 'kte'])

        for h_ in range(4):
            sk = take_w()
            S.add('pool', lambda h, sk=sk, h_=h_: h.dma_start(out=wbuf[sk][:, :, 256:512],
                                                              in_=winv[:, :, 1024 + h_ * 256:1024 + (h_ + 1) * 256]),
                  w=['w%d' % sk], dma='w%d' % sk)
            sv = take_w()
            load_w(sv, winv[:, :, 2048 + h_ * 512:2048 + (h_ + 1) * 512])
            S.add('dve', lambda h: h.memset(Sf, 0.0), w=['Sf'])
            for c in range(NPF):
                i = c % 2
                W = wsets[i]
                R = 'g%d' % i
                bk, bv_ = bank(), bank()
                for k in range(KD):
                    S.add('pe', lambda h, k=k, c=c, bk=bk, sk=sk: h.matmul(
                        ps[bk][:, 0:256], lhsT=xpT[:, k, c * 128:(c + 1) * 128], rhs=wbuf[sk][:, k, 256:512],
                        start=(k == 0), stop=(k == KD - 1)), r=[XPT[c], 'w%d' % sk], w=[PS(bk)])
                    S.add('pe', lambda h, k=k, c=c, bv_=bv_, sv=sv: h.matmul(
                        ps[bv_][:, :], lhsT=xpT[:, k, c * 128:(c + 1) * 128], rhs=wbuf[sv][:, k, :],
                        start=(k == 0), stop=(k == KD - 1)), r=[XPT[c], 'w%d' % sv], w=[PS(bv_)])
                S.add('dve', lambda h, W=W, bk=bk: h.tensor_copy(out=W.qk[:, 256:512], in_=ps[bk][:, 0:256]),
                      r=[PS(bk)], w=[R + 'qk'])
                S.add('act', lambda h, i=i, bv_=bv_: h.copy(out=v3[i], in_=ps[bv_][:, :]), r=[PS(bv_)], w=['gv%d' % i])
                gate_common(W, i, gTp, 'gTp', c, h_, CM(C_TRIU_S))
                bb = bank()
                for j in range(2):
                    S.add('pe', lambda h, j=j, bb=bb, W=W: h.matmul(
                        ps[bb][:, 2 * j:2 * j + 2], lhsT=W.nl[:, j * 128:(j + 1) * 128], rhs=cm[:, 784:786],
                        start=True, stop=True), r=[R + 'nl', 'cm'], w=[PS(bb)])
                S.add('act', lambda h, bb=bb, i=i: h.activation(out=dec[i], in_=ps[bb][:, 0:4], func=AF.Exp),
                      r=[PS(bb)], w=['dec%d' % i])
                for j in range(2):
                    bu = bank()
                    S.add('pe', lambda h, j=j, bu=bu, W=W, i=i: h.matmul(
                        ps[bu][:, :], lhsT=W.kte[:, j * 128:(j + 1) * 128], rhs=v3[i], start=True, stop=True),
                        r=[R + 'kte', 'gv%d' % i], w=[PS(bu)])
                    S.add('dve', lambda h, j=j, bu=bu, i=i: h.scalar_tensor_tensor(
                        out=Sf[:, j, :], in0=Sf[:, j, :], scalar=dec[i][:, 2 * j:2 * j + 1], op0=ALU.mult,
                        in1=ps[bu][:, :], op1=ALU.add), r=[PS(bu), 'dec%d' % i, 'Sf'], w=['Sf'])
            give_w(sk)
            give_w(sv)
            S.add('sp', lambda h, h_=h_: h.dma_start(out=sscr[h_].rearrange("(j p) v -> p j v", p=128), in_=Sf),
                  r=['Sf'], w=['sscr%d' % h_], dma='sscr%d' % h_)
        S.retire(XPT + ['gTp'] + ['dec0', 'dec1'])
        A.release(xpT, gTp, *dec)

        gated = A.alloc([NT, D], BF16)
        gTm = gate_T(xT, XT_ALL, T, 'gTm')
        gng = A.alloc([512], F32)
        Sb = A.alloc([2, 512], BF16)
        s0 = [A.alloc([2, 512], F32) for _ in range(3)]
        Qm = [A.alloc([2, 128], F32) for _ in range(2)]
        kteM = [A.alloc([256], BF16) for _ in range(2)]

        class Ck:
            pass

        def mk(h_, c, slots):
            ck = Ck()
            ck.h, ck.c, ck.slots = h_, c, slots
            ck.sample = (c == NT - 1)
            if ck.sample:
                ck.W, ck.R, ck.v, ck.VR = wsets[2], 'g2', v3[3], 'gv3'
            else:
                ck.W, ck.R, ck.v, ck.VR = wsets[c % 2], 'g%d' % (c % 2), v3[c % 3], 'gv%d' % (c % 3)
            return ck

        def proj(ck, slot):
            b = bank()
            c = ck.c
            for k in range(KD):
                S.add('pe', lambda h, k=k: h.matmul(
                    ps[b][:, :], lhsT=xT[:, k, c * 128:(c + 1) * 128], rhs=wbuf[slot][:, k, :],
                    start=(k == 0), stop=(k == KD - 1)), r=[XT(c), 'w%d' % slot], w=[PS(b)])
            return b

        def P_qk(ck):
            W, R = ck.W, ck.R
            bq = proj(ck, ck.slots[0])
            S.add('act', lambda h: h.mul(out=W.qk[:, 0:256], in_=ps[bq][:, 0:256], mul=1.0 / 16), r=[PS(bq)], w=[R + 'qk'])
            S.add('dve', lambda h: h.tensor_copy(out=W.qk[:, 256:512], in_=ps[bq][:, 256:512]), r=[PS(bq)], w=[R + 'qk'])

        def P_v(ck):
            bv_ = proj(ck, ck.slots[1])
            S.add('act', lambda h: h.copy(out=ck.v, in_=ps[bv_][:, :]), r=[PS(bv_)], w=[ck.VR])

        def P_r(ck):
            W, R, h_ = ck.W, ck.R, ck.h
            br = proj(ck, ck.slots[2])
            S.add('act', lambda h: h.activation(out=W.sg, in_=ps[br][:, :], func=AF.Exp, scale=-1.0), r=[PS(br)], w=['gsg'])
            S.add('act', lambda h: h.activation(out=W.sg, in_=W.sg, func=AF.Ln, bias=cm[:, 785:786], scale=1.0),
                  r=['gsg', 'cm'], w=['gsg'])
            S.add('act', lambda h: h.activation(out=W.sg, in_=W.sg, func=AF.Exp, scale=-1.0), r=['gsg'], w=['gsg'])
            S.add('dve', lambda h: h.tensor_tensor(out=W.t1, in0=ps[br][:, :], in1=W.sg, op=ALU.mult),
                  r=[PS(br), 'gsg'], w=['gt1'])
            S.add('pool', lambda h: h.tensor_tensor(out=W.srg, in0=W.t1, in1=gng, op=ALU.mult),
                  r=['gt1', 'gng'], w=[R + 'srg'])

        def G01(ck):
            W, R, c, h_ = ck.W, ck.R, ck.c, ck.h
            bg = bank()
            S.add('pe', lambda h: h.matmul(ps[bg][:, 0:256], lhsT=gTm[0:32, c * 128:(c + 1) * 128],
                                           rhs=wg[0:32, h_ * 256:(h_ + 1) * 256], start=True, stop=True),
                  r=['wg', 'gTm'], w=[PS(bg)])
            S.add('act', lambda h: h.activation(out=W.nl, in_=ps[bg][:, 0:256], func=AF.Exp, scale=-1.0),
                  r=[PS(bg)], w=[R + 'nl'])
            S.add('act', lambda h: h.activation(out=W.nl, in_=W.nl, func=AF.Ln, bias=cm[:, 785:786], scale=1.0),
                  r=[R + 'nl', 'cm'], w=[R + 'nl'])

        def G23(ck):
            W, R = ck.W, ck.R
            tri_u = CM(C_BDU_S) if ck.sample else CM(C_TRIU_S)
            tri = CM(C_BD_S) if ck.sample else CM(C_TRI_S)
            brc = bank()
            S.add('pe', lambda h: h.matmul(ps[brc][:, 0:256], lhsT=tri_u, rhs=W.nl, start=True, stop=True),
                  r=['cm', R + 'nl'], w=[PS(brc)])
            bbt = bank()
            for j in range(2):
                S.add('pe', lambda h, j=j: h.matmul(ps[bbt][:, j * 128:(j + 1) * 128], lhsT=W.nl[:, j * 128:(j + 1) * 128],
                                                     rhs=tri, start=True, stop=True), r=[R + 'nl', 'cm'], w=[PS(bbt)])
            S.add('act', lambda h: h.activation(out=W.eend, in_=ps[brc][:, 0:256], func=AF.Exp),
                  r=[PS(brc)], w=[R + 'eend'])
            S.add('act', lambda h: h.activation(out=W.epos.rearrange("p a b -> p (a b)"), in_=ps[bbt][:, 0:256], func=AF.Exp),
                  r=[PS(bbt)], w=[R + 'epos'])
            S.add('act', lambda h: h.activation(out=W.eneg.rearrange("p a b -> p (a b)"), in_=ps[bbt][:, 0:256], func=AF.Exp, scale=-1.0),
                  r=[PS(bbt)], w=[R + 'eneg'])
            S.add('pool', lambda h: h.tensor_tensor(out=W.kte, in0=W.qk[:, 256:512], in1=W.eend, op=ALU.mult),
                  r=[R + 'qk', R + 'eend'], w=[R + 'kte'])

        def G45(ck):
            W, R = ck.W, ck.R
            btr = bank()
            for j in range(4):
                S.add('pe', lambda h, j=j: h.transpose(out=psb[btr][:, j * 128:(j + 1) * 128],
                                                        in_=W.qk[:, j * 128:(j + 1) * 128], identity=identb),
                      r=[R + 'qk', 'identb'], w=[PS(btr)])
            S.add('dve', lambda h: h.tensor_tensor(out=W.qdT.rearrange("p a b -> p (a b)"), in0=psb[btr][:, 0:256],
                                                   in1=W.epos.rearrange("p a b -> p (a b)"), op=ALU.mult),
                  r=[PS(btr), R + 'epos'], w=[R + 'qdT'])
            S.add('dve', lambda h: h.tensor_tensor(out=W.kiT.rearrange("p a b -> p (a b)"), in0=psb[btr][:, 256:512],
                                                   in1=W.eneg.rearrange("p a b -> p (a b)"), op=ALU.mult),
                  r=[PS(btr), R + 'eneg'], w=[R + 'kiT'])

        def B12(ck):
            W, R = ck.W, ck.R
            bs = bank()
            for j in range(2):
                S.add('pe', lambda h, j=j: h.matmul(ps[bs][:, 0:128], lhsT=W.kiT[:, j, :], rhs=W.qdT[:, j, :],
                                                     start=(j == 0), stop=(j == 1)), r=[R + 'kiT', R + 'qdT'], w=[PS(bs)])
            m01 = CM(C_BD01) if ck.sample else CM(C_TRI01)
            S.add('dve', lambda h: h.tensor_tensor(out=W.scm, in0=ps[bs][:, 0:128], in1=m01, op=ALU.mult),
                  r=[PS(bs), 'cm'], w=[R + 'scm'])

        def B3(ck):
            W, R, c, h_ = ck.W, ck.R, ck.c, ck.h
            vv, VR = ck.v, ck.VR
            bo = bank()
            pinned.add(bo)
            ck.bo = bo
            S.add('pe', lambda h: h.matmul(ps[bo][:, :], lhsT=W.scm, rhs=vv, start=True, stop=False),
                  r=[R + 'scm', VR], w=[PS(bo)])
            if not ck.sample:
                for j in range(2):
                    S.add('pe', lambda h, j=j: h.matmul(ps[bo][:, :], lhsT=W.qdT[:, j, :], rhs=Sb[:, j, :],
                                                         start=False, stop=(j == 1)), r=[R + 'qdT', 'Sb'], w=[PS(bo)])
                for j in range(2):
                    bu = bank()
                    S.add('pe', lambda h, j=j, bu=bu: h.matmul(ps[bu][:, :], lhsT=W.kte[:, j * 128:(j + 1) * 128], rhs=vv,
                                                               start=True, stop=True), r=[R + 'kte', VR], w=[PS(bu)])
                    S.add('dve', lambda h, j=j, bu=bu: h.scalar_tensor_tensor(
                        out=Sf[:, j, :], in0=Sf[:, j, :], scalar=W.epos[:, j, 127:128], op0=ALU.mult,
                        in1=ps[bu][:, :], op1=ALU.add), r=[PS(bu), R + 'epos', 'Sf'], w=['Sf'])
                    S.add('act', lambda h, j=j: h.copy(out=Sb[:, j, :], in_=Sf[:, j, :]), r=['Sf'], w=['Sb'])
                if c == NT - 2:
                    S.add('sp', lambda h: h.dma_start(out=gst_p[h_].rearrange("(j p) v -> p j v", p=128), in_=Sf),
                          r=['Sf'], dma='gstp')

        def s0_load(ck, q_):
            sb_ = q_ % 3
            h_ = ck.h
            S.add('sp', lambda h: h.dma_start(out=s0[sb_], in_=st_in[q_, h_].rearrange("(j p) v -> p j v", p=128)),
                  w=['s0_%d' % sb_], dma='s0_%d' % sb_)

        def unit(ck, q_):
            if DBG_NOUNIT:
                return
            W, R, h_ = ck.W, ck.R, ck.h
            vv, VR, bo = ck.v, ck.VR, ck.bo
            sb_ = q_ % 3
            qb_ = q_ % 2
            if q_ + 1 < 16:
                s0_load(ck, q_ + 1)
            S.add('pool', lambda h: h.memset(Qm[qb_], 0.0), w=['Qm%d' % qb_])
            S.add('pool', lambda h: h.tensor_copy(out=Qm[qb_][:, :, 8 * q_:8 * q_ + 8], in_=W.qdT[:, :, 8 * q_:8 * q_ + 8]),
                  r=[R + 'qdT'], w=['Qm%d' % qb_])
            for j in range(2):
                S.add('pe', lambda h, j=j: h.matmul(
                    ps[bo][:, :], lhsT=Qm[qb_][:, j, :], rhs=s0[sb_][:, j, :], start=False,
                    stop=(q_ == 15 and j == 1)), r=['Qm%d' % qb_, 's0_%d' % sb_], w=[PS(bo)])
            S.add('dve', lambda h: h.tensor_scalar(
                out=kteM[qb_], in0=W.kte, scalar1=cm[:, 768 + q_:769 + q_], scalar2=None, op0=ALU.mult),
                r=[R + 'kte', 'cm'], w=['kteM%d' % qb_])
            for j in range(2):
                bu = bank()
                S.add('pe', lambda h, j=j, bu=bu: h.matmul(
                    ps[bu][:, :], lhsT=kteM[qb_][:, j * 128:(j + 1) * 128], rhs=vv, start=True, stop=True),
                    r=['kteM%d' % qb_, VR], w=[PS(bu)])
                S.add('dve', lambda h, j=j, bu=bu: h.scalar_tensor_tensor(
                    out=s0[sb_][:, j, :], in0=s0[sb_][:, j, :], scalar=W.epos[:, j, 8 * q_ + 7:8 * q_ + 8],
                    op0=ALU.mult, in1=ps[bu][:, :], op1=ALU.add),
                    r=[PS(bu), R + 'epos', 's0_%d' % sb_], w=['s0_%d' % sb_])
            S.add('sp', lambda h: h.dma_start(out=gst_s[q_, h_].rearrange("(j p) v -> p j v", p=128), in_=s0[sb_]),
                  r=['s0_%d' % sb_], dma='s0_%d' % sb_)

        def B4(ck):
            W, R, c, h_, bo = ck.W, ck.R, ck.c, ck.h, ck.bo
            S.add('act', lambda h: h.activation(out=W.junk, in_=ps[bo][:, :], func=AF.Square, accum_out=W.sm[:, 0:1]),
                  r=[PS(bo)], w=['gjunk', R + 'sm'])
            S.add('act', lambda h: h.activation(out=W.sm[:, 1:2], in_=W.sm[:, 0:1], func=AF.Ln, bias=eps_t[:, 1:2], scale=1.0 / 512),
                  r=[R + 'sm', 'eps'], w=[R + 'sm'])
            S.add('act', lambda h: h.activation(out=W.sm[:, 2:3], in_=W.sm[:, 1:2], func=AF.Exp, scale=-0.5),
                  r=[R + 'sm'], w=[R + 'sm'])
            S.add('dve', lambda h: h.scalar_tensor_tensor(out=gated[:, c, h_ * 512:(h_ + 1) * 512], in0=ps[bo][:, :],
                                                          scalar=W.sm[:, 2:3], op0=ALU.mult, in1=W.srg, op1=ALU.mult),
                  r=[PS(bo), R + 'sm', R + 'srg'], w=['gated%d' % c])
            pinned.discard(bo)

        def issue_head(h_):
            sqk = take_w()
            S.add('pool', lambda h: h.dma_start(out=wbuf[sqk][:, :, 0:256], in_=winv[:, :, h_ * 256:(h_ + 1) * 256]),
                  w=['w%d' % sqk], dma='w%d' % sqk)
            S.add('pool', lambda h: h.dma_start(out=wbuf[sqk][:, :, 256:512],
                                                in_=winv[:, :, 1024 + h_ * 256:1024 + (h_ + 1) * 256]),
                  r=['w%d' % sqk], w=['w%d' % sqk], dma='w%d' % sqk)
            sv = take_w()
            load_w(sv, winv[:, :, 2048 + h_ * 512:2048 + (h_ + 1) * 512])
            sr = take_w()
            load_w(sr, winv[:, :, 4096 + h_ * 512:4096 + (h_ + 1) * 512])
            return (sqk, sv, sr)

        for h_ in range(4):
            slots = issue_head(h_)
            S.add('sp', lambda h, h_=h_: h.dma_start(out=gng, in_=gla_ng[:, h_ * 512:(h_ + 1) * 512].partition_broadcast(128)),
                  w=['gng'], dma='c_gng')
            S.add('sp', lambda h, h_=h_: h.dma_start(out=Sf, in_=sscr[h_].rearrange("(j p) v -> p j v", p=128)),
                  r=['sscr%d' % h_], w=['Sf'], dma='sfl')
            S.add('act', lambda h: h.copy(out=Sb.rearrange("p a b -> p (a b)"), in_=Sf.rearrange("p a b -> p (a b)")),
                  r=['Sf'], w=['Sb'])
            cks = {c: mk(h_, c, slots) for c in range(NT)}
            ck8 = cks[NT - 1]
            s0_load(ck8, 0)
            P_qk(ck8)
            P_v(ck8)
            P_r(ck8)
            G01(ck8)
            G23(ck8)
            G45(ck8)
            B12(ck8)
            B3(ck8)
            NPR = NT - 1
            for c in range(-2, NPR):
                p = cks.get(c + 2) if c + 2 < NPR else None
                g = cks.get(c + 1) if c + 1 < NPR else None
                b_ = cks.get(c) if c >= 0 else None
                if p:
                    P_qk(p)
                if b_:
                    B12(b_)
                if g:
                    G23(g)
                if p:
                    P_v(p)
                if b_:
                    B3(b_)
                    B4(b_)
                    unit(ck8, 2 * c)
                if g:
                    G45(g)
                if p:
                    P_r(p)
                    G01(p)
                if b_:
                    unit(ck8, 2 * c + 1)
            B4(ck8)
            for s_ in slots:
                give_w(s_)
        for tc in range(NT):
            transpose_bf16_chunk(gated[:, tc, :], 'gated%d' % tc, xT, XT(tc), tc)
        S.retire_prefix('g0', 'g1', 'g2', 'gsg', 'gt1', 'gjunk', 'gv', 'gated', 'gTm', 'gng', 'Sf', 'Sb', 's0_', 'Qm', 'kteM', 'wg')
        A.release(gated, gTm, gng, Sf, Sb, *s0, *Qm, *kteM, wg, wg16, *v3, sgS, t1S, junkS)
        for w_ in wsets:
            A.release(*w_.all)

    def sg_layer():
        winv = sg_w_in.rearrange("(k p) f -> p k f", p=128)
        uT = A.alloc([KD, T], BF16)
        binT = A.alloc([16], F32)
        S.add('sp', lambda h: h.dma_start(out=binT, in_=sg_binT), w=['binT'], dma='c_binT')
        wtmp = A.alloc([8, 128], F32)
        WT = [A.alloc([8, 128], BF16) for _ in range(2)]
        for v_ in range(2):
            src = sg_wsT if v_ == 0 else sg_wsTs
            S.add('sp', lambda h, src=src: h.dma_start(out=wtmp, in_=src), w=['wtmp'], dma='c_wtmp')
            m01 = CM(C_TRI01) if v_ == 0 else CM(C_BD01)
            for g in range(8):
                S.add('dve', lambda h, g=g, v_=v_, m01=m01: h.tensor_tensor(out=WT[v_][:, g, :], in0=wtmp[:, g, :], in1=m01, op=ALU.mult),
                      r=['wtmp', 'cm'], w=['WT%d' % v_])
        S.retire(['wtmp'])
        A.release(wtmp)
        bsp1 = A.alloc([8, 128], F32)
        bsp = [bsp1, bsp1]

        def load_bsp(v_):
            S.add('sp', lambda h: h.dma_start(out=bsp1, in_=sg_bsp[v_].partition_broadcast(128)),
                  w=['bsp'], dma='c_bsp')

        load_bsp(0)
        bv = A.alloc([D], F32)
        vg = A.alloc([D], F32)
        vb = A.alloc([D], F32)
        S.add('sp', lambda h: h.dma_start(out=bv, in_=sg_bv.partition_broadcast(128)), w=['sgbv'], dma='c_sgbv')
        S.add('sp', lambda h: h.dma_start(out=vg, in_=sg_vg.partition_broadcast(128)), w=['sgvg'], dma='c_sgvg')
        S.add('sp', lambda h: h.dma_start(out=vb, in_=sg_vb.partition_broadcast(128)), w=['sgvb'], dma='c_sgvb')

        slots = {}

        def issue(cb, col0):
            s_ = take_w()
            load_w(s_, winv[:, :, col0 + cb * 512:col0 + (cb + 1) * 512])
            slots[cb] = s_

        nxt = 0
        while nxt < 4 and free_w:
            issue(nxt, 0)
            nxt += 1
        for cb in range(4):
            if cb not in slots:
                issue(cb, 0)
                nxt = cb + 1
            s_ = slots[cb]
            for fc in range(4):
                bs = [bank() for _ in range(3)]
                for k in range(KD):
                    for tt in range(3):
                        S.add('pe', lambda h, k=k, tt=tt, fc=fc, b=bs[tt], s_=s_: h.matmul(
                            ps[b][:, 0:TT], lhsT=wbuf[s_][:, k, fc * 128:(fc + 1) * 128],
                            rhs=xT[:, k, tt * TT:(tt + 1) * TT], start=(k == 0), stop=(k == KD - 1)),
                            r=['w%d' % s_] + XT_ALL[3 * tt:3 * tt + 3], w=[PS(bs[tt])])
                f_ = cb * 4 + fc
                for tt in range(3):
                    S.add('act', lambda h, tt=tt, b=bs[tt], f_=f_: h.activation(
                        out=uT[:, f_, tt * TT:(tt + 1) * TT], in_=ps[b][:, 0:TT], func=AF.Gelu,
                        bias=binT[:, f_:f_ + 1], scale=1.0), r=[PS(bs[tt]), 'binT'], w=['uT'])
            give_w(s_)
        vs = []
        for cb in range(4):
            s_ = take_w()
            load_w(s_, winv[:, :, 2048 + cb * 512:2048 + (cb + 1) * 512])
            vs.append(s_)
        vt = [A.alloc([D], F32) for _ in range(2)]
        vnb = [A.alloc([D], BF16) for _ in range(2)]
        tmp = [A.alloc([512], F32) for _ in range(2)]
        mt = [A.alloc([4, 128], F32) for _ in range(2)]
        st = [A.alloc([4, 6], F32) for _ in range(2)]
        sm = [A.alloc([8], F32) for _ in range(2)]

        def p_stage(tc):
            i = tc % 2
            VT = 'sv%dvt' % i
            bs = [bank() for _ in range(4)]
            for k in range(KD):
                for cb in range(4):
                    S.add('pe', lambda h, k=k, cb=cb, b=bs[cb], tc=tc: h.matmul(
                        ps[b][:, :], lhsT=xT[:, k, tc * 128:(tc + 1) * 128], rhs=wbuf[vs[cb]][:, k, :],
                        start=(k == 0), stop=(k == KD - 1)), r=[XT(tc), 'w%d' % vs[cb]], w=[PS(bs[cb])])
            for cb in range(4):
                sl = slice(cb * 512, (cb + 1) * 512)
                S.add('dve', lambda h, b=bs[cb], sl=sl, i=i: h.tensor_tensor(out=vt[i][:, sl], in0=ps[b][:, :], in1=bv[:, sl], op=ALU.add),
                      r=[PS(bs[cb]), 'sgbv'], w=[VT + str(cb)])
                S.add('act', lambda h, sl=sl, i=i: h.activation(out=vt[i][:, sl], in_=vt[i][:, sl], func=AF.Gelu),
                      r=[VT + str(cb)], w=[VT + str(cb)])

        def l_stage(tc):
            i = tc % 2
            sample = (tc == NT - 1)
            V = 'sv%d' % i
            VT = 'sv%dvt' % i
            ln_stats(vt[i], [VT + str(c_) for c_ in range(4)], st[i], sm[i], V, LN_EPS)
            for c in range(4):
                j = (tc * 4 + c) % 2
                sl = slice(c * 512, (c + 1) * 512)
                S.add('act', lambda h, i=i, sl=sl, j=j: h.activation(out=tmp[j], in_=vt[i][:, sl], func=AF.Identity,
                                                                  scale=sm[i][:, 3:4], bias=sm[i][:, 4:5]),
                      r=[VT + str(c), V + 'sm'], w=['svtmp%d' % j])
                S.add('pool', lambda h, j=j, sl=sl: h.tensor_tensor(out=tmp[j], in0=tmp[j], in1=vg[:, sl], op=ALU.mult),
                      r=['svtmp%d' % j, 'sgvg'], w=['svtmp%d' % j])
                if sample:
                    S.add('dve', lambda h, j=j, sl=sl, i=i: h.tensor_tensor(out=vt[i][:, sl], in0=tmp[j], in1=vb[:, sl], op=ALU.add),
                          r=['svtmp%d' % j, 'sgvb', VT + str(c)], w=[VT + str(c)])
                    S.add('act', lambda h, sl=sl, i=i: h.copy(out=vnb[i][:, sl], in_=vt[i][:, sl]), r=[VT + str(c)], w=[V + 'vnb' + str(c)])
                else:
                    S.add('dve', lambda h, j=j, sl=sl, i=i: h.tensor_tensor(out=vnb[i][:, sl], in0=tmp[j], in1=vb[:, sl], op=ALU.add),
                          r=['svtmp%d' % j, 'sgvb'], w=[V + 'vnb' + str(c)])
            if sample:
                S.add('sp', lambda h, i=i: h.dma_start(out=sgv, in_=vt[i]), r=[VT + str(c_) for c_ in range(4)], dma='sgvout')

        def m_stage(tc):
            i = tc % 2
            sample = (tc == NT - 1)
            V = 'sv%d' % i
            v_ = 1 if sample else 0
            for dg in range(4):
                b = bank()
                for q_ in range(4):
                    dc = dg * 4 + q_
                    S.add('pe', lambda h, q_=q_, dc=dc, b=b, i=i, v_=v_: h.matmul(
                        ps[b][:, q_ * 128:(q_ + 1) * 128], lhsT=vnb[i][:, dc * 128:(dc + 1) * 128],
                        rhs=WT[v_][:, dc // 2, :], start=True, stop=True), r=[V + 'vnb' + str(dg), 'WT%d' % v_], w=[PS(b)])
                mi = dg % 2
                bias_ap = bsp[v_][:, 2 * dg:2 * dg + 2, :].unsqueeze(2).broadcast_to([128, 2, 2, 128])
                S.add('dve', lambda h, b=b, mi=mi, bias_ap=bias_ap: h.tensor_tensor(
                    out=mt[mi].rearrange("p (a c) t -> p a c t", a=2), in0=ps[b][:, :].rearrange("p (a c t) -> p a c t", a=2, c=2),
                    in1=bias_ap, op=ALU.add), r=[PS(b), 'bsp'], w=['mt%d' % mi])
                S.add('pool', lambda h, mi=mi, dg=dg, tc=tc: h.tensor_tensor(
                    out=xT[:, 4 * dg:4 * dg + 4, tc * 128:(tc + 1) * 128], in0=mt[mi],
                    in1=uT[:, 4 * dg:4 * dg + 4, tc * 128:(tc + 1) * 128], op=ALU.mult),
                    r=['mt%d' % mi, 'uT', XT(tc)], w=[XT(tc)])

        p_stage(0)
        p_stage(1)
        l_stage(0)
        for tc in range(NT):
            if tc + 2 < NT:
                p_stage(tc + 2)
            if tc + 1 < NT:
                l_stage(tc + 1)
            if tc == NT - 1:
                load_bsp(1)
            m_stage(tc)
        for s_ in vs:
            give_w(s_)
        S.retire_prefix('uT', 'binT', 'wtmp', 'WT', 'bsp', 'sgbv', 'sgvg', 'sgvb', 'sv', 'mt')
        A.release(uT, binT, *WT, bsp1, bv, vg, vb, *vt, *vnb, *tmp, *mt, *st, *sm)

    load_xT(xm, NT, xT, XT)
    last = [p for p in ("gla", "mlp0", "sg", "mlp1") if p in phases][-1]
    if "gla" in phases:
        gla_layer()
    alloc_xres(xm)
    if "gla" in phases:
        out_proj(gla_w_out)
        layer_norm(0, ln1g[0:1, :], ln1b[0:1, :], final=(last == "gla"))
    if "mlp0" in phases:
        mlp(0)
        layer_norm(1, ln2g[0:1, :], ln2b[0:1, :], final=(last == "mlp0"))
    if "sg" in phases:
        xres = xres_box[0]
        for tc in range(NT):
            S.add('sp', lambda h, tc=tc, xres=xres: h.dma_start(out=xspill[tc * 128:(tc + 1) * 128, :], in_=xres[:, tc, :]),
                  r=XRA(tc), w=['xspill%d' % tc], dma='xsp%d' % (tc % 3))
        free_xres()
        sg_layer()
        xres = A.alloc([NT, D], F32)
        xres_box[0] = xres
        for tc in range(NT):
            S.add('sp', lambda h, tc=tc, xres=xres: h.dma_start(out=xres[:, tc, :], in_=xspill[tc * 128:(tc + 1) * 128, :]),
                  r=['xspill%d' % tc], w=XRA(tc), dma='xres%d' % (tc % 3))
        out_proj(sg_w_out)
        layer_norm(2, ln1g[1:2, :], ln1b[1:2, :], final=(last == "sg"))
    if "mlp1" in phases:
        mlp(1)
        layer_norm(3, ln2g[1:2, :], ln2b[1:2, :], final=True)

    S.emit(nc, es)
    es.close()
    return nc, S, A


def prep_shared(inp):
    f = lambda a: np.ascontiguousarray(np.asarray(a, dtype=np.float32))
    wg_aug = np.zeros((32, 1024), np.float32)
    wg_aug[0:16] = inp["gla_w_gate"][0]
    wg_aug[16] = inp["gla_b_gate"][0]
    ws = np.asarray(inp["sg_w_spatial"][0])
    wsT = ws.transpose(2, 0, 1)
    wsTs = np.tile(ws[:, :8, :8].transpose(2, 0, 1), (16, 1, 16))
    bsp = np.asarray(inp["sg_b_spatial"][0])
    bsp2 = np.stack([bsp, np.tile(bsp[:, :8], (1, 16))])
    b_in = np.asarray(inp["sg_b_in"][0])
    d = dict(
        gla_w_in=f(inp["gla_w_in"][0]), wg_aug=wg_aug, gla_ng=f(np.asarray(inp["gla_norm_g"][0]).reshape(1, D)),
        gla_w_out=f(inp["gla_w_out"][0]), sg_w_in=f(inp["sg_w_in"][0]),
        sg_binT=f(b_in[:D].reshape(16, 128).T), sg_bv=f(b_in[D:].reshape(1, D)),
        sg_vg=f(np.asarray(inp["sg_v_norm_g"][0]).reshape(1, D)), sg_vb=f(np.asarray(inp["sg_v_norm_b"][0]).reshape(1, D)),
        sg_wsT=f(wsT), sg_wsTs=f(wsTs), sg_bsp=f(bsp2), sg_w_out=f(inp["sg_w_out"][0]),
        mlp_w1=f(inp["mlp_w1"]), mlp_w2=f(inp["mlp_w2"]),
        ln1_g=f(inp["ln1_g"]), ln1_b=f(inp["ln1_b"]), ln2_g=f(inp["ln2_g"]), ln2_b=f(inp["ln2_b"]),
        ident=np.eye(128, dtype=np.float32), cmask=make_consts(),
    )
    lnT = np.zeros((4, 128, 2, 16), np.float32)
    for n_, (gk, bk, li) in enumerate([("ln1_g", "ln1_b", 0), ("ln2_g", "ln2_b", 0), ("ln1_g", "ln1_b", 1), ("ln2_g", "ln2_b", 1)]):
        lnT[n_, :, 0, :] = np.asarray(inp[gk][li]).reshape(16, 128).T
        lnT[n_, :, 1, :] = np.asarray(inp[bk][li]).reshape(16, 128).T
    d["lnT"] = lnT
    return d


def prep_core(inp, c):
    xpr = np.asarray(inp["x_prompt"])
    xs = np.asarray(inp["x_sample"])
    b, hf = c // 2, c % 2
    xm = np.concatenate([xpr[b, hf * 1024:(hf + 1) * 1024], xs[16 * c:16 * (c + 1)].reshape(128, D)], axis=0)
    if hf == 1:
        xp = xpr[b, 0:1024]
    else:
        xp = np.zeros((1024, D), np.float32)
    st = np.asarray(inp["state_gla"])[0, 16 * c:16 * (c + 1)]
    return dict(xm=np.ascontiguousarray(xm, dtype=np.float32), xp=np.ascontiguousarray(xp, dtype=np.float32),
                st=np.ascontiguousarray(st, dtype=np.float32))


_CACHE = {}


def kernel(**inputs):
    if "nc" not in _CACHE:
        _CACHE["nc"] = build()[0]
    nc = _CACHE["nc"]
    shared = prep_shared(inputs)
    in_maps = []
    for c in range(8):
        m = dict(shared)
        m.update(prep_core(inputs, c))
        in_maps.append(m)
    res = run_bass_kernel_spmd(nc, in_maps, core_ids=list(range(8)))
    R = res.results
    y_prompt = np.zeros((4, 2048, D), np.float32)
    y_sample = np.zeros((128, 8, D), np.float32)
    gp = np.zeros((1, 4, 4, 256, 512), np.float32)
    gs = np.zeros((1, 128, 4, 256, 512), np.float32)
    sgv = np.zeros((1, 128, 8, D), np.float32)
    for c in range(8):
        b, hf = c // 2, c % 2
        yc = R[c]["y"]
        y_prompt[b, hf * 1024:(hf + 1) * 1024] = yc[:1024]
        y_sample[16 * c:16 * (c + 1)] = yc[1024:].reshape(16, 8, D)
        if hf == 1:
            gp[0, b] = R[c]["gst_p"]
        gs[0, 16 * c:16 * (c + 1)] = R[c]["gst_s"]
        sgv[0, 16 * c:16 * (c + 1)] = R[c]["sgv"].reshape(16, 8, D)
    return (y_prompt, y_sample, gp, gs, sgv)
```

```python
import numpy as np
from contextlib import ExitStack
import concourse.bass as bass
import concourse.mybir as mybir
from concourse.bass_utils import run_bass_kernel_spmd

F32 = mybir.dt.float32
BF16 = mybir.dt.bfloat16
AF = mybir.ActivationFunctionType
ALU = mybir.AluOpType

D = 2048
KD = 16
NT = 9
NPF = 8
T = NT * 128
TT = 384
DFF = 8192
ALPHA = float((2.0 * 2) ** 0.25)
LN_EPS = 1e-5
HN_EPS = 1e-6
import os as _os2
DBG_NOUNIT = bool(_os2.environ.get('DBG_NOUNIT'))
import os as _os
NO_SAME_ENG_SYNC = bool(_os.environ.get('NO_SAME_ENG_SYNC'))


class Sched:
    def __init__(self):
        self.ops = []
        self.res = {}
        self.ghost = set()

    def add(self, eng, fn, r=(), w=(), dma=None):
        i = len(self.ops)
        deps = set()
        for name in r:
            st = self.res.get(name)
            if st is None:
                st = self.res[name] = [None, list(self.ghost)]
            if st[0] is not None:
                deps.add(st[0])
            if name.startswith('ps'):
                deps.update(d for d in st[1] if self.ops[d]['eng'] != eng)
        for name in w:
            st = self.res.get(name)
            if st is None:
                st = self.res[name] = [None, list(self.ghost)]
            if st[0] is not None:
                deps.add(st[0])
            deps.update(st[1])
        for name in r:
            self.res[name][1].append(i)
        for name in w:
            self.res[name] = [i, []]
        deps.discard(i)
        self.ops.append(dict(i=i, eng=eng, fn=fn, deps=deps, dma=dma, signal=False))
        return i

    def retire(self, names):
        for n in names:
            st = self.res.pop(n, None)
            if st is None:
                continue
            if st[0] is not None:
                self.ghost.add(st[0])
            self.ghost.update(st[1])
        best = {}
        for d in self.ghost:
            p = self.ops[d]
            key = ('d', p['dma']) if p['dma'] else ('c', p['eng'])
            if key not in best or best[key] < d:
                best[key] = d
        self.ghost = set(best.values())

    def retire_prefix(self, *prefixes):
        self.retire([n for n in list(self.res) if any(n.startswith(p) for p in prefixes)])

    def emit(self, nc, es, final_eng='sp'):
        ops = self.ops
        last_dma = {}
        for op in ops:
            if op['dma']:
                last_dma[op['dma']] = op['i']
        fin = dict(i=len(ops), eng=final_eng, fn=None, deps=set(last_dma.values()), dma=None, signal=False)
        ops.append(fin)
        for op in ops:
            best = {}
            for d in op['deps']:
                p = ops[d]
                key = ('d', p['dma']) if p['dma'] else ('c', p['eng'])
                if key not in best or best[key] < d:
                    best[key] = d
            rd = []
            for key, d in best.items():
                p = ops[d]
                if p['dma'] is None and p['eng'] == 'pe' and op['eng'] == 'pe' and op['dma'] is None:
                    continue
                if NO_SAME_ENG_SYNC and p['dma'] is None and op['dma'] is None and p['eng'] == op['eng']:
                    continue
                if p['dma'] is None:
                    p['signal'] = True
                rd.append(d)
            op['rdeps'] = rd
        cnt = {}
        dcnt = {}
        for op in ops:
            if op['dma']:
                dcnt[op['dma']] = dcnt.get(op['dma'], 0) + 16
                op['sval'] = dcnt[op['dma']]
            elif op['signal']:
                cnt[op['eng']] = cnt.get(op['eng'], 0) + 1
                op['sval'] = cnt[op['eng']]
        engs = ['pe', 'act', 'dve', 'pool', 'sp']
        sems = {e: es.enter_context(nc.semaphore("s_" + e)) for e in engs}
        dsems = {k: es.enter_context(nc.semaphore("d_%d" % n)) for n, k in enumerate(sorted(dcnt))}
        self.nsem = len(sems) + len(dsems)
        self.maxcnt = dict(cnt)
        block = es.enter_context(nc.Block())
        per = {e: [op for op in ops if op['eng'] == e] for e in engs}

        def run(e, h):
            waited = {}
            for op in per[e]:
                for d in op['rdeps']:
                    p = ops[d]
                    if p['dma']:
                        s, v, k = dsems[p['dma']], p['sval'], ('d', p['dma'])
                    else:
                        s, v, k = sems[p['eng']], p['sval'], ('c', p['eng'])
                    if waited.get(k, 0) >= v:
                        continue
                    waited[k] = v
                    h.wait_ge(s, v)
                if op['fn'] is None:
                    continue
                ins = op['fn'](h)
                if op['dma']:
                    ins.then_inc(dsems[op['dma']], 16)
                elif op['signal']:
                    ins.then_inc(sems[e], 1)

        @block.tensor
        def _(h):
            run('pe', h)

        @block.scalar
        def _(h):
            run('act', h)

        @block.vector
        def _(h):
            run('dve', h)

        @block.gpsimd
        def _(h):
            run('pool', h)

        @block.sync
        def _(h):
            run('sp', h)


class Arena:
    def __init__(self, ap_f32):
        self.ap = ap_f32
        self.n = ap_f32.shape[1]
        self.free = [(0, self.n)]
        self.live = {}
        self.peak = 0

    def alloc(self, shape, dtype, name=None):
        n = int(np.prod(shape))
        words = n if dtype == F32 else (n + 1) // 2
        words = (words + 1) // 2 * 2
        for idx, (o, sz) in enumerate(self.free):
            if sz >= words:
                break
        else:
            raise AssertionError(("arena overflow", name, words, self.free))
        if sz == words:
            self.free.pop(idx)
        else:
            self.free[idx] = (o + words, sz - words)
        self.peak = max(self.peak, o + words)
        v = self.ap[:, o:o + words]
        if dtype != F32:
            v = v.bitcast(dtype)
        v = v[:, 0:n]
        if len(shape) == 2:
            v = v.rearrange("p (a b) -> p a b", a=shape[0])
        elif len(shape) == 3:
            v = v.rearrange("p (a b c) -> p a b c", a=shape[0], b=shape[1])
        self.live[id(v)] = (o, words, v)
        return v

    def release(self, *views):
        for v in views:
            o, words, _ = self.live.pop(id(v))
            self.free.append((o, words))
        self.free.sort()
        merged = []
        for o, sz in self.free:
            if merged and merged[-1][0] + merged[-1][1] == o:
                merged[-1] = (merged[-1][0], merged[-1][1] + sz)
            else:
                merged.append((o, sz))
        self.free = merged


ARENA_WORDS = 53000

C_TRI_S, C_TRIU_S, C_BD_S, C_BDU_S, C_TRI01, C_BD01 = range(6)


def make_consts():
    p = np.arange(128)
    s, t = p[:, None], p[None, :]
    same = (s // 8) == (t // 8)
    tri = (s <= t)
    triu = (s > t)
    m = np.zeros((128, 6 * 128 + 16 + 2), np.float32)
    m[:, 0:128] = tri * (-1.0 / 16)
    m[:, 128:256] = triu * (-1.0 / 16)
    m[:, 256:384] = (tri & same) * (-1.0 / 16)
    m[:, 384:512] = (triu & same) * (-1.0 / 16)
    m[:, 512:640] = tri
    m[:, 640:768] = tri & same
    m[:, 768:784] = (p[:, None] // 8) == np.arange(16)[None, :]
    m[:, 784] = -1.0 / 16
    m[:, 785] = 1.0
    return m


def build(phases=("gla", "mlp0", "sg", "mlp1"), dbg=False):
    nc = bass.Bass("TRN2", target_bir_lowering=False)

    def din(name, shape):
        return nc.dram_tensor(name, list(shape), F32, kind="ExternalInput").ap()

    def dout(name, shape):
        return nc.dram_tensor(name, list(shape), F32, kind="ExternalOutput").ap()

    xm = din("xm", [T, D])
    xp = din("xp", [NPF * 128, D])
    st_in = din("st", [16, 4, 256, 512])
    gla_w_in = din("gla_w_in", [D, 6160])
    wg_aug = din("wg_aug", [32, 1024])
    gla_ng = din("gla_ng", [1, D])
    gla_w_out = din("gla_w_out", [D, D])
    sg_w_in = din("sg_w_in", [D, 2 * D])
    sg_binT = din("sg_binT", [128, 16])
    sg_bv = din("sg_bv", [1, D])
    sg_vg = din("sg_vg", [1, D])
    sg_vb = din("sg_vb", [1, D])
    sg_wsT = din("sg_wsT", [128, 8, 128])
    sg_wsTs = din("sg_wsTs", [128, 8, 128])
    sg_bsp = din("sg_bsp", [2, 8, 128])
    sg_w_out = din("sg_w_out", [D, D])
    w1 = din("mlp_w1", [2, D, DFF])
    w2 = din("mlp_w2", [2, DFF, D])
    ln1g = din("ln1_g", [2, D])
    ln1b = din("ln1_b", [2, D])
    ln2g = din("ln2_g", [2, D])
    ln2b = din("ln2_b", [2, D])
    ident_d = din("ident", [128, 128])
    lnT_d = din("lnT", [4, 128, 2, 16])
    cm_d = din("cmask", [128, 786])
    y = dout("y", [T, D])
    gst_p = dout("gst_p", [4, 256, 512])
    gst_s = dout("gst_s", [16, 4, 256, 512])
    sgv = dout("sgv", [128, D])
    xspill = nc.dram_tensor("xspill", [T, D], F32, kind="Internal").ap()
    sscr = nc.dram_tensor("sscr", [4, 256, 512], F32, kind="Internal").ap()

    S = Sched()
    es = ExitStack()
    arena_t = es.enter_context(nc.sbuf_tensor("arena", [128, ARENA_WORDS], F32))
    A = Arena(arena_t[:])
    ps = [es.enter_context(nc.psum_tensor("ps%d" % i, [128, 512], F32)) for i in range(8)]
    psb = [p_[:].bitcast(BF16) for p_ in ps]
    bank_ctr = [0]

    pinned = set()

    def bank():
        while True:
            b = bank_ctr[0] % 8
            bank_ctr[0] += 1
            if b not in pinned:
                return b

    def PS(b):
        return 'ps%d' % b

    ident = A.alloc([128], F32)
    identb = A.alloc([128], BF16)
    NWS = 4
    wbuf = [A.alloc([16, 512], BF16) for _ in range(NWS)]
    xT = A.alloc([KD, T], BF16)
    free_w = list(range(NWS))

    def take_w():
        return free_w.pop(0)

    def give_w(s_):
        free_w.append(s_)

    def XT(tc, k):
        return 'xT%d_%d' % (tc, k)

    def XTG(tc, g):
        return [XT(tc, 4 * g + i_) for i_ in range(4)]

    def XTT(tt, k):
        return [XT(3 * tt + i_, k) for i_ in range(3)]

    S.add('sp', lambda h: h.dma_start(out=ident, in_=ident_d), w=['ident'], dma='c_ident')
    S.add('pool', lambda h: h.dma_start(out=identb, in_=ident_d), w=['identb'], dma='c_identb')

    def load_w(slot, src_ap, dst=None):
        if dst is None:
            a, b = src_ap.shape[1], src_ap.shape[2]
            dst = wbuf[slot]
            if (a, b) != (16, 512):
                dst = dst.rearrange("p a b -> p (a b)").rearrange("p (a b) -> p a b", a=a)
        S.add('pool', lambda h: h.dma_start(out=dst, in_=src_ap), w=['w%d' % slot], dma='w%d' % slot)
        return dst

    cp_ctr = [0]

    def evac_copy(out, in_, r, w):
        cp_ctr[0] += 1
        if cp_ctr[0] % 2:
            S.add('act', lambda h: h.copy(out=out, in_=in_), r=r, w=w)
        else:
            S.add('dve', lambda h: h.tensor_copy(out=out, in_=in_), r=r, w=w)

    def transpose_f32_chunk(src, src_res, dstT, dst_res, tc):
        for g in range(4):
            b = bank()
            for i in range(4):
                kc = 4 * g + i
                S.add('pe', lambda h, kc=kc, i=i, b=b: h.transpose(
                    out=ps[b][:, i * 128:(i + 1) * 128], in_=src[:, kc * 128:(kc + 1) * 128], identity=ident),
                    r=[src_res[g] if isinstance(src_res, list) else src_res, 'ident'], w=[PS(b)])
            evac_copy(dstT[:, 4 * g:4 * g + 4, tc * 128:(tc + 1) * 128],
                      ps[b][:].rearrange("p (a t) -> p a t", a=4), r=[PS(b)], w=dst_res(tc, g))

    def transpose_bf16_chunk(src, src_res, dstT, dst_res, tc):
        for g in range(4):
            b = bank()
            for i in range(4):
                kc = 4 * g + i
                S.add('pe', lambda h, kc=kc, i=i, b=b: h.transpose(
                    out=psb[b][:, i * 128:(i + 1) * 128], in_=src[:, kc * 128:(kc + 1) * 128], identity=identb),
                    r=[src_res, 'identb'], w=[PS(b)])
            evac_copy(dstT[:, 4 * g:4 * g + 4, tc * 128:(tc + 1) * 128],
                      psb[b][:, 0:512].rearrange("p (a t) -> p a t", a=4), r=[PS(b)], w=dst_res(tc, g))

    def load_xT(src_dram, nchunks, dstT, dst_res_fn):
        stg = [A.alloc([D], F32) for _ in range(2)]
        for tc in range(nchunks):
            i = tc % 2
            S.add('sp', lambda h, tc=tc, i=i: h.dma_start(out=stg[i], in_=src_dram[tc * 128:(tc + 1) * 128, :]),
                  w=['xstg%d' % i], dma='xstg%d' % i)
            transpose_f32_chunk(stg[i], 'xstg%d' % i, dstT, dst_res_fn, tc)
        S.retire_prefix('xstg')
        A.release(*stg)

    xres_box = [None]

    def XR(tc, c):
        return 'xres%d_%d' % (tc, c)

    def XRA(tc):
        return ['xres%d_%d' % (tc, c) for c in range(4)]

    def alloc_xres(src_dram):
        xres = A.alloc([NT, D], F32)
        xres_box[0] = xres
        for tc in range(NT):
            S.add('sp', lambda h, tc=tc: h.dma_start(out=xres[:, tc, :], in_=src_dram[tc * 128:(tc + 1) * 128, :]),
                  w=XRA(tc), dma='xres%d' % tc)

    def free_xres():
        S.retire([n for tc in range(NT) for n in XRA(tc)])
        A.release(xres_box[0])
        xres_box[0] = None

    def out_proj(w_dram):
        xres = xres_box[0]
        wv = w_dram.rearrange("(k p) d -> p k d", p=128)
        slots = {}

        def issue(cb):
            s_ = take_w()
            load_w(s_, wv[:, :, cb * 512:(cb + 1) * 512])
            slots[cb] = s_

        nxt = 0
        while nxt < 4 and free_w:
            issue(nxt)
            nxt += 1
        for cb in range(4):
            if cb not in slots:
                issue(cb)
                nxt = cb + 1
            s_ = slots[cb]
            for tc in range(NT):
                b = bank()
                for k in range(KD):
                    S.add('pe', lambda h, k=k, tc=tc, b=b, s_=s_: h.matmul(
                        ps[b][:, :], lhsT=xT[:, k, tc * 128:(tc + 1) * 128], rhs=wbuf[s_][:, k, :],
                        start=(k == 0), stop=(k == KD - 1)),
                        r=[XT(tc, k), 'w%d' % s_], w=[PS(b)])
                dst = xres[:, tc, cb * 512:(cb + 1) * 512]
                S.add('dve', lambda h, dst=dst, b=b: h.scalar_tensor_tensor(
                    out=dst, in0=dst, scalar=ALPHA, op0=ALU.mult, in1=ps[b][:, :], op1=ALU.add),
                    r=[PS(b), XR(tc, cb)], w=[XR(tc, cb)])
            give_w(s_)
            if nxt < 4 and free_w:
                issue(nxt)
                nxt += 1

    def mlp(layer):
        xres = xres_box[0]
        hT = [A.alloc([4, T], BF16) for _ in range(2)]
        rtmp = [A.alloc([TT], BF16) for _ in range(2)]
        NFB = DFF // 512
        w1v = w1[layer].rearrange("(k p) f -> p k f", p=128)
        w2v = w2[layer].rearrange("(fb c p) d -> fb p c d", p=128, c=4)
        slots = {}

        def issue1(fb):
            s1 = take_w()
            load_w(s1, w1v[:, :, fb * 512:(fb + 1) * 512])
            slots[fb] = [s1, None, None]

        def issue2(fb):
            s2 = take_w()
            d2 = load_w(s2, w2v[fb])
            slots[fb][1] = s2
            slots[fb][2] = d2

        def stage_a(fb):
            s1 = slots[fb][0]
            hs = fb % 2
            for fc in range(4):
                bs = [bank() for _ in range(3)]
                for k in range(KD):
                    for tt in range(3):
                        S.add('pe', lambda h, k=k, tt=tt, fc=fc, b=bs[tt]: h.matmul(
                            ps[b][:, 0:TT], lhsT=wbuf[s1][:, k, fc * 128:(fc + 1) * 128],
                            rhs=xT[:, k, tt * TT:(tt + 1) * TT], start=(k == 0), stop=(k == KD - 1)),
                            r=['w%d' % s1] + XTT(tt, k), w=[PS(bs[tt])])
                for tt in range(3):
                    rt = (fc * 3 + tt) % 2
                    S.add('act', lambda h, tt=tt, b=bs[tt], rt=rt: h.activation(
                        out=rtmp[rt], in_=ps[b][:, 0:TT], func=AF.Relu),
                        r=[PS(bs[tt])], w=['rtmp%d' % rt])
                    eng = 'pool' if tt == 1 else 'dve'
                    S.add(eng, lambda h, tt=tt, fc=fc, rt=rt: h.tensor_tensor(
                        out=hT[hs][:, fc, tt * TT:(tt + 1) * TT], in0=rtmp[rt], in1=rtmp[rt], op=ALU.mult),
                        r=['rtmp%d' % rt], w=['hT%d' % hs])

        def stage_b(fb):
            s2, d2 = slots[fb][1], slots[fb][2]
            hs = fb % 2
            for tc in range(NT):
                for cb in range(4):
                    b = bank()
                    for fc in range(4):
                        S.add('pe', lambda h, fc=fc, tc=tc, cb=cb, b=b: h.matmul(
                            ps[b][:, :], lhsT=hT[hs][:, fc, tc * 128:(tc + 1) * 128],
                            rhs=d2[:, fc, cb * 512:(cb + 1) * 512], start=(fc == 0), stop=(fc == 3)),
                            r=['hT%d' % hs, 'w%d' % s2], w=[PS(b)])
                    dst = xres[:, tc, cb * 512:(cb + 1) * 512]
                    if fb == 0:
                        S.add('dve', lambda h, dst=dst, b=b: h.scalar_tensor_tensor(
                            out=dst, in0=dst, scalar=ALPHA, op0=ALU.mult, in1=ps[b][:, :], op1=ALU.add),
                            r=[PS(b), XR(tc, cb)], w=[XR(tc, cb)])
                    else:
                        S.add('dve', lambda h, dst=dst, b=b: h.tensor_tensor(
                            out=dst, in0=dst, in1=ps[b][:, :], op=ALU.add),
                            r=[PS(b), XR(tc, cb)], w=[XR(tc, cb)])

        issue1(0)
        issue2(0)
        for fb in range(NFB + 1):
            if fb < NFB:
                stage_a(fb)
                give_w(slots[fb][0])
                if fb + 1 < NFB:
                    issue1(fb + 1)
            if fb >= 1:
                stage_b(fb - 1)
                give_w(slots[fb - 1][1])
            if fb + 1 < NFB:
                issue2(fb + 1)
        S.retire_prefix('hT', 'rtmp')
        A.release(*hT, *rtmp)

    def ln_stats(z, zr, st, sm, tag, eps):
        for c in range(4):
            S.add('dve', lambda h, c=c: h.bn_stats(out=st[:, c, :], in_=z[:, c * 512:(c + 1) * 512]),
                  r=[zr[c]], w=[tag + 'st'])
        S.add('dve', lambda h: h.bn_aggr(out=sm[:, 0:2], in_=st), r=[tag + 'st'], w=[tag + 'sm'])
        S.add('act', lambda h: h.activation(out=sm[:, 2:3], in_=sm[:, 1:2], func=AF.Ln, bias=eps_t[:, 0:1], scale=1.0),
              r=[tag + 'sm', 'eps'], w=[tag + 'sm'])
        S.add('act', lambda h: h.activation(out=sm[:, 3:4], in_=sm[:, 2:3], func=AF.Exp, scale=-0.5),
              r=[tag + 'sm'], w=[tag + 'sm'])
        S.add('dve', lambda h: h.tensor_scalar(out=sm[:, 4:5], in0=sm[:, 0:1], scalar1=sm[:, 3:4],
                                               scalar2=-1.0, op0=ALU.mult, op1=ALU.mult),
              r=[tag + 'sm'], w=[tag + 'sm'])

    def layer_norm(idx, g_dram, b_dram, final):
        xres = xres_box[0]
        gt = A.alloc([D], F32)
        bt = A.alloc([D], F32)
        gbc = A.alloc([2, 16], F32)
        st = [A.alloc([4, 6], F32) for _ in range(3)]
        sm = [A.alloc([8], F32) for _ in range(3)]
        S.add('sp', lambda h: h.dma_start(out=gt, in_=g_dram.partition_broadcast(128)), w=['ln_g'], dma='ln_g')
        S.add('sp', lambda h: h.dma_start(out=bt, in_=b_dram.partition_broadcast(128)), w=['ln_b'], dma='ln_b')
        S.add('sp', lambda h: h.dma_start(out=gbc, in_=lnT_d[idx]), w=['ln_c'], dma='ln_c')

        def st_stage(tc):
            i = tc % 3
            ln_stats(xres[:, tc, :], XRA(tc), st[i], sm[i], 'ln%d' % i, LN_EPS)

        def nrm_stage(tc):
            i = tc % 3
            z = xres[:, tc, :]
            for c in range(4):
                zr = XR(tc, c)
                zc = z[:, c * 512:(c + 1) * 512]
                S.add('act', lambda h, i=i, zc=zc: h.activation(out=zc, in_=zc, func=AF.Identity,
                                                             scale=sm[i][:, 3:4], bias=sm[i][:, 4:5]),
                      r=[zr, 'ln%dsm' % i], w=[zr])

        def t_stage(tc):
            z = xres[:, tc, :]
            for g in range(4):
                zr = XR(tc, g)
                b = bank()
                for q_ in range(4):
                    kc = 4 * g + q_
                    S.add('pe', lambda h, kc=kc, q_=q_, b=b: h.transpose(
                        out=ps[b][:, q_ * 128:(q_ + 1) * 128], in_=z[:, kc * 128:(kc + 1) * 128], identity=ident),
                        r=[zr, 'ident'], w=[PS(b)])
                for q_ in range(4):
                    kc = 4 * g + q_
                    dst = xT[:, kc, tc * 128:(tc + 1) * 128]
                    src = ps[b][:, q_ * 128:(q_ + 1) * 128]
                    if g % 2 == 0:
                        S.add('act', lambda h, kc=kc, dst=dst, src=src: h.activation(
                            out=dst, in_=src, func=AF.Identity, scale=gbc[:, 0, kc:kc + 1], bias=gbc[:, 1, kc:kc + 1]),
                            r=[PS(b), 'ln_c'], w=[XT(tc, kc)])
                    else:
                        S.add('dve', lambda h, kc=kc, dst=dst, src=src: h.tensor_scalar(
                            out=dst, in0=src, scalar1=gbc[:, 0, kc:kc + 1], scalar2=gbc[:, 1, kc:kc + 1],
                            op0=ALU.mult, op1=ALU.add), r=[PS(b), 'ln_c'], w=[XT(tc, kc)])

        def gb_stage(tc):
            z = xres[:, tc, :]
            for c in range(4):
                zr = XR(tc, c)
                zc = z[:, c * 512:(c + 1) * 512]
                S.add('pool', lambda h, c=c, zc=zc: h.tensor_tensor(out=zc, in0=zc, in1=gt[:, c * 512:(c + 1) * 512], op=ALU.mult),
                      r=[zr, 'ln_g'], w=[zr])
                S.add('dve', lambda h, c=c, zc=zc: h.tensor_tensor(out=zc, in0=zc, in1=bt[:, c * 512:(c + 1) * 512], op=ALU.add),
                      r=[zr, 'ln_b'], w=[zr])
            if final:
                S.add('sp', lambda h, tc=tc, z=z: h.dma_start(out=y[tc * 128:(tc + 1) * 128, :], in_=z),
                      r=XRA(tc), dma='yout%d' % (tc % 3))

        st_stage(0)
        st_stage(1)
        nrm_stage(0)
        for tc in range(NT):
            if tc + 2 < NT:
                st_stage(tc + 2)
            if tc + 1 < NT:
                nrm_stage(tc + 1)
            if not final:
                t_stage(tc)
            gb_stage(tc)
        S.retire_prefix('ln')
        A.release(gt, bt, gbc, *st, *sm)

    cm = A.alloc([786], F32)
    S.add('sp', lambda h: h.dma_start(out=cm, in_=cm_d), w=['cm'], dma='c_cm')
    eps_t = A.alloc([2], F32)
    S.add('dve', lambda h: h.memset(eps_t[:, 0:1], LN_EPS), w=['eps'])
    S.add('dve', lambda h: h.memset(eps_t[:, 1:2], HN_EPS), w=['eps'])

    def CM(i):
        return cm[:, i * 128:(i + 1) * 128]

    def gla_layer():
        winv = gla_w_in.rearrange("(k p) f -> p k f", p=128)
        wg = A.alloc([1024], F32)
        S.add('sp', lambda h: h.dma_start(out=wg[0:32, :], in_=wg_aug), w=['wg'], dma='c_wg')
        wg16 = A.alloc([16, 16], BF16)
        S.add('pool', lambda h: h.dma_start(out=wg16, in_=winv[:, :, 6144:6160]), w=['wg16'], dma='c_wg16')
        identb_r = ['identb']

        def gate_T(srcT, src_res_list, ntok, name):
            gTa = A.alloc([ntok], F32)
            S.add('dve', lambda h: h.memset(gTa[0:32, :], 1.0), w=[name])
            ntt = ntok // TT if ntok % TT == 0 else None
            tiles = [(i * TT, TT) for i in range(ntok // TT)] if ntt else [(i * 512, 512) for i in range(ntok // 512)]
            for (o, n) in tiles:
                b = bank()
                for k in range(KD):
                    if src_res_list is None:
                        rr = [XT(tc_, k) for tc_ in range(o // 128, (o + n - 1) // 128 + 1)]
                    else:
                        rr = sorted(set(src_res_list[(o // 128):((o + n - 1) // 128) + 1]))
                    S.add('pe', lambda h, k=k, b=b, o=o, n=n: h.matmul(
                        ps[b][0:16, 0:n], lhsT=wg16[:, k, :], rhs=srcT[:, k, o:o + n],
                        start=(k == 0), stop=(k == KD - 1)), r=['wg16'] + rr, w=[PS(b)])
                S.add('act', lambda h, b=b, o=o, n=n: h.copy(out=gTa[0:16, o:o + n], in_=ps[b][0:16, 0:n]),
                      r=[PS(b)], w=[name])
            return gTa

        class WS:
            pass

        sgS = A.alloc([512], F32)
        t1S = A.alloc([512], BF16)
        junkS = A.alloc([512], BF16)
        v3 = [A.alloc([512], BF16) for _ in range(4)]

        def make_ws():
            w_ = WS()
            w_.qk = A.alloc([512], BF16)
            w_.sg = sgS
            w_.t1 = t1S
            w_.srg = A.alloc([512], BF16)
            w_.nl = A.alloc([256], F32)
            w_.eend = A.alloc([256], F32)
            w_.kte = A.alloc([256], BF16)
            w_.epos = A.alloc([2, 128], F32)
            w_.eneg = A.alloc([2, 128], F32)
            w_.qdT = A.alloc([2, 128], BF16)
            w_.kiT = A.alloc([2, 128], BF16)
            w_.scm = A.alloc([128], BF16)
            w_.junk = junkS
            w_.sm = A.alloc([8], F32)
            w_.all = [w_.qk, w_.srg, w_.nl, w_.eend, w_.kte, w_.epos, w_.eneg,
                      w_.qdT, w_.kiT, w_.scm, w_.sm]
            return w_

        xpT = A.alloc([KD, NPF * 128], BF16)
        load_xT(xp, NPF, xpT, lambda tc, g: ['xpT%d' % tc])
        XPT = ['xpT%d' % tc for tc in range(NPF)]
        gTp = gate_T(xpT, XPT, NPF * 128, 'gTp')
        Sf = A.alloc([2, 512], F32)
        wsets = [make_ws() for _ in range(3)]
        dec = [A.alloc([4], F32) for _ in range(2)]

        def gate_common(W, i, gsrc, gres, c, h_, tri_u):
            R = 'g%d' % i
            bg = bank()
            S.add('pe', lambda h, bg=bg: h.matmul(ps[bg][:, 0:256], lhsT=gsrc[0:32, c * 128:(c + 1) * 128],
                                                   rhs=wg[0:32, h_ * 256:(h_ + 1) * 256], start=True, stop=True),
                  r=['wg', gres], w=[PS(bg)])
            S.add('act', lambda h, bg=bg: h.activation(out=W.nl, in_=ps[bg][:, 0:256], func=AF.Exp, scale=-1.0),
                  r=[PS(bg)], w=[R + 'nl'])
            S.add('act', lambda h: h.activation(out=W.nl, in_=W.nl, func=AF.Ln, bias=cm[:, 785:786], scale=1.0),
                  r=[R + 'nl', 'cm'], w=[R + 'nl'])
            brc = bank()
            S.add('pe', lambda h, brc=brc: h.matmul(ps[brc][:, 0:256], lhsT=tri_u, rhs=W.nl, start=True, stop=True),
                  r=['cm', R + 'nl'], w=[PS(brc)])
            S.add('act', lambda h, brc=brc: h.activation(out=W.eend, in_=ps[brc][:, 0:256], func=AF.Exp),
                  r=[PS(brc)], w=[R + 'eend'])
            S.add('pool', lambda h: h.tensor_tensor(out=W.kte, in0=W.qk[:, 256:512], in1=W.eend, op=ALU.mult),
                  r=[R + 'qk', R + 'eend'], w=[R + 'kte'])

        def pf_Pk(h_, c, sk):
            W, R = wsets[c % 2], 'g%d' % (c % 2)
            bk = bank()
            for k in range(KD):
                S.add('pe', lambda h, k=k: h.matmul(
                    ps[bk][:, 0:256], lhsT=xpT[:, k, c * 128:(c + 1) * 128], rhs=wbuf[sk][:, k, 256:512],
                    start=(k == 0), stop=(k == KD - 1)), r=[XPT[c], 'w%d' % sk], w=[PS(bk)])
            S.add('dve', lambda h: h.tensor_copy(out=W.qk[:, 256:512], in_=ps[bk][:, 0:256]), r=[PS(bk)], w=[R + 'qk'])

        def pf_Pv(h_, c, sv):
            vv, VR = v3[c % 3], 'gv%d' % (c % 3)
            bv_ = bank()
            for k in range(KD):
                S.add('pe', lambda h, k=k: h.matmul(
                    ps[bv_][:, :], lhsT=xpT[:, k, c * 128:(c + 1) * 128], rhs=wbuf[sv][:, k, :],
                    start=(k == 0), stop=(k == KD - 1)), r=[XPT[c], 'w%d' % sv], w=[PS(bv_)])
            S.add('act', lambda h: h.copy(out=vv, in_=ps[bv_][:, :]), r=[PS(bv_)], w=[VR])

        def pf_G01(h_, c):
            W, R = wsets[c % 2], 'g%d' % (c % 2)
            bg = bank()
            S.add('pe', lambda h: h.matmul(ps[bg][:, 0:256], lhsT=gTp[0:32, c * 128:(c + 1) * 128],
                                           rhs=wg[0:32, h_ * 256:(h_ + 1) * 256], start=True, stop=True),
                  r=['wg', 'gTp'], w=[PS(bg)])
            S.add('act', lambda h: h.activation(out=W.nl, in_=ps[bg][:, 0:256], func=AF.Exp, scale=-1.0),
                  r=[PS(bg)], w=[R + 'nl'])
            S.add('act', lambda h: h.activation(out=W.nl, in_=W.nl, func=AF.Ln, bias=cm[:, 785:786], scale=1.0),
                  r=[R + 'nl', 'cm'], w=[R + 'nl'])

        def pf_G2(h_, c):
            W, R, i = wsets[c % 2], 'g%d' % (c % 2), c % 2
            brc = bank()
            S.add('pe', lambda h: h.matmul(ps[brc][:, 0:256], lhsT=CM(C_TRIU_S), rhs=W.nl, start=True, stop=True),
                  r=['cm', R + 'nl'], w=[PS(brc)])
            bb = bank()
            for j in range(2):
                S.add('pe', lambda h, j=j: h.matmul(
                    ps[bb][:, 2 * j:2 * j + 2], lhsT=W.nl[:, j * 128:(j + 1) * 128], rhs=cm[:, 784:786],
                    start=True, stop=True), r=[R + 'nl', 'cm'], w=[PS(bb)])
            S.add('act', lambda h: h.activation(out=W.eend, in_=ps[brc][:, 0:256], func=AF.Exp),
                  r=[PS(brc)], w=[R + 'eend'])
            S.add('act', lambda h: h.activation(out=dec[i], in_=ps[bb][:, 0:4], func=AF.Exp),
                  r=[PS(bb)], w=['dec%d' % i])
            S.add('pool', lambda h: h.tensor_tensor(out=W.kte, in0=W.qk[:, 256:512], in1=W.eend, op=ALU.mult),
                  r=[R + 'qk', R + 'eend'], w=[R + 'kte'])

        def pf_U(h_, c):
            W, R, i = wsets[c % 2], 'g%d' % (c % 2), c % 2
            vv, VR = v3[c % 3], 'gv%d' % (c % 3)
            for j in range(2):
                bu = bank()
                S.add('pe', lambda h, j=j, bu=bu: h.matmul(
                    ps[bu][:, :], lhsT=W.kte[:, j * 128:(j + 1) * 128], rhs=vv, start=True, stop=True),
                    r=[R + 'kte', VR], w=[PS(bu)])
                S.add('dve', lambda h, j=j, bu=bu: h.scalar_tensor_tensor(
                    out=Sf[:, j, :], in0=Sf[:, j, :], scalar=dec[i][:, 2 * j:2 * j + 1], op0=ALU.mult,
                    in1=ps[bu][:, :], op1=ALU.add), r=[PS(bu), 'dec%d' % i, 'Sf'], w=['Sf'])

        for h_ in range(4):
            sk = take_w()
            S.add('pool', lambda h, sk=sk, h_=h_: h.dma_start(out=wbuf[sk][:, :, 256:512],
                                                              in_=winv[:, :, 1024 + h_ * 256:1024 + (h_ + 1) * 256]),
                  w=['w%d' % sk], dma='w%d' % sk)
            sv = take_w()
            load_w(sv, winv[:, :, 2048 + h_ * 512:2048 + (h_ + 1) * 512])
            S.add('dve', lambda h: h.memset(Sf, 0.0), w=['Sf'])
            for c in range(-2, NPF):
                if 0 <= c + 2 < NPF:
                    pf_Pk(h_, c + 2, sk)
                if 0 <= c + 1 < NPF:
                    pf_G2(h_, c + 1)
                if 0 <= c + 2 < NPF:
                    pf_Pv(h_, c + 2, sv)
                if c >= 0:
                    pf_U(h_, c)
                if 0 <= c + 2 < NPF:
                    pf_G01(h_, c + 2)
            give_w(sk)
            give_w(sv)
            S.add('sp', lambda h, h_=h_: h.dma_start(out=sscr[h_].rearrange("(j p) v -> p j v", p=128), in_=Sf),
                  r=['Sf'], w=['sscr%d' % h_], dma='sscr%d' % h_)
        S.retire(XPT + ['gTp'] + ['dec0', 'dec1'])
        A.release(xpT, gTp, *dec)

        gated = A.alloc([NT, D], BF16)
        gTm = gate_T(xT, None, T, 'gTm')
        gng = A.alloc([512], F32)
        Sb = A.alloc([2, 512], BF16)
        s0 = [A.alloc([2, 512], F32) for _ in range(3)]
        Qm = [A.alloc([2, 128], F32) for _ in range(2)]
        kteM = [A.alloc([256], BF16) for _ in range(2)]

        class Ck:
            pass

        def mk(h_, c, slots):
            ck = Ck()
            ck.h, ck.c, ck.slots = h_, c, slots
            ck.sample = (c == NT - 1)
            if ck.sample:
                ck.W, ck.R, ck.v, ck.VR = wsets[2], 'g2', v3[3], 'gv3'
            else:
                ck.W, ck.R, ck.v, ck.VR = wsets[c % 2], 'g%d' % (c % 2), v3[c % 3], 'gv%d' % (c % 3)
            return ck

        def proj(ck, slot):
            b = bank()
            c = ck.c
            for k in range(KD):
                S.add('pe', lambda h, k=k: h.matmul(
                    ps[b][:, :], lhsT=xT[:, k, c * 128:(c + 1) * 128], rhs=wbuf[slot][:, k, :],
                    start=(k == 0), stop=(k == KD - 1)), r=[XT(c, k), 'w%d' % slot], w=[PS(b)])
            return b

        def P_qk(ck):
            W, R = ck.W, ck.R
            bq = proj(ck, ck.slots[0])
            S.add('act', lambda h: h.mul(out=W.qk[:, 0:256], in_=ps[bq][:, 0:256], mul=1.0 / 16), r=[PS(bq)], w=[R + 'qk'])
            S.add('act', lambda h: h.copy(out=W.qk[:, 256:512], in_=ps[bq][:, 256:512]), r=[PS(bq)], w=[R + 'qk'])

        def P_v(ck):
            bv_ = proj(ck, ck.slots[1])
            S.add('act', lambda h: h.copy(out=ck.v, in_=ps[bv_][:, :]), r=[PS(bv_)], w=[ck.VR])

        def P_r(ck):
            W, R, h_ = ck.W, ck.R, ck.h
            br = proj(ck, ck.slots[2])
            S.add('act', lambda h: h.activation(out=W.sg, in_=ps[br][:, :], func=AF.Exp, scale=-1.0), r=[PS(br)], w=['gsg'])
            S.add('act', lambda h: h.activation(out=W.sg, in_=W.sg, func=AF.Ln, bias=cm[:, 785:786], scale=1.0),
                  r=['gsg', 'cm'], w=['gsg'])
            S.add('act', lambda h: h.activation(out=W.sg, in_=W.sg, func=AF.Exp, scale=-1.0), r=['gsg'], w=['gsg'])
            S.add('dve', lambda h: h.tensor_tensor(out=W.t1, in0=ps[br][:, :], in1=W.sg, op=ALU.mult),
                  r=[PS(br), 'gsg'], w=['gt1'])
            S.add('pool', lambda h: h.tensor_tensor(out=W.srg, in0=W.t1, in1=gng, op=ALU.mult),
                  r=['gt1', 'gng'], w=[R + 'srg'])

        def G01(ck):
            W, R, c, h_ = ck.W, ck.R, ck.c, ck.h
            bg = bank()
            S.add('pe', lambda h: h.matmul(ps[bg][:, 0:256], lhsT=gTm[0:32, c * 128:(c + 1) * 128],
                                           rhs=wg[0:32, h_ * 256:(h_ + 1) * 256], start=True, stop=True),
                  r=['wg', 'gTm'], w=[PS(bg)])
            S.add('act', lambda h: h.activation(out=W.nl, in_=ps[bg][:, 0:256], func=AF.Exp, scale=-1.0),
                  r=[PS(bg)], w=[R + 'nl'])
            S.add('act', lambda h: h.activation(out=W.nl, in_=W.nl, func=AF.Ln, bias=cm[:, 785:786], scale=1.0),
                  r=[R + 'nl', 'cm'], w=[R + 'nl'])

        def G23(ck):
            W, R = ck.W, ck.R
            tri_u = CM(C_BDU_S) if ck.sample else CM(C_TRIU_S)
            tri = CM(C_BD_S) if ck.sample else CM(C_TRI_S)
            brc = bank()
            S.add('pe', lambda h: h.matmul(ps[brc][:, 0:256], lhsT=tri_u, rhs=W.nl, start=True, stop=True),
                  r=['cm', R + 'nl'], w=[PS(brc)])
            bbt = bank()
            for j in range(2):
                S.add('pe', lambda h, j=j: h.matmul(ps[bbt][:, j * 128:(j + 1) * 128], lhsT=W.nl[:, j * 128:(j + 1) * 128],
                                                     rhs=tri, start=True, stop=True), r=[R + 'nl', 'cm'], w=[PS(bbt)])
            S.add('act', lambda h: h.activation(out=W.eend, in_=ps[brc][:, 0:256], func=AF.Exp),
                  r=[PS(brc)], w=[R + 'eend'])
            S.add('act', lambda h: h.activation(out=W.epos.rearrange("p a b -> p (a b)"), in_=ps[bbt][:, 0:256], func=AF.Exp),
                  r=[PS(bbt)], w=[R + 'epos'])
            S.add('act', lambda h: h.activation(out=W.eneg.rearrange("p a b -> p (a b)"), in_=ps[bbt][:, 0:256], func=AF.Exp, scale=-1.0),
                  r=[PS(bbt)], w=[R + 'eneg'])
            S.add('pool', lambda h: h.tensor_tensor(out=W.kte, in0=W.qk[:, 256:512], in1=W.eend, op=ALU.mult),
                  r=[R + 'qk', R + 'eend'], w=[R + 'kte'])

        def G45(ck):
            W, R = ck.W, ck.R
            btr = bank()
            for j in range(4):
                S.add('pe', lambda h, j=j: h.transpose(out=psb[btr][:, j * 128:(j + 1) * 128],
                                                        in_=W.qk[:, j * 128:(j + 1) * 128], identity=identb),
                      r=[R + 'qk', 'identb'], w=[PS(btr)])
            S.add('dve', lambda h: h.tensor_tensor(out=W.qdT.rearrange("p a b -> p (a b)"), in0=psb[btr][:, 0:256],
                                                   in1=W.epos.rearrange("p a b -> p (a b)"), op=ALU.mult),
                  r=[PS(btr), R + 'epos'], w=[R + 'qdT'])
            S.add('dve', lambda h: h.tensor_tensor(out=W.kiT.rearrange("p a b -> p (a b)"), in0=psb[btr][:, 256:512],
                                                   in1=W.eneg.rearrange("p a b -> p (a b)"), op=ALU.mult),
                  r=[PS(btr), R + 'eneg'], w=[R + 'kiT'])

        def B12(ck):
            W, R = ck.W, ck.R
            bs = bank()
            for j in range(2):
                S.add('pe', lambda h, j=j: h.matmul(ps[bs][:, 0:128], lhsT=W.kiT[:, j, :], rhs=W.qdT[:, j, :],
                                                     start=(j == 0), stop=(j == 1)), r=[R + 'kiT', R + 'qdT'], w=[PS(bs)])
            m01 = CM(C_BD01) if ck.sample else CM(C_TRI01)
            S.add('dve', lambda h: h.tensor_tensor(out=W.scm, in0=ps[bs][:, 0:128], in1=m01, op=ALU.mult),
                  r=[PS(bs), 'cm'], w=[R + 'scm'])

        def B3(ck):
            W, R, c, h_ = ck.W, ck.R, ck.c, ck.h
            vv, VR = ck.v, ck.VR
            bo = bank()
            pinned.add(bo)
            ck.bo = bo
            S.add('pe', lambda h: h.matmul(ps[bo][:, :], lhsT=W.scm, rhs=vv, start=True, stop=False),
                  r=[R + 'scm', VR], w=[PS(bo)])
            if not ck.sample:
                for j in range(2):
                    S.add('pe', lambda h, j=j: h.matmul(ps[bo][:, :], lhsT=W.qdT[:, j, :], rhs=Sb[:, j, :],
                                                         start=False, stop=(j == 1)), r=[R + 'qdT', 'Sb'], w=[PS(bo)])
                for j in range(2):
                    bu = bank()
                    S.add('pe', lambda h, j=j, bu=bu: h.matmul(ps[bu][:, :], lhsT=W.kte[:, j * 128:(j + 1) * 128], rhs=vv,
                                                               start=True, stop=True), r=[R + 'kte', VR], w=[PS(bu)])
                    S.add('dve', lambda h, j=j, bu=bu: h.scalar_tensor_tensor(
                        out=Sf[:, j, :], in0=Sf[:, j, :], scalar=W.epos[:, j, 127:128], op0=ALU.mult,
                        in1=ps[bu][:, :], op1=ALU.add), r=[PS(bu), R + 'epos', 'Sf'], w=['Sf'])
                    S.add('act', lambda h, j=j: h.copy(out=Sb[:, j, :], in_=Sf[:, j, :]), r=['Sf'], w=['Sb'])
                if c == NT - 2:
                    S.add('sp', lambda h: h.dma_start(out=gst_p[h_].rearrange("(j p) v -> p j v", p=128), in_=Sf),
                          r=['Sf'], dma='gstp')

        def s0_load(ck, q_):
            sb_ = q_ % 3
            h_ = ck.h
            S.add('sp', lambda h: h.dma_start(out=s0[sb_], in_=st_in[q_, h_].rearrange("(j p) v -> p j v", p=128)),
                  w=['s0_%d' % sb_], dma='s0_%d' % sb_)

        def unit(ck, q_):
            if DBG_NOUNIT:
                return
            W, R, h_ = ck.W, ck.R, ck.h
            vv, VR, bo = ck.v, ck.VR, ck.bo
            sb_ = q_ % 3
            qb_ = q_ % 2
            if q_ + 1 < 16:
                s0_load(ck, q_ + 1)
            S.add('pool', lambda h: h.memset(Qm[qb_], 0.0), w=['Qm%d' % qb_])
            S.add('pool', lambda h: h.tensor_copy(out=Qm[qb_][:, :, 8 * q_:8 * q_ + 8], in_=W.qdT[:, :, 8 * q_:8 * q_ + 8]),
                  r=[R + 'qdT'], w=['Qm%d' % qb_])
            for j in range(2):
                S.add('pe', lambda h, j=j: h.matmul(
                    ps[bo][:, :], lhsT=Qm[qb_][:, j, :], rhs=s0[sb_][:, j, :], start=False,
                    stop=(q_ == 15 and j == 1)), r=['Qm%d' % qb_, 's0_%d' % sb_], w=[PS(bo)])
            S.add('dve', lambda h: h.tensor_scalar(
                out=kteM[qb_], in0=W.kte, scalar1=cm[:, 768 + q_:769 + q_], scalar2=None, op0=ALU.mult),
                r=[R + 'kte', 'cm'], w=['kteM%d' % qb_])
            for j in range(2):
                bu = bank()
                S.add('pe', lambda h, j=j, bu=bu: h.matmul(
                    ps[bu][:, :], lhsT=kteM[qb_][:, j * 128:(j + 1) * 128], rhs=vv, start=True, stop=True),
                    r=['kteM%d' % qb_, VR], w=[PS(bu)])
                S.add('dve', lambda h, j=j, bu=bu: h.scalar_tensor_tensor(
                    out=s0[sb_][:, j, :], in0=s0[sb_][:, j, :], scalar=W.epos[:, j, 8 * q_ + 7:8 * q_ + 8],
                    op0=ALU.mult, in1=ps[bu][:, :], op1=ALU.add),
                    r=[PS(bu), R + 'epos', 's0_%d' % sb_], w=['s0_%d' % sb_])
            S.add('sp', lambda h: h.dma_start(out=gst_s[q_, h_].rearrange("(j p) v -> p j v", p=128), in_=s0[sb_]),
                  r=['s0_%d' % sb_], dma='s0_%d' % sb_)

        def B4(ck):
            W, R, c, h_, bo = ck.W, ck.R, ck.c, ck.h, ck.bo
            S.add('act', lambda h: h.activation(out=W.junk, in_=ps[bo][:, :], func=AF.Square, accum_out=W.sm[:, 0:1]),
                  r=[PS(bo)], w=['gjunk', R + 'sm'])
            S.add('act', lambda h: h.activation(out=W.sm[:, 1:2], in_=W.sm[:, 0:1], func=AF.Ln, bias=eps_t[:, 1:2], scale=1.0 / 512),
                  r=[R + 'sm', 'eps'], w=[R + 'sm'])
            S.add('act', lambda h: h.activation(out=W.sm[:, 2:3], in_=W.sm[:, 1:2], func=AF.Exp, scale=-0.5),
                  r=[R + 'sm'], w=[R + 'sm'])
            S.add('dve', lambda h: h.scalar_tensor_tensor(out=gated[:, c, h_ * 512:(h_ + 1) * 512], in0=ps[bo][:, :],
                                                          scalar=W.sm[:, 2:3], op0=ALU.mult, in1=W.srg, op1=ALU.mult),
                  r=[PS(bo), R + 'sm', R + 'srg'], w=['gated%d' % c])
            pinned.discard(bo)

        def issue_head(h_):
            sqk = take_w()
            S.add('pool', lambda h: h.dma_start(out=wbuf[sqk][:, :, 0:256], in_=winv[:, :, h_ * 256:(h_ + 1) * 256]),
                  w=['w%d' % sqk], dma='w%d' % sqk)
            S.add('pool', lambda h: h.dma_start(out=wbuf[sqk][:, :, 256:512],
                                                in_=winv[:, :, 1024 + h_ * 256:1024 + (h_ + 1) * 256]),
                  r=['w%d' % sqk], w=['w%d' % sqk], dma='w%d' % sqk)
            sv = take_w()
            load_w(sv, winv[:, :, 2048 + h_ * 512:2048 + (h_ + 1) * 512])
            sr = take_w()
            load_w(sr, winv[:, :, 4096 + h_ * 512:4096 + (h_ + 1) * 512])
            return (sqk, sv, sr)

        for h_ in range(4):
            slots = issue_head(h_)
            S.add('sp', lambda h, h_=h_: h.dma_start(out=gng, in_=gla_ng[:, h_ * 512:(h_ + 1) * 512].partition_broadcast(128)),
                  w=['gng'], dma='c_gng')
            S.add('sp', lambda h, h_=h_: h.dma_start(out=Sf, in_=sscr[h_].rearrange("(j p) v -> p j v", p=128)),
                  r=['sscr%d' % h_], w=['Sf'], dma='sfl')
            S.add('act', lambda h: h.copy(out=Sb.rearrange("p a b -> p (a b)"), in_=Sf.rearrange("p a b -> p (a b)")),
                  r=['Sf'], w=['Sb'])
            cks = {c: mk(h_, c, slots) for c in range(NT)}
            ck8 = cks[NT - 1]
            s0_load(ck8, 0)
            P_qk(ck8)
            P_v(ck8)
            P_r(ck8)
            G01(ck8)
            G23(ck8)
            G45(ck8)
            B12(ck8)
            B3(ck8)
            NPR = NT - 1
            for c in range(-2, NPR):
                p = cks.get(c + 2) if c + 2 < NPR else None
                g = cks.get(c + 1) if c + 1 < NPR else None
                b_ = cks.get(c) if c >= 0 else None
                if p:
                    P_qk(p)
                if b_:
                    B12(b_)
                if g:
                    G23(g)
                if p:
                    P_v(p)
                if b_:
                    B3(b_)
                    B4(b_)
                    unit(ck8, 2 * c)
                if g:
                    G45(g)
                if p:
                    P_r(p)
                    G01(p)
                if b_:
                    unit(ck8, 2 * c + 1)
            B4(ck8)
            for s_ in slots:
                give_w(s_)
        for tc in range(NT):
            transpose_bf16_chunk(gated[:, tc, :], 'gated%d' % tc, xT, XTG, tc)
        S.retire_prefix('g0', 'g1', 'g2', 'gsg', 'gt1', 'gjunk', 'gv', 'gated', 'gTm', 'gng', 'Sf', 'Sb', 's0_', 'Qm', 'kteM', 'wg')
        A.release(gated, gTm, gng, Sf, Sb, *s0, *Qm, *kteM, wg, wg16, *v3, sgS, t1S, junkS)
        for w_ in wsets:
            A.release(*w_.all)

    def sg_layer():
        winv = sg_w_in.rearrange("(k p) f -> p k f", p=128)
        uT = A.alloc([KD, T], BF16)
        binT = A.alloc([16], F32)
        S.add('sp', lambda h: h.dma_start(out=binT, in_=sg_binT), w=['binT'], dma='c_binT')
        wtmp = A.alloc([8, 128], F32)
        WT = [A.alloc([8, 128], BF16) for _ in range(2)]
        for v_ in range(2):
            src = sg_wsT if v_ == 0 else sg_wsTs
            S.add('sp', lambda h, src=src: h.dma_start(out=wtmp, in_=src), w=['wtmp'], dma='c_wtmp')
            m01 = CM(C_TRI01) if v_ == 0 else CM(C_BD01)
            for g in range(8):
                S.add('dve', lambda h, g=g, v_=v_, m01=m01: h.tensor_tensor(out=WT[v_][:, g, :], in0=wtmp[:, g, :], in1=m01, op=ALU.mult),
                      r=['wtmp', 'cm'], w=['WT%d' % v_])
        S.retire(['wtmp'])
        A.release(wtmp)
        bsp1 = A.alloc([8, 128], F32)
        bsp = [bsp1, bsp1]

        def load_bsp(v_):
            S.add('sp', lambda h: h.dma_start(out=bsp1, in_=sg_bsp[v_].partition_broadcast(128)),
                  w=['bsp'], dma='c_bsp')

        load_bsp(0)
        bv = A.alloc([D], F32)
        vg = A.alloc([D], F32)
        vb = A.alloc([D], F32)
        S.add('sp', lambda h: h.dma_start(out=bv, in_=sg_bv.partition_broadcast(128)), w=['sgbv'], dma='c_sgbv')
        S.add('sp', lambda h: h.dma_start(out=vg, in_=sg_vg.partition_broadcast(128)), w=['sgvg'], dma='c_sgvg')
        S.add('sp', lambda h: h.dma_start(out=vb, in_=sg_vb.partition_broadcast(128)), w=['sgvb'], dma='c_sgvb')

        slots = {}

        def issue(cb, col0):
            s_ = take_w()
            load_w(s_, winv[:, :, col0 + cb * 512:col0 + (cb + 1) * 512])
            slots[cb] = s_

        nxt = 0
        while nxt < 4 and free_w:
            issue(nxt, 0)
            nxt += 1
        for cb in range(4):
            if cb not in slots:
                issue(cb, 0)
                nxt = cb + 1
            s_ = slots[cb]
            for fc in range(4):
                bs = [bank() for _ in range(3)]
                for k in range(KD):
                    for tt in range(3):
                        S.add('pe', lambda h, k=k, tt=tt, fc=fc, b=bs[tt], s_=s_: h.matmul(
                            ps[b][:, 0:TT], lhsT=wbuf[s_][:, k, fc * 128:(fc + 1) * 128],
                            rhs=xT[:, k, tt * TT:(tt + 1) * TT], start=(k == 0), stop=(k == KD - 1)),
                            r=['w%d' % s_] + XTT(tt, k), w=[PS(bs[tt])])
                f_ = cb * 4 + fc
                for tt in range(3):
                    S.add('act', lambda h, tt=tt, b=bs[tt], f_=f_: h.activation(
                        out=uT[:, f_, tt * TT:(tt + 1) * TT], in_=ps[b][:, 0:TT], func=AF.Gelu,
                        bias=binT[:, f_:f_ + 1], scale=1.0), r=[PS(bs[tt]), 'binT'], w=['uT'])
            give_w(s_)
        vs = []
        for cb in range(4):
            s_ = take_w()
            load_w(s_, winv[:, :, 2048 + cb * 512:2048 + (cb + 1) * 512])
            vs.append(s_)
        vt = [A.alloc([D], F32) for _ in range(2)]
        vnb = [A.alloc([D], BF16) for _ in range(2)]
        tmp = [A.alloc([512], F32) for _ in range(2)]
        mt = [A.alloc([4, 128], F32) for _ in range(2)]
        st = [A.alloc([4, 6], F32) for _ in range(2)]
        sm = [A.alloc([8], F32) for _ in range(2)]

        def p_stage(tc):
            i = tc % 2
            VT = 'sv%dvt' % i
            bs = [bank() for _ in range(4)]
            for k in range(KD):
                for cb in range(4):
                    S.add('pe', lambda h, k=k, cb=cb, b=bs[cb], tc=tc: h.matmul(
                        ps[b][:, :], lhsT=xT[:, k, tc * 128:(tc + 1) * 128], rhs=wbuf[vs[cb]][:, k, :],
                        start=(k == 0), stop=(k == KD - 1)), r=[XT(tc, k), 'w%d' % vs[cb]], w=[PS(bs[cb])])
            for cb in range(4):
                sl = slice(cb * 512, (cb + 1) * 512)
                S.add('dve', lambda h, b=bs[cb], sl=sl, i=i: h.tensor_tensor(out=vt[i][:, sl], in0=ps[b][:, :], in1=bv[:, sl], op=ALU.add),
                      r=[PS(bs[cb]), 'sgbv'], w=[VT + str(cb)])
                S.add('act', lambda h, sl=sl, i=i: h.activation(out=vt[i][:, sl], in_=vt[i][:, sl], func=AF.Gelu),
                      r=[VT + str(cb)], w=[VT + str(cb)])

        def l_stage(tc):
            i = tc % 2
            sample = (tc == NT - 1)
            V = 'sv%d' % i
            VT = 'sv%dvt' % i
            ln_stats(vt[i], [VT + str(c_) for c_ in range(4)], st[i], sm[i], V, LN_EPS)
            for c in range(4):
                j = (tc * 4 + c) % 2
                sl = slice(c * 512, (c + 1) * 512)
                S.add('act', lambda h, i=i, sl=sl, j=j: h.activation(out=tmp[j], in_=vt[i][:, sl], func=AF.Identity,
                                                                  scale=sm[i][:, 3:4], bias=sm[i][:, 4:5]),
                      r=[VT + str(c), V + 'sm'], w=['svtmp%d' % j])
                S.add('pool', lambda h, j=j, sl=sl: h.tensor_tensor(out=tmp[j], in0=tmp[j], in1=vg[:, sl], op=ALU.mult),
                      r=['svtmp%d' % j, 'sgvg'], w=['svtmp%d' % j])
                if sample:
                    S.add('dve', lambda h, j=j, sl=sl, i=i: h.tensor_tensor(out=vt[i][:, sl], in0=tmp[j], in1=vb[:, sl], op=ALU.add),
                          r=['svtmp%d' % j, 'sgvb', VT + str(c)], w=[VT + str(c)])
                    S.add('act', lambda h, sl=sl, i=i: h.copy(out=vnb[i][:, sl], in_=vt[i][:, sl]), r=[VT + str(c)], w=[V + 'vnb' + str(c)])
                else:
                    S.add('dve', lambda h, j=j, sl=sl, i=i: h.tensor_tensor(out=vnb[i][:, sl], in0=tmp[j], in1=vb[:, sl], op=ALU.add),
                          r=['svtmp%d' % j, 'sgvb'], w=[V + 'vnb' + str(c)])
            if sample:
                S.add('sp', lambda h, i=i: h.dma_start(out=sgv, in_=vt[i]), r=[VT + str(c_) for c_ in range(4)], dma='sgvout')

        def m_stage(tc):
            i = tc % 2
            sample = (tc == NT - 1)
            V = 'sv%d' % i
            v_ = 1 if sample else 0
            for dg in range(4):
                b = bank()
                for q_ in range(4):
                    dc = dg * 4 + q_
                    S.add('pe', lambda h, q_=q_, dc=dc, b=b, i=i, v_=v_: h.matmul(
                        ps[b][:, q_ * 128:(q_ + 1) * 128], lhsT=vnb[i][:, dc * 128:(dc + 1) * 128],
                        rhs=WT[v_][:, dc // 2, :], start=True, stop=True), r=[V + 'vnb' + str(dg), 'WT%d' % v_], w=[PS(b)])
                mi = dg % 2
                bias_ap = bsp[v_][:, 2 * dg:2 * dg + 2, :].unsqueeze(2).broadcast_to([128, 2, 2, 128])
                S.add('dve', lambda h, b=b, mi=mi, bias_ap=bias_ap: h.tensor_tensor(
                    out=mt[mi].rearrange("p (a c) t -> p a c t", a=2), in0=ps[b][:, :].rearrange("p (a c t) -> p a c t", a=2, c=2),
                    in1=bias_ap, op=ALU.add), r=[PS(b), 'bsp'], w=['mt%d' % mi])
                S.add('pool', lambda h, mi=mi, dg=dg, tc=tc: h.tensor_tensor(
                    out=xT[:, 4 * dg:4 * dg + 4, tc * 128:(tc + 1) * 128], in0=mt[mi],
                    in1=uT[:, 4 * dg:4 * dg + 4, tc * 128:(tc + 1) * 128], op=ALU.mult),
                    r=['mt%d' % mi, 'uT'], w=XTG(tc, dg))

        p_stage(0)
        p_stage(1)
        l_stage(0)
        for tc in range(NT):
            if tc + 2 < NT:
                p_stage(tc + 2)
            if tc + 1 < NT:
                l_stage(tc + 1)
            if tc == NT - 1:
                load_bsp(1)
            m_stage(tc)
        for s_ in vs:
            give_w(s_)
        S.retire_prefix('uT', 'binT', 'wtmp', 'WT', 'bsp', 'sgbv', 'sgvg', 'sgvb', 'sv', 'mt')
        A.release(uT, binT, *WT, bsp1, bv, vg, vb, *vt, *vnb, *tmp, *mt, *st, *sm)

    load_xT(xm, NT, xT, XTG)
    last = [p for p in ("gla", "mlp0", "sg", "mlp1") if p in phases][-1]
    if "gla" in phases:
        gla_layer()
    alloc_xres(xm)
    if "gla" in phases:
        out_proj(gla_w_out)
        layer_norm(0, ln1g[0:1, :], ln1b[0:1, :], final=(last == "gla"))
    if "mlp0" in phases:
        mlp(0)
        layer_norm(1, ln2g[0:1, :], ln2b[0:1, :], final=(last == "mlp0"))
    if "sg" in phases:
        xres = xres_box[0]
        for tc in range(NT):
            S.add('sp', lambda h, tc=tc, xres=xres: h.dma_start(out=xspill[tc * 128:(tc + 1) * 128, :], in_=xres[:, tc, :]),
                  r=XRA(tc), w=['xspill%d' % tc], dma='xsp%d' % tc)
        free_xres()
        sg_layer()
        xres = A.alloc([NT, D], F32)
        xres_box[0] = xres
        for tc in range(NT):
            S.add('sp', lambda h, tc=tc, xres=xres: h.dma_start(out=xres[:, tc, :], in_=xspill[tc * 128:(tc + 1) * 128, :]),
                  r=['xspill%d' % tc], w=XRA(tc), dma='xres%d' % tc)
        out_proj(sg_w_out)
        layer_norm(2, ln1g[1:2, :], ln1b[1:2, :], final=(last == "sg"))
    if "mlp1" in phases:
        mlp(1)
        layer_norm(3, ln2g[1:2, :], ln2b[1:2, :], final=True)

    S.emit(nc, es)
    es.close()
    return nc, S, A


def prep_shared(inp):
    f = lambda a: np.ascontiguousarray(np.asarray(a, dtype=np.float32))
    wg_aug = np.zeros((32, 1024), np.float32)
    wg_aug[0:16] = inp["gla_w_gate"][0]
    wg_aug[16] = inp["gla_b_gate"][0]
    ws = np.asarray(inp["sg_w_spatial"][0])
    wsT = ws.transpose(2, 0, 1)
    wsTs = np.tile(ws[:, :8, :8].transpose(2, 0, 1), (16, 1, 16))
    bsp = np.asarray(inp["sg_b_spatial"][0])
    bsp2 = np.stack([bsp, np.tile(bsp[:, :8], (1, 16))])
    b_in = np.asarray(inp["sg_b_in"][0])
    d = dict(
        gla_w_in=f(inp["gla_w_in"][0]), wg_aug=wg_aug, gla_ng=f(np.asarray(inp["gla_norm_g"][0]).reshape(1, D)),
        gla_w_out=f(inp["gla_w_out"][0]), sg_w_in=f(inp["sg_w_in"][0]),
        sg_binT=f(b_in[:D].reshape(16, 128).T), sg_bv=f(b_in[D:].reshape(1, D)),
        sg_vg=f(np.asarray(inp["sg_v_norm_g"][0]).reshape(1, D)), sg_vb=f(np.asarray(inp["sg_v_norm_b"][0]).reshape(1, D)),
        sg_wsT=f(wsT), sg_wsTs=f(wsTs), sg_bsp=f(bsp2), sg_w_out=f(inp["sg_w_out"][0]),
        mlp_w1=f(inp["mlp_w1"]), mlp_w2=f(inp["mlp_w2"]),
        ln1_g=f(inp["ln1_g"]), ln1_b=f(inp["ln1_b"]), ln2_g=f(inp["ln2_g"]), ln2_b=f(inp["ln2_b"]),
        ident=np.eye(128, dtype=np.float32), cmask=make_consts(),
    )
    lnT = np.zeros((4, 128, 2, 16), np.float32)
    for n_, (gk, bk, li) in enumerate([("ln1_g", "ln1_b", 0), ("ln2_g", "ln2_b", 0), ("ln1_g", "ln1_b", 1), ("ln2_g", "ln2_b", 1)]):
        lnT[n_, :, 0, :] = np.asarray(inp[gk][li]).reshape(16, 128).T
        lnT[n_, :, 1, :] = np.asarray(inp[bk][li]).reshape(16, 128).T
    d["lnT"] = lnT
    return d


def prep_core(inp, c):
    xpr = np.asarray(inp["x_prompt"])
    xs = np.asarray(inp["x_sample"])
    b, hf = c // 2, c % 2
    xm = np.concatenate([xpr[b, hf * 1024:(hf + 1) * 1024], xs[16 * c:16 * (c + 1)].reshape(128, D)], axis=0)
    if hf == 1:
        xp = xpr[b, 0:1024]
    else:
        xp = np.zeros((1024, D), np.float32)
    st = np.asarray(inp["state_gla"])[0, 16 * c:16 * (c + 1)]
    return dict(xm=np.ascontiguousarray(xm, dtype=np.float32), xp=np.ascontiguousarray(xp, dtype=np.float32),
                st=np.ascontiguousarray(st, dtype=np.float32))


_CACHE = {}


def kernel(**inputs):
    if "nc" not in _CACHE:
        _CACHE["nc"] = build()[0]
    nc = _CACHE["nc"]
    shared = prep_shared(inputs)
    in_maps = []
    for c in range(8):
        m = dict(shared)
        m.update(prep_core(inputs, c))
        in_maps.append(m)
    res = run_bass_kernel_spmd(nc, in_maps, core_ids=list(range(8)))
    R = res.results
    y_prompt = np.zeros((4, 2048, D), np.float32)
    y_sample = np.zeros((128, 8, D), np.float32)
    gp = np.zeros((1, 4, 4, 256, 512), np.float32)
    gs = np.zeros((1, 128, 4, 256, 512), np.float32)
    sgv = np.zeros((1, 128, 8, D), np.float32)
    for c in range(8):
        b, hf = c // 2, c % 2
        yc = R[c]["y"]
        y_prompt[b, hf * 1024:(hf + 1) * 1024] = yc[:1024]
        y_sample[16 * c:16 * (c + 1)] = yc[1024:].reshape(16, 8, D)
        if hf == 1:
            gp[0, b] = R[c]["gst_p"]
        gs[0, 16 * c:16 * (c + 1)] = R[c]["gst_s"]
        sgv[0, 16 * c:16 * (c + 1)] = R[c]["sgv"].reshape(16, 8, D)
    return (y_prompt, y_sample, gp, gs, sgv)
```

```python
import numpy as np
from contextlib import ExitStack
import concourse.bass as bass
import concourse.mybir as mybir
from concourse.bass_utils import run_bass_kernel_spmd

F32 = mybir.dt.float32
BF16 = mybir.dt.bfloat16
AF = mybir.ActivationFunctionType
ALU = mybir.AluOpType

D = 2048
KD = 16
NT = 9
NPF = 8
T = NT * 128
TT = 384
DFF = 8192
ALPHA = float((2.0 * 2) ** 0.25)
LN_EPS = 1e-5
HN_EPS = 1e-6
import os as _os2
DBG_NOUNIT = bool(_os2.environ.get('DBG_NOUNIT'))
import os as _os
NO_SAME_ENG_SYNC = bool(_os.environ.get('NO_SAME_ENG_SYNC'))


class Sched:
    def __init__(self):
        self.ops = []
        self.res = {}
        self.ghost = set()

    def add(self, eng, fn, r=(), w=(), dma=None):
        i = len(self.ops)
        deps = set()
        for name in r:
            st = self.res.get(name)
            if st is None:
                st = self.res[name] = [None, list(self.ghost)]
            if st[0] is not None:
                deps.add(st[0])
            if name.startswith('ps'):
                deps.update(d for d in st[1] if self.ops[d]['eng'] != eng)
        for name in w:
            st = self.res.get(name)
            if st is None:
                st = self.res[name] = [None, list(self.ghost)]
            if st[0] is not None:
                deps.add(st[0])
            deps.update(st[1])
        for name in r:
            self.res[name][1].append(i)
        for name in w:
            self.res[name] = [i, []]
        deps.discard(i)
        self.ops.append(dict(i=i, eng=eng, fn=fn, deps=deps, dma=dma, signal=False))
        return i

    def retire(self, names):
        for n in names:
            st = self.res.pop(n, None)
            if st is None:
                continue
            if st[0] is not None:
                self.ghost.add(st[0])
            self.ghost.update(st[1])
        best = {}
        for d in self.ghost:
            p = self.ops[d]
            key = ('d', p['dma']) if p['dma'] else ('c', p['eng'])
            if key not in best or best[key] < d:
                best[key] = d
        self.ghost = set(best.values())

    def retire_prefix(self, *prefixes):
        self.retire([n for n in list(self.res) if any(n.startswith(p) for p in prefixes)])

    def emit(self, nc, es, final_eng='sp'):
        ops = self.ops
        last_dma = {}
        for op in ops:
            if op['dma']:
                last_dma[op['dma']] = op['i']
        fin = dict(i=len(ops), eng=final_eng, fn=None, deps=set(last_dma.values()), dma=None, signal=False)
        ops.append(fin)
        for op in ops:
            best = {}
            for d in op['deps']:
                p = ops[d]
                key = ('d', p['dma']) if p['dma'] else ('c', p['eng'])
                if key not in best or best[key] < d:
                    best[key] = d
            rd = []
            for key, d in best.items():
                p = ops[d]
                if p['dma'] is None and p['eng'] == 'pe' and op['eng'] == 'pe' and op['dma'] is None:
                    continue
                if NO_SAME_ENG_SYNC and p['dma'] is None and op['dma'] is None and p['eng'] == op['eng']:
                    continue
                if p['dma'] is None:
                    p['signal'] = True
                rd.append(d)
            op['rdeps'] = rd
        cnt = {}
        dcnt = {}
        for op in ops:
            if op['dma']:
                dcnt[op['dma']] = dcnt.get(op['dma'], 0) + 16
                op['sval'] = dcnt[op['dma']]
            elif op['signal']:
                cnt[op['eng']] = cnt.get(op['eng'], 0) + 1
                op['sval'] = cnt[op['eng']]
        engs = ['pe', 'act', 'dve', 'pool', 'sp']
        sems = {e: es.enter_context(nc.semaphore("s_" + e)) for e in engs}
        dsems = {k: es.enter_context(nc.semaphore("d_%d" % n)) for n, k in enumerate(sorted(dcnt))}
        self.nsem = len(sems) + len(dsems)
        self.maxcnt = dict(cnt)
        block = es.enter_context(nc.Block())
        per = {e: [op for op in ops if op['eng'] == e] for e in engs}

        def run(e, h):
            waited = {}
            for op in per[e]:
                for d in op['rdeps']:
                    p = ops[d]
                    if p['dma']:
                        s, v, k = dsems[p['dma']], p['sval'], ('d', p['dma'])
                    else:
                        s, v, k = sems[p['eng']], p['sval'], ('c', p['eng'])
                    if waited.get(k, 0) >= v:
                        continue
                    waited[k] = v
                    h.wait_ge(s, v)
                if op['fn'] is None:
                    continue
                ins = op['fn'](h)
                if op['dma']:
                    ins.then_inc(dsems[op['dma']], 16)
                elif op['signal']:
                    ins.then_inc(sems[e], 1)

        @block.tensor
        def _(h):
            run('pe', h)

        @block.scalar
        def _(h):
            run('act', h)

        @block.vector
        def _(h):
            run('dve', h)

        @block.gpsimd
        def _(h):
            run('pool', h)

        @block.sync
        def _(h):
            run('sp', h)


class Arena:
    def __init__(self, ap_f32):
        self.ap = ap_f32
        self.n = ap_f32.shape[1]
        self.free = [(0, self.n)]
        self.live = {}
        self.peak = 0

    def alloc(self, shape, dtype, name=None):
        n = int(np.prod(shape))
        words = n if dtype == F32 else (n + 1) // 2
        words = (words + 1) // 2 * 2
        for idx, (o, sz) in enumerate(self.free):
            if sz >= words:
                break
        else:
            raise AssertionError(("arena overflow", name, words, self.free))
        if sz == words:
            self.free.pop(idx)
        else:
            self.free[idx] = (o + words, sz - words)
        self.peak = max(self.peak, o + words)
        v = self.ap[:, o:o + words]
        if dtype != F32:
            v = v.bitcast(dtype)
        v = v[:, 0:n]
        if len(shape) == 2:
            v = v.rearrange("p (a b) -> p a b", a=shape[0])
        elif len(shape) == 3:
            v = v.rearrange("p (a b c) -> p a b c", a=shape[0], b=shape[1])
        self.live[id(v)] = (o, words, v)
        return v

    def release(self, *views):
        for v in views:
            o, words, _ = self.live.pop(id(v))
            self.free.append((o, words))
        self.free.sort()
        merged = []
        for o, sz in self.free:
            if merged and merged[-1][0] + merged[-1][1] == o:
                merged[-1] = (merged[-1][0], merged[-1][1] + sz)
            else:
                merged.append((o, sz))
        self.free = merged


ARENA_WORDS = 53000

C_TRI_S, C_TRIU_S, C_BD_S, C_BDU_S, C_TRI01, C_BD01 = range(6)


def make_consts():
    p = np.arange(128)
    s, t = p[:, None], p[None, :]
    same = (s // 8) == (t // 8)
    tri = (s <= t)
    triu = (s > t)
    m = np.zeros((128, 6 * 128 + 16 + 2), np.float32)
    m[:, 0:128] = tri * (-1.0 / 16)
    m[:, 128:256] = triu * (-1.0 / 16)
    m[:, 256:384] = (tri & same) * (-1.0 / 16)
    m[:, 384:512] = (triu & same) * (-1.0 / 16)
    m[:, 512:640] = tri
    m[:, 640:768] = tri & same
    m[:, 768:784] = (p[:, None] // 8) == np.arange(16)[None, :]
    m[:, 784] = -1.0 / 16
    m[:, 785] = 1.0
    return m


def build(phases=("gla", "mlp0", "sg", "mlp1"), dbg=False):
    nc = bass.Bass("TRN2", target_bir_lowering=False)

    def din(name, shape):
        return nc.dram_tensor(name, list(shape), F32, kind="ExternalInput").ap()

    def dout(name, shape):
        return nc.dram_tensor(name, list(shape), F32, kind="ExternalOutput").ap()

    xm = din("xm", [T, D])
    xp = din("xp", [NPF * 128, D])
    st_in = din("st", [16, 4, 256, 512])
    gla_w_in = din("gla_w_in", [D, 6160])
    wg_aug = din("wg_aug", [32, 1024])
    gla_ng = din("gla_ng", [1, D])
    gla_w_out = din("gla_w_out", [D, D])
    sg_w_in = din("sg_w_in", [D, 2 * D])
    sg_binT = din("sg_binT", [128, 16])
    sg_bv = din("sg_bv", [1, D])
    sg_vg = din("sg_vg", [1, D])
    sg_vb = din("sg_vb", [1, D])
    sg_wsT = din("sg_wsT", [128, 8, 128])
    sg_wsTs = din("sg_wsTs", [128, 8, 128])
    sg_bsp = din("sg_bsp", [2, 8, 128])
    sg_w_out = din("sg_w_out", [D, D])
    w1 = din("mlp_w1", [2, D, DFF])
    w2 = din("mlp_w2", [2, DFF, D])
    ln1g = din("ln1_g", [2, D])
    ln1b = din("ln1_b", [2, D])
    ln2g = din("ln2_g", [2, D])
    ln2b = din("ln2_b", [2, D])
    ident_d = din("ident", [128, 128])
    lnT_d = din("lnT", [4, 128, 2, 16])
    cm_d = din("cmask", [128, 786])
    y = dout("y", [T, D])
    gst_p = dout("gst_p", [4, 256, 512])
    gst_s = dout("gst_s", [16, 4, 256, 512])
    sgv = dout("sgv", [128, D])
    xspill = nc.dram_tensor("xspill", [T, D], F32, kind="Internal").ap()
    sscr = nc.dram_tensor("sscr", [4, 256, 512], F32, kind="Internal").ap()

    S = Sched()
    es = ExitStack()
    arena_t = es.enter_context(nc.sbuf_tensor("arena", [128, ARENA_WORDS], F32))
    A = Arena(arena_t[:])
    ps = [es.enter_context(nc.psum_tensor("ps%d" % i, [128, 512], F32)) for i in range(8)]
    psb = [p_[:].bitcast(BF16) for p_ in ps]
    bank_ctr = [0]

    pinned = set()

    def bank():
        while True:
            b = bank_ctr[0] % 8
            bank_ctr[0] += 1
            if b not in pinned:
                return b

    def PS(b):
        return 'ps%d' % b

    ident = A.alloc([128], F32)
    identb = A.alloc([128], BF16)
    NWS = 4
    wbuf = [A.alloc([16, 512], BF16) for _ in range(NWS)]
    xT = A.alloc([KD, T], BF16)
    free_w = list(range(NWS))

    wq = []
    wq_pos = [0]

    def wq_add(parts):
        wq.append(dict(parts=parts, slot=None))
        return len(wq) - 1

    def wq_issue_pending():
        while free_w and wq_pos[0] < len(wq):
            e = wq[wq_pos[0]]
            wq_pos[0] += 1
            slot = free_w.pop(0)
            e['slot'] = slot
            for n_, (dst_fn, src_ap) in enumerate(e['parts']):
                dst = dst_fn(slot)
                S.add('pool', lambda h, dst=dst, src_ap=src_ap: h.dma_start(out=dst, in_=src_ap),
                      r=(['w%d' % slot] if n_ else []), w=['w%d' % slot], dma='w%d' % slot)

    def wq_get(idx):
        if wq[idx]['slot'] is None:
            wq_issue_pending()
        assert wq[idx]['slot'] is not None, ("weight block not issuable", idx, wq_pos[0], free_w)
        return wq[idx]['slot']

    def give_w(s_):
        free_w.append(s_)
        wq_issue_pending()

    def full_dst(a, b):
        def f(slot):
            dst = wbuf[slot]
            if (a, b) != (16, 512):
                dst = dst.rearrange("p a b -> p (a b)").rearrange("p (a b) -> p a b", a=a)
            return dst
        return f

    def XT(tc, k):
        return 'xT%d_%d' % (tc, k)

    def XTG(tc, g):
        return [XT(tc, 4 * g + i_) for i_ in range(4)]

    def XTT(tt, k):
        return [XT(3 * tt + i_, k) for i_ in range(3)]

    S.add('sp', lambda h: h.dma_start(out=ident, in_=ident_d), w=['ident'], dma='c_ident')
    S.add('pool', lambda h: h.dma_start(out=identb, in_=ident_d), w=['identb'], dma='c_identb')

    def load_w(slot, src_ap, dst=None):
        if dst is None:
            a, b = src_ap.shape[1], src_ap.shape[2]
            dst = wbuf[slot]
            if (a, b) != (16, 512):
                dst = dst.rearrange("p a b -> p (a b)").rearrange("p (a b) -> p a b", a=a)
        S.add('pool', lambda h: h.dma_start(out=dst, in_=src_ap), w=['w%d' % slot], dma='w%d' % slot)
        return dst

    cp_ctr = [0]

    def evac_copy(out, in_, r, w):
        cp_ctr[0] += 1
        if cp_ctr[0] % 2:
            S.add('act', lambda h: h.copy(out=out, in_=in_), r=r, w=w)
        else:
            S.add('dve', lambda h: h.tensor_copy(out=out, in_=in_), r=r, w=w)

    def transpose_f32_chunk(src, src_res, dstT, dst_res, tc):
        for g in range(4):
            b = bank()
            for i in range(4):
                kc = 4 * g + i
                S.add('pe', lambda h, kc=kc, i=i, b=b: h.transpose(
                    out=ps[b][:, i * 128:(i + 1) * 128], in_=src[:, kc * 128:(kc + 1) * 128], identity=ident),
                    r=[src_res[g] if isinstance(src_res, list) else src_res, 'ident'], w=[PS(b)])
            evac_copy(dstT[:, 4 * g:4 * g + 4, tc * 128:(tc + 1) * 128],
                      ps[b][:].rearrange("p (a t) -> p a t", a=4), r=[PS(b)], w=dst_res(tc, g))

    def transpose_bf16_chunk(src, src_res, dstT, dst_res, tc):
        for g in range(4):
            b = bank()
            for i in range(4):
                kc = 4 * g + i
                S.add('pe', lambda h, kc=kc, i=i, b=b: h.transpose(
                    out=psb[b][:, i * 128:(i + 1) * 128], in_=src[:, kc * 128:(kc + 1) * 128], identity=identb),
                    r=[src_res, 'identb'], w=[PS(b)])
            evac_copy(dstT[:, 4 * g:4 * g + 4, tc * 128:(tc + 1) * 128],
                      psb[b][:, 0:512].rearrange("p (a t) -> p a t", a=4), r=[PS(b)], w=dst_res(tc, g))

    def load_xT(src_dram, nchunks, dstT, dst_res_fn):
        stg = [A.alloc([D], F32) for _ in range(2)]
        for tc in range(nchunks):
            i = tc % 2
            S.add('sp', lambda h, tc=tc, i=i: h.dma_start(out=stg[i], in_=src_dram[tc * 128:(tc + 1) * 128, :]),
                  w=['xstg%d' % i], dma='xstg%d' % i)
            transpose_f32_chunk(stg[i], 'xstg%d' % i, dstT, dst_res_fn, tc)
        S.retire_prefix('xstg')
        A.release(*stg)

    xres_box = [None]

    def XR(tc, c):
        return 'xres%d_%d' % (tc, c)

    def XRA(tc):
        return ['xres%d_%d' % (tc, c) for c in range(4)]

    def alloc_xres(src_dram):
        xres = A.alloc([NT, D], F32)
        xres_box[0] = xres
        for tc in range(NT):
            S.add('sp', lambda h, tc=tc: h.dma_start(out=xres[:, tc, :], in_=src_dram[tc * 128:(tc + 1) * 128, :]),
                  w=XRA(tc), dma='xres%d' % tc)

    def free_xres():
        S.retire([n for tc in range(NT) for n in XRA(tc)])
        A.release(xres_box[0])
        xres_box[0] = None

    def out_proj(plan):
        xres = xres_box[0]
        for cb in range(4):
            s_ = wq_get(plan[cb])
            for tc in range(NT):
                b = bank()
                for k in range(KD):
                    S.add('pe', lambda h, k=k, tc=tc, b=b, s_=s_: h.matmul(
                        ps[b][:, :], lhsT=xT[:, k, tc * 128:(tc + 1) * 128], rhs=wbuf[s_][:, k, :],
                        start=(k == 0), stop=(k == KD - 1)),
                        r=[XT(tc, k), 'w%d' % s_], w=[PS(b)])
                dst = xres[:, tc, cb * 512:(cb + 1) * 512]
                S.add('dve', lambda h, dst=dst, b=b: h.scalar_tensor_tensor(
                    out=dst, in0=dst, scalar=ALPHA, op0=ALU.mult, in1=ps[b][:, :], op1=ALU.add),
                    r=[PS(b), XR(tc, cb)], w=[XR(tc, cb)])
            give_w(s_)

    def mlp(layer):
        xres = xres_box[0]
        hT = [A.alloc([4, T], BF16) for _ in range(2)]
        rtmp = [A.alloc([TT], BF16) for _ in range(2)]
        NFB = DFF // 512
        w1v = w1[layer].rearrange("(k p) f -> p k f", p=128)
        w2v = w2[layer].rearrange("(fb c p) d -> fb p c d", p=128, c=4)
        slots = {}

        plan = PLAN['mlp%d' % layer]

        def issue1(fb):
            slots[fb] = [wq_get(plan[fb][0]), None, None]

        def issue2(fb):
            s2 = wq_get(plan[fb][1])
            slots[fb][1] = s2
            slots[fb][2] = full_dst(4, 2048)(s2)

        def stage_a(fb):
            s1 = slots[fb][0]
            hs = fb % 2
            for fc in range(4):
                bs = [bank() for _ in range(3)]
                for k in range(KD):
                    for tt in range(3):
                        S.add('pe', lambda h, k=k, tt=tt, fc=fc, b=bs[tt]: h.matmul(
                            ps[b][:, 0:TT], lhsT=wbuf[s1][:, k, fc * 128:(fc + 1) * 128],
                            rhs=xT[:, k, tt * TT:(tt + 1) * TT], start=(k == 0), stop=(k == KD - 1)),
                            r=['w%d' % s1] + XTT(tt, k), w=[PS(bs[tt])])
                for tt in range(3):
                    rt = (fc * 3 + tt) % 2
                    S.add('act', lambda h, tt=tt, b=bs[tt], rt=rt: h.activation(
                        out=rtmp[rt], in_=ps[b][:, 0:TT], func=AF.Relu),
                        r=[PS(bs[tt])], w=['rtmp%d' % rt])
                    eng = 'pool' if tt == 1 else 'dve'
                    S.add(eng, lambda h, tt=tt, fc=fc, rt=rt: h.tensor_tensor(
                        out=hT[hs][:, fc, tt * TT:(tt + 1) * TT], in0=rtmp[rt], in1=rtmp[rt], op=ALU.mult),
                        r=['rtmp%d' % rt], w=['hT%d' % hs])

        def stage_b(fb):
            s2, d2 = slots[fb][1], slots[fb][2]
            hs = fb % 2
            for tc in range(NT):
                for cb in range(4):
                    b = bank()
                    for fc in range(4):
                        S.add('pe', lambda h, fc=fc, tc=tc, cb=cb, b=b: h.matmul(
                            ps[b][:, :], lhsT=hT[hs][:, fc, tc * 128:(tc + 1) * 128],
                            rhs=d2[:, fc, cb * 512:(cb + 1) * 512], start=(fc == 0), stop=(fc == 3)),
                            r=['hT%d' % hs, 'w%d' % s2], w=[PS(b)])
                    dst = xres[:, tc, cb * 512:(cb + 1) * 512]
                    if fb == 0:
                        S.add('dve', lambda h, dst=dst, b=b: h.scalar_tensor_tensor(
                            out=dst, in0=dst, scalar=ALPHA, op0=ALU.mult, in1=ps[b][:, :], op1=ALU.add),
                            r=[PS(b), XR(tc, cb)], w=[XR(tc, cb)])
                    else:
                        S.add('dve', lambda h, dst=dst, b=b: h.tensor_tensor(
                            out=dst, in0=dst, in1=ps[b][:, :], op=ALU.add),
                            r=[PS(b), XR(tc, cb)], w=[XR(tc, cb)])

        issue1(0)
        issue2(0)
        for fb in range(NFB + 1):
            if fb < NFB:
                if fb > 0:
                    issue1(fb)
                stage_a(fb)
                give_w(slots[fb][0])
            if fb >= 1:
                if fb - 1 > 0:
                    issue2(fb - 1)
                stage_b(fb - 1)
                give_w(slots[fb - 1][1])
        S.retire_prefix('hT', 'rtmp')
        A.release(*hT, *rtmp)

    def ln_stats(z, zr, st, sm, tag, eps):
        for c in range(4):
            S.add('dve', lambda h, c=c: h.bn_stats(out=st[:, c, :], in_=z[:, c * 512:(c + 1) * 512]),
                  r=[zr[c]], w=[tag + 'st'])
        S.add('dve', lambda h: h.bn_aggr(out=sm[:, 0:2], in_=st), r=[tag + 'st'], w=[tag + 'sm'])
        S.add('act', lambda h: h.activation(out=sm[:, 2:3], in_=sm[:, 1:2], func=AF.Ln, bias=eps_t[:, 0:1], scale=1.0),
              r=[tag + 'sm', 'eps'], w=[tag + 'sm'])
        S.add('act', lambda h: h.activation(out=sm[:, 3:4], in_=sm[:, 2:3], func=AF.Exp, scale=-0.5),
              r=[tag + 'sm'], w=[tag + 'sm'])
        S.add('dve', lambda h: h.tensor_scalar(out=sm[:, 4:5], in0=sm[:, 0:1], scalar1=sm[:, 3:4],
                                               scalar2=-1.0, op0=ALU.mult, op1=ALU.mult),
              r=[tag + 'sm'], w=[tag + 'sm'])

    def layer_norm(idx, g_dram, b_dram, final):
        xres = xres_box[0]
        gt = A.alloc([D], F32)
        bt = A.alloc([D], F32)
        gbc = A.alloc([2, 16], F32)
        st = [A.alloc([4, 6], F32) for _ in range(3)]
        sm = [A.alloc([8], F32) for _ in range(3)]
        S.add('sp', lambda h: h.dma_start(out=gt, in_=g_dram.partition_broadcast(128)), w=['ln_g'], dma='ln_g')
        S.add('sp', lambda h: h.dma_start(out=bt, in_=b_dram.partition_broadcast(128)), w=['ln_b'], dma='ln_b')
        S.add('sp', lambda h: h.dma_start(out=gbc, in_=lnT_d[idx]), w=['ln_c'], dma='ln_c')

        def st_stage(tc):
            i = tc % 3
            ln_stats(xres[:, tc, :], XRA(tc), st[i], sm[i], 'ln%d' % i, LN_EPS)

        def nrm_stage(tc):
            i = tc % 3
            z = xres[:, tc, :]
            for c in range(4):
                zr = XR(tc, c)
                zc = z[:, c * 512:(c + 1) * 512]
                S.add('act', lambda h, i=i, zc=zc: h.activation(out=zc, in_=zc, func=AF.Identity,
                                                             scale=sm[i][:, 3:4], bias=sm[i][:, 4:5]),
                      r=[zr, 'ln%dsm' % i], w=[zr])

        def t_stage(tc):
            z = xres[:, tc, :]
            for g in range(4):
                zr = XR(tc, g)
                b = bank()
                for q_ in range(4):
                    kc = 4 * g + q_
                    S.add('pe', lambda h, kc=kc, q_=q_, b=b: h.transpose(
                        out=ps[b][:, q_ * 128:(q_ + 1) * 128], in_=z[:, kc * 128:(kc + 1) * 128], identity=ident),
                        r=[zr, 'ident'], w=[PS(b)])
                for q_ in range(4):
                    kc = 4 * g + q_
                    dst = xT[:, kc, tc * 128:(tc + 1) * 128]
                    src = ps[b][:, q_ * 128:(q_ + 1) * 128]
                    if g % 2 == 0:
                        S.add('act', lambda h, kc=kc, dst=dst, src=src: h.activation(
                            out=dst, in_=src, func=AF.Identity, scale=gbc[:, 0, kc:kc + 1], bias=gbc[:, 1, kc:kc + 1]),
                            r=[PS(b), 'ln_c'], w=[XT(tc, kc)])
                    else:
                        S.add('dve', lambda h, kc=kc, dst=dst, src=src: h.tensor_scalar(
                            out=dst, in0=src, scalar1=gbc[:, 0, kc:kc + 1], scalar2=gbc[:, 1, kc:kc + 1],
                            op0=ALU.mult, op1=ALU.add), r=[PS(b), 'ln_c'], w=[XT(tc, kc)])

        def gb_stage(tc):
            z = xres[:, tc, :]
            for c in range(4):
                zr = XR(tc, c)
                zc = z[:, c * 512:(c + 1) * 512]
                S.add('pool', lambda h, c=c, zc=zc: h.tensor_tensor(out=zc, in0=zc, in1=gt[:, c * 512:(c + 1) * 512], op=ALU.mult),
                      r=[zr, 'ln_g'], w=[zr])
                S.add('dve', lambda h, c=c, zc=zc: h.tensor_tensor(out=zc, in0=zc, in1=bt[:, c * 512:(c + 1) * 512], op=ALU.add),
                      r=[zr, 'ln_b'], w=[zr])
            if final:
                S.add('sp', lambda h, tc=tc, z=z: h.dma_start(out=y[tc * 128:(tc + 1) * 128, :], in_=z),
                      r=XRA(tc), dma='yout%d' % (tc % 3))

        st_stage(0)
        st_stage(1)
        nrm_stage(0)
        for tc in range(NT):
            if tc + 2 < NT:
                st_stage(tc + 2)
            if tc + 1 < NT:
                nrm_stage(tc + 1)
            if not final:
                t_stage(tc)
            gb_stage(tc)
        S.retire_prefix('ln')
        A.release(gt, bt, gbc, *st, *sm)

    cm = A.alloc([786], F32)
    S.add('sp', lambda h: h.dma_start(out=cm, in_=cm_d), w=['cm'], dma='c_cm')
    eps_t = A.alloc([2], F32)
    S.add('dve', lambda h: h.memset(eps_t[:, 0:1], LN_EPS), w=['eps'])
    S.add('dve', lambda h: h.memset(eps_t[:, 1:2], HN_EPS), w=['eps'])

    def CM(i):
        return cm[:, i * 128:(i + 1) * 128]

    def gla_layer():
        winv = gla_w_in.rearrange("(k p) f -> p k f", p=128)
        wg = A.alloc([1024], F32)
        S.add('sp', lambda h: h.dma_start(out=wg[0:32, :], in_=wg_aug), w=['wg'], dma='c_wg')
        wg16 = A.alloc([16, 16], BF16)
        S.add('pool', lambda h: h.dma_start(out=wg16, in_=winv[:, :, 6144:6160]), w=['wg16'], dma='c_wg16')
        identb_r = ['identb']

        def gate_T(srcT, src_res_list, ntok, name):
            gTa = A.alloc([ntok], F32)
            S.add('dve', lambda h: h.memset(gTa[0:32, :], 1.0), w=[name])
            ntt = ntok // TT if ntok % TT == 0 else None
            tiles = [(i * TT, TT) for i in range(ntok // TT)] if ntt else [(i * 512, 512) for i in range(ntok // 512)]
            for (o, n) in tiles:
                b = bank()
                for k in range(KD):
                    if src_res_list is None:
                        rr = [XT(tc_, k) for tc_ in range(o // 128, (o + n - 1) // 128 + 1)]
                    else:
                        rr = sorted(set(src_res_list[(o // 128):((o + n - 1) // 128) + 1]))
                    S.add('pe', lambda h, k=k, b=b, o=o, n=n: h.matmul(
                        ps[b][0:16, 0:n], lhsT=wg16[:, k, :], rhs=srcT[:, k, o:o + n],
                        start=(k == 0), stop=(k == KD - 1)), r=['wg16'] + rr, w=[PS(b)])
                S.add('act', lambda h, b=b, o=o, n=n: h.copy(out=gTa[0:16, o:o + n], in_=ps[b][0:16, 0:n]),
                      r=[PS(b)], w=[name])
            return gTa

        class WS:
            pass

        sgS = A.alloc([512], F32)
        t1S = A.alloc([512], BF16)
        junkS = A.alloc([512], BF16)
        v3 = [A.alloc([512], BF16) for _ in range(4)]

        def make_ws():
            w_ = WS()
            w_.qk = A.alloc([512], BF16)
            w_.sg = sgS
            w_.t1 = t1S
            w_.srg = A.alloc([512], BF16)
            w_.nl = A.alloc([256], F32)
            w_.eend = A.alloc([256], F32)
            w_.kte = A.alloc([256], BF16)
            w_.epos = A.alloc([2, 128], F32)
            w_.eneg = A.alloc([2, 128], F32)
            w_.qdT = A.alloc([2, 128], BF16)
            w_.kiT = A.alloc([2, 128], BF16)
            w_.scm = A.alloc([128], BF16)
            w_.junk = junkS
            w_.sm = A.alloc([8], F32)
            w_.all = [w_.qk, w_.srg, w_.nl, w_.eend, w_.kte, w_.epos, w_.eneg,
                      w_.qdT, w_.kiT, w_.scm, w_.sm]
            return w_

        xpT = A.alloc([KD, NPF * 128], BF16)
        load_xT(xp, NPF, xpT, lambda tc, g: ['xpT%d' % tc])
        XPT = ['xpT%d' % tc for tc in range(NPF)]
        gTp = gate_T(xpT, XPT, NPF * 128, 'gTp')
        Sf = A.alloc([2, 512], F32)
        wsets = [make_ws() for _ in range(3)]
        dec = [A.alloc([4], F32) for _ in range(2)]

        def gate_common(W, i, gsrc, gres, c, h_, tri_u):
            R = 'g%d' % i
            bg = bank()
            S.add('pe', lambda h, bg=bg: h.matmul(ps[bg][:, 0:256], lhsT=gsrc[0:32, c * 128:(c + 1) * 128],
                                                   rhs=wg[0:32, h_ * 256:(h_ + 1) * 256], start=True, stop=True),
                  r=['wg', gres], w=[PS(bg)])
            S.add('act', lambda h, bg=bg: h.activation(out=W.nl, in_=ps[bg][:, 0:256], func=AF.Exp, scale=-1.0),
                  r=[PS(bg)], w=[R + 'nl'])
            S.add('act', lambda h: h.activation(out=W.nl, in_=W.nl, func=AF.Ln, bias=cm[:, 785:786], scale=1.0),
                  r=[R + 'nl', 'cm'], w=[R + 'nl'])
            brc = bank()
            S.add('pe', lambda h, brc=brc: h.matmul(ps[brc][:, 0:256], lhsT=tri_u, rhs=W.nl, start=True, stop=True),
                  r=['cm', R + 'nl'], w=[PS(brc)])
            S.add('act', lambda h, brc=brc: h.activation(out=W.eend, in_=ps[brc][:, 0:256], func=AF.Exp),
                  r=[PS(brc)], w=[R + 'eend'])
            S.add('pool', lambda h: h.tensor_tensor(out=W.kte, in0=W.qk[:, 256:512], in1=W.eend, op=ALU.mult),
                  r=[R + 'qk', R + 'eend'], w=[R + 'kte'])

        def pf_Pk(h_, c, sk):
            W, R = wsets[c % 2], 'g%d' % (c % 2)
            bk = bank()
            for k in range(KD):
                S.add('pe', lambda h, k=k: h.matmul(
                    ps[bk][:, 0:256], lhsT=xpT[:, k, c * 128:(c + 1) * 128], rhs=wbuf[sk][:, k, 256:512],
                    start=(k == 0), stop=(k == KD - 1)), r=[XPT[c], 'w%d' % sk], w=[PS(bk)])
            S.add('dve', lambda h: h.tensor_copy(out=W.qk[:, 256:512], in_=ps[bk][:, 0:256]), r=[PS(bk)], w=[R + 'qk'])

        def pf_Pv(h_, c, sv):
            vv, VR = v3[c % 3], 'gv%d' % (c % 3)
            bv_ = bank()
            for k in range(KD):
                S.add('pe', lambda h, k=k: h.matmul(
                    ps[bv_][:, :], lhsT=xpT[:, k, c * 128:(c + 1) * 128], rhs=wbuf[sv][:, k, :],
                    start=(k == 0), stop=(k == KD - 1)), r=[XPT[c], 'w%d' % sv], w=[PS(bv_)])
            S.add('act', lambda h: h.copy(out=vv, in_=ps[bv_][:, :]), r=[PS(bv_)], w=[VR])

        def pf_G01(h_, c):
            W, R = wsets[c % 2], 'g%d' % (c % 2)
            bg = bank()
            S.add('pe', lambda h: h.matmul(ps[bg][:, 0:256], lhsT=gTp[0:32, c * 128:(c + 1) * 128],
                                           rhs=wg[0:32, h_ * 256:(h_ + 1) * 256], start=True, stop=True),
                  r=['wg', 'gTp'], w=[PS(bg)])
            S.add('act', lambda h: h.activation(out=W.nl, in_=ps[bg][:, 0:256], func=AF.Exp, scale=-1.0),
                  r=[PS(bg)], w=[R + 'nl'])
            S.add('act', lambda h: h.activation(out=W.nl, in_=W.nl, func=AF.Ln, bias=cm[:, 785:786], scale=1.0),
                  r=[R + 'nl', 'cm'], w=[R + 'nl'])

        def pf_G2(h_, c):
            W, R, i = wsets[c % 2], 'g%d' % (c % 2), c % 2
            brc = bank()
            S.add('pe', lambda h: h.matmul(ps[brc][:, 0:256], lhsT=CM(C_TRIU_S), rhs=W.nl, start=True, stop=True),
                  r=['cm', R + 'nl'], w=[PS(brc)])
            bb = bank()
            for j in range(2):
                S.add('pe', lambda h, j=j: h.matmul(
                    ps[bb][:, 2 * j:2 * j + 2], lhsT=W.nl[:, j * 128:(j + 1) * 128], rhs=cm[:, 784:786],
                    start=True, stop=True), r=[R + 'nl', 'cm'], w=[PS(bb)])
            S.add('act', lambda h: h.activation(out=W.eend, in_=ps[brc][:, 0:256], func=AF.Exp),
                  r=[PS(brc)], w=[R + 'eend'])
            S.add('act', lambda h: h.activation(out=dec[i], in_=ps[bb][:, 0:4], func=AF.Exp),
                  r=[PS(bb)], w=['dec%d' % i])
            S.add('pool', lambda h: h.tensor_tensor(out=W.kte, in0=W.qk[:, 256:512], in1=W.eend, op=ALU.mult),
                  r=[R + 'qk', R + 'eend'], w=[R + 'kte'])

        def pf_U(h_, c):
            W, R, i = wsets[c % 2], 'g%d' % (c % 2), c % 2
            vv, VR = v3[c % 3], 'gv%d' % (c % 3)
            for j in range(2):
                bu = bank()
                S.add('pe', lambda h, j=j, bu=bu: h.matmul(
                    ps[bu][:, :], lhsT=W.kte[:, j * 128:(j + 1) * 128], rhs=vv, start=True, stop=True),
                    r=[R + 'kte', VR], w=[PS(bu)])
                S.add('dve', lambda h, j=j, bu=bu: h.scalar_tensor_tensor(
                    out=Sf[:, j, :], in0=Sf[:, j, :], scalar=dec[i][:, 2 * j:2 * j + 1], op0=ALU.mult,
                    in1=ps[bu][:, :], op1=ALU.add), r=[PS(bu), 'dec%d' % i, 'Sf'], w=['Sf'])

        for h_ in range(4):
            sk = wq_get(PLAN['pf'][h_][0])
            sv = wq_get(PLAN['pf'][h_][1])
            S.add('dve', lambda h: h.memset(Sf, 0.0), w=['Sf'])
            for c in range(-2, NPF):
                if 0 <= c + 2 < NPF:
                    pf_Pk(h_, c + 2, sk)
                if 0 <= c + 1 < NPF:
                    pf_G2(h_, c + 1)
                if 0 <= c + 2 < NPF:
                    pf_Pv(h_, c + 2, sv)
                if c >= 0:
                    pf_U(h_, c)
                if 0 <= c + 2 < NPF:
                    pf_G01(h_, c + 2)
            give_w(sk)
            give_w(sv)
            S.add('sp', lambda h, h_=h_: h.dma_start(out=sscr[h_].rearrange("(j p) v -> p j v", p=128), in_=Sf),
                  r=['Sf'], w=['sscr%d' % h_], dma='sscr%d' % h_)
        S.retire(XPT + ['gTp'] + ['dec0', 'dec1'])
        A.release(xpT, gTp, *dec)

        gated = A.alloc([NT, D], BF16)
        gTm = gate_T(xT, None, T, 'gTm')
        gng = A.alloc([512], F32)
        Sb = A.alloc([2, 512], BF16)
        s0 = [A.alloc([2, 512], F32) for _ in range(3)]
        Qm = [A.alloc([2, 128], F32) for _ in range(2)]
        kteM = [A.alloc([256], BF16) for _ in range(2)]

        class Ck:
            pass

        def mk(h_, c, slots):
            ck = Ck()
            ck.h, ck.c, ck.slots = h_, c, slots
            ck.sample = (c == NT - 1)
            if ck.sample:
                ck.W, ck.R, ck.v, ck.VR = wsets[2], 'g2', v3[3], 'gv3'
            else:
                ck.W, ck.R, ck.v, ck.VR = wsets[c % 2], 'g%d' % (c % 2), v3[c % 3], 'gv%d' % (c % 3)
            return ck

        def proj(ck, slot):
            b = bank()
            c = ck.c
            for k in range(KD):
                S.add('pe', lambda h, k=k: h.matmul(
                    ps[b][:, :], lhsT=xT[:, k, c * 128:(c + 1) * 128], rhs=wbuf[slot][:, k, :],
                    start=(k == 0), stop=(k == KD - 1)), r=[XT(c, k), 'w%d' % slot], w=[PS(b)])
            return b

        def P_qk(ck):
            W, R = ck.W, ck.R
            bq = proj(ck, ck.slots[0])
            S.add('act', lambda h: h.mul(out=W.qk[:, 0:256], in_=ps[bq][:, 0:256], mul=1.0 / 16), r=[PS(bq)], w=[R + 'qk'])
            S.add('act', lambda h: h.copy(out=W.qk[:, 256:512], in_=ps[bq][:, 256:512]), r=[PS(bq)], w=[R + 'qk'])

        def P_v(ck):
            bv_ = proj(ck, ck.slots[1])
            S.add('act', lambda h: h.copy(out=ck.v, in_=ps[bv_][:, :]), r=[PS(bv_)], w=[ck.VR])

        def P_r(ck):
            W, R, h_ = ck.W, ck.R, ck.h
            br = proj(ck, ck.slots[2])
            S.add('act', lambda h: h.activation(out=W.sg, in_=ps[br][:, :], func=AF.Exp, scale=-1.0), r=[PS(br)], w=['gsg'])
            S.add('act', lambda h: h.activation(out=W.sg, in_=W.sg, func=AF.Ln, bias=cm[:, 785:786], scale=1.0),
                  r=['gsg', 'cm'], w=['gsg'])
            S.add('act', lambda h: h.activation(out=W.sg, in_=W.sg, func=AF.Exp, scale=-1.0), r=['gsg'], w=['gsg'])
            S.add('dve', lambda h: h.tensor_tensor(out=W.t1, in0=ps[br][:, :], in1=W.sg, op=ALU.mult),
                  r=[PS(br), 'gsg'], w=['gt1'])
            S.add('pool', lambda h: h.tensor_tensor(out=W.srg, in0=W.t1, in1=gng, op=ALU.mult),
                  r=['gt1', 'gng'], w=[R + 'srg'])

        def G01(ck):
            W, R, c, h_ = ck.W, ck.R, ck.c, ck.h
            bg = bank()
            S.add('pe', lambda h: h.matmul(ps[bg][:, 0:256], lhsT=gTm[0:32, c * 128:(c + 1) * 128],
                                           rhs=wg[0:32, h_ * 256:(h_ + 1) * 256], start=True, stop=True),
                  r=['wg', 'gTm'], w=[PS(bg)])
            S.add('act', lambda h: h.activation(out=W.nl, in_=ps[bg][:, 0:256], func=AF.Exp, scale=-1.0),
                  r=[PS(bg)], w=[R + 'nl'])
            S.add('act', lambda h: h.activation(out=W.nl, in_=W.nl, func=AF.Ln, bias=cm[:, 785:786], scale=1.0),
                  r=[R + 'nl', 'cm'], w=[R + 'nl'])

        def G23(ck):
            W, R = ck.W, ck.R
            tri_u = CM(C_BDU_S) if ck.sample else CM(C_TRIU_S)
            tri = CM(C_BD_S) if ck.sample else CM(C_TRI_S)
            brc = bank()
            S.add('pe', lambda h: h.matmul(ps[brc][:, 0:256], lhsT=tri_u, rhs=W.nl, start=True, stop=True),
                  r=['cm', R + 'nl'], w=[PS(brc)])
            bbt = bank()
            for j in range(2):
                S.add('pe', lambda h, j=j: h.matmul(ps[bbt][:, j * 128:(j + 1) * 128], lhsT=W.nl[:, j * 128:(j + 1) * 128],
                                                     rhs=tri, start=True, stop=True), r=[R + 'nl', 'cm'], w=[PS(bbt)])
            S.add('act', lambda h: h.activation(out=W.eend, in_=ps[brc][:, 0:256], func=AF.Exp),
                  r=[PS(brc)], w=[R + 'eend'])
            S.add('act', lambda h: h.activation(out=W.epos.rearrange("p a b -> p (a b)"), in_=ps[bbt][:, 0:256], func=AF.Exp),
                  r=[PS(bbt)], w=[R + 'epos'])
            S.add('act', lambda h: h.activation(out=W.eneg.rearrange("p a b -> p (a b)"), in_=ps[bbt][:, 0:256], func=AF.Exp, scale=-1.0),
                  r=[PS(bbt)], w=[R + 'eneg'])
            S.add('pool', lambda h: h.tensor_tensor(out=W.kte, in0=W.qk[:, 256:512], in1=W.eend, op=ALU.mult),
                  r=[R + 'qk', R + 'eend'], w=[R + 'kte'])

        def G45(ck):
            W, R = ck.W, ck.R
            btr = bank()
            for j in range(4):
                S.add('pe', lambda h, j=j: h.transpose(out=psb[btr][:, j * 128:(j + 1) * 128],
                                                        in_=W.qk[:, j * 128:(j + 1) * 128], identity=identb),
                      r=[R + 'qk', 'identb'], w=[PS(btr)])
            S.add('dve', lambda h: h.tensor_tensor(out=W.qdT.rearrange("p a b -> p (a b)"), in0=psb[btr][:, 0:256],
                                                   in1=W.epos.rearrange("p a b -> p (a b)"), op=ALU.mult),
                  r=[PS(btr), R + 'epos'], w=[R + 'qdT'])
            S.add('dve', lambda h: h.tensor_tensor(out=W.kiT.rearrange("p a b -> p (a b)"), in0=psb[btr][:, 256:512],
                                                   in1=W.eneg.rearrange("p a b -> p (a b)"), op=ALU.mult),
                  r=[PS(btr), R + 'eneg'], w=[R + 'kiT'])

        def B12(ck):
            W, R = ck.W, ck.R
            bs = bank()
            for j in range(2):
                S.add('pe', lambda h, j=j: h.matmul(ps[bs][:, 0:128], lhsT=W.kiT[:, j, :], rhs=W.qdT[:, j, :],
                                                     start=(j == 0), stop=(j == 1)), r=[R + 'kiT', R + 'qdT'], w=[PS(bs)])
            m01 = CM(C_BD01) if ck.sample else CM(C_TRI01)
            S.add('dve', lambda h: h.tensor_tensor(out=W.scm, in0=ps[bs][:, 0:128], in1=m01, op=ALU.mult),
                  r=[PS(bs), 'cm'], w=[R + 'scm'])

        def B3(ck):
            W, R, c, h_ = ck.W, ck.R, ck.c, ck.h
            vv, VR = ck.v, ck.VR
            bo = bank()
            pinned.add(bo)
            ck.bo = bo
            S.add('pe', lambda h: h.matmul(ps[bo][:, :], lhsT=W.scm, rhs=vv, start=True, stop=False),
                  r=[R + 'scm', VR], w=[PS(bo)])
            if not ck.sample:
                for j in range(2):
                    S.add('pe', lambda h, j=j: h.matmul(ps[bo][:, :], lhsT=W.qdT[:, j, :], rhs=Sb[:, j, :],
                                                         start=False, stop=(j == 1)), r=[R + 'qdT', 'Sb'], w=[PS(bo)])
                for j in range(2):
                    bu = bank()
                    S.add('pe', lambda h, j=j, bu=bu: h.matmul(ps[bu][:, :], lhsT=W.kte[:, j * 128:(j + 1) * 128], rhs=vv,
                                                               start=True, stop=True), r=[R + 'kte', VR], w=[PS(bu)])
                    S.add('dve', lambda h, j=j, bu=bu: h.scalar_tensor_tensor(
                        out=Sf[:, j, :], in0=Sf[:, j, :], scalar=W.epos[:, j, 127:128], op0=ALU.mult,
                        in1=ps[bu][:, :], op1=ALU.add), r=[PS(bu), R + 'epos', 'Sf'], w=['Sf'])
                    S.add('act', lambda h, j=j: h.copy(out=Sb[:, j, :], in_=Sf[:, j, :]), r=['Sf'], w=['Sb'])
                if c == NT - 2:
                    S.add('sp', lambda h: h.dma_start(out=gst_p[h_].rearrange("(j p) v -> p j v", p=128), in_=Sf),
                          r=['Sf'], dma='gstp')

        def s0_load(ck, q_):
            sb_ = q_ % 3
            h_ = ck.h
            S.add('sp', lambda h: h.dma_start(out=s0[sb_], in_=st_in[q_, h_].rearrange("(j p) v -> p j v", p=128)),
                  w=['s0_%d' % sb_], dma='s0_%d' % sb_)

        def unit(ck, q_):
            if DBG_NOUNIT:
                return
            W, R, h_ = ck.W, ck.R, ck.h
            vv, VR, bo = ck.v, ck.VR, ck.bo
            sb_ = q_ % 3
            qb_ = q_ % 2
            if q_ + 1 < 16:
                s0_load(ck, q_ + 1)
            S.add('pool', lambda h: h.memset(Qm[qb_], 0.0), w=['Qm%d' % qb_])
            S.add('pool', lambda h: h.tensor_copy(out=Qm[qb_][:, :, 8 * q_:8 * q_ + 8], in_=W.qdT[:, :, 8 * q_:8 * q_ + 8]),
                  r=[R + 'qdT'], w=['Qm%d' % qb_])
            for j in range(2):
                S.add('pe', lambda h, j=j: h.matmul(
                    ps[bo][:, :], lhsT=Qm[qb_][:, j, :], rhs=s0[sb_][:, j, :], start=False,
                    stop=(q_ == 15 and j == 1)), r=['Qm%d' % qb_, 's0_%d' % sb_], w=[PS(bo)])
            S.add('dve', lambda h: h.tensor_scalar(
                out=kteM[qb_], in0=W.kte, scalar1=cm[:, 768 + q_:769 + q_], scalar2=None, op0=ALU.mult),
                r=[R + 'kte', 'cm'], w=['kteM%d' % qb_])
            for j in range(2):
                bu = bank()
                S.add('pe', lambda h, j=j, bu=bu: h.matmul(
                    ps[bu][:, :], lhsT=kteM[qb_][:, j * 128:(j + 1) * 128], rhs=vv, start=True, stop=True),
                    r=['kteM%d' % qb_, VR], w=[PS(bu)])
                S.add('dve', lambda h, j=j, bu=bu: h.scalar_tensor_tensor(
                    out=s0[sb_][:, j, :], in0=s0[sb_][:, j, :], scalar=W.epos[:, j, 8 * q_ + 7:8 * q_ + 8],
                    op0=ALU.mult, in1=ps[bu][:, :], op1=ALU.add),
                    r=[PS(bu), R + 'epos', 's0_%d' % sb_], w=['s0_%d' % sb_])
            S.add('sp', lambda h: h.dma_start(out=gst_s[q_, h_].rearrange("(j p) v -> p j v", p=128), in_=s0[sb_]),
                  r=['s0_%d' % sb_], dma='s0_%d' % sb_)

        def B4(ck):
            W, R, c, h_, bo = ck.W, ck.R, ck.c, ck.h, ck.bo
            S.add('act', lambda h: h.activation(out=W.junk, in_=ps[bo][:, :], func=AF.Square, accum_out=W.sm[:, 0:1]),
                  r=[PS(bo)], w=['gjunk', R + 'sm'])
            S.add('act', lambda h: h.activation(out=W.sm[:, 1:2], in_=W.sm[:, 0:1], func=AF.Ln, bias=eps_t[:, 1:2], scale=1.0 / 512),
                  r=[R + 'sm', 'eps'], w=[R + 'sm'])
            S.add('act', lambda h: h.activation(out=W.sm[:, 2:3], in_=W.sm[:, 1:2], func=AF.Exp, scale=-0.5),
                  r=[R + 'sm'], w=[R + 'sm'])
            S.add('dve', lambda h: h.scalar_tensor_tensor(out=gated[:, c, h_ * 512:(h_ + 1) * 512], in0=ps[bo][:, :],
                                                          scalar=W.sm[:, 2:3], op0=ALU.mult, in1=W.srg, op1=ALU.mult),
                  r=[PS(bo), R + 'sm', R + 'srg'], w=['gated%d' % c])
            pinned.discard(bo)

        def issue_head(h_):
            return tuple(wq_get(i_) for i_ in PLAN['main'][h_])

        for h_ in range(4):
            slots = issue_head(h_)
            S.add('sp', lambda h, h_=h_: h.dma_start(out=gng, in_=gla_ng[:, h_ * 512:(h_ + 1) * 512].partition_broadcast(128)),
                  w=['gng'], dma='c_gng')
            S.add('sp', lambda h, h_=h_: h.dma_start(out=Sf, in_=sscr[h_].rearrange("(j p) v -> p j v", p=128)),
                  r=['sscr%d' % h_], w=['Sf'], dma='sfl')
            S.add('act', lambda h: h.copy(out=Sb.rearrange("p a b -> p (a b)"), in_=Sf.rearrange("p a b -> p (a b)")),
                  r=['Sf'], w=['Sb'])
            cks = {c: mk(h_, c, slots) for c in range(NT)}
            ck8 = cks[NT - 1]
            s0_load(ck8, 0)
            P_qk(ck8)
            P_v(ck8)
            P_r(ck8)
            G01(ck8)
            G23(ck8)
            G45(ck8)
            B12(ck8)
            B3(ck8)
            NPR = NT - 1
            for c in range(-2, NPR):
                p = cks.get(c + 2) if c + 2 < NPR else None
                g = cks.get(c + 1) if c + 1 < NPR else None
                b_ = cks.get(c) if c >= 0 else None
                if p:
                    P_qk(p)
                if b_:
                    B12(b_)
                if g:
                    G23(g)
                if p:
                    P_v(p)
                if b_:
                    B3(b_)
                    B4(b_)
                    unit(ck8, 2 * c)
                if g:
                    G45(g)
                if p:
                    P_r(p)
                    G01(p)
                if b_:
                    unit(ck8, 2 * c + 1)
            B4(ck8)
            for s_ in slots:
                give_w(s_)
        for tc in range(NT):
            transpose_bf16_chunk(gated[:, tc, :], 'gated%d' % tc, xT, XTG, tc)
        S.retire_prefix('g0', 'g1', 'g2', 'gsg', 'gt1', 'gjunk', 'gv', 'gated', 'gTm', 'gng', 'Sf', 'Sb', 's0_', 'Qm', 'kteM', 'wg')
        A.release(gated, gTm, gng, Sf, Sb, *s0, *Qm, *kteM, wg, wg16, *v3, sgS, t1S, junkS)
        for w_ in wsets:
            A.release(*w_.all)

    def sg_layer():
        winv = sg_w_in.rearrange("(k p) f -> p k f", p=128)
        uT = A.alloc([KD, T], BF16)
        binT = A.alloc([16], F32)
        S.add('sp', lambda h: h.dma_start(out=binT, in_=sg_binT), w=['binT'], dma='c_binT')
        wtmp = A.alloc([8, 128], F32)
        WT = [A.alloc([8, 128], BF16) for _ in range(2)]
        for v_ in range(2):
            src = sg_wsT if v_ == 0 else sg_wsTs
            S.add('sp', lambda h, src=src: h.dma_start(out=wtmp, in_=src), w=['wtmp'], dma='c_wtmp')
            m01 = CM(C_TRI01) if v_ == 0 else CM(C_BD01)
            for g in range(8):
                S.add('dve', lambda h, g=g, v_=v_, m01=m01: h.tensor_tensor(out=WT[v_][:, g, :], in0=wtmp[:, g, :], in1=m01, op=ALU.mult),
                      r=['wtmp', 'cm'], w=['WT%d' % v_])
        S.retire(['wtmp'])
        A.release(wtmp)
        bsp1 = A.alloc([8, 128], F32)
        bsp = [bsp1, bsp1]

        def load_bsp(v_):
            S.add('sp', lambda h: h.dma_start(out=bsp1, in_=sg_bsp[v_].partition_broadcast(128)),
                  w=['bsp'], dma='c_bsp')

        load_bsp(0)
        bv = A.alloc([D], F32)
        vg = A.alloc([D], F32)
        vb = A.alloc([D], F32)
        S.add('sp', lambda h: h.dma_start(out=bv, in_=sg_bv.partition_broadcast(128)), w=['sgbv'], dma='c_sgbv')
        S.add('sp', lambda h: h.dma_start(out=vg, in_=sg_vg.partition_broadcast(128)), w=['sgvg'], dma='c_sgvg')
        S.add('sp', lambda h: h.dma_start(out=vb, in_=sg_vb.partition_broadcast(128)), w=['sgvb'], dma='c_sgvb')

        for cb in range(4):
            s_ = wq_get(PLAN['sg_u'][cb])
            for fc in range(4):
                bs = [bank() for _ in range(3)]
                for k in range(KD):
                    for tt in range(3):
                        S.add('pe', lambda h, k=k, tt=tt, fc=fc, b=bs[tt], s_=s_: h.matmul(
                            ps[b][:, 0:TT], lhsT=wbuf[s_][:, k, fc * 128:(fc + 1) * 128],
                            rhs=xT[:, k, tt * TT:(tt + 1) * TT], start=(k == 0), stop=(k == KD - 1)),
                            r=['w%d' % s_] + XTT(tt, k), w=[PS(bs[tt])])
                f_ = cb * 4 + fc
                for tt in range(3):
                    S.add('act', lambda h, tt=tt, b=bs[tt], f_=f_: h.activation(
                        out=uT[:, f_, tt * TT:(tt + 1) * TT], in_=ps[b][:, 0:TT], func=AF.Gelu,
                        bias=binT[:, f_:f_ + 1], scale=1.0), r=[PS(bs[tt]), 'binT'], w=['uT'])
            give_w(s_)
        vs = [wq_get(PLAN['sg_v'][cb]) for cb in range(4)]
        vt = [A.alloc([D], F32) for _ in range(2)]
        vnb = [A.alloc([D], BF16) for _ in range(2)]
        tmp = [A.alloc([512], F32) for _ in range(2)]
        mt = [A.alloc([4, 128], F32) for _ in range(2)]
        st = [A.alloc([4, 6], F32) for _ in range(2)]
        sm = [A.alloc([8], F32) for _ in range(2)]

        def p_stage(tc):
            i = tc % 2
            VT = 'sv%dvt' % i
            bs = [bank() for _ in range(4)]
            for k in range(KD):
                for cb in range(4):
                    S.add('pe', lambda h, k=k, cb=cb, b=bs[cb], tc=tc: h.matmul(
                        ps[b][:, :], lhsT=xT[:, k, tc * 128:(tc + 1) * 128], rhs=wbuf[vs[cb]][:, k, :],
                        start=(k == 0), stop=(k == KD - 1)), r=[XT(tc, k), 'w%d' % vs[cb]], w=[PS(bs[cb])])
            for cb in range(4):
                sl = slice(cb * 512, (cb + 1) * 512)
                S.add('dve', lambda h, b=bs[cb], sl=sl, i=i: h.tensor_tensor(out=vt[i][:, sl], in0=ps[b][:, :], in1=bv[:, sl], op=ALU.add),
                      r=[PS(bs[cb]), 'sgbv'], w=[VT + str(cb)])
                S.add('act', lambda h, sl=sl, i=i: h.activation(out=vt[i][:, sl], in_=vt[i][:, sl], func=AF.Gelu),
                      r=[VT + str(cb)], w=[VT + str(cb)])

        def l_stage(tc):
            i = tc % 2
            sample = (tc == NT - 1)
            V = 'sv%d' % i
            VT = 'sv%dvt' % i
            ln_stats(vt[i], [VT + str(c_) for c_ in range(4)], st[i], sm[i], V, LN_EPS)
            for c in range(4):
                j = (tc * 4 + c) % 2
                sl = slice(c * 512, (c + 1) * 512)
                S.add('act', lambda h, i=i, sl=sl, j=j: h.activation(out=tmp[j], in_=vt[i][:, sl], func=AF.Identity,
                                                                  scale=sm[i][:, 3:4], bias=sm[i][:, 4:5]),
                      r=[VT + str(c), V + 'sm'], w=['svtmp%d' % j])
                S.add('pool', lambda h, j=j, sl=sl: h.tensor_tensor(out=tmp[j], in0=tmp[j], in1=vg[:, sl], op=ALU.mult),
                      r=['svtmp%d' % j, 'sgvg'], w=['svtmp%d' % j])
                if sample:
                    S.add('dve', lambda h, j=j, sl=sl, i=i: h.tensor_tensor(out=vt[i][:, sl], in0=tmp[j], in1=vb[:, sl], op=ALU.add),
                          r=['svtmp%d' % j, 'sgvb', VT + str(c)], w=[VT + str(c)])
                    S.add('act', lambda h, sl=sl, i=i: h.copy(out=vnb[i][:, sl], in_=vt[i][:, sl]), r=[VT + str(c)], w=[V + 'vnb' + str(c)])
                else:
                    S.add('dve', lambda h, j=j, sl=sl, i=i: h.tensor_tensor(out=vnb[i][:, sl], in0=tmp[j], in1=vb[:, sl], op=ALU.add),
                          r=['svtmp%d' % j, 'sgvb'], w=[V + 'vnb' + str(c)])
            if sample:
                S.add('sp', lambda h, i=i: h.dma_start(out=sgv, in_=vt[i]), r=[VT + str(c_) for c_ in range(4)], dma='sgvout')

        def m_stage(tc):
            i = tc % 2
            sample = (tc == NT - 1)
            V = 'sv%d' % i
            v_ = 1 if sample else 0
            for dg in range(4):
                b = bank()
                for q_ in range(4):
                    dc = dg * 4 + q_
                    S.add('pe', lambda h, q_=q_, dc=dc, b=b, i=i, v_=v_: h.matmul(
                        ps[b][:, q_ * 128:(q_ + 1) * 128], lhsT=vnb[i][:, dc * 128:(dc + 1) * 128],
                        rhs=WT[v_][:, dc // 2, :], start=True, stop=True), r=[V + 'vnb' + str(dg), 'WT%d' % v_], w=[PS(b)])
                mi = dg % 2
                bias_ap = bsp[v_][:, 2 * dg:2 * dg + 2, :].unsqueeze(2).broadcast_to([128, 2, 2, 128])
                S.add('dve', lambda h, b=b, mi=mi, bias_ap=bias_ap: h.tensor_tensor(
                    out=mt[mi].rearrange("p (a c) t -> p a c t", a=2), in0=ps[b][:, :].rearrange("p (a c t) -> p a c t", a=2, c=2),
                    in1=bias_ap, op=ALU.add), r=[PS(b), 'bsp'], w=['mt%d' % mi])
                S.add('pool', lambda h, mi=mi, dg=dg, tc=tc: h.tensor_tensor(
                    out=xT[:, 4 * dg:4 * dg + 4, tc * 128:(tc + 1) * 128], in0=mt[mi],
                    in1=uT[:, 4 * dg:4 * dg + 4, tc * 128:(tc + 1) * 128], op=ALU.mult),
                    r=['mt%d' % mi, 'uT'], w=XTG(tc, dg))

        p_stage(0)
        p_stage(1)
        l_stage(0)
        for tc in range(NT):
            if tc + 2 < NT:
                p_stage(tc + 2)
                if tc + 2 == NT - 1:
                    for s_ in vs:
                        give_w(s_)
            if tc + 1 < NT:
                l_stage(tc + 1)
            if tc == NT - 1:
                load_bsp(1)
            m_stage(tc)
        S.retire_prefix('uT', 'binT', 'wtmp', 'WT', 'bsp', 'sgbv', 'sgvg', 'sgvb', 'sv', 'mt')
        A.release(uT, binT, *WT, bsp1, bv, vg, vb, *vt, *vnb, *tmp, *mt, *st, *sm)

    winv_g = gla_w_in.rearrange("(k p) f -> p k f", p=128)
    winv_s = sg_w_in.rearrange("(k p) f -> p k f", p=128)
    PLAN = {}
    if "gla" in phases:
        PLAN['pf'] = []
        for h_ in range(4):
            ik = wq_add([(lambda sl: wbuf[sl][:, :, 256:512], winv_g[:, :, 1024 + h_ * 256:1024 + (h_ + 1) * 256])])
            iv = wq_add([(full_dst(16, 512), winv_g[:, :, 2048 + h_ * 512:2048 + (h_ + 1) * 512])])
            PLAN['pf'].append((ik, iv))
        PLAN['main'] = []
        for h_ in range(4):
            iqk = wq_add([(lambda sl: wbuf[sl][:, :, 0:256], winv_g[:, :, h_ * 256:(h_ + 1) * 256]),
                          (lambda sl: wbuf[sl][:, :, 256:512], winv_g[:, :, 1024 + h_ * 256:1024 + (h_ + 1) * 256])])
            iv = wq_add([(full_dst(16, 512), winv_g[:, :, 2048 + h_ * 512:2048 + (h_ + 1) * 512])])
            ir = wq_add([(full_dst(16, 512), winv_g[:, :, 4096 + h_ * 512:4096 + (h_ + 1) * 512])])
            PLAN['main'].append((iqk, iv, ir))
        wv_ = gla_w_out.rearrange("(k p) d -> p k d", p=128)
        PLAN['gla_out'] = [wq_add([(full_dst(16, 512), wv_[:, :, cb * 512:(cb + 1) * 512])]) for cb in range(4)]

    def plan_mlp(layer):
        w1v = w1[layer].rearrange("(k p) f -> p k f", p=128)
        w2v = w2[layer].rearrange("(fb c p) d -> fb p c d", p=128, c=4)
        out = []
        for fb in range(DFF // 512):
            i1 = wq_add([(full_dst(16, 512), w1v[:, :, fb * 512:(fb + 1) * 512])])
            i2 = wq_add([(full_dst(4, 2048), w2v[fb])])
            out.append((i1, i2))
        return out

    if "mlp0" in phases:
        PLAN['mlp0'] = plan_mlp(0)
    if "sg" in phases:
        PLAN['sg_u'] = [wq_add([(full_dst(16, 512), winv_s[:, :, cb * 512:(cb + 1) * 512])]) for cb in range(4)]
        PLAN['sg_v'] = [wq_add([(full_dst(16, 512), winv_s[:, :, 2048 + cb * 512:2048 + (cb + 1) * 512])]) for cb in range(4)]
        wv_ = sg_w_out.rearrange("(k p) d -> p k d", p=128)
        PLAN['sg_out'] = [wq_add([(full_dst(16, 512), wv_[:, :, cb * 512:(cb + 1) * 512])]) for cb in range(4)]
    if "mlp1" in phases:
        PLAN['mlp1'] = plan_mlp(1)
    wq_issue_pending()

    load_xT(xm, NT, xT, XTG)
    last = [p for p in ("gla", "mlp0", "sg", "mlp1") if p in phases][-1]
    if "gla" in phases:
        gla_layer()
    alloc_xres(xm)
    if "gla" in phases:
        out_proj(PLAN['gla_out'])
        layer_norm(0, ln1g[0:1, :], ln1b[0:1, :], final=(last == "gla"))
    if "mlp0" in phases:
        mlp(0)
        layer_norm(1, ln2g[0:1, :], ln2b[0:1, :], final=(last == "mlp0"))
    if "sg" in phases:
        xres = xres_box[0]
        for tc in range(NT):
            S.add('sp', lambda h, tc=tc, xres=xres: h.dma_start(out=xspill[tc * 128:(tc + 1) * 128, :], in_=xres[:, tc, :]),
                  r=XRA(tc), w=['xspill%d' % tc], dma='xsp%d' % tc)
        free_xres()
        sg_layer()
        xres = A.alloc([NT, D], F32)
        xres_box[0] = xres
        for tc in range(NT):
            S.add('sp', lambda h, tc=tc, xres=xres: h.dma_start(out=xres[:, tc, :], in_=xspill[tc * 128:(tc + 1) * 128, :]),
                  r=['xspill%d' % tc], w=XRA(tc), dma='xres%d' % tc)
        out_proj(PLAN['sg_out'])
        layer_norm(2, ln1g[1:2, :], ln1b[1:2, :], final=(last == "sg"))
    if "mlp1" in phases:
        mlp(1)
        layer_norm(3, ln2g[1:2, :], ln2b[1:2, :], final=True)

    S.emit(nc, es)
    es.close()
    return nc, S, A


def prep_shared(inp):
    f = lambda a: np.ascontiguousarray(np.asarray(a, dtype=np.float32))
    wg_aug = np.zeros((32, 1024), np.float32)
    wg_aug[0:16] = inp["gla_w_gate"][0]
    wg_aug[16] = inp["gla_b_gate"][0]
    ws = np.asarray(inp["sg_w_spatial"][0])
    wsT = ws.transpose(2, 0, 1)
    wsTs = np.tile(ws[:, :8, :8].transpose(2, 0, 1), (16, 1, 16))
    bsp = np.asarray(inp["sg_b_spatial"][0])
    bsp2 = np.stack([bsp, np.tile(bsp[:, :8], (1, 16))])
    b_in = np.asarray(inp["sg_b_in"][0])
    d = dict(
        gla_w_in=f(inp["gla_w_in"][0]), wg_aug=wg_aug, gla_ng=f(np.asarray(inp["gla_norm_g"][0]).reshape(1, D)),
        gla_w_out=f(inp["gla_w_out"][0]), sg_w_in=f(inp["sg_w_in"][0]),
        sg_binT=f(b_in[:D].reshape(16, 128).T), sg_bv=f(b_in[D:].reshape(1, D)),
        sg_vg=f(np.asarray(inp["sg_v_norm_g"][0]).reshape(1, D)), sg_vb=f(np.asarray(inp["sg_v_norm_b"][0]).reshape(1, D)),
        sg_wsT=f(wsT), sg_wsTs=f(wsTs), sg_bsp=f(bsp2), sg_w_out=f(inp["sg_w_out"][0]),
        mlp_w1=f(inp["mlp_w1"]), mlp_w2=f(inp["mlp_w2"]),
        ln1_g=f(inp["ln1_g"]), ln1_b=f(inp["ln1_b"]), ln2_g=f(inp["ln2_g"]), ln2_b=f(inp["ln2_b"]),
        ident=np.eye(128, dtype=np.float32), cmask=make_consts(),
    )
    lnT = np.zeros((4, 128, 2, 16), np.float32)
    for n_, (gk, bk, li) in enumerate([("ln1_g", "ln1_b", 0), ("ln2_g", "ln2_b", 0), ("ln1_g", "ln1_b", 1), ("ln2_g", "ln2_b", 1)]):
        lnT[n_, :, 0, :] = np.asarray(inp[gk][li]).reshape(16, 128).T
        lnT[n_, :, 1, :] = np.asarray(inp[bk][li]).reshape(16, 128).T
    d["lnT"] = lnT
    return d


def prep_core(inp, c):
    xpr = np.asarray(inp["x_prompt"])
    xs = np.asarray(inp["x_sample"])
    b, hf = c // 2, c % 2
    xm = np.concatenate([xpr[b, hf * 1024:(hf + 1) * 1024], xs[16 * c:16 * (c + 1)].reshape(128, D)], axis=0)
    if hf == 1:
        xp = xpr[b, 0:1024]
    else:
        xp = np.zeros((1024, D), np.float32)
    st = np.asarray(inp["state_gla"])[0, 16 * c:16 * (c + 1)]
    return dict(xm=np.ascontiguousarray(xm, dtype=np.float32), xp=np.ascontiguousarray(xp, dtype=np.float32),
                st=np.ascontiguousarray(st, dtype=np.float32))


_CACHE = {}


def kernel(**inputs):
    if "nc" not in _CACHE:
        _CACHE["nc"] = build()[0]
    nc = _CACHE["nc"]
    shared = prep_shared(inputs)
    in_maps = []
    for c in range(8):
        m = dict(shared)
        m.update(prep_core(inputs, c))
        in_maps.append(m)
    res = run_bass_kernel_spmd(nc, in_maps, core_ids=list(range(8)))
    R = res.results
    y_prompt = np.zeros((4, 2048, D), np.float32)
    y_sample = np.zeros((128, 8, D), np.float32)
    gp = np.zeros((1, 4, 4, 256, 512), np.float32)
    gs = np.zeros((1, 128, 4, 256, 512), np.float32)
    sgv = np.zeros((1, 128, 8, D), np.float32)
    for c in range(8):
        b, hf = c // 2, c % 2
        yc = R[c]["y"]
        y_prompt[b, hf * 1024:(hf + 1) * 1024] = yc[:1024]
        y_sample[16 * c:16 * (c + 1)] = yc[1024:].reshape(16, 8, D)
        if hf == 1:
            gp[0, b] = R[c]["gst_p"]
        gs[0, 16 * c:16 * (c + 1)] = R[c]["gst_s"]
        sgv[0, 16 * c:16 * (c + 1)] = R[c]["sgv"].reshape(16, 8, D)
    return (y_prompt, y_sample, gp, gs, sgv)
```

```python
import numpy as np
from contextlib import ExitStack
import concourse.bass as bass
import concourse.mybir as mybir
from concourse.bass_utils import run_bass_kernel_spmd

F32 = mybir.dt.float32
BF16 = mybir.dt.bfloat16
AF = mybir.ActivationFunctionType
ALU = mybir.AluOpType

D = 2048
KD = 16
NT = 9
NPF = 8
T = NT * 128
TT = 384
DFF = 8192
ALPHA = float((2.0 * 2) ** 0.25)
LN_EPS = 1e-5
HN_EPS = 1e-6
import os as _os2
DBG_NOUNIT = bool(_os2.environ.get('DBG_NOUNIT'))
import os as _os
NO_SAME_ENG_SYNC = bool(_os.environ.get('NO_SAME_ENG_SYNC'))


class Sched:
    def __init__(self):
        self.ops = []
        self.res = {}
        self.ghost = set()

    def add(self, eng, fn, r=(), w=(), dma=None):
        i = len(self.ops)
        deps = set()
        for name in r:
            st = self.res.get(name)
            if st is None:
                st = self.res[name] = [None, list(self.ghost)]
            if st[0] is not None:
                deps.add(st[0])
            if name.startswith('ps'):
                deps.update(d for d in st[1] if self.ops[d]['eng'] != eng)
        for name in w:
            st = self.res.get(name)
            if st is None:
                st = self.res[name] = [None, list(self.ghost)]
            if st[0] is not None:
                deps.add(st[0])
            deps.update(st[1])
        for name in r:
            self.res[name][1].append(i)
        for name in w:
            self.res[name] = [i, []]
        deps.discard(i)
        self.ops.append(dict(i=i, eng=eng, fn=fn, deps=deps, dma=dma, signal=False))
        return i

    def retire(self, names):
        for n in names:
            st = self.res.pop(n, None)
            if st is None:
                continue
            if st[0] is not None:
                self.ghost.add(st[0])
            self.ghost.update(st[1])
        best = {}
        for d in self.ghost:
            p = self.ops[d]
            key = ('d', p['dma']) if p['dma'] else ('c', p['eng'])
            if key not in best or best[key] < d:
                best[key] = d
        self.ghost = set(best.values())

    def retire_prefix(self, *prefixes):
        self.retire([n for n in list(self.res) if any(n.startswith(p) for p in prefixes)])

    def emit(self, nc, es, final_eng='sp'):
        ops = self.ops
        last_dma = {}
        for op in ops:
            if op['dma']:
                last_dma[op['dma']] = op['i']
        fin = dict(i=len(ops), eng=final_eng, fn=None, deps=set(last_dma.values()), dma=None, signal=False)
        ops.append(fin)
        for op in ops:
            best = {}
            for d in op['deps']:
                p = ops[d]
                key = ('d', p['dma']) if p['dma'] else ('c', p['eng'])
                if key not in best or best[key] < d:
                    best[key] = d
            rd = []
            for key, d in best.items():
                p = ops[d]
                if p['dma'] is None and p['eng'] == 'pe' and op['eng'] == 'pe' and op['dma'] is None:
                    continue
                if NO_SAME_ENG_SYNC and p['dma'] is None and op['dma'] is None and p['eng'] == op['eng']:
                    continue
                if p['dma'] is None:
                    p['signal'] = True
                rd.append(d)
            op['rdeps'] = rd
        cnt = {}
        dcnt = {}
        for op in ops:
            if op['dma']:
                dcnt[op['dma']] = dcnt.get(op['dma'], 0) + 16
                op['sval'] = dcnt[op['dma']]
            elif op['signal']:
                cnt[op['eng']] = cnt.get(op['eng'], 0) + 1
                op['sval'] = cnt[op['eng']]
        engs = ['pe', 'act', 'dve', 'pool', 'sp']
        sems = {e: es.enter_context(nc.semaphore("s_" + e)) for e in engs}
        dsems = {k: es.enter_context(nc.semaphore("d_%d" % n)) for n, k in enumerate(sorted(dcnt))}
        self.nsem = len(sems) + len(dsems)
        self.maxcnt = dict(cnt)
        block = es.enter_context(nc.Block())
        per = {e: [op for op in ops if op['eng'] == e] for e in engs}

        def run(e, h):
            waited = {}
            for op in per[e]:
                for d in op['rdeps']:
                    p = ops[d]
                    if p['dma']:
                        s, v, k = dsems[p['dma']], p['sval'], ('d', p['dma'])
                    else:
                        s, v, k = sems[p['eng']], p['sval'], ('c', p['eng'])
                    if waited.get(k, 0) >= v:
                        continue
                    waited[k] = v
                    h.wait_ge(s, v)
                if op['fn'] is None:
                    continue
                ins = op['fn'](h)
                if op['dma']:
                    ins.then_inc(dsems[op['dma']], 16)
                elif op['signal']:
                    ins.then_inc(sems[e], 1)

        @block.tensor
        def _(h):
            run('pe', h)

        @block.scalar
        def _(h):
            run('act', h)

        @block.vector
        def _(h):
            run('dve', h)

        @block.gpsimd
        def _(h):
            run('pool', h)

        @block.sync
        def _(h):
            run('sp', h)


class Arena:
    def __init__(self, ap_f32):
        self.ap = ap_f32
        self.n = ap_f32.shape[1]
        self.free = [(0, self.n)]
        self.live = {}
        self.peak = 0

    def alloc(self, shape, dtype, name=None):
        n = int(np.prod(shape))
        words = n if dtype == F32 else (n + 1) // 2
        words = (words + 1) // 2 * 2
        for idx, (o, sz) in enumerate(self.free):
            if sz >= words:
                break
        else:
            raise AssertionError(("arena overflow", name, words, self.free))
        if sz == words:
            self.free.pop(idx)
        else:
            self.free[idx] = (o + words, sz - words)
        self.peak = max(self.peak, o + words)
        v = self.ap[:, o:o + words]
        if dtype != F32:
            v = v.bitcast(dtype)
        v = v[:, 0:n]
        if len(shape) == 2:
            v = v.rearrange("p (a b) -> p a b", a=shape[0])
        elif len(shape) == 3:
            v = v.rearrange("p (a b c) -> p a b c", a=shape[0], b=shape[1])
        self.live[id(v)] = (o, words, v)
        return v

    def release(self, *views):
        for v in views:
            o, words, _ = self.live.pop(id(v))
            self.free.append((o, words))
        self.free.sort()
        merged = []
        for o, sz in self.free:
            if merged and merged[-1][0] + merged[-1][1] == o:
                merged[-1] = (merged[-1][0], merged[-1][1] + sz)
            else:
                merged.append((o, sz))
        self.free = merged


ARENA_WORDS = 53000

C_TRI_S, C_TRIU_S, C_BD_S, C_BDU_S, C_TRI01, C_BD01 = range(6)


def make_consts():
    p = np.arange(128)
    s, t = p[:, None], p[None, :]
    same = (s // 8) == (t // 8)
    tri = (s <= t)
    triu = (s > t)
    m = np.zeros((128, 6 * 128 + 16 + 4), np.float32)
    m[:, 0:128] = tri * (-1.0 / 16)
    m[:, 128:256] = triu * (-1.0 / 16)
    m[:, 256:384] = (tri & same) * (-1.0 / 16)
    m[:, 384:512] = (triu & same) * (-1.0 / 16)
    m[:, 512:640] = tri
    m[:, 640:768] = tri & same
    m[:, 768:784] = (p[:, None] // 8) == np.arange(16)[None, :]
    m[:, 784] = -1.0 / 16
    m[:, 785] = 1.0
    m[:, 786] = -1.0 / 16
    m[:, 787] = -1.0 / 16
    return m


def build(phases=("gla", "mlp0", "sg", "mlp1"), dbg=False):
    nc = bass.Bass("TRN2", target_bir_lowering=False)

    def din(name, shape):
        return nc.dram_tensor(name, list(shape), F32, kind="ExternalInput").ap()

    def dout(name, shape):
        return nc.dram_tensor(name, list(shape), F32, kind="ExternalOutput").ap()

    xm = din("xm", [T, D])
    xp = din("xp", [NPF * 128, D])
    st_in = din("st", [16, 4, 256, 512])
    gla_w_in = din("gla_w_in", [D, 6160])
    wg_aug = din("wg_aug", [32, 1024])
    gla_ng = din("gla_ng", [1, D])
    gla_w_out = din("gla_w_out", [D, D])
    sg_w_in = din("sg_w_in", [D, 2 * D])
    sg_binT = din("sg_binT", [128, 16])
    sg_bv = din("sg_bv", [1, D])
    sg_vg = din("sg_vg", [1, D])
    sg_vb = din("sg_vb", [1, D])
    sg_wsT = din("sg_wsT", [128, 8, 128])
    sg_wsTs = din("sg_wsTs", [128, 8, 128])
    sg_bsp = din("sg_bsp", [2, 8, 128])
    sg_w_out = din("sg_w_out", [D, D])
    w1 = din("mlp_w1", [2, D, DFF])
    w2 = din("mlp_w2", [2, DFF, D])
    ln1g = din("ln1_g", [2, D])
    ln1b = din("ln1_b", [2, D])
    ln2g = din("ln2_g", [2, D])
    ln2b = din("ln2_b", [2, D])
    ident_d = din("ident", [128, 128])
    lnT_d = din("lnT", [4, 128, 2, 16])
    cm_d = din("cmask", [128, 788])
    y = dout("y", [T, D])
    gst_p = dout("gst_p", [4, 256, 512])
    gst_s = dout("gst_s", [16, 4, 256, 512])
    sgv = dout("sgv", [128, D])
    xspill = nc.dram_tensor("xspill", [T, D], F32, kind="Internal").ap()
    sscr = nc.dram_tensor("sscr", [4, 256, 512], F32, kind="Internal").ap()

    S = Sched()
    es = ExitStack()
    arena_t = es.enter_context(nc.sbuf_tensor("arena", [128, ARENA_WORDS], F32))
    A = Arena(arena_t[:])
    ps = [es.enter_context(nc.psum_tensor("ps%d" % i, [128, 512], F32)) for i in range(8)]
    psb = [p_[:].bitcast(BF16) for p_ in ps]
    bank_ctr = [0]

    pinned = set()

    def bank():
        while True:
            b = bank_ctr[0] % 8
            bank_ctr[0] += 1
            if b not in pinned:
                return b

    def PS(b):
        return 'ps%d' % b

    ident = A.alloc([128], F32)
    identb = A.alloc([128], BF16)
    NWS = 4
    wbuf = [A.alloc([16, 512], BF16) for _ in range(NWS)]
    xT = A.alloc([KD, T], BF16)
    free_w = list(range(NWS))

    wq = []
    wq_pos = [0]

    def wq_add(parts):
        wq.append(dict(parts=parts, slot=None))
        return len(wq) - 1

    def wq_issue_pending():
        while free_w and wq_pos[0] < len(wq):
            e = wq[wq_pos[0]]
            wq_pos[0] += 1
            slot = free_w.pop(0)
            e['slot'] = slot
            for n_, (dst_fn, src_ap) in enumerate(e['parts']):
                dst = dst_fn(slot)
                S.add('pool', lambda h, dst=dst, src_ap=src_ap: h.dma_start(out=dst, in_=src_ap),
                      r=(['w%d' % slot] if n_ else []), w=['w%d' % slot], dma='w%d' % slot)

    def wq_get(idx):
        if wq[idx]['slot'] is None:
            wq_issue_pending()
        assert wq[idx]['slot'] is not None, ("weight block not issuable", idx, wq_pos[0], free_w)
        return wq[idx]['slot']

    def give_w(s_):
        free_w.append(s_)
        wq_issue_pending()

    def full_dst(a, b):
        def f(slot):
            dst = wbuf[slot]
            if (a, b) != (16, 512):
                dst = dst.rearrange("p a b -> p (a b)").rearrange("p (a b) -> p a b", a=a)
            return dst
        return f

    def XT(tc, k):
        return 'xT%d_%d' % (tc, k)

    def XTG(tc, g):
        return [XT(tc, 4 * g + i_) for i_ in range(4)]

    def XTT(tt, k):
        return [XT(3 * tt + i_, k) for i_ in range(3)]

    S.add('sp', lambda h: h.dma_start(out=ident, in_=ident_d), w=['ident'], dma='c_ident')
    S.add('pool', lambda h: h.dma_start(out=identb, in_=ident_d), w=['identb'], dma='c_identb')

    def load_w(slot, src_ap, dst=None):
        if dst is None:
            a, b = src_ap.shape[1], src_ap.shape[2]
            dst = wbuf[slot]
            if (a, b) != (16, 512):
                dst = dst.rearrange("p a b -> p (a b)").rearrange("p (a b) -> p a b", a=a)
        S.add('pool', lambda h: h.dma_start(out=dst, in_=src_ap), w=['w%d' % slot], dma='w%d' % slot)
        return dst

    cp_ctr = [0]

    def evac_copy(out, in_, r, w):
        cp_ctr[0] += 1
        if cp_ctr[0] % 2:
            S.add('act', lambda h: h.copy(out=out, in_=in_), r=r, w=w)
        else:
            S.add('dve', lambda h: h.tensor_copy(out=out, in_=in_), r=r, w=w)

    def transpose_f32_chunk(src, src_res, dstT, dst_res, tc):
        for g in range(4):
            b = bank()
            for i in range(4):
                kc = 4 * g + i
                S.add('pe', lambda h, kc=kc, i=i, b=b: h.transpose(
                    out=ps[b][:, i * 128:(i + 1) * 128], in_=src[:, kc * 128:(kc + 1) * 128], identity=ident),
                    r=[src_res[g] if isinstance(src_res, list) else src_res, 'ident'], w=[PS(b)])
            evac_copy(dstT[:, 4 * g:4 * g + 4, tc * 128:(tc + 1) * 128],
                      ps[b][:].rearrange("p (a t) -> p a t", a=4), r=[PS(b)], w=dst_res(tc, g))

    def transpose_bf16_chunk(src, src_res, dstT, dst_res, tc):
        for g in range(4):
            b = bank()
            for i in range(4):
                kc = 4 * g + i
                S.add('pe', lambda h, kc=kc, i=i, b=b: h.transpose(
                    out=psb[b][:, i * 128:(i + 1) * 128], in_=src[:, kc * 128:(kc + 1) * 128], identity=identb),
                    r=[src_res, 'identb'], w=[PS(b)])
            evac_copy(dstT[:, 4 * g:4 * g + 4, tc * 128:(tc + 1) * 128],
                      psb[b][:, 0:512].rearrange("p (a t) -> p a t", a=4), r=[PS(b)], w=dst_res(tc, g))

    def load_xT(src_dram, nchunks, dstT, dst_res_fn):
        stg = [A.alloc([D], F32) for _ in range(2)]
        for tc in range(nchunks):
            i = tc % 2
            S.add('sp', lambda h, tc=tc, i=i: h.dma_start(out=stg[i], in_=src_dram[tc * 128:(tc + 1) * 128, :]),
                  w=['xstg%d' % i], dma='xstg%d' % i)
            transpose_f32_chunk(stg[i], 'xstg%d' % i, dstT, dst_res_fn, tc)
        S.retire_prefix('xstg')
        A.release(*stg)

    xres_box = [None]

    def XR(tc, c):
        return 'xres%d_%d' % (tc, c)

    def XRA(tc):
        return ['xres%d_%d' % (tc, c) for c in range(4)]

    def alloc_xres(src_dram):
        xres = A.alloc([NT, D], F32)
        xres_box[0] = xres
        for tc in range(NT):
            S.add('sp', lambda h, tc=tc: h.dma_start(out=xres[:, tc, :], in_=src_dram[tc * 128:(tc + 1) * 128, :]),
                  w=XRA(tc), dma='xres%d' % tc)

    def free_xres():
        S.retire([n for tc in range(NT) for n in XRA(tc)])
        A.release(xres_box[0])
        xres_box[0] = None

    def out_proj(plan):
        xres = xres_box[0]
        for cb in range(4):
            s_ = wq_get(plan[cb])
            for tc in range(NT):
                b = bank()
                for k in range(KD):
                    S.add('pe', lambda h, k=k, tc=tc, b=b, s_=s_: h.matmul(
                        ps[b][:, :], lhsT=xT[:, k, tc * 128:(tc + 1) * 128], rhs=wbuf[s_][:, k, :],
                        start=(k == 0), stop=(k == KD - 1)),
                        r=[XT(tc, k), 'w%d' % s_], w=[PS(b)])
                dst = xres[:, tc, cb * 512:(cb + 1) * 512]
                S.add('dve', lambda h, dst=dst, b=b: h.scalar_tensor_tensor(
                    out=dst, in0=dst, scalar=ALPHA, op0=ALU.mult, in1=ps[b][:, :], op1=ALU.add),
                    r=[PS(b), XR(tc, cb)], w=[XR(tc, cb)])
            give_w(s_)

    def mlp(layer):
        xres = xres_box[0]
        hT = [A.alloc([4, T], BF16) for _ in range(2)]
        rtmp = [A.alloc([TT], BF16) for _ in range(2)]
        NFB = DFF // 512
        w1v = w1[layer].rearrange("(k p) f -> p k f", p=128)
        w2v = w2[layer].rearrange("(fb c p) d -> fb p c d", p=128, c=4)
        slots = {}

        plan = PLAN['mlp%d' % layer]

        def issue1(fb):
            slots[fb] = [wq_get(plan[fb][0]), None, None]

        def issue2(fb):
            s2 = wq_get(plan[fb][1])
            slots[fb][1] = s2
            slots[fb][2] = full_dst(4, 2048)(s2)

        def stage_a(fb):
            s1 = slots[fb][0]
            hs = fb % 2
            for fc in range(4):
                bs = [bank() for _ in range(3)]
                for k in range(KD):
                    for tt in range(3):
                        S.add('pe', lambda h, k=k, tt=tt, fc=fc, b=bs[tt]: h.matmul(
                            ps[b][:, 0:TT], lhsT=wbuf[s1][:, k, fc * 128:(fc + 1) * 128],
                            rhs=xT[:, k, tt * TT:(tt + 1) * TT], start=(k == 0), stop=(k == KD - 1)),
                            r=['w%d' % s1] + XTT(tt, k), w=[PS(bs[tt])])
                for tt in range(3):
                    rt = (fc * 3 + tt) % 2
                    S.add('act', lambda h, tt=tt, b=bs[tt], rt=rt: h.activation(
                        out=rtmp[rt], in_=ps[b][:, 0:TT], func=AF.Relu),
                        r=[PS(bs[tt])], w=['rtmp%d' % rt])
                    eng = 'pool' if tt == 1 else 'dve'
                    S.add(eng, lambda h, tt=tt, fc=fc, rt=rt: h.tensor_tensor(
                        out=hT[hs][:, fc, tt * TT:(tt + 1) * TT], in0=rtmp[rt], in1=rtmp[rt], op=ALU.mult),
                        r=['rtmp%d' % rt], w=['hT%d' % hs])

        def stage_b(fb):
            s2, d2 = slots[fb][1], slots[fb][2]
            hs = fb % 2
            for tc in range(NT):
                for cb in range(4):
                    b = bank()
                    for fc in range(4):
                        S.add('pe', lambda h, fc=fc, tc=tc, cb=cb, b=b: h.matmul(
                            ps[b][:, :], lhsT=hT[hs][:, fc, tc * 128:(tc + 1) * 128],
                            rhs=d2[:, fc, cb * 512:(cb + 1) * 512], start=(fc == 0), stop=(fc == 3)),
                            r=['hT%d' % hs, 'w%d' % s2], w=[PS(b)])
                    dst = xres[:, tc, cb * 512:(cb + 1) * 512]
                    if fb == 0:
                        S.add('dve', lambda h, dst=dst, b=b: h.scalar_tensor_tensor(
                            out=dst, in0=dst, scalar=ALPHA, op0=ALU.mult, in1=ps[b][:, :], op1=ALU.add),
                            r=[PS(b), XR(tc, cb)], w=[XR(tc, cb)])
                    else:
                        S.add('dve', lambda h, dst=dst, b=b: h.tensor_tensor(
                            out=dst, in0=dst, in1=ps[b][:, :], op=ALU.add),
                            r=[PS(b), XR(tc, cb)], w=[XR(tc, cb)])

        issue1(0)
        issue2(0)
        for fb in range(NFB + 1):
            if fb < NFB:
                if fb > 0:
                    issue1(fb)
                stage_a(fb)
                give_w(slots[fb][0])
            if fb >= 1:
                if fb - 1 > 0:
                    issue2(fb - 1)
                stage_b(fb - 1)
                give_w(slots[fb - 1][1])
        S.retire_prefix('hT', 'rtmp')
        A.release(*hT, *rtmp)

    def ln_stats(z, zr, st, sm, tag, eps):
        for c in range(4):
            S.add('dve', lambda h, c=c: h.bn_stats(out=st[:, c, :], in_=z[:, c * 512:(c + 1) * 512]),
                  r=[zr[c]], w=[tag + 'st'])
        S.add('dve', lambda h: h.bn_aggr(out=sm[:, 0:2], in_=st), r=[tag + 'st'], w=[tag + 'sm'])
        S.add('act', lambda h: h.activation(out=sm[:, 2:3], in_=sm[:, 1:2], func=AF.Ln, bias=eps_t[:, 0:1], scale=1.0),
              r=[tag + 'sm', 'eps'], w=[tag + 'sm'])
        S.add('act', lambda h: h.activation(out=sm[:, 3:4], in_=sm[:, 2:3], func=AF.Exp, scale=-0.5),
              r=[tag + 'sm'], w=[tag + 'sm'])
        S.add('dve', lambda h: h.tensor_scalar(out=sm[:, 4:5], in0=sm[:, 0:1], scalar1=sm[:, 3:4],
                                               scalar2=-1.0, op0=ALU.mult, op1=ALU.mult),
              r=[tag + 'sm'], w=[tag + 'sm'])

    def layer_norm(idx, g_dram, b_dram, final):
        xres = xres_box[0]
        gt = A.alloc([D], F32)
        bt = A.alloc([D], F32)
        gbc = A.alloc([2, 16], F32)
        st = [A.alloc([4, 6], F32) for _ in range(3)]
        sm = [A.alloc([8], F32) for _ in range(3)]
        S.add('sp', lambda h: h.dma_start(out=gt, in_=g_dram.partition_broadcast(128)), w=['ln_g'], dma='ln_g')
        S.add('sp', lambda h: h.dma_start(out=bt, in_=b_dram.partition_broadcast(128)), w=['ln_b'], dma='ln_b')
        S.add('sp', lambda h: h.dma_start(out=gbc, in_=lnT_d[idx]), w=['ln_c'], dma='ln_c')

        def st_stage(tc):
            i = tc % 3
            ln_stats(xres[:, tc, :], XRA(tc), st[i], sm[i], 'ln%d' % i, LN_EPS)

        def nrm_stage(tc):
            i = tc % 3
            z = xres[:, tc, :]
            for c in range(4):
                zr = XR(tc, c)
                zc = z[:, c * 512:(c + 1) * 512]
                S.add('act', lambda h, i=i, zc=zc: h.activation(out=zc, in_=zc, func=AF.Identity,
                                                             scale=sm[i][:, 3:4], bias=sm[i][:, 4:5]),
                      r=[zr, 'ln%dsm' % i], w=[zr])

        def t_stage(tc):
            z = xres[:, tc, :]
            for g in range(4):
                zr = XR(tc, g)
                b = bank()
                for q_ in range(4):
                    kc = 4 * g + q_
                    S.add('pe', lambda h, kc=kc, q_=q_, b=b: h.transpose(
                        out=ps[b][:, q_ * 128:(q_ + 1) * 128], in_=z[:, kc * 128:(kc + 1) * 128], identity=ident),
                        r=[zr, 'ident'], w=[PS(b)])
                for q_ in range(4):
                    kc = 4 * g + q_
                    dst = xT[:, kc, tc * 128:(tc + 1) * 128]
                    src = ps[b][:, q_ * 128:(q_ + 1) * 128]
                    if g % 2 == 0:
                        S.add('act', lambda h, kc=kc, dst=dst, src=src: h.activation(
                            out=dst, in_=src, func=AF.Identity, scale=gbc[:, 0, kc:kc + 1], bias=gbc[:, 1, kc:kc + 1]),
                            r=[PS(b), 'ln_c'], w=[XT(tc, kc)])
                    else:
                        S.add('dve', lambda h, kc=kc, dst=dst, src=src: h.tensor_scalar(
                            out=dst, in0=src, scalar1=gbc[:, 0, kc:kc + 1], scalar2=gbc[:, 1, kc:kc + 1],
                            op0=ALU.mult, op1=ALU.add), r=[PS(b), 'ln_c'], w=[XT(tc, kc)])

        def gb_stage(tc):
            z = xres[:, tc, :]
            for c in range(4):
                zr = XR(tc, c)
                zc = z[:, c * 512:(c + 1) * 512]
                S.add('pool', lambda h, c=c, zc=zc: h.tensor_tensor(out=zc, in0=zc, in1=gt[:, c * 512:(c + 1) * 512], op=ALU.mult),
                      r=[zr, 'ln_g'], w=[zr])
                S.add('dve', lambda h, c=c, zc=zc: h.tensor_tensor(out=zc, in0=zc, in1=bt[:, c * 512:(c + 1) * 512], op=ALU.add),
                      r=[zr, 'ln_b'], w=[zr])
            if final:
                S.add('sp', lambda h, tc=tc, z=z: h.dma_start(out=y[tc * 128:(tc + 1) * 128, :], in_=z),
                      r=XRA(tc), dma='yout%d' % (tc % 3))

        st_stage(0)
        st_stage(1)
        nrm_stage(0)
        for tc in range(NT):
            if tc + 2 < NT:
                st_stage(tc + 2)
            if tc + 1 < NT:
                nrm_stage(tc + 1)
            if not final:
                t_stage(tc)
            gb_stage(tc)
        S.retire_prefix('ln')
        A.release(gt, bt, gbc, *st, *sm)

    cm = A.alloc([788], F32)
    S.add('sp', lambda h: h.dma_start(out=cm, in_=cm_d), w=['cm'], dma='c_cm')
    eps_t = A.alloc([2], F32)
    S.add('dve', lambda h: h.memset(eps_t[:, 0:1], LN_EPS), w=['eps'])
    S.add('dve', lambda h: h.memset(eps_t[:, 1:2], HN_EPS), w=['eps'])

    def CM(i):
        return cm[:, i * 128:(i + 1) * 128]

    def gla_layer():
        winv = gla_w_in.rearrange("(k p) f -> p k f", p=128)
        wg = A.alloc([1024], F32)
        S.add('sp', lambda h: h.dma_start(out=wg[0:32, :], in_=wg_aug), w=['wg'], dma='c_wg')
        wg16 = A.alloc([16, 16], BF16)
        S.add('pool', lambda h: h.dma_start(out=wg16, in_=winv[:, :, 6144:6160]), w=['wg16'], dma='c_wg16')
        identb_r = ['identb']

        def gate_T(srcT, src_res_list, ntok, name):
            gTa = A.alloc([ntok], F32)
            S.add('dve', lambda h: h.memset(gTa[0:32, :], 1.0), w=[name])
            ntt = ntok // TT if ntok % TT == 0 else None
            tiles = [(i * TT, TT) for i in range(ntok // TT)] if ntt else [(i * 512, 512) for i in range(ntok // 512)]
            for (o, n) in tiles:
                b = bank()
                for k in range(KD):
                    if src_res_list is None:
                        rr = [XT(tc_, k) for tc_ in range(o // 128, (o + n - 1) // 128 + 1)]
                    else:
                        rr = sorted(set(src_res_list[(o // 128):((o + n - 1) // 128) + 1]))
                    S.add('pe', lambda h, k=k, b=b, o=o, n=n: h.matmul(
                        ps[b][0:16, 0:n], lhsT=wg16[:, k, :], rhs=srcT[:, k, o:o + n],
                        start=(k == 0), stop=(k == KD - 1)), r=['wg16'] + rr, w=[PS(b)])
                S.add('act', lambda h, b=b, o=o, n=n: h.copy(out=gTa[0:16, o:o + n], in_=ps[b][0:16, 0:n]),
                      r=[PS(b)], w=[name])
            return gTa

        class WS:
            pass

        sgS = A.alloc([512], F32)
        t1S = A.alloc([512], BF16)
        junkS = A.alloc([512], BF16)
        v3 = [A.alloc([512], BF16) for _ in range(4)]

        def make_ws():
            w_ = WS()
            w_.qk = A.alloc([512], BF16)
            w_.sg = sgS
            w_.t1 = t1S
            w_.srg = A.alloc([512], BF16)
            w_.nl = A.alloc([256], F32)
            w_.eend = A.alloc([256], F32)
            w_.kte = A.alloc([256], BF16)
            w_.epos = A.alloc([2, 128], F32)
            w_.eneg = A.alloc([2, 128], F32)
            w_.qdT = A.alloc([2, 128], BF16)
            w_.kiT = A.alloc([2, 128], BF16)
            w_.scm = A.alloc([128], BF16)
            w_.junk = junkS
            w_.sm = A.alloc([8], F32)
            w_.all = [w_.qk, w_.srg, w_.nl, w_.eend, w_.kte, w_.epos, w_.eneg,
                      w_.qdT, w_.kiT, w_.scm, w_.sm]
            return w_

        xpT = A.alloc([KD, NPF * 128], BF16)
        load_xT(xp, NPF, xpT, lambda tc, g: ['xpT%d' % tc])
        XPT = ['xpT%d' % tc for tc in range(NPF)]
        gTp = gate_T(xpT, XPT, NPF * 128, 'gTp')
        Sf = A.alloc([2, 512], F32)
        wsets = [make_ws() for _ in range(3)]
        dec = [A.alloc([4], F32) for _ in range(2)]

        def gate_common(W, i, gsrc, gres, c, h_, tri_u):
            R = 'g%d' % i
            bg = bank()
            S.add('pe', lambda h, bg=bg: h.matmul(ps[bg][:, 0:256], lhsT=gsrc[0:32, c * 128:(c + 1) * 128],
                                                   rhs=wg[0:32, h_ * 256:(h_ + 1) * 256], start=True, stop=True),
                  r=['wg', gres], w=[PS(bg)])
            S.add('act', lambda h, bg=bg: h.activation(out=W.nl, in_=ps[bg][:, 0:256], func=AF.Exp, scale=-1.0),
                  r=[PS(bg)], w=[R + 'nl'])
            S.add('act', lambda h: h.activation(out=W.nl, in_=W.nl, func=AF.Ln, bias=cm[:, 785:786], scale=1.0),
                  r=[R + 'nl', 'cm'], w=[R + 'nl'])
            brc = bank()
            S.add('pe', lambda h, brc=brc: h.matmul(ps[brc][:, 0:256], lhsT=tri_u, rhs=W.nl, start=True, stop=True),
                  r=['cm', R + 'nl'], w=[PS(brc)])
            S.add('act', lambda h, brc=brc: h.activation(out=W.eend, in_=ps[brc][:, 0:256], func=AF.Exp),
                  r=[PS(brc)], w=[R + 'eend'])
            S.add('pool', lambda h: h.tensor_tensor(out=W.kte, in0=W.qk[:, 256:512], in1=W.eend, op=ALU.mult),
                  r=[R + 'qk', R + 'eend'], w=[R + 'kte'])

        def pf_Pk(h_, c, sk):
            W, R = wsets[c % 2], 'g%d' % (c % 2)
            bk = bank()
            for k in range(KD):
                S.add('pe', lambda h, k=k: h.matmul(
                    ps[bk][:, 0:256], lhsT=xpT[:, k, c * 128:(c + 1) * 128], rhs=wbuf[sk][:, k, 256:512],
                    start=(k == 0), stop=(k == KD - 1)), r=[XPT[c], 'w%d' % sk], w=[PS(bk)])
            S.add('dve', lambda h: h.tensor_copy(out=W.qk[:, 256:512], in_=ps[bk][:, 0:256]), r=[PS(bk)], w=[R + 'qk'])

        def pf_Pv(h_, c, sv):
            vv, VR = v3[c % 3], 'gv%d' % (c % 3)
            bv_ = bank()
            for k in range(KD):
                S.add('pe', lambda h, k=k: h.matmul(
                    ps[bv_][:, :], lhsT=xpT[:, k, c * 128:(c + 1) * 128], rhs=wbuf[sv][:, k, :],
                    start=(k == 0), stop=(k == KD - 1)), r=[XPT[c], 'w%d' % sv], w=[PS(bv_)])
            S.add('act', lambda h: h.copy(out=vv, in_=ps[bv_][:, :]), r=[PS(bv_)], w=[VR])

        def pf_G01(h_, c):
            W, R = wsets[c % 2], 'g%d' % (c % 2)
            bg = bank()
            S.add('pe', lambda h: h.matmul(ps[bg][:, 0:256], lhsT=gTp[0:32, c * 128:(c + 1) * 128],
                                           rhs=wg[0:32, h_ * 256:(h_ + 1) * 256], start=True, stop=True),
                  r=['wg', 'gTp'], w=[PS(bg)])
            S.add('act', lambda h: h.activation(out=W.nl, in_=ps[bg][:, 0:256], func=AF.Exp, scale=-1.0),
                  r=[PS(bg)], w=[R + 'nl'])
            S.add('act', lambda h: h.activation(out=W.nl, in_=W.nl, func=AF.Ln, bias=cm[:, 785:786], scale=1.0),
                  r=[R + 'nl', 'cm'], w=[R + 'nl'])

        def pf_G2(h_, c):
            W, R, i = wsets[c % 2], 'g%d' % (c % 2), c % 2
            brc = bank()
            S.add('pe', lambda h: h.matmul(ps[brc][:, 0:256], lhsT=CM(C_TRIU_S), rhs=W.nl, start=True, stop=True),
                  r=['cm', R + 'nl'], w=[PS(brc)])
            bb = bank()
            for j in range(2):
                S.add('pe', lambda h, j=j: h.matmul(
                    ps[bb][:, 2 * j:2 * j + 2], lhsT=W.nl[:, j * 128:(j + 1) * 128], rhs=cm[:, 786:788],
                    start=True, stop=True), r=[R + 'nl', 'cm'], w=[PS(bb)])
            S.add('act', lambda h: h.activation(out=W.eend, in_=ps[brc][:, 0:256], func=AF.Exp),
                  r=[PS(brc)], w=[R + 'eend'])
            S.add('act', lambda h: h.activation(out=dec[i], in_=ps[bb][:, 0:4], func=AF.Exp),
                  r=[PS(bb)], w=['dec%d' % i])
            S.add('pool', lambda h: h.tensor_tensor(out=W.kte, in0=W.qk[:, 256:512], in1=W.eend, op=ALU.mult),
                  r=[R + 'qk', R + 'eend'], w=[R + 'kte'])

        def pf_U(h_, c):
            W, R, i = wsets[c % 2], 'g%d' % (c % 2), c % 2
            vv, VR = v3[c % 3], 'gv%d' % (c % 3)
            for j in range(2):
                bu = bank()
                S.add('pe', lambda h, j=j, bu=bu: h.matmul(
                    ps[bu][:, :], lhsT=W.kte[:, j * 128:(j + 1) * 128], rhs=vv, start=True, stop=True),
                    r=[R + 'kte', VR], w=[PS(bu)])
                S.add('dve', lambda h, j=j, bu=bu: h.scalar_tensor_tensor(
                    out=Sf[:, j, :], in0=Sf[:, j, :], scalar=dec[i][:, 2 * j:2 * j + 1], op0=ALU.mult,
                    in1=ps[bu][:, :], op1=ALU.add), r=[PS(bu), 'dec%d' % i, 'Sf'], w=['Sf'])

        for h_ in range(4):
            sk = wq_get(PLAN['pf'][h_][0])
            sv = wq_get(PLAN['pf'][h_][1])
            S.add('dve', lambda h: h.memset(Sf, 0.0), w=['Sf'])
            for c in range(-2, NPF):
                if 0 <= c + 2 < NPF:
                    pf_Pk(h_, c + 2, sk)
                if 0 <= c + 1 < NPF:
                    pf_G2(h_, c + 1)
                if 0 <= c + 2 < NPF:
                    pf_Pv(h_, c + 2, sv)
                if c >= 0:
                    pf_U(h_, c)
                if 0 <= c + 2 < NPF:
                    pf_G01(h_, c + 2)
            give_w(sk)
            give_w(sv)
            S.add('sp', lambda h, h_=h_: h.dma_start(out=sscr[h_].rearrange("(j p) v -> p j v", p=128), in_=Sf),
                  r=['Sf'], w=['sscr%d' % h_], dma='sscr%d' % h_)
        S.retire(XPT + ['gTp'] + ['dec0', 'dec1'])
        A.release(xpT, gTp, *dec)

        gated = A.alloc([NT, D], BF16)
        gTm = gate_T(xT, None, T, 'gTm')
        gng = A.alloc([512], F32)
        Sb = A.alloc([2, 512], BF16)
        s0 = [A.alloc([2, 512], F32) for _ in range(3)]
        Qm = [A.alloc([2, 128], F32) for _ in range(2)]
        kteM = [A.alloc([256], BF16) for _ in range(2)]

        class Ck:
            pass

        def mk(h_, c, slots):
            ck = Ck()
            ck.h, ck.c, ck.slots = h_, c, slots
            ck.sample = (c == NT - 1)
            if ck.sample:
                ck.W, ck.R, ck.v, ck.VR = wsets[2], 'g2', v3[3], 'gv3'
            else:
                ck.W, ck.R, ck.v, ck.VR = wsets[c % 2], 'g%d' % (c % 2), v3[c % 3], 'gv%d' % (c % 3)
            return ck

        def proj(ck, slot):
            b = bank()
            c = ck.c
            for k in range(KD):
                S.add('pe', lambda h, k=k: h.matmul(
                    ps[b][:, :], lhsT=xT[:, k, c * 128:(c + 1) * 128], rhs=wbuf[slot][:, k, :],
                    start=(k == 0), stop=(k == KD - 1)), r=[XT(c, k), 'w%d' % slot], w=[PS(b)])
            return b

        def P_qk(ck):
            W, R = ck.W, ck.R
            bq = proj(ck, ck.slots[0])
            S.add('act', lambda h: h.mul(out=W.qk[:, 0:256], in_=ps[bq][:, 0:256], mul=1.0 / 16), r=[PS(bq)], w=[R + 'qk'])
            S.add('act', lambda h: h.copy(out=W.qk[:, 256:512], in_=ps[bq][:, 256:512]), r=[PS(bq)], w=[R + 'qk'])

        def P_v(ck):
            bv_ = proj(ck, ck.slots[1])
            S.add('act', lambda h: h.copy(out=ck.v, in_=ps[bv_][:, :]), r=[PS(bv_)], w=[ck.VR])

        def P_r(ck):
            W, R, h_ = ck.W, ck.R, ck.h
            br = proj(ck, ck.slots[2])
            S.add('act', lambda h: h.activation(out=W.sg, in_=ps[br][:, :], func=AF.Exp, scale=-1.0), r=[PS(br)], w=['gsg'])
            S.add('act', lambda h: h.activation(out=W.sg, in_=W.sg, func=AF.Ln, bias=cm[:, 785:786], scale=1.0),
                  r=['gsg', 'cm'], w=['gsg'])
            S.add('act', lambda h: h.activation(out=W.sg, in_=W.sg, func=AF.Exp, scale=-1.0), r=['gsg'], w=['gsg'])
            S.add('dve', lambda h: h.tensor_tensor(out=W.t1, in0=ps[br][:, :], in1=W.sg, op=ALU.mult),
                  r=[PS(br), 'gsg'], w=['gt1'])
            S.add('pool', lambda h: h.tensor_tensor(out=W.srg, in0=W.t1, in1=gng, op=ALU.mult),
                  r=['gt1', 'gng'], w=[R + 'srg'])

        def G01(ck):
            W, R, c, h_ = ck.W, ck.R, ck.c, ck.h
            bg = bank()
            S.add('pe', lambda h: h.matmul(ps[bg][:, 0:256], lhsT=gTm[0:32, c * 128:(c + 1) * 128],
                                           rhs=wg[0:32, h_ * 256:(h_ + 1) * 256], start=True, stop=True),
                  r=['wg', 'gTm'], w=[PS(bg)])
            S.add('act', lambda h: h.activation(out=W.nl, in_=ps[bg][:, 0:256], func=AF.Exp, scale=-1.0),
                  r=[PS(bg)], w=[R + 'nl'])
            S.add('act', lambda h: h.activation(out=W.nl, in_=W.nl, func=AF.Ln, bias=cm[:, 785:786], scale=1.0),
                  r=[R + 'nl', 'cm'], w=[R + 'nl'])

        def G23(ck):
            W, R = ck.W, ck.R
            tri_u = CM(C_BDU_S) if ck.sample else CM(C_TRIU_S)
            tri = CM(C_BD_S) if ck.sample else CM(C_TRI_S)
            brc = bank()
            S.add('pe', lambda h: h.matmul(ps[brc][:, 0:256], lhsT=tri_u, rhs=W.nl, start=True, stop=True),
                  r=['cm', R + 'nl'], w=[PS(brc)])
            bbt = bank()
            for j in range(2):
                S.add('pe', lambda h, j=j: h.matmul(ps[bbt][:, j * 128:(j + 1) * 128], lhsT=W.nl[:, j * 128:(j + 1) * 128],
                                                     rhs=tri, start=True, stop=True), r=[R + 'nl', 'cm'], w=[PS(bbt)])
            S.add('act', lambda h: h.activation(out=W.eend, in_=ps[brc][:, 0:256], func=AF.Exp),
                  r=[PS(brc)], w=[R + 'eend'])
            S.add('act', lambda h: h.activation(out=W.epos.rearrange("p a b -> p (a b)"), in_=ps[bbt][:, 0:256], func=AF.Exp),
                  r=[PS(bbt)], w=[R + 'epos'])
            S.add('act', lambda h: h.activation(out=W.eneg.rearrange("p a b -> p (a b)"), in_=ps[bbt][:, 0:256], func=AF.Exp, scale=-1.0),
                  r=[PS(bbt)], w=[R + 'eneg'])
            S.add('pool', lambda h: h.tensor_tensor(out=W.kte, in0=W.qk[:, 256:512], in1=W.eend, op=ALU.mult),
                  r=[R + 'qk', R + 'eend'], w=[R + 'kte'])

        def G45(ck):
            W, R = ck.W, ck.R
            btr = bank()
            for j in range(4):
                S.add('pe', lambda h, j=j: h.transpose(out=psb[btr][:, j * 128:(j + 1) * 128],
                                                        in_=W.qk[:, j * 128:(j + 1) * 128], identity=identb),
                      r=[R + 'qk', 'identb'], w=[PS(btr)])
            S.add('dve', lambda h: h.tensor_tensor(out=W.qdT.rearrange("p a b -> p (a b)"), in0=psb[btr][:, 0:256],
                                                   in1=W.epos.rearrange("p a b -> p (a b)"), op=ALU.mult),
                  r=[PS(btr), R + 'epos'], w=[R + 'qdT'])
            S.add('dve', lambda h: h.tensor_tensor(out=W.kiT.rearrange("p a b -> p (a b)"), in0=psb[btr][:, 256:512],
                                                   in1=W.eneg.rearrange("p a b -> p (a b)"), op=ALU.mult),
                  r=[PS(btr), R + 'eneg'], w=[R + 'kiT'])

        def B12(ck):
            W, R = ck.W, ck.R
            bs = bank()
            for j in range(2):
                S.add('pe', lambda h, j=j: h.matmul(ps[bs][:, 0:128], lhsT=W.kiT[:, j, :], rhs=W.qdT[:, j, :],
                                                     start=(j == 0), stop=(j == 1)), r=[R + 'kiT', R + 'qdT'], w=[PS(bs)])
            m01 = CM(C_BD01) if ck.sample else CM(C_TRI01)
            S.add('dve', lambda h: h.tensor_tensor(out=W.scm, in0=ps[bs][:, 0:128], in1=m01, op=ALU.mult),
                  r=[PS(bs), 'cm'], w=[R + 'scm'])

        def B3(ck):
            W, R, c, h_ = ck.W, ck.R, ck.c, ck.h
            vv, VR = ck.v, ck.VR
            bo = bank()
            pinned.add(bo)
            ck.bo = bo
            S.add('pe', lambda h: h.matmul(ps[bo][:, :], lhsT=W.scm, rhs=vv, start=True, stop=False),
                  r=[R + 'scm', VR], w=[PS(bo)])
            if not ck.sample:
                for j in range(2):
                    S.add('pe', lambda h, j=j: h.matmul(ps[bo][:, :], lhsT=W.qdT[:, j, :], rhs=Sb[:, j, :],
                                                         start=False, stop=(j == 1)), r=[R + 'qdT', 'Sb'], w=[PS(bo)])
                for j in range(2):
                    bu = bank()
                    S.add('pe', lambda h, j=j, bu=bu: h.matmul(ps[bu][:, :], lhsT=W.kte[:, j * 128:(j + 1) * 128], rhs=vv,
                                                               start=True, stop=True), r=[R + 'kte', VR], w=[PS(bu)])
                    S.add('dve', lambda h, j=j, bu=bu: h.scalar_tensor_tensor(
                        out=Sf[:, j, :], in0=Sf[:, j, :], scalar=W.epos[:, j, 127:128], op0=ALU.mult,
                        in1=ps[bu][:, :], op1=ALU.add), r=[PS(bu), R + 'epos', 'Sf'], w=['Sf'])
                    S.add('act', lambda h, j=j: h.copy(out=Sb[:, j, :], in_=Sf[:, j, :]), r=['Sf'], w=['Sb'])
                if c == NT - 2:
                    S.add('sp', lambda h: h.dma_start(out=gst_p[h_].rearrange("(j p) v -> p j v", p=128), in_=Sf),
                          r=['Sf'], dma='gstp')

        def s0_load(ck, q_):
            sb_ = q_ % 3
            h_ = ck.h
            S.add('sp', lambda h: h.dma_start(out=s0[sb_], in_=st_in[q_, h_].rearrange("(j p) v -> p j v", p=128)),
                  w=['s0_%d' % sb_], dma='s0_%d' % sb_)

        def unit_prep(ck, q_):
            if DBG_NOUNIT or q_ >= 16:
                return
            W, R = ck.W, ck.R
            qb_ = q_ % 2
            S.add('pool', lambda h: h.memset(Qm[qb_], 0.0), w=['Qm%d' % qb_])
            S.add('pool', lambda h: h.tensor_copy(out=Qm[qb_][:, :, 8 * q_:8 * q_ + 8], in_=W.qdT[:, :, 8 * q_:8 * q_ + 8]),
                  r=[R + 'qdT'], w=['Qm%d' % qb_])
            S.add('dve', lambda h: h.tensor_scalar(
                out=kteM[qb_], in0=W.kte, scalar1=cm[:, 768 + q_:769 + q_], scalar2=None, op0=ALU.mult),
                r=[R + 'kte', 'cm'], w=['kteM%d' % qb_])

        def unit(ck, q_):
            if DBG_NOUNIT:
                return
            W, R, h_ = ck.W, ck.R, ck.h
            vv, VR, bo = ck.v, ck.VR, ck.bo
            sb_ = q_ % 3
            qb_ = q_ % 2
            if q_ + 1 < 16:
                s0_load(ck, q_ + 1)
            for j in range(2):
                S.add('pe', lambda h, j=j: h.matmul(
                    ps[bo][:, :], lhsT=Qm[qb_][:, j, :], rhs=s0[sb_][:, j, :], start=False,
                    stop=(q_ == 15 and j == 1)), r=['Qm%d' % qb_, 's0_%d' % sb_], w=[PS(bo)])
            for j in range(2):
                bu = bank()
                S.add('pe', lambda h, j=j, bu=bu: h.matmul(
                    ps[bu][:, :], lhsT=kteM[qb_][:, j * 128:(j + 1) * 128], rhs=vv, start=True, stop=True),
                    r=['kteM%d' % qb_, VR], w=[PS(bu)])
                S.add('dve', lambda h, j=j, bu=bu: h.scalar_tensor_tensor(
                    out=s0[sb_][:, j, :], in0=s0[sb_][:, j, :], scalar=W.epos[:, j, 8 * q_ + 7:8 * q_ + 8],
                    op0=ALU.mult, in1=ps[bu][:, :], op1=ALU.add),
                    r=[PS(bu), R + 'epos', 's0_%d' % sb_], w=['s0_%d' % sb_])
            S.add('sp', lambda h: h.dma_start(out=gst_s[q_, h_].rearrange("(j p) v -> p j v", p=128), in_=s0[sb_]),
                  r=['s0_%d' % sb_], dma='s0_%d' % sb_)

        def B4(ck):
            W, R, c, h_, bo = ck.W, ck.R, ck.c, ck.h, ck.bo
            S.add('act', lambda h: h.activation(out=W.junk, in_=ps[bo][:, :], func=AF.Square, accum_out=W.sm[:, 0:1]),
                  r=[PS(bo)], w=['gjunk', R + 'sm'])
            S.add('act', lambda h: h.activation(out=W.sm[:, 1:2], in_=W.sm[:, 0:1], func=AF.Ln, bias=eps_t[:, 1:2], scale=1.0 / 512),
                  r=[R + 'sm', 'eps'], w=[R + 'sm'])
            S.add('act', lambda h: h.activation(out=W.sm[:, 2:3], in_=W.sm[:, 1:2], func=AF.Exp, scale=-0.5),
                  r=[R + 'sm'], w=[R + 'sm'])
            S.add('dve', lambda h: h.scalar_tensor_tensor(out=gated[:, c, h_ * 512:(h_ + 1) * 512], in0=ps[bo][:, :],
                                                          scalar=W.sm[:, 2:3], op0=ALU.mult, in1=W.srg, op1=ALU.mult),
                  r=[PS(bo), R + 'sm', R + 'srg'], w=['gated%d' % c])
            pinned.discard(bo)

        def issue_head(h_):
            return tuple(wq_get(i_) for i_ in PLAN['main'][h_])

        for h_ in range(4):
            slots = issue_head(h_)
            S.add('sp', lambda h, h_=h_: h.dma_start(out=gng, in_=gla_ng[:, h_ * 512:(h_ + 1) * 512].partition_broadcast(128)),
                  w=['gng'], dma='c_gng')
            S.add('sp', lambda h, h_=h_: h.dma_start(out=Sf, in_=sscr[h_].rearrange("(j p) v -> p j v", p=128)),
                  r=['sscr%d' % h_], w=['Sf'], dma='sfl')
            S.add('act', lambda h: h.copy(out=Sb.rearrange("p a b -> p (a b)"), in_=Sf.rearrange("p a b -> p (a b)")),
                  r=['Sf'], w=['Sb'])
            cks = {c: mk(h_, c, slots) for c in range(NT)}
            ck8 = cks[NT - 1]
            s0_load(ck8, 0)
            L = [NT - 1] + list(range(NT - 1))
            for i in range(-2, len(L)):
                p = cks[L[i + 2]] if 0 <= i + 2 < len(L) else None
                g = cks[L[i + 1]] if 0 <= i + 1 < len(L) else None
                b_ = cks[L[i]] if 0 <= i < len(L) else None
                u0 = 2 * (i - 1)
                if p:
                    P_qk(p)
                if b_ and not b_.sample:
                    unit(ck8, u0)
                if b_:
                    B12(b_)
                if g:
                    G23(g)
                if p:
                    P_v(p)
                if b_:
                    B3(b_)
                    if not b_.sample:
                        unit(ck8, u0 + 1)
                        B4(b_)
                if g:
                    G45(g)
                if p:
                    P_r(p)
                    G01(p)
                if b_:
                    unit_prep(ck8, u0 + 2)
                    unit_prep(ck8, u0 + 3)
            B4(ck8)
            for s_ in slots:
                give_w(s_)
        for tc in range(NT):
            transpose_bf16_chunk(gated[:, tc, :], 'gated%d' % tc, xT, XTG, tc)
        S.retire_prefix('g0', 'g1', 'g2', 'gsg', 'gt1', 'gjunk', 'gv', 'gated', 'gTm', 'gng', 'Sf', 'Sb', 's0_', 'Qm', 'kteM', 'wg')
        A.release(gated, gTm, gng, Sf, Sb, *s0, *Qm, *kteM, wg, wg16, *v3, sgS, t1S, junkS)
        for w_ in wsets:
            A.release(*w_.all)

    def sg_layer():
        winv = sg_w_in.rearrange("(k p) f -> p k f", p=128)
        uT = A.alloc([KD, T], BF16)
        binT = A.alloc([16], F32)
        S.add('sp', lambda h: h.dma_start(out=binT, in_=sg_binT), w=['binT'], dma='c_binT')
        wtmp = A.alloc([8, 128], F32)
        WT = [A.alloc([8, 128], BF16) for _ in range(2)]
        for v_ in range(2):
            src = sg_wsT if v_ == 0 else sg_wsTs
            S.add('sp', lambda h, src=src: h.dma_start(out=wtmp, in_=src), w=['wtmp'], dma='c_wtmp')
            m01 = CM(C_TRI01) if v_ == 0 else CM(C_BD01)
            for g in range(8):
                S.add('dve', lambda h, g=g, v_=v_, m01=m01: h.tensor_tensor(out=WT[v_][:, g, :], in0=wtmp[:, g, :], in1=m01, op=ALU.mult),
                      r=['wtmp', 'cm'], w=['WT%d' % v_])
        S.retire(['wtmp'])
        A.release(wtmp)
        bsp1 = A.alloc([8, 128], F32)
        bsp = [bsp1, bsp1]

        def load_bsp(v_):
            S.add('sp', lambda h: h.dma_start(out=bsp1, in_=sg_bsp[v_].partition_broadcast(128)),
                  w=['bsp'], dma='c_bsp')

        load_bsp(0)
        bv = A.alloc([D], F32)
        vg = A.alloc([D], F32)
        vb = A.alloc([D], F32)
        S.add('sp', lambda h: h.dma_start(out=bv, in_=sg_bv.partition_broadcast(128)), w=['sgbv'], dma='c_sgbv')
        S.add('sp', lambda h: h.dma_start(out=vg, in_=sg_vg.partition_broadcast(128)), w=['sgvg'], dma='c_sgvg')
        S.add('sp', lambda h: h.dma_start(out=vb, in_=sg_vb.partition_broadcast(128)), w=['sgvb'], dma='c_sgvb')

        for cb in range(4):
            s_ = wq_get(PLAN['sg_u'][cb])
            for fc in range(4):
                bs = [bank() for _ in range(3)]
                for k in range(KD):
                    for tt in range(3):
                        S.add('pe', lambda h, k=k, tt=tt, fc=fc, b=bs[tt], s_=s_: h.matmul(
                            ps[b][:, 0:TT], lhsT=wbuf[s_][:, k, fc * 128:(fc + 1) * 128],
                            rhs=xT[:, k, tt * TT:(tt + 1) * TT], start=(k == 0), stop=(k == KD - 1)),
                            r=['w%d' % s_] + XTT(tt, k), w=[PS(bs[tt])])
                f_ = cb * 4 + fc
                for tt in range(3):
                    S.add('act', lambda h, tt=tt, b=bs[tt], f_=f_: h.activation(
                        out=uT[:, f_, tt * TT:(tt + 1) * TT], in_=ps[b][:, 0:TT], func=AF.Gelu,
                        bias=binT[:, f_:f_ + 1], scale=1.0), r=[PS(bs[tt]), 'binT'], w=['uT'])
            give_w(s_)
        vs = [wq_get(PLAN['sg_v'][cb]) for cb in range(4)]
        vt = [A.alloc([D], F32) for _ in range(2)]
        vnb = [A.alloc([D], BF16) for _ in range(2)]
        tmp = [A.alloc([512], F32) for _ in range(2)]
        mt = [A.alloc([4, 128], F32) for _ in range(2)]
        st = [A.alloc([4, 6], F32) for _ in range(2)]
        sm = [A.alloc([8], F32) for _ in range(2)]

        def p_stage(tc):
            i = tc % 2
            VT = 'sv%dvt' % i
            bs = [bank() for _ in range(4)]
            for k in range(KD):
                for cb in range(4):
                    S.add('pe', lambda h, k=k, cb=cb, b=bs[cb], tc=tc: h.matmul(
                        ps[b][:, :], lhsT=xT[:, k, tc * 128:(tc + 1) * 128], rhs=wbuf[vs[cb]][:, k, :],
                        start=(k == 0), stop=(k == KD - 1)), r=[XT(tc, k), 'w%d' % vs[cb]], w=[PS(bs[cb])])
            for cb in range(4):
                sl = slice(cb * 512, (cb + 1) * 512)
                S.add('dve', lambda h, b=bs[cb], sl=sl, i=i: h.tensor_tensor(out=vt[i][:, sl], in0=ps[b][:, :], in1=bv[:, sl], op=ALU.add),
                      r=[PS(bs[cb]), 'sgbv'], w=[VT + str(cb)])
                S.add('act', lambda h, sl=sl, i=i: h.activation(out=vt[i][:, sl], in_=vt[i][:, sl], func=AF.Gelu),
                      r=[VT + str(cb)], w=[VT + str(cb)])

        def l_stage(tc):
            i = tc % 2
            sample = (tc == NT - 1)
            V = 'sv%d' % i
            VT = 'sv%dvt' % i
            ln_stats(vt[i], [VT + str(c_) for c_ in range(4)], st[i], sm[i], V, LN_EPS)
            for c in range(4):
                j = (tc * 4 + c) % 2
                sl = slice(c * 512, (c + 1) * 512)
                S.add('act', lambda h, i=i, sl=sl, j=j: h.activation(out=tmp[j], in_=vt[i][:, sl], func=AF.Identity,
                                                                  scale=sm[i][:, 3:4], bias=sm[i][:, 4:5]),
                      r=[VT + str(c), V + 'sm'], w=['svtmp%d' % j])
                S.add('pool', lambda h, j=j, sl=sl: h.tensor_tensor(out=tmp[j], in0=tmp[j], in1=vg[:, sl], op=ALU.mult),
                      r=['svtmp%d' % j, 'sgvg'], w=['svtmp%d' % j])
                if sample:
                    S.add('dve', lambda h, j=j, sl=sl, i=i: h.tensor_tensor(out=vt[i][:, sl], in0=tmp[j], in1=vb[:, sl], op=ALU.add),
                          r=['svtmp%d' % j, 'sgvb', VT + str(c)], w=[VT + str(c)])
                    S.add('act', lambda h, sl=sl, i=i: h.copy(out=vnb[i][:, sl], in_=vt[i][:, sl]), r=[VT + str(c)], w=[V + 'vnb' + str(c)])
                else:
                    S.add('dve', lambda h, j=j, sl=sl, i=i: h.tensor_tensor(out=vnb[i][:, sl], in0=tmp[j], in1=vb[:, sl], op=ALU.add),
                          r=['svtmp%d' % j, 'sgvb'], w=[V + 'vnb' + str(c)])
            if sample:
                S.add('sp', lambda h, i=i: h.dma_start(out=sgv, in_=vt[i]), r=[VT + str(c_) for c_ in range(4)], dma='sgvout')

        def m_stage(tc):
            i = tc % 2
            sample = (tc == NT - 1)
            V = 'sv%d' % i
            v_ = 1 if sample else 0
            for dg in range(4):
                b = bank()
                for q_ in range(4):
                    dc = dg * 4 + q_
                    S.add('pe', lambda h, q_=q_, dc=dc, b=b, i=i, v_=v_: h.matmul(
                        ps[b][:, q_ * 128:(q_ + 1) * 128], lhsT=vnb[i][:, dc * 128:(dc + 1) * 128],
                        rhs=WT[v_][:, dc // 2, :], start=True, stop=True), r=[V + 'vnb' + str(dg), 'WT%d' % v_], w=[PS(b)])
                mi = dg % 2
                bias_ap = bsp[v_][:, 2 * dg:2 * dg + 2, :].unsqueeze(2).broadcast_to([128, 2, 2, 128])
                S.add('dve', lambda h, b=b, mi=mi, bias_ap=bias_ap: h.tensor_tensor(
                    out=mt[mi].rearrange("p (a c) t -> p a c t", a=2), in0=ps[b][:, :].rearrange("p (a c t) -> p a c t", a=2, c=2),
                    in1=bias_ap, op=ALU.add), r=[PS(b), 'bsp'], w=['mt%d' % mi])
                S.add('pool', lambda h, mi=mi, dg=dg, tc=tc: h.tensor_tensor(
                    out=xT[:, 4 * dg:4 * dg + 4, tc * 128:(tc + 1) * 128], in0=mt[mi],
                    in1=uT[:, 4 * dg:4 * dg + 4, tc * 128:(tc + 1) * 128], op=ALU.mult),
                    r=['mt%d' % mi, 'uT'], w=XTG(tc, dg))

        p_stage(0)
        p_stage(1)
        l_stage(0)
        for tc in range(NT):
            if tc + 2 < NT:
                p_stage(tc + 2)
                if tc + 2 == NT - 1:
                    for s_ in vs:
                        give_w(s_)
            if tc + 1 < NT:
                l_stage(tc + 1)
            if tc == NT - 1:
                load_bsp(1)
            m_stage(tc)
        S.retire_prefix('uT', 'binT', 'wtmp', 'WT', 'bsp', 'sgbv', 'sgvg', 'sgvb', 'sv', 'mt')
        A.release(uT, binT, *WT, bsp1, bv, vg, vb, *vt, *vnb, *tmp, *mt, *st, *sm)

    winv_g = gla_w_in.rearrange("(k p) f -> p k f", p=128)
    winv_s = sg_w_in.rearrange("(k p) f -> p k f", p=128)
    PLAN = {}
    if "gla" in phases:
        PLAN['pf'] = []
        for h_ in range(4):
            ik = wq_add([(lambda sl: wbuf[sl][:, :, 256:512], winv_g[:, :, 1024 + h_ * 256:1024 + (h_ + 1) * 256])])
            iv = wq_add([(full_dst(16, 512), winv_g[:, :, 2048 + h_ * 512:2048 + (h_ + 1) * 512])])
            PLAN['pf'].append((ik, iv))
        PLAN['main'] = []
        for h_ in range(4):
            iqk = wq_add([(lambda sl: wbuf[sl][:, :, 0:256], winv_g[:, :, h_ * 256:(h_ + 1) * 256]),
                          (lambda sl: wbuf[sl][:, :, 256:512], winv_g[:, :, 1024 + h_ * 256:1024 + (h_ + 1) * 256])])
            iv = wq_add([(full_dst(16, 512), winv_g[:, :, 2048 + h_ * 512:2048 + (h_ + 1) * 512])])
            ir = wq_add([(full_dst(16, 512), winv_g[:, :, 4096 + h_ * 512:4096 + (h_ + 1) * 512])])
            PLAN['main'].append((iqk, iv, ir))
        wv_ = gla_w_out.rearrange("(k p) d -> p k d", p=128)
        PLAN['gla_out'] = [wq_add([(full_dst(16, 512), wv_[:, :, cb * 512:(cb + 1) * 512])]) for cb in range(4)]

    def plan_mlp(layer):
        w1v = w1[layer].rearrange("(k p) f -> p k f", p=128)
        w2v = w2[layer].rearrange("(fb c p) d -> fb p c d", p=128, c=4)
        out = []
        for fb in range(DFF // 512):
            i1 = wq_add([(full_dst(16, 512), w1v[:, :, fb * 512:(fb + 1) * 512])])
            i2 = wq_add([(full_dst(4, 2048), w2v[fb])])
            out.append((i1, i2))
        return out

    if "mlp0" in phases:
        PLAN['mlp0'] = plan_mlp(0)
    if "sg" in phases:
        PLAN['sg_u'] = [wq_add([(full_dst(16, 512), winv_s[:, :, cb * 512:(cb + 1) * 512])]) for cb in range(4)]
        PLAN['sg_v'] = [wq_add([(full_dst(16, 512), winv_s[:, :, 2048 + cb * 512:2048 + (cb + 1) * 512])]) for cb in range(4)]
        wv_ = sg_w_out.rearrange("(k p) d -> p k d", p=128)
        PLAN['sg_out'] = [wq_add([(full_dst(16, 512), wv_[:, :, cb * 512:(cb + 1) * 512])]) for cb in range(4)]
    if "mlp1" in phases:
        PLAN['mlp1'] = plan_mlp(1)
    wq_issue_pending()

    load_xT(xm, NT, xT, XTG)
    last = [p for p in ("gla", "mlp0", "sg", "mlp1") if p in phases][-1]
    if "gla" in phases:
        gla_layer()
    alloc_xres(xm)
    if "gla" in phases:
        out_proj(PLAN['gla_out'])
        layer_norm(0, ln1g[0:1, :], ln1b[0:1, :], final=(last == "gla"))
    if "mlp0" in phases:
        mlp(0)
        layer_norm(1, ln2g[0:1, :], ln2b[0:1, :], final=(last == "mlp0"))
    if "sg" in phases:
        xres = xres_box[0]
        for tc in range(NT):
            S.add('sp', lambda h, tc=tc, xres=xres: h.dma_start(out=xspill[tc * 128:(tc + 1) * 128, :], in_=xres[:, tc, :]),
                  r=XRA(tc), w=['xspill%d' % tc], dma='xsp%d' % tc)
        free_xres()
        sg_layer()
        xres = A.alloc([NT, D], F32)
        xres_box[0] = xres
        for tc in range(NT):
            S.add('sp', lambda h, tc=tc, xres=xres: h.dma_start(out=xres[:, tc, :], in_=xspill[tc * 128:(tc + 1) * 128, :]),
                  r=['xspill%d' % tc], w=XRA(tc), dma='xres%d' % tc)
        out_proj(PLAN['sg_out'])
        layer_norm(2, ln1g[1:2, :], ln1b[1:2, :], final=(last == "sg"))
    if "mlp1" in phases:
        mlp(1)
        layer_norm(3, ln2g[1:2, :], ln2b[1:2, :], final=True)

    S.emit(nc, es)
    es.close()
    return nc, S, A


def prep_shared(inp):
    f = lambda a: np.ascontiguousarray(np.asarray(a, dtype=np.float32))
    wg_aug = np.zeros((32, 1024), np.float32)
    wg_aug[0:16] = inp["gla_w_gate"][0]
    wg_aug[16] = inp["gla_b_gate"][0]
    ws = np.asarray(inp["sg_w_spatial"][0])
    wsT = ws.transpose(2, 0, 1)
    wsTs = np.tile(ws[:, :8, :8].transpose(2, 0, 1), (16, 1, 16))
    bsp = np.asarray(inp["sg_b_spatial"][0])
    bsp2 = np.stack([bsp, np.tile(bsp[:, :8], (1, 16))])
    b_in = np.asarray(inp["sg_b_in"][0])
    d = dict(
        gla_w_in=f(inp["gla_w_in"][0]), wg_aug=wg_aug, gla_ng=f(np.asarray(inp["gla_norm_g"][0]).reshape(1, D)),
        gla_w_out=f(inp["gla_w_out"][0]), sg_w_in=f(inp["sg_w_in"][0]),
        sg_binT=f(b_in[:D].reshape(16, 128).T), sg_bv=f(b_in[D:].reshape(1, D)),
        sg_vg=f(np.asarray(inp["sg_v_norm_g"][0]).reshape(1, D)), sg_vb=f(np.asarray(inp["sg_v_norm_b"][0]).reshape(1, D)),
        sg_wsT=f(wsT), sg_wsTs=f(wsTs), sg_bsp=f(bsp2), sg_w_out=f(inp["sg_w_out"][0]),
        mlp_w1=f(inp["mlp_w1"]), mlp_w2=f(inp["mlp_w2"]),
        ln1_g=f(inp["ln1_g"]), ln1_b=f(inp["ln1_b"]), ln2_g=f(inp["ln2_g"]), ln2_b=f(inp["ln2_b"]),
        ident=np.eye(128, dtype=np.float32), cmask=make_consts(),
    )
    lnT = np.zeros((4, 128, 2, 16), np.float32)
    for n_, (gk, bk, li) in enumerate([("ln1_g", "ln1_b", 0), ("ln2_g", "ln2_b", 0), ("ln1_g", "ln1_b", 1), ("ln2_g", "ln2_b", 1)]):
        lnT[n_, :, 0, :] = np.asarray(inp[gk][li]).reshape(16, 128).T
        lnT[n_, :, 1, :] = np.asarray(inp[bk][li]).reshape(16, 128).T
    d["lnT"] = lnT
    return d


def prep_core(inp, c):
    xpr = np.asarray(inp["x_prompt"])
    xs = np.asarray(inp["x_sample"])
    b, hf = c // 2, c % 2
    xm = np.concatenate([xpr[b, hf * 1024:(hf + 1) * 1024], xs[16 * c:16 * (c + 1)].reshape(128, D)], axis=0)
    if hf == 1:
        xp = xpr[b, 0:1024]
    else:
        xp = np.zeros((1024, D), np.float32)
    st = np.asarray(inp["state_gla"])[0, 16 * c:16 * (c + 1)]
    return dict(xm=np.ascontiguousarray(xm, dtype=np.float32), xp=np.ascontiguousarray(xp, dtype=np.float32),
                st=np.ascontiguousarray(st, dtype=np.float32))


_CACHE = {}


def kernel(**inputs):
    if "nc" not in _CACHE:
        _CACHE["nc"] = build()[0]
    nc = _CACHE["nc"]
    shared = prep_shared(inputs)
    in_maps = []
    for c in range(8):
        m = dict(shared)
        m.update(prep_core(inputs, c))
        in_maps.append(m)
    res = run_bass_kernel_spmd(nc, in_maps, core_ids=list(range(8)))
    R = res.results
    y_prompt = np.zeros((4, 2048, D), np.float32)
    y_sample = np.zeros((128, 8, D), np.float32)
    gp = np.zeros((1, 4, 4, 256, 512), np.float32)
    gs = np.zeros((1, 128, 4, 256, 512), np.float32)
    sgv = np.zeros((1, 128, 8, D), np.float32)
    for c in range(8):
        b, hf = c // 2, c % 2
        yc = R[c]["y"]
        y_prompt[b, hf * 1024:(hf + 1) * 1024] = yc[:1024]
        y_sample[16 * c:16 * (c + 1)] = yc[1024:].reshape(16, 8, D)
        if hf == 1:
            gp[0, b] = R[c]["gst_p"]
        gs[0, 16 * c:16 * (c + 1)] = R[c]["gst_s"]
        sgv[0, 16 * c:16 * (c + 1)] = R[c]["sgv"].reshape(16, 8, D)
    return (y_prompt, y_sample, gp, gs, sgv)
```

```python
import numpy as np
from contextlib import ExitStack
import concourse.bass as bass
import concourse.mybir as mybir
from concourse.bass_utils import run_bass_kernel_spmd

F32 = mybir.dt.float32
BF16 = mybir.dt.bfloat16
AF = mybir.ActivationFunctionType
ALU = mybir.AluOpType

D = 2048
KD = 16
NT = 9
NPF = 8
T = NT * 128
TT = 384
DFF = 8192
ALPHA = float((2.0 * 2) ** 0.25)
LN_EPS = 1e-5
HN_EPS = 1e-6
import os as _os2
DBG_NOUNIT = bool(_os2.environ.get('DBG_NOUNIT'))
import os as _os
NO_SAME_ENG_SYNC = bool(_os.environ.get('NO_SAME_ENG_SYNC'))


class Sched:
    def __init__(self):
        self.ops = []
        self.res = {}
        self.ghost = set()

    def add(self, eng, fn, r=(), w=(), dma=None):
        i = len(self.ops)
        deps = set()
        for name in r:
            st = self.res.get(name)
            if st is None:
                st = self.res[name] = [None, list(self.ghost)]
            if st[0] is not None:
                deps.add(st[0])
            if name.startswith('ps'):
                deps.update(d for d in st[1] if self.ops[d]['eng'] != eng)
        for name in w:
            st = self.res.get(name)
            if st is None:
                st = self.res[name] = [None, list(self.ghost)]
            if st[0] is not None:
                deps.add(st[0])
            deps.update(st[1])
        for name in r:
            self.res[name][1].append(i)
        for name in w:
            self.res[name] = [i, []]
        deps.discard(i)
        self.ops.append(dict(i=i, eng=eng, fn=fn, deps=deps, dma=dma, signal=False))
        return i

    def retire(self, names):
        for n in names:
            st = self.res.pop(n, None)
            if st is None:
                continue
            if st[0] is not None:
                self.ghost.add(st[0])
            self.ghost.update(st[1])
        best = {}
        for d in self.ghost:
            p = self.ops[d]
            key = ('d', p['dma']) if p['dma'] else ('c', p['eng'])
            if key not in best or best[key] < d:
                best[key] = d
        self.ghost = set(best.values())

    def retire_prefix(self, *prefixes):
        self.retire([n for n in list(self.res) if any(n.startswith(p) for p in prefixes)])

    def emit(self, nc, es, final_eng='sp'):
        ops = self.ops
        last_dma = {}
        for op in ops:
            if op['dma']:
                last_dma[op['dma']] = op['i']
        fin = dict(i=len(ops), eng=final_eng, fn=None, deps=set(last_dma.values()), dma=None, signal=False)
        ops.append(fin)
        for op in ops:
            best = {}
            for d in op['deps']:
                p = ops[d]
                key = ('d', p['dma']) if p['dma'] else ('c', p['eng'])
                if key not in best or best[key] < d:
                    best[key] = d
            rd = []
            for key, d in best.items():
                p = ops[d]
                if p['dma'] is None and p['eng'] == 'pe' and op['eng'] == 'pe' and op['dma'] is None:
                    continue
                if NO_SAME_ENG_SYNC and p['dma'] is None and op['dma'] is None and p['eng'] == op['eng']:
                    continue
                if p['dma'] is None:
                    p['signal'] = True
                rd.append(d)
            op['rdeps'] = rd
        cnt = {}
        dcnt = {}
        for op in ops:
            if op['dma']:
                dcnt[op['dma']] = dcnt.get(op['dma'], 0) + 16
                op['sval'] = dcnt[op['dma']]
            elif op['signal']:
                cnt[op['eng']] = cnt.get(op['eng'], 0) + 1
                op['sval'] = cnt[op['eng']]
        engs = ['pe', 'act', 'dve', 'pool', 'sp']
        sems = {e: es.enter_context(nc.semaphore("s_" + e)) for e in engs}
        dsems = {k: es.enter_context(nc.semaphore("d_%d" % n)) for n, k in enumerate(sorted(dcnt))}
        self.nsem = len(sems) + len(dsems)
        self.maxcnt = dict(cnt)
        block = es.enter_context(nc.Block())
        per = {e: [op for op in ops if op['eng'] == e] for e in engs}

        def run(e, h):
            waited = {}
            for op in per[e]:
                for d in op['rdeps']:
                    p = ops[d]
                    if p['dma']:
                        s, v, k = dsems[p['dma']], p['sval'], ('d', p['dma'])
                    else:
                        s, v, k = sems[p['eng']], p['sval'], ('c', p['eng'])
                    if waited.get(k, 0) >= v:
                        continue
                    waited[k] = v
                    h.wait_ge(s, v)
                if op['fn'] is None:
                    continue
                ins = op['fn'](h)
                if op['dma']:
                    ins.then_inc(dsems[op['dma']], 16)
                elif op['signal']:
                    ins.then_inc(sems[e], 1)

        @block.tensor
        def _(h):
            run('pe', h)

        @block.scalar
        def _(h):
            run('act', h)

        @block.vector
        def _(h):
            run('dve', h)

        @block.gpsimd
        def _(h):
            run('pool', h)

        @block.sync
        def _(h):
            run('sp', h)


class Arena:
    def __init__(self, ap_f32):
        self.ap = ap_f32
        self.n = ap_f32.shape[1]
        self.free = [(0, self.n)]
        self.live = {}
        self.peak = 0

    def alloc(self, shape, dtype, name=None):
        n = int(np.prod(shape))
        words = n if dtype == F32 else (n + 1) // 2
        words = (words + 1) // 2 * 2
        for idx, (o, sz) in enumerate(self.free):
            if sz >= words:
                break
        else:
            raise AssertionError(("arena overflow", name, words, self.free))
        if sz == words:
            self.free.pop(idx)
        else:
            self.free[idx] = (o + words, sz - words)
        self.peak = max(self.peak, o + words)
        v = self.ap[:, o:o + words]
        if dtype != F32:
            v = v.bitcast(dtype)
        v = v[:, 0:n]
        if len(shape) == 2:
            v = v.rearrange("p (a b) -> p a b", a=shape[0])
        elif len(shape) == 3:
            v = v.rearrange("p (a b c) -> p a b c", a=shape[0], b=shape[1])
        self.live[id(v)] = (o, words, v)
        return v

    def release(self, *views):
        for v in views:
            o, words, _ = self.live.pop(id(v))
            self.free.append((o, words))
        self.free.sort()
        merged = []
        for o, sz in self.free:
            if merged and merged[-1][0] + merged[-1][1] == o:
                merged[-1] = (merged[-1][0], merged[-1][1] + sz)
            else:
                merged.append((o, sz))
        self.free = merged


ARENA_WORDS = 53000

C_TRI_S, C_TRIU_S, C_BD_S, C_BDU_S, C_TRI01, C_BD01 = range(6)


def make_consts():
    p = np.arange(128)
    s, t = p[:, None], p[None, :]
    same = (s // 8) == (t // 8)
    tri = (s <= t)
    triu = (s > t)
    m = np.zeros((128, 6 * 128 + 16 + 4), np.float32)
    m[:, 0:128] = tri * (-1.0 / 16)
    m[:, 128:256] = triu * (-1.0 / 16)
    m[:, 256:384] = (tri & same) * (-1.0 / 16)
    m[:, 384:512] = (triu & same) * (-1.0 / 16)
    m[:, 512:640] = tri
    m[:, 640:768] = tri & same
    m[:, 768:784] = (p[:, None] // 8) == np.arange(16)[None, :]
    m[:, 784] = -1.0 / 16
    m[:, 785] = 1.0
    m[:, 786] = -1.0 / 16
    m[:, 787] = -1.0 / 16
    return m


def build(phases=("gla", "mlp0", "sg", "mlp1"), dbg=False):
    nc = bass.Bass("TRN2", target_bir_lowering=False)

    def din(name, shape):
        return nc.dram_tensor(name, list(shape), F32, kind="ExternalInput").ap()

    def dout(name, shape):
        return nc.dram_tensor(name, list(shape), F32, kind="ExternalOutput").ap()

    xm = din("xm", [T, D])
    xp = din("xp", [NPF * 128, D])
    st_in = din("st", [16, 4, 256, 512])
    gla_w_in = din("gla_w_in", [D, 6160])
    wg_aug = din("wg_aug", [32, 1024])
    gla_ng = din("gla_ng", [1, D])
    gla_w_out = din("gla_w_out", [D, D])
    sg_w_in = din("sg_w_in", [D, 2 * D])
    sg_binT = din("sg_binT", [128, 16])
    sg_bv = din("sg_bv", [1, D])
    sg_vg = din("sg_vg", [1, D])
    sg_vb = din("sg_vb", [1, D])
    sg_wsT = din("sg_wsT", [128, 8, 128])
    sg_wsTs = din("sg_wsTs", [128, 8, 128])
    sg_bsp = din("sg_bsp", [2, 8, 128])
    sg_w_out = din("sg_w_out", [D, D])
    w1 = din("mlp_w1", [2, D, DFF])
    w2 = din("mlp_w2", [2, DFF, D])
    ln1g = din("ln1_g", [2, D])
    ln1b = din("ln1_b", [2, D])
    ln2g = din("ln2_g", [2, D])
    ln2b = din("ln2_b", [2, D])
    ident_d = din("ident", [128, 128])
    lnT_d = din("lnT", [4, 128, 2, 16])
    cm_d = din("cmask", [128, 788])
    y = dout("y", [T, D])
    gst_p = dout("gst_p", [4, 256, 512])
    gst_s = dout("gst_s", [16, 4, 256, 512])
    sgv = dout("sgv", [128, D])
    xspill = nc.dram_tensor("xspill", [T, D], F32, kind="Internal").ap()
    sscr = nc.dram_tensor("sscr", [4, 256, 512], F32, kind="Internal").ap()

    S = Sched()
    es = ExitStack()
    arena_t = es.enter_context(nc.sbuf_tensor("arena", [128, ARENA_WORDS], F32))
    A = Arena(arena_t[:])
    ps = [es.enter_context(nc.psum_tensor("ps%d" % i, [128, 512], F32)) for i in range(8)]
    psb = [p_[:].bitcast(BF16) for p_ in ps]
    bank_ctr = [0]

    pinned = set()

    def bank():
        while True:
            b = bank_ctr[0] % 8
            bank_ctr[0] += 1
            if b not in pinned:
                return b

    def PS(b):
        return 'ps%d' % b

    ident = A.alloc([128], F32)
    identb = A.alloc([128], BF16)
    NWS = 4
    wbuf = [A.alloc([16, 512], BF16) for _ in range(NWS)]
    xT = A.alloc([KD, T], BF16)
    free_w = list(range(NWS))

    wq = []
    wq_pos = [0]

    def wq_add(parts):
        wq.append(dict(parts=parts, slot=None))
        return len(wq) - 1

    def wq_issue_pending():
        while free_w and wq_pos[0] < len(wq):
            e = wq[wq_pos[0]]
            wq_pos[0] += 1
            slot = free_w.pop(0)
            e['slot'] = slot
            for n_, (dst_fn, src_ap) in enumerate(e['parts']):
                dst = dst_fn(slot)
                S.add('pool', lambda h, dst=dst, src_ap=src_ap: h.dma_start(out=dst, in_=src_ap),
                      r=(['w%d' % slot] if n_ else []), w=['w%d' % slot], dma='w%d' % slot)

    def wq_get(idx):
        if wq[idx]['slot'] is None:
            wq_issue_pending()
        assert wq[idx]['slot'] is not None, ("weight block not issuable", idx, wq_pos[0], free_w)
        return wq[idx]['slot']

    def give_w(s_):
        free_w.append(s_)
        wq_issue_pending()

    def full_dst(a, b):
        def f(slot):
            dst = wbuf[slot]
            if (a, b) != (16, 512):
                dst = dst.rearrange("p a b -> p (a b)").rearrange("p (a b) -> p a b", a=a)
            return dst
        return f

    def XT(tc, k):
        return 'xT%d_%d' % (tc, k)

    def XTG(tc, g):
        return [XT(tc, 4 * g + i_) for i_ in range(4)]

    def XTT(tt, k):
        return [XT(3 * tt + i_, k) for i_ in range(3)]

    S.add('sp', lambda h: h.dma_start(out=ident, in_=ident_d), w=['ident'], dma='c_ident')
    S.add('pool', lambda h: h.dma_start(out=identb, in_=ident_d), w=['identb'], dma='c_identb')

    def load_w(slot, src_ap, dst=None):
        if dst is None:
            a, b = src_ap.shape[1], src_ap.shape[2]
            dst = wbuf[slot]
            if (a, b) != (16, 512):
                dst = dst.rearrange("p a b -> p (a b)").rearrange("p (a b) -> p a b", a=a)
        S.add('pool', lambda h: h.dma_start(out=dst, in_=src_ap), w=['w%d' % slot], dma='w%d' % slot)
        return dst

    cp_ctr = [0]

    def evac_copy(out, in_, r, w):
        cp_ctr[0] += 1
        if cp_ctr[0] % 2:
            S.add('act', lambda h: h.copy(out=out, in_=in_), r=r, w=w)
        else:
            S.add('dve', lambda h: h.tensor_copy(out=out, in_=in_), r=r, w=w)

    def transpose_f32_chunk(src, src_res, dstT, dst_res, tc):
        for g in range(4):
            b = bank()
            for i in range(4):
                kc = 4 * g + i
                S.add('pe', lambda h, kc=kc, i=i, b=b: h.transpose(
                    out=ps[b][:, i * 128:(i + 1) * 128], in_=src[:, kc * 128:(kc + 1) * 128], identity=ident),
                    r=[src_res[g] if isinstance(src_res, list) else src_res, 'ident'], w=[PS(b)])
            evac_copy(dstT[:, 4 * g:4 * g + 4, tc * 128:(tc + 1) * 128],
                      ps[b][:].rearrange("p (a t) -> p a t", a=4), r=[PS(b)], w=dst_res(tc, g))

    def transpose_bf16_chunk(src, src_res, dstT, dst_res, tc):
        for g in range(4):
            b = bank()
            for i in range(4):
                kc = 4 * g + i
                S.add('pe', lambda h, kc=kc, i=i, b=b: h.transpose(
                    out=psb[b][:, i * 128:(i + 1) * 128], in_=src[:, kc * 128:(kc + 1) * 128], identity=identb),
                    r=[src_res, 'identb'], w=[PS(b)])
            evac_copy(dstT[:, 4 * g:4 * g + 4, tc * 128:(tc + 1) * 128],
                      psb[b][:, 0:512].rearrange("p (a t) -> p a t", a=4), r=[PS(b)], w=dst_res(tc, g))

    def load_xT(src_dram, nchunks, dstT, dst_res_fn):
        stg = [A.alloc([D], F32) for _ in range(2)]
        for tc in range(nchunks):
            i = tc % 2
            S.add('sp', lambda h, tc=tc, i=i: h.dma_start(out=stg[i], in_=src_dram[tc * 128:(tc + 1) * 128, :]),
                  w=['xstg%d' % i], dma='xstg%d' % i)
            transpose_f32_chunk(stg[i], 'xstg%d' % i, dstT, dst_res_fn, tc)
        S.retire_prefix('xstg')
        A.release(*stg)

    xres_box = [None]

    def XR(tc, c):
        return 'xres%d_%d' % (tc, c)

    def XRA(tc):
        return ['xres%d_%d' % (tc, c) for c in range(4)]

    def alloc_xres(src_dram):
        xres = A.alloc([NT, D], F32)
        xres_box[0] = xres
        for tc in range(NT):
            S.add('sp', lambda h, tc=tc: h.dma_start(out=xres[:, tc, :], in_=src_dram[tc * 128:(tc + 1) * 128, :]),
                  w=XRA(tc), dma='xres%d' % tc)

    def free_xres():
        S.retire([n for tc in range(NT) for n in XRA(tc)])
        A.release(xres_box[0])
        xres_box[0] = None

    def out_proj(plan):
        xres = xres_box[0]
        for cb in range(4):
            s_ = wq_get(plan[cb])
            for tc in range(NT):
                b = bank()
                for k in range(KD):
                    S.add('pe', lambda h, k=k, tc=tc, b=b, s_=s_: h.matmul(
                        ps[b][:, :], lhsT=xT[:, k, tc * 128:(tc + 1) * 128], rhs=wbuf[s_][:, k, :],
                        start=(k == 0), stop=(k == KD - 1)),
                        r=[XT(tc, k), 'w%d' % s_], w=[PS(b)])
                dst = xres[:, tc, cb * 512:(cb + 1) * 512]
                S.add('dve', lambda h, dst=dst, b=b: h.scalar_tensor_tensor(
                    out=dst, in0=dst, scalar=ALPHA, op0=ALU.mult, in1=ps[b][:, :], op1=ALU.add),
                    r=[PS(b), XR(tc, cb)], w=[XR(tc, cb)])
            give_w(s_)

    def mlp(layer):
        xres = xres_box[0]
        hT = [A.alloc([4, T], BF16) for _ in range(2)]
        rtmp = [A.alloc([TT], BF16) for _ in range(2)]
        NFB = DFF // 512
        w1v = w1[layer].rearrange("(k p) f -> p k f", p=128)
        w2v = w2[layer].rearrange("(fb c p) d -> fb p c d", p=128, c=4)
        slots = {}

        plan = PLAN['mlp%d' % layer]

        def issue1(fb):
            slots[fb] = [wq_get(plan[fb][0]), None, None]

        def issue2(fb):
            s2 = wq_get(plan[fb][1])
            slots[fb][1] = s2
            slots[fb][2] = full_dst(4, 2048)(s2)

        def stage_a(fb):
            s1 = slots[fb][0]
            hs = fb % 2
            for fc in range(4):
                bs = [bank() for _ in range(3)]
                for k in range(KD):
                    for tt in range(3):
                        S.add('pe', lambda h, k=k, tt=tt, fc=fc, b=bs[tt]: h.matmul(
                            ps[b][:, 0:TT], lhsT=wbuf[s1][:, k, fc * 128:(fc + 1) * 128],
                            rhs=xT[:, k, tt * TT:(tt + 1) * TT], start=(k == 0), stop=(k == KD - 1)),
                            r=['w%d' % s1] + XTT(tt, k), w=[PS(bs[tt])])
                for tt in range(3):
                    rt = (fc * 3 + tt) % 2
                    S.add('act', lambda h, tt=tt, b=bs[tt], rt=rt: h.activation(
                        out=rtmp[rt], in_=ps[b][:, 0:TT], func=AF.Relu),
                        r=[PS(bs[tt])], w=['rtmp%d' % rt])
                    eng = 'pool' if tt == 1 else 'dve'
                    S.add(eng, lambda h, tt=tt, fc=fc, rt=rt: h.tensor_tensor(
                        out=hT[hs][:, fc, tt * TT:(tt + 1) * TT], in0=rtmp[rt], in1=rtmp[rt], op=ALU.mult),
                        r=['rtmp%d' % rt], w=['hT%d' % hs])

        def stage_b(fb):
            s2, d2 = slots[fb][1], slots[fb][2]
            hs = fb % 2
            for tc in range(NT):
                for cb in range(4):
                    b = bank()
                    for fc in range(4):
                        S.add('pe', lambda h, fc=fc, tc=tc, cb=cb, b=b: h.matmul(
                            ps[b][:, :], lhsT=hT[hs][:, fc, tc * 128:(tc + 1) * 128],
                            rhs=d2[:, fc, cb * 512:(cb + 1) * 512], start=(fc == 0), stop=(fc == 3)),
                            r=['hT%d' % hs, 'w%d' % s2], w=[PS(b)])
                    dst = xres[:, tc, cb * 512:(cb + 1) * 512]
                    if fb == 0:
                        S.add('dve', lambda h, dst=dst, b=b: h.scalar_tensor_tensor(
                            out=dst, in0=dst, scalar=ALPHA, op0=ALU.mult, in1=ps[b][:, :], op1=ALU.add),
                            r=[PS(b), XR(tc, cb)], w=[XR(tc, cb)])
                    else:
                        S.add('dve', lambda h, dst=dst, b=b: h.tensor_tensor(
                            out=dst, in0=dst, in1=ps[b][:, :], op=ALU.add),
                            r=[PS(b), XR(tc, cb)], w=[XR(tc, cb)])

        issue1(0)
        issue2(0)
        for fb in range(NFB + 1):
            if fb < NFB:
                if fb > 0:
                    issue1(fb)
                stage_a(fb)
                give_w(slots[fb][0])
            if fb >= 1:
                if fb - 1 > 0:
                    issue2(fb - 1)
                stage_b(fb - 1)
                give_w(slots[fb - 1][1])
        S.retire_prefix('hT', 'rtmp')
        A.release(*hT, *rtmp)

    def ln_stats(z, zr, st, sm, tag, eps):
        for c in range(4):
            S.add('dve', lambda h, c=c: h.bn_stats(out=st[:, c, :], in_=z[:, c * 512:(c + 1) * 512]),
                  r=[zr[c]], w=[tag + 'st'])
        S.add('dve', lambda h: h.bn_aggr(out=sm[:, 0:2], in_=st), r=[tag + 'st'], w=[tag + 'sm'])
        S.add('act', lambda h: h.activation(out=sm[:, 2:3], in_=sm[:, 1:2], func=AF.Ln, bias=eps_t[:, 0:1], scale=1.0),
              r=[tag + 'sm', 'eps'], w=[tag + 'sm'])
        S.add('act', lambda h: h.activation(out=sm[:, 3:4], in_=sm[:, 2:3], func=AF.Exp, scale=-0.5),
              r=[tag + 'sm'], w=[tag + 'sm'])
        S.add('dve', lambda h: h.tensor_scalar(out=sm[:, 4:5], in0=sm[:, 0:1], scalar1=sm[:, 3:4],
                                               scalar2=-1.0, op0=ALU.mult, op1=ALU.mult),
              r=[tag + 'sm'], w=[tag + 'sm'])

    def layer_norm(idx, g_dram, b_dram, final):
        xres = xres_box[0]
        gt = A.alloc([D], F32)
        bt = A.alloc([D], F32)
        gbc = A.alloc([2, 16], F32)
        st = [A.alloc([4, 6], F32) for _ in range(3)]
        sm = [A.alloc([8], F32) for _ in range(3)]
        S.add('sp', lambda h: h.dma_start(out=gt, in_=g_dram.partition_broadcast(128)), w=['ln_g'], dma='ln_g')
        S.add('sp', lambda h: h.dma_start(out=bt, in_=b_dram.partition_broadcast(128)), w=['ln_b'], dma='ln_b')
        S.add('sp', lambda h: h.dma_start(out=gbc, in_=lnT_d[idx]), w=['ln_c'], dma='ln_c')

        def st_stage(tc):
            i = tc % 3
            ln_stats(xres[:, tc, :], XRA(tc), st[i], sm[i], 'ln%d' % i, LN_EPS)

        def nrm_stage(tc):
            i = tc % 3
            z = xres[:, tc, :]
            for c in range(4):
                zr = XR(tc, c)
                zc = z[:, c * 512:(c + 1) * 512]
                S.add('act', lambda h, i=i, zc=zc: h.activation(out=zc, in_=zc, func=AF.Identity,
                                                             scale=sm[i][:, 3:4], bias=sm[i][:, 4:5]),
                      r=[zr, 'ln%dsm' % i], w=[zr])

        def t_stage(tc):
            z = xres[:, tc, :]
            for g in range(4):
                zr = XR(tc, g)
                b = bank()
                for q_ in range(4):
                    kc = 4 * g + q_
                    S.add('pe', lambda h, kc=kc, q_=q_, b=b: h.transpose(
                        out=ps[b][:, q_ * 128:(q_ + 1) * 128], in_=z[:, kc * 128:(kc + 1) * 128], identity=ident),
                        r=[zr, 'ident'], w=[PS(b)])
                for q_ in range(4):
                    kc = 4 * g + q_
                    dst = xT[:, kc, tc * 128:(tc + 1) * 128]
                    src = ps[b][:, q_ * 128:(q_ + 1) * 128]
                    if g % 2 == 0:
                        S.add('act', lambda h, kc=kc, dst=dst, src=src: h.activation(
                            out=dst, in_=src, func=AF.Identity, scale=gbc[:, 0, kc:kc + 1], bias=gbc[:, 1, kc:kc + 1]),
                            r=[PS(b), 'ln_c'], w=[XT(tc, kc)])
                    else:
                        S.add('dve', lambda h, kc=kc, dst=dst, src=src: h.tensor_scalar(
                            out=dst, in0=src, scalar1=gbc[:, 0, kc:kc + 1], scalar2=gbc[:, 1, kc:kc + 1],
                            op0=ALU.mult, op1=ALU.add), r=[PS(b), 'ln_c'], w=[XT(tc, kc)])

        def gb_stage(tc):
            z = xres[:, tc, :]
            for c in range(4):
                zr = XR(tc, c)
                zc = z[:, c * 512:(c + 1) * 512]
                S.add('pool', lambda h, c=c, zc=zc: h.tensor_tensor(out=zc, in0=zc, in1=gt[:, c * 512:(c + 1) * 512], op=ALU.mult),
                      r=[zr, 'ln_g'], w=[zr])
                S.add('dve', lambda h, c=c, zc=zc: h.tensor_tensor(out=zc, in0=zc, in1=bt[:, c * 512:(c + 1) * 512], op=ALU.add),
                      r=[zr, 'ln_b'], w=[zr])
            if final:
                S.add('sp', lambda h, tc=tc, z=z: h.dma_start(out=y[tc * 128:(tc + 1) * 128, :], in_=z),
                      r=XRA(tc), dma='yout%d' % (tc % 3))

        st_stage(0)
        st_stage(1)
        nrm_stage(0)
        for tc in range(NT):
            if tc + 2 < NT:
                st_stage(tc + 2)
            if tc + 1 < NT:
                nrm_stage(tc + 1)
            if not final:
                t_stage(tc)
            else:
                gb_stage(tc)
        if not final:
            for tc in range(NT):
                gb_stage(tc)
        S.retire_prefix('ln')
        A.release(gt, bt, gbc, *st, *sm)

    cm = A.alloc([788], F32)
    S.add('sp', lambda h: h.dma_start(out=cm, in_=cm_d), w=['cm'], dma='c_cm')
    eps_t = A.alloc([2], F32)
    S.add('dve', lambda h: h.memset(eps_t[:, 0:1], LN_EPS), w=['eps'])
    S.add('dve', lambda h: h.memset(eps_t[:, 1:2], HN_EPS), w=['eps'])

    def CM(i):
        return cm[:, i * 128:(i + 1) * 128]

    def gla_layer():
        winv = gla_w_in.rearrange("(k p) f -> p k f", p=128)
        wg = A.alloc([1024], F32)
        S.add('sp', lambda h: h.dma_start(out=wg[0:32, :], in_=wg_aug), w=['wg'], dma='c_wg')
        wg16 = A.alloc([16, 16], BF16)
        S.add('pool', lambda h: h.dma_start(out=wg16, in_=winv[:, :, 6144:6160]), w=['wg16'], dma='c_wg16')
        identb_r = ['identb']

        def gate_T(srcT, src_res_list, ntok, name):
            gTa = A.alloc([ntok], F32)
            S.add('dve', lambda h: h.memset(gTa[0:32, :], 1.0), w=[name])
            ntt = ntok // TT if ntok % TT == 0 else None
            tiles = [(i * TT, TT) for i in range(ntok // TT)] if ntt else [(i * 512, 512) for i in range(ntok // 512)]
            for (o, n) in tiles:
                b = bank()
                for k in range(KD):
                    if src_res_list is None:
                        rr = [XT(tc_, k) for tc_ in range(o // 128, (o + n - 1) // 128 + 1)]
                    else:
                        rr = sorted(set(src_res_list[(o // 128):((o + n - 1) // 128) + 1]))
                    S.add('pe', lambda h, k=k, b=b, o=o, n=n: h.matmul(
                        ps[b][0:16, 0:n], lhsT=wg16[:, k, :], rhs=srcT[:, k, o:o + n],
                        start=(k == 0), stop=(k == KD - 1)), r=['wg16'] + rr, w=[PS(b)])
                S.add('act', lambda h, b=b, o=o, n=n: h.copy(out=gTa[0:16, o:o + n], in_=ps[b][0:16, 0:n]),
                      r=[PS(b)], w=[name])
            return gTa

        class WS:
            pass

        sgS = A.alloc([512], F32)
        t1S = A.alloc([512], BF16)
        junkS = A.alloc([512], BF16)
        v3 = [A.alloc([512], BF16) for _ in range(4)]

        def make_ws():
            w_ = WS()
            w_.qk = A.alloc([512], BF16)
            w_.sg = sgS
            w_.t1 = t1S
            w_.srg = A.alloc([512], BF16)
            w_.nl = A.alloc([256], F32)
            w_.eend = A.alloc([256], F32)
            w_.kte = A.alloc([256], BF16)
            w_.epos = A.alloc([2, 128], F32)
            w_.eneg = A.alloc([2, 128], F32)
            w_.qdT = A.alloc([2, 128], BF16)
            w_.kiT = A.alloc([2, 128], BF16)
            w_.scm = A.alloc([128], BF16)
            w_.junk = junkS
            w_.sm = A.alloc([8], F32)
            w_.all = [w_.qk, w_.srg, w_.nl, w_.eend, w_.kte, w_.epos, w_.eneg,
                      w_.qdT, w_.kiT, w_.scm, w_.sm]
            return w_

        xpT = A.alloc([KD, NPF * 128], BF16)
        load_xT(xp, NPF, xpT, lambda tc, g: ['xpT%d' % tc])
        XPT = ['xpT%d' % tc for tc in range(NPF)]
        gTp = gate_T(xpT, XPT, NPF * 128, 'gTp')
        Sf = A.alloc([2, 512], F32)
        wsets = [make_ws() for _ in range(3)]
        dec = [A.alloc([4], F32) for _ in range(2)]

        def gate_common(W, i, gsrc, gres, c, h_, tri_u):
            R = 'g%d' % i
            bg = bank()
            S.add('pe', lambda h, bg=bg: h.matmul(ps[bg][:, 0:256], lhsT=gsrc[0:32, c * 128:(c + 1) * 128],
                                                   rhs=wg[0:32, h_ * 256:(h_ + 1) * 256], start=True, stop=True),
                  r=['wg', gres], w=[PS(bg)])
            S.add('act', lambda h, bg=bg: h.activation(out=W.nl, in_=ps[bg][:, 0:256], func=AF.Exp, scale=-1.0),
                  r=[PS(bg)], w=[R + 'nl'])
            S.add('act', lambda h: h.activation(out=W.nl, in_=W.nl, func=AF.Ln, bias=cm[:, 785:786], scale=1.0),
                  r=[R + 'nl', 'cm'], w=[R + 'nl'])
            brc = bank()
            S.add('pe', lambda h, brc=brc: h.matmul(ps[brc][:, 0:256], lhsT=tri_u, rhs=W.nl, start=True, stop=True),
                  r=['cm', R + 'nl'], w=[PS(brc)])
            S.add('act', lambda h, brc=brc: h.activation(out=W.eend, in_=ps[brc][:, 0:256], func=AF.Exp),
                  r=[PS(brc)], w=[R + 'eend'])
            S.add('pool', lambda h: h.tensor_tensor(out=W.kte, in0=W.qk[:, 256:512], in1=W.eend, op=ALU.mult),
                  r=[R + 'qk', R + 'eend'], w=[R + 'kte'])

        def pf_Pk(h_, c, sk):
            W, R = wsets[c % 2], 'g%d' % (c % 2)
            bk = bank()
            for k in range(KD):
                S.add('pe', lambda h, k=k: h.matmul(
                    ps[bk][:, 0:256], lhsT=xpT[:, k, c * 128:(c + 1) * 128], rhs=wbuf[sk][:, k, 256:512],
                    start=(k == 0), stop=(k == KD - 1)), r=[XPT[c], 'w%d' % sk], w=[PS(bk)])
            S.add('dve', lambda h: h.tensor_copy(out=W.qk[:, 256:512], in_=ps[bk][:, 0:256]), r=[PS(bk)], w=[R + 'qk'])

        def pf_Pv(h_, c, sv):
            vv, VR = v3[c % 3], 'gv%d' % (c % 3)
            bv_ = bank()
            for k in range(KD):
                S.add('pe', lambda h, k=k: h.matmul(
                    ps[bv_][:, :], lhsT=xpT[:, k, c * 128:(c + 1) * 128], rhs=wbuf[sv][:, k, :],
                    start=(k == 0), stop=(k == KD - 1)), r=[XPT[c], 'w%d' % sv], w=[PS(bv_)])
            S.add('act', lambda h: h.copy(out=vv, in_=ps[bv_][:, :]), r=[PS(bv_)], w=[VR])

        def pf_G01(h_, c):
            W, R = wsets[c % 2], 'g%d' % (c % 2)
            bg = bank()
            S.add('pe', lambda h: h.matmul(ps[bg][:, 0:256], lhsT=gTp[0:32, c * 128:(c + 1) * 128],
                                           rhs=wg[0:32, h_ * 256:(h_ + 1) * 256], start=True, stop=True),
                  r=['wg', 'gTp'], w=[PS(bg)])
            S.add('act', lambda h: h.activation(out=W.nl, in_=ps[bg][:, 0:256], func=AF.Exp, scale=-1.0),
                  r=[PS(bg)], w=[R + 'nl'])
            S.add('act', lambda h: h.activation(out=W.nl, in_=W.nl, func=AF.Ln, bias=cm[:, 785:786], scale=1.0),
                  r=[R + 'nl', 'cm'], w=[R + 'nl'])

        def pf_G2(h_, c):
            W, R, i = wsets[c % 2], 'g%d' % (c % 2), c % 2
            brc = bank()
            S.add('pe', lambda h: h.matmul(ps[brc][:, 0:256], lhsT=CM(C_TRIU_S), rhs=W.nl, start=True, stop=True),
                  r=['cm', R + 'nl'], w=[PS(brc)])
            bb = bank()
            for j in range(2):
                S.add('pe', lambda h, j=j: h.matmul(
                    ps[bb][:, 2 * j:2 * j + 2], lhsT=W.nl[:, j * 128:(j + 1) * 128], rhs=cm[:, 786:788],
                    start=True, stop=True), r=[R + 'nl', 'cm'], w=[PS(bb)])
            S.add('act', lambda h: h.activation(out=W.eend, in_=ps[brc][:, 0:256], func=AF.Exp),
                  r=[PS(brc)], w=[R + 'eend'])
            S.add('act', lambda h: h.activation(out=dec[i], in_=ps[bb][:, 0:4], func=AF.Exp),
                  r=[PS(bb)], w=['dec%d' % i])
            S.add('pool', lambda h: h.tensor_tensor(out=W.kte, in0=W.qk[:, 256:512], in1=W.eend, op=ALU.mult),
                  r=[R + 'qk', R + 'eend'], w=[R + 'kte'])

        def pf_U(h_, c):
            W, R, i = wsets[c % 2], 'g%d' % (c % 2), c % 2
            vv, VR = v3[c % 3], 'gv%d' % (c % 3)
            for j in range(2):
                bu = bank()
                S.add('pe', lambda h, j=j, bu=bu: h.matmul(
                    ps[bu][:, :], lhsT=W.kte[:, j * 128:(j + 1) * 128], rhs=vv, start=True, stop=True),
                    r=[R + 'kte', VR], w=[PS(bu)])
                S.add('dve', lambda h, j=j, bu=bu: h.scalar_tensor_tensor(
                    out=Sf[:, j, :], in0=Sf[:, j, :], scalar=dec[i][:, 2 * j:2 * j + 1], op0=ALU.mult,
                    in1=ps[bu][:, :], op1=ALU.add), r=[PS(bu), 'dec%d' % i, 'Sf'], w=['Sf'])

        for h_ in range(4):
            sk = wq_get(PLAN['pf'][h_][0])
            sv = wq_get(PLAN['pf'][h_][1])
            S.add('dve', lambda h: h.memset(Sf, 0.0), w=['Sf'])
            for c in range(-2, NPF):
                if 0 <= c + 2 < NPF:
                    pf_Pk(h_, c + 2, sk)
                if 0 <= c + 1 < NPF:
                    pf_G2(h_, c + 1)
                if 0 <= c + 2 < NPF:
                    pf_Pv(h_, c + 2, sv)
                if c >= 0:
                    pf_U(h_, c)
                if 0 <= c + 2 < NPF:
                    pf_G01(h_, c + 2)
            give_w(sk)
            give_w(sv)
            S.add('sp', lambda h, h_=h_: h.dma_start(out=sscr[h_].rearrange("(j p) v -> p j v", p=128), in_=Sf),
                  r=['Sf'], w=['sscr%d' % h_], dma='sscr%d' % h_)
        S.retire(XPT + ['gTp'] + ['dec0', 'dec1'])
        A.release(xpT, gTp, *dec)

        gated = A.alloc([NT, D], BF16)
        gTm = gate_T(xT, None, T, 'gTm')
        gng = A.alloc([512], F32)
        Sb = A.alloc([2, 512], BF16)
        s0 = [A.alloc([2, 512], F32) for _ in range(3)]
        Qm = [A.alloc([2, 128], F32) for _ in range(2)]
        kteM = [A.alloc([256], BF16) for _ in range(2)]

        class Ck:
            pass

        def mk(h_, c, slots):
            ck = Ck()
            ck.h, ck.c, ck.slots = h_, c, slots
            ck.sample = (c == NT - 1)
            if ck.sample:
                ck.W, ck.R, ck.v, ck.VR = wsets[2], 'g2', v3[3], 'gv3'
            else:
                ck.W, ck.R, ck.v, ck.VR = wsets[c % 2], 'g%d' % (c % 2), v3[c % 3], 'gv%d' % (c % 3)
            return ck

        def proj(ck, slot):
            b = bank()
            c = ck.c
            for k in range(KD):
                S.add('pe', lambda h, k=k: h.matmul(
                    ps[b][:, :], lhsT=xT[:, k, c * 128:(c + 1) * 128], rhs=wbuf[slot][:, k, :],
                    start=(k == 0), stop=(k == KD - 1)), r=[XT(c, k), 'w%d' % slot], w=[PS(b)])
            return b

        def P_qk(ck):
            W, R = ck.W, ck.R
            bq = proj(ck, ck.slots[0])
            S.add('act', lambda h: h.mul(out=W.qk[:, 0:256], in_=ps[bq][:, 0:256], mul=1.0 / 16), r=[PS(bq)], w=[R + 'qk'])
            S.add('act', lambda h: h.copy(out=W.qk[:, 256:512], in_=ps[bq][:, 256:512]), r=[PS(bq)], w=[R + 'qk'])

        def P_v(ck):
            bv_ = proj(ck, ck.slots[1])
            S.add('act', lambda h: h.copy(out=ck.v, in_=ps[bv_][:, :]), r=[PS(bv_)], w=[ck.VR])

        def P_r(ck):
            W, R, h_ = ck.W, ck.R, ck.h
            br = proj(ck, ck.slots[2])
            S.add('act', lambda h: h.activation(out=W.sg, in_=ps[br][:, :], func=AF.Exp, scale=-1.0), r=[PS(br)], w=['gsg'])
            S.add('act', lambda h: h.activation(out=W.sg, in_=W.sg, func=AF.Ln, bias=cm[:, 785:786], scale=1.0),
                  r=['gsg', 'cm'], w=['gsg'])
            S.add('act', lambda h: h.activation(out=W.sg, in_=W.sg, func=AF.Exp, scale=-1.0), r=['gsg'], w=['gsg'])
            S.add('dve', lambda h: h.tensor_tensor(out=W.t1, in0=ps[br][:, :], in1=W.sg, op=ALU.mult),
                  r=[PS(br), 'gsg'], w=['gt1'])
            S.add('pool', lambda h: h.tensor_tensor(out=W.srg, in0=W.t1, in1=gng, op=ALU.mult),
                  r=['gt1', 'gng'], w=[R + 'srg'])

        def G01(ck):
            W, R, c, h_ = ck.W, ck.R, ck.c, ck.h
            bg = bank()
            S.add('pe', lambda h: h.matmul(ps[bg][:, 0:256], lhsT=gTm[0:32, c * 128:(c + 1) * 128],
                                           rhs=wg[0:32, h_ * 256:(h_ + 1) * 256], start=True, stop=True),
                  r=['wg', 'gTm'], w=[PS(bg)])
            S.add('act', lambda h: h.activation(out=W.nl, in_=ps[bg][:, 0:256], func=AF.Exp, scale=-1.0),
                  r=[PS(bg)], w=[R + 'nl'])
            S.add('act', lambda h: h.activation(out=W.nl, in_=W.nl, func=AF.Ln, bias=cm[:, 785:786], scale=1.0),
                  r=[R + 'nl', 'cm'], w=[R + 'nl'])

        def G23(ck):
            W, R = ck.W, ck.R
            tri_u = CM(C_BDU_S) if ck.sample else CM(C_TRIU_S)
            tri = CM(C_BD_S) if ck.sample else CM(C_TRI_S)
            brc = bank()
            S.add('pe', lambda h: h.matmul(ps[brc][:, 0:256], lhsT=tri_u, rhs=W.nl, start=True, stop=True),
                  r=['cm', R + 'nl'], w=[PS(brc)])
            bbt = bank()
            for j in range(2):
                S.add('pe', lambda h, j=j: h.matmul(ps[bbt][:, j * 128:(j + 1) * 128], lhsT=W.nl[:, j * 128:(j + 1) * 128],
                                                     rhs=tri, start=True, stop=True), r=[R + 'nl', 'cm'], w=[PS(bbt)])
            S.add('act', lambda h: h.activation(out=W.eend, in_=ps[brc][:, 0:256], func=AF.Exp),
                  r=[PS(brc)], w=[R + 'eend'])
            S.add('act', lambda h: h.activation(out=W.epos.rearrange("p a b -> p (a b)"), in_=ps[bbt][:, 0:256], func=AF.Exp),
                  r=[PS(bbt)], w=[R + 'epos'])
            S.add('act', lambda h: h.activation(out=W.eneg.rearrange("p a b -> p (a b)"), in_=ps[bbt][:, 0:256], func=AF.Exp, scale=-1.0),
                  r=[PS(bbt)], w=[R + 'eneg'])
            S.add('pool', lambda h: h.tensor_tensor(out=W.kte, in0=W.qk[:, 256:512], in1=W.eend, op=ALU.mult),
                  r=[R + 'qk', R + 'eend'], w=[R + 'kte'])

        def G45(ck):
            W, R = ck.W, ck.R
            btr = bank()
            for j in range(4):
                S.add('pe', lambda h, j=j: h.transpose(out=psb[btr][:, j * 128:(j + 1) * 128],
                                                        in_=W.qk[:, j * 128:(j + 1) * 128], identity=identb),
                      r=[R + 'qk', 'identb'], w=[PS(btr)])
            S.add('dve', lambda h: h.tensor_tensor(out=W.qdT.rearrange("p a b -> p (a b)"), in0=psb[btr][:, 0:256],
                                                   in1=W.epos.rearrange("p a b -> p (a b)"), op=ALU.mult),
                  r=[PS(btr), R + 'epos'], w=[R + 'qdT'])
            S.add('dve', lambda h: h.tensor_tensor(out=W.kiT.rearrange("p a b -> p (a b)"), in0=psb[btr][:, 256:512],
                                                   in1=W.eneg.rearrange("p a b -> p (a b)"), op=ALU.mult),
                  r=[PS(btr), R + 'eneg'], w=[R + 'kiT'])

        def B12(ck):
            W, R = ck.W, ck.R
            bs = bank()
            for j in range(2):
                S.add('pe', lambda h, j=j: h.matmul(ps[bs][:, 0:128], lhsT=W.kiT[:, j, :], rhs=W.qdT[:, j, :],
                                                     start=(j == 0), stop=(j == 1)), r=[R + 'kiT', R + 'qdT'], w=[PS(bs)])
            m01 = CM(C_BD01) if ck.sample else CM(C_TRI01)
            S.add('dve', lambda h: h.tensor_tensor(out=W.scm, in0=ps[bs][:, 0:128], in1=m01, op=ALU.mult),
                  r=[PS(bs), 'cm'], w=[R + 'scm'])

        def B3(ck):
            W, R, c, h_ = ck.W, ck.R, ck.c, ck.h
            vv, VR = ck.v, ck.VR
            bo = bank()
            pinned.add(bo)
            ck.bo = bo
            S.add('pe', lambda h: h.matmul(ps[bo][:, :], lhsT=W.scm, rhs=vv, start=True, stop=False),
                  r=[R + 'scm', VR], w=[PS(bo)])
            if not ck.sample:
                for j in range(2):
                    S.add('pe', lambda h, j=j: h.matmul(ps[bo][:, :], lhsT=W.qdT[:, j, :], rhs=Sb[:, j, :],
                                                         start=False, stop=(j == 1)), r=[R + 'qdT', 'Sb'], w=[PS(bo)])
                for j in range(2):
                    bu = bank()
                    S.add('pe', lambda h, j=j, bu=bu: h.matmul(ps[bu][:, :], lhsT=W.kte[:, j * 128:(j + 1) * 128], rhs=vv,
                                                               start=True, stop=True), r=[R + 'kte', VR], w=[PS(bu)])
                    S.add('dve', lambda h, j=j, bu=bu: h.scalar_tensor_tensor(
                        out=Sf[:, j, :], in0=Sf[:, j, :], scalar=W.epos[:, j, 127:128], op0=ALU.mult,
                        in1=ps[bu][:, :], op1=ALU.add), r=[PS(bu), R + 'epos', 'Sf'], w=['Sf'])
                    S.add('act', lambda h, j=j: h.copy(out=Sb[:, j, :], in_=Sf[:, j, :]), r=['Sf'], w=['Sb'])
                if c == NT - 2:
                    S.add('sp', lambda h: h.dma_start(out=gst_p[h_].rearrange("(j p) v -> p j v", p=128), in_=Sf),
                          r=['Sf'], dma='gstp')

        def s0_load(ck, q_):
            sb_ = q_ % 3
            h_ = ck.h
            S.add('sp', lambda h: h.dma_start(out=s0[sb_], in_=st_in[q_, h_].rearrange("(j p) v -> p j v", p=128)),
                  w=['s0_%d' % sb_], dma='s0_%d' % sb_)

        def unit_prep(ck, q_):
            if DBG_NOUNIT or q_ >= 16:
                return
            W, R = ck.W, ck.R
            qb_ = q_ % 2
            S.add('pool', lambda h: h.memset(Qm[qb_], 0.0), w=['Qm%d' % qb_])
            S.add('pool', lambda h: h.tensor_copy(out=Qm[qb_][:, :, 8 * q_:8 * q_ + 8], in_=W.qdT[:, :, 8 * q_:8 * q_ + 8]),
                  r=[R + 'qdT'], w=['Qm%d' % qb_])
            S.add('dve', lambda h: h.tensor_scalar(
                out=kteM[qb_], in0=W.kte, scalar1=cm[:, 768 + q_:769 + q_], scalar2=None, op0=ALU.mult),
                r=[R + 'kte', 'cm'], w=['kteM%d' % qb_])

        def unit(ck, q_):
            if DBG_NOUNIT:
                return
            W, R, h_ = ck.W, ck.R, ck.h
            vv, VR, bo = ck.v, ck.VR, ck.bo
            sb_ = q_ % 3
            qb_ = q_ % 2
            if q_ + 1 < 16:
                s0_load(ck, q_ + 1)
            for j in range(2):
                S.add('pe', lambda h, j=j: h.matmul(
                    ps[bo][:, :], lhsT=Qm[qb_][:, j, :], rhs=s0[sb_][:, j, :], start=False,
                    stop=(q_ == 15 and j == 1)), r=['Qm%d' % qb_, 's0_%d' % sb_], w=[PS(bo)])
            for j in range(2):
                bu = bank()
                S.add('pe', lambda h, j=j, bu=bu: h.matmul(
                    ps[bu][:, :], lhsT=kteM[qb_][:, j * 128:(j + 1) * 128], rhs=vv, start=True, stop=True),
                    r=['kteM%d' % qb_, VR], w=[PS(bu)])
                S.add('dve', lambda h, j=j, bu=bu: h.scalar_tensor_tensor(
                    out=s0[sb_][:, j, :], in0=s0[sb_][:, j, :], scalar=W.epos[:, j, 8 * q_ + 7:8 * q_ + 8],
                    op0=ALU.mult, in1=ps[bu][:, :], op1=ALU.add),
                    r=[PS(bu), R + 'epos', 's0_%d' % sb_], w=['s0_%d' % sb_])
            S.add('sp', lambda h: h.dma_start(out=gst_s[q_, h_].rearrange("(j p) v -> p j v", p=128), in_=s0[sb_]),
                  r=['s0_%d' % sb_], dma='s0_%d' % sb_)

        def B4(ck):
            W, R, c, h_, bo = ck.W, ck.R, ck.c, ck.h, ck.bo
            S.add('act', lambda h: h.activation(out=W.junk, in_=ps[bo][:, :], func=AF.Square, accum_out=W.sm[:, 0:1]),
                  r=[PS(bo)], w=['gjunk', R + 'sm'])
            S.add('act', lambda h: h.activation(out=W.sm[:, 1:2], in_=W.sm[:, 0:1], func=AF.Ln, bias=eps_t[:, 1:2], scale=1.0 / 512),
                  r=[R + 'sm', 'eps'], w=[R + 'sm'])
            S.add('act', lambda h: h.activation(out=W.sm[:, 2:3], in_=W.sm[:, 1:2], func=AF.Exp, scale=-0.5),
                  r=[R + 'sm'], w=[R + 'sm'])
            S.add('dve', lambda h: h.scalar_tensor_tensor(out=gated[:, c, h_ * 512:(h_ + 1) * 512], in0=ps[bo][:, :],
                                                          scalar=W.sm[:, 2:3], op0=ALU.mult, in1=W.srg, op1=ALU.mult),
                  r=[PS(bo), R + 'sm', R + 'srg'], w=['gated%d' % c])
            pinned.discard(bo)

        def issue_head(h_):
            return tuple(wq_get(i_) for i_ in PLAN['main'][h_])

        for h_ in range(4):
            slots = issue_head(h_)
            S.add('sp', lambda h, h_=h_: h.dma_start(out=gng, in_=gla_ng[:, h_ * 512:(h_ + 1) * 512].partition_broadcast(128)),
                  w=['gng'], dma='c_gng')
            S.add('sp', lambda h, h_=h_: h.dma_start(out=Sf, in_=sscr[h_].rearrange("(j p) v -> p j v", p=128)),
                  r=['sscr%d' % h_], w=['Sf'], dma='sfl')
            S.add('act', lambda h: h.copy(out=Sb.rearrange("p a b -> p (a b)"), in_=Sf.rearrange("p a b -> p (a b)")),
                  r=['Sf'], w=['Sb'])
            cks = {c: mk(h_, c, slots) for c in range(NT)}
            ck8 = cks[NT - 1]
            s0_load(ck8, 0)
            L = [NT - 1] + list(range(NT - 1))
            for i in range(-2, len(L)):
                p = cks[L[i + 2]] if 0 <= i + 2 < len(L) else None
                g = cks[L[i + 1]] if 0 <= i + 1 < len(L) else None
                b_ = cks[L[i]] if 0 <= i < len(L) else None
                u0 = 2 * (i - 1)
                if p:
                    P_qk(p)
                if b_ and not b_.sample:
                    unit(ck8, u0)
                if b_:
                    B12(b_)
                if g:
                    G23(g)
                if p:
                    P_v(p)
                if b_:
                    B3(b_)
                    if not b_.sample:
                        unit(ck8, u0 + 1)
                        B4(b_)
                if g:
                    G45(g)
                if p:
                    P_r(p)
                    G01(p)
                if b_:
                    unit_prep(ck8, u0 + 2)
                    unit_prep(ck8, u0 + 3)
            B4(ck8)
            for s_ in slots:
                give_w(s_)
        for tc in range(NT):
            transpose_bf16_chunk(gated[:, tc, :], 'gated%d' % tc, xT, XTG, tc)
        S.retire_prefix('g0', 'g1', 'g2', 'gsg', 'gt1', 'gjunk', 'gv', 'gated', 'gTm', 'gng', 'Sf', 'Sb', 's0_', 'Qm', 'kteM', 'wg')
        A.release(gated, gTm, gng, Sf, Sb, *s0, *Qm, *kteM, wg, wg16, *v3, sgS, t1S, junkS)
        for w_ in wsets:
            A.release(*w_.all)

    def sg_layer():
        winv = sg_w_in.rearrange("(k p) f -> p k f", p=128)
        uT = A.alloc([KD, T], BF16)
        binT = A.alloc([16], F32)
        S.add('sp', lambda h: h.dma_start(out=binT, in_=sg_binT), w=['binT'], dma='c_binT')
        wtmp = A.alloc([8, 128], F32)
        WT = [A.alloc([8, 128], BF16) for _ in range(2)]
        for v_ in range(2):
            src = sg_wsT if v_ == 0 else sg_wsTs
            S.add('sp', lambda h, src=src: h.dma_start(out=wtmp, in_=src), w=['wtmp'], dma='c_wtmp')
            m01 = CM(C_TRI01) if v_ == 0 else CM(C_BD01)
            for g in range(8):
                S.add('dve', lambda h, g=g, v_=v_, m01=m01: h.tensor_tensor(out=WT[v_][:, g, :], in0=wtmp[:, g, :], in1=m01, op=ALU.mult),
                      r=['wtmp', 'cm'], w=['WT%d' % v_])
        S.retire(['wtmp'])
        A.release(wtmp)
        bsp1 = A.alloc([8, 128], F32)
        bsp = [bsp1, bsp1]

        def load_bsp(v_):
            S.add('sp', lambda h: h.dma_start(out=bsp1, in_=sg_bsp[v_].partition_broadcast(128)),
                  w=['bsp'], dma='c_bsp')

        load_bsp(0)
        bv = A.alloc([D], F32)
        vg = A.alloc([D], F32)
        vb = A.alloc([D], F32)
        S.add('sp', lambda h: h.dma_start(out=bv, in_=sg_bv.partition_broadcast(128)), w=['sgbv'], dma='c_sgbv')
        S.add('sp', lambda h: h.dma_start(out=vg, in_=sg_vg.partition_broadcast(128)), w=['sgvg'], dma='c_sgvg')
        S.add('sp', lambda h: h.dma_start(out=vb, in_=sg_vb.partition_broadcast(128)), w=['sgvb'], dma='c_sgvb')

        for cb in range(4):
            s_ = wq_get(PLAN['sg_u'][cb])
            for fc in range(4):
                bs = [bank() for _ in range(3)]
                for k in range(KD):
                    for tt in range(3):
                        S.add('pe', lambda h, k=k, tt=tt, fc=fc, b=bs[tt], s_=s_: h.matmul(
                            ps[b][:, 0:TT], lhsT=wbuf[s_][:, k, fc * 128:(fc + 1) * 128],
                            rhs=xT[:, k, tt * TT:(tt + 1) * TT], start=(k == 0), stop=(k == KD - 1)),
                            r=['w%d' % s_] + XTT(tt, k), w=[PS(bs[tt])])
                f_ = cb * 4 + fc
                for tt in range(3):
                    S.add('act', lambda h, tt=tt, b=bs[tt], f_=f_: h.activation(
                        out=uT[:, f_, tt * TT:(tt + 1) * TT], in_=ps[b][:, 0:TT], func=AF.Gelu,
                        bias=binT[:, f_:f_ + 1], scale=1.0), r=[PS(bs[tt]), 'binT'], w=['uT'])
            give_w(s_)
        vs = [wq_get(PLAN['sg_v'][cb]) for cb in range(4)]
        vt = [A.alloc([D], F32) for _ in range(2)]
        vnb = [A.alloc([D], BF16) for _ in range(2)]
        tmp = [A.alloc([512], F32) for _ in range(2)]
        mt = [A.alloc([4, 128], F32) for _ in range(2)]
        st = [A.alloc([4, 6], F32) for _ in range(2)]
        sm = [A.alloc([8], F32) for _ in range(2)]

        def p_stage(tc):
            i = tc % 2
            VT = 'sv%dvt' % i
            bs = [bank() for _ in range(4)]
            for k in range(KD):
                for cb in range(4):
                    S.add('pe', lambda h, k=k, cb=cb, b=bs[cb], tc=tc: h.matmul(
                        ps[b][:, :], lhsT=xT[:, k, tc * 128:(tc + 1) * 128], rhs=wbuf[vs[cb]][:, k, :],
                        start=(k == 0), stop=(k == KD - 1)), r=[XT(tc, k), 'w%d' % vs[cb]], w=[PS(bs[cb])])
            for cb in range(4):
                sl = slice(cb * 512, (cb + 1) * 512)
                S.add('dve', lambda h, b=bs[cb], sl=sl, i=i: h.tensor_tensor(out=vt[i][:, sl], in0=ps[b][:, :], in1=bv[:, sl], op=ALU.add),
                      r=[PS(bs[cb]), 'sgbv'], w=[VT + str(cb)])
                S.add('act', lambda h, sl=sl, i=i: h.activation(out=vt[i][:, sl], in_=vt[i][:, sl], func=AF.Gelu),
                      r=[VT + str(cb)], w=[VT + str(cb)])

        def l_stage(tc):
            i = tc % 2
            sample = (tc == NT - 1)
            V = 'sv%d' % i
            VT = 'sv%dvt' % i
            ln_stats(vt[i], [VT + str(c_) for c_ in range(4)], st[i], sm[i], V, LN_EPS)
            for c in range(4):
                j = (tc * 4 + c) % 2
                sl = slice(c * 512, (c + 1) * 512)
                S.add('act', lambda h, i=i, sl=sl, j=j: h.activation(out=tmp[j], in_=vt[i][:, sl], func=AF.Identity,
                                                                  scale=sm[i][:, 3:4], bias=sm[i][:, 4:5]),
                      r=[VT + str(c), V + 'sm'], w=['svtmp%d' % j])
                S.add('pool', lambda h, j=j, sl=sl: h.tensor_tensor(out=tmp[j], in0=tmp[j], in1=vg[:, sl], op=ALU.mult),
                      r=['svtmp%d' % j, 'sgvg'], w=['svtmp%d' % j])
                if sample:
                    S.add('dve', lambda h, j=j, sl=sl, i=i: h.tensor_tensor(out=vt[i][:, sl], in0=tmp[j], in1=vb[:, sl], op=ALU.add),
                          r=['svtmp%d' % j, 'sgvb', VT + str(c)], w=[VT + str(c)])
                    S.add('act', lambda h, sl=sl, i=i: h.copy(out=vnb[i][:, sl], in_=vt[i][:, sl]), r=[VT + str(c)], w=[V + 'vnb' + str(c)])
                else:
                    S.add('dve', lambda h, j=j, sl=sl, i=i: h.tensor_tensor(out=vnb[i][:, sl], in0=tmp[j], in1=vb[:, sl], op=ALU.add),
                          r=['svtmp%d' % j, 'sgvb'], w=[V + 'vnb' + str(c)])
            if sample:
                S.add('sp', lambda h, i=i: h.dma_start(out=sgv, in_=vt[i]), r=[VT + str(c_) for c_ in range(4)], dma='sgvout')

        def m_stage(tc):
            i = tc % 2
            sample = (tc == NT - 1)
            V = 'sv%d' % i
            v_ = 1 if sample else 0
            for dg in range(4):
                b = bank()
                for q_ in range(4):
                    dc = dg * 4 + q_
                    S.add('pe', lambda h, q_=q_, dc=dc, b=b, i=i, v_=v_: h.matmul(
                        ps[b][:, q_ * 128:(q_ + 1) * 128], lhsT=vnb[i][:, dc * 128:(dc + 1) * 128],
                        rhs=WT[v_][:, dc // 2, :], start=True, stop=True), r=[V + 'vnb' + str(dg), 'WT%d' % v_], w=[PS(b)])
                mi = dg % 2
                bias_ap = bsp[v_][:, 2 * dg:2 * dg + 2, :].unsqueeze(2).broadcast_to([128, 2, 2, 128])
                S.add('dve', lambda h, b=b, mi=mi, bias_ap=bias_ap: h.tensor_tensor(
                    out=mt[mi].rearrange("p (a c) t -> p a c t", a=2), in0=ps[b][:, :].rearrange("p (a c t) -> p a c t", a=2, c=2),
                    in1=bias_ap, op=ALU.add), r=[PS(b), 'bsp'], w=['mt%d' % mi])
                S.add('pool', lambda h, mi=mi, dg=dg, tc=tc: h.tensor_tensor(
                    out=xT[:, 4 * dg:4 * dg + 4, tc * 128:(tc + 1) * 128], in0=mt[mi],
                    in1=uT[:, 4 * dg:4 * dg + 4, tc * 128:(tc + 1) * 128], op=ALU.mult),
                    r=['mt%d' % mi, 'uT'], w=XTG(tc, dg))

        p_stage(0)
        p_stage(1)
        l_stage(0)
        for tc in range(NT):
            if tc + 2 < NT:
                p_stage(tc + 2)
                if tc + 2 == NT - 1:
                    for s_ in vs:
                        give_w(s_)
            if tc + 1 < NT:
                l_stage(tc + 1)
            if tc == NT - 1:
                load_bsp(1)
            m_stage(tc)
        S.retire_prefix('uT', 'binT', 'wtmp', 'WT', 'bsp', 'sgbv', 'sgvg', 'sgvb', 'sv', 'mt')
        A.release(uT, binT, *WT, bsp1, bv, vg, vb, *vt, *vnb, *tmp, *mt, *st, *sm)

    winv_g = gla_w_in.rearrange("(k p) f -> p k f", p=128)
    winv_s = sg_w_in.rearrange("(k p) f -> p k f", p=128)
    PLAN = {}
    if "gla" in phases:
        PLAN['pf'] = []
        for h_ in range(4):
            ik = wq_add([(lambda sl: wbuf[sl][:, :, 256:512], winv_g[:, :, 1024 + h_ * 256:1024 + (h_ + 1) * 256])])
            iv = wq_add([(full_dst(16, 512), winv_g[:, :, 2048 + h_ * 512:2048 + (h_ + 1) * 512])])
            PLAN['pf'].append((ik, iv))
        PLAN['main'] = []
        for h_ in range(4):
            iqk = wq_add([(lambda sl: wbuf[sl][:, :, 0:256], winv_g[:, :, h_ * 256:(h_ + 1) * 256]),
                          (lambda sl: wbuf[sl][:, :, 256:512], winv_g[:, :, 1024 + h_ * 256:1024 + (h_ + 1) * 256])])
            iv = wq_add([(full_dst(16, 512), winv_g[:, :, 2048 + h_ * 512:2048 + (h_ + 1) * 512])])
            ir = wq_add([(full_dst(16, 512), winv_g[:, :, 4096 + h_ * 512:4096 + (h_ + 1) * 512])])
            PLAN['main'].append((iqk, iv, ir))
        wv_ = gla_w_out.rearrange("(k p) d -> p k d", p=128)
        PLAN['gla_out'] = [wq_add([(full_dst(16, 512), wv_[:, :, cb * 512:(cb + 1) * 512])]) for cb in range(4)]

    def plan_mlp(layer):
        w1v = w1[layer].rearrange("(k p) f -> p k f", p=128)
        w2v = w2[layer].rearrange("(fb c p) d -> fb p c d", p=128, c=4)
        out = []
        for fb in range(DFF // 512):
            i1 = wq_add([(full_dst(16, 512), w1v[:, :, fb * 512:(fb + 1) * 512])])
            i2 = wq_add([(full_dst(4, 2048), w2v[fb])])
            out.append((i1, i2))
        return out

    if "mlp0" in phases:
        PLAN['mlp0'] = plan_mlp(0)
    if "sg" in phases:
        PLAN['sg_u'] = [wq_add([(full_dst(16, 512), winv_s[:, :, cb * 512:(cb + 1) * 512])]) for cb in range(4)]
        PLAN['sg_v'] = [wq_add([(full_dst(16, 512), winv_s[:, :, 2048 + cb * 512:2048 + (cb + 1) * 512])]) for cb in range(4)]
        wv_ = sg_w_out.rearrange("(k p) d -> p k d", p=128)
        PLAN['sg_out'] = [wq_add([(full_dst(16, 512), wv_[:, :, cb * 512:(cb + 1) * 512])]) for cb in range(4)]
    if "mlp1" in phases:
        PLAN['mlp1'] = plan_mlp(1)
    wq_issue_pending()

    load_xT(xm, NT, xT, XTG)
    last = [p for p in ("gla", "mlp0", "sg", "mlp1") if p in phases][-1]
    if "gla" in phases:
        gla_layer()
    alloc_xres(xm)
    if "gla" in phases:
        out_proj(PLAN['gla_out'])
        layer_norm(0, ln1g[0:1, :], ln1b[0:1, :], final=(last == "gla"))
    if "mlp0" in phases:
        mlp(0)
        layer_norm(1, ln2g[0:1, :], ln2b[0:1, :], final=(last == "mlp0"))
    if "sg" in phases:
        xres = xres_box[0]
        for tc in range(NT):
            S.add('sp', lambda h, tc=tc, xres=xres: h.dma_start(out=xspill[tc * 128:(tc + 1) * 128, :], in_=xres[:, tc, :]),
                  r=XRA(tc), w=['xspill%d' % tc], dma='xsp%d' % tc)
        free_xres()
        sg_layer()
        xres = A.alloc([NT, D], F32)
        xres_box[0] = xres
        for tc in range(NT):
            S.add('sp', lambda h, tc=tc, xres=xres: h.dma_start(out=xres[:, tc, :], in_=xspill[tc * 128:(tc + 1) * 128, :]),
                  r=['xspill%d' % tc], w=XRA(tc), dma='xres%d' % tc)
        out_proj(PLAN['sg_out'])
        layer_norm(2, ln1g[1:2, :], ln1b[1:2, :], final=(last == "sg"))
    if "mlp1" in phases:
        mlp(1)
        layer_norm(3, ln2g[1:2, :], ln2b[1:2, :], final=True)

    S.emit(nc, es)
    es.close()
    return nc, S, A


def prep_shared(inp):
    f = lambda a: np.ascontiguousarray(np.asarray(a, dtype=np.float32))
    wg_aug = np.zeros((32, 1024), np.float32)
    wg_aug[0:16] = inp["gla_w_gate"][0]
    wg_aug[16] = inp["gla_b_gate"][0]
    ws = np.asarray(inp["sg_w_spatial"][0])
    wsT = ws.transpose(2, 0, 1)
    wsTs = np.tile(ws[:, :8, :8].transpose(2, 0, 1), (16, 1, 16))
    bsp = np.asarray(inp["sg_b_spatial"][0])
    bsp2 = np.stack([bsp, np.tile(bsp[:, :8], (1, 16))])
    b_in = np.asarray(inp["sg_b_in"][0])
    d = dict(
        gla_w_in=f(inp["gla_w_in"][0]), wg_aug=wg_aug, gla_ng=f(np.asarray(inp["gla_norm_g"][0]).reshape(1, D)),
        gla_w_out=f(inp["gla_w_out"][0]), sg_w_in=f(inp["sg_w_in"][0]),
        sg_binT=f(b_in[:D].reshape(16, 128).T), sg_bv=f(b_in[D:].reshape(1, D)),
        sg_vg=f(np.asarray(inp["sg_v_norm_g"][0]).reshape(1, D)), sg_vb=f(np.asarray(inp["sg_v_norm_b"][0]).reshape(1, D)),
        sg_wsT=f(wsT), sg_wsTs=f(wsTs), sg_bsp=f(bsp2), sg_w_out=f(inp["sg_w_out"][0]),
        mlp_w1=f(inp["mlp_w1"]), mlp_w2=f(inp["mlp_w2"]),
        ln1_g=f(inp["ln1_g"]), ln1_b=f(inp["ln1_b"]), ln2_g=f(inp["ln2_g"]), ln2_b=f(inp["ln2_b"]),
        ident=np.eye(128, dtype=np.float32), cmask=make_consts(),
    )
    lnT = np.zeros((4, 128, 2, 16), np.float32)
    for n_, (gk, bk, li) in enumerate([("ln1_g", "ln1_b", 0), ("ln2_g", "ln2_b", 0), ("ln1_g", "ln1_b", 1), ("ln2_g", "ln2_b", 1)]):
        lnT[n_, :, 0, :] = np.asarray(inp[gk][li]).reshape(16, 128).T
        lnT[n_, :, 1, :] = np.asarray(inp[bk][li]).reshape(16, 128).T
    d["lnT"] = lnT
    return d


def prep_core(inp, c):
    xpr = np.asarray(inp["x_prompt"])
    xs = np.asarray(inp["x_sample"])
    b, hf = c // 2, c % 2
    xm = np.concatenate([xpr[b, hf * 1024:(hf + 1) * 1024], xs[16 * c:16 * (c + 1)].reshape(128, D)], axis=0)
    if hf == 1:
        xp = xpr[b, 0:1024]
    else:
        xp = np.zeros((1024, D), np.float32)
    st = np.asarray(inp["state_gla"])[0, 16 * c:16 * (c + 1)]
    return dict(xm=np.ascontiguousarray(xm, dtype=np.float32), xp=np.ascontiguousarray(xp, dtype=np.float32),
                st=np.ascontiguousarray(st, dtype=np.float32))


_CACHE = {}


def kernel(**inputs):
    if "nc" not in _CACHE:
        _CACHE["nc"] = build()[0]
    nc = _CACHE["nc"]
    shared = prep_shared(inputs)
    in_maps = []
    for c in range(8):
        m = dict(shared)
        m.update(prep_core(inputs, c))
        in_maps.append(m)
    res = run_bass_kernel_spmd(nc, in_maps, core_ids=list(range(8)))
    R = res.results
    y_prompt = np.zeros((4, 2048, D), np.float32)
    y_sample = np.zeros((128, 8, D), np.float32)
    gp = np.zeros((1, 4, 4, 256, 512), np.float32)
    gs = np.zeros((1, 128, 4, 256, 512), np.float32)
    sgv = np.zeros((1, 128, 8, D), np.float32)
    for c in range(8):
        b, hf = c // 2, c % 2
        yc = R[c]["y"]
        y_prompt[b, hf * 1024:(hf + 1) * 1024] = yc[:1024]
        y_sample[16 * c:16 * (c + 1)] = yc[1024:].reshape(16, 8, D)
        if hf == 1:
            gp[0, b] = R[c]["gst_p"]
        gs[0, 16 * c:16 * (c + 1)] = R[c]["gst_s"]
        sgv[0, 16 * c:16 * (c + 1)] = R[c]["sgv"].reshape(16, 8, D)
    return (y_prompt, y_sample, gp, gs, sgv)
```

```python
import numpy as np
from contextlib import ExitStack
import concourse.bass as bass
import concourse.mybir as mybir
from concourse.bass_utils import run_bass_kernel_spmd

F32 = mybir.dt.float32
BF16 = mybir.dt.bfloat16
AF = mybir.ActivationFunctionType
ALU = mybir.AluOpType

D = 2048
KD = 16
NT = 9
NPF = 8
T = NT * 128
TT = 384
DFF = 8192
ALPHA = float((2.0 * 2) ** 0.25)
LN_EPS = 1e-5
HN_EPS = 1e-6
import os as _os2
DBG_NOUNIT = bool(_os2.environ.get('DBG_NOUNIT'))
import os as _os
NO_SAME_ENG_SYNC = bool(_os.environ.get('NO_SAME_ENG_SYNC'))


class Sched:
    def __init__(self):
        self.ops = []
        self.res = {}
        self.ghost = set()

    def add(self, eng, fn, r=(), w=(), dma=None):
        i = len(self.ops)
        deps = set()
        for name in r:
            st = self.res.get(name)
            if st is None:
                st = self.res[name] = [None, list(self.ghost)]
            if st[0] is not None:
                deps.add(st[0])
            if name.startswith('ps'):
                deps.update(d for d in st[1] if self.ops[d]['eng'] != eng)
        for name in w:
            st = self.res.get(name)
            if st is None:
                st = self.res[name] = [None, list(self.ghost)]
            if st[0] is not None:
                deps.add(st[0])
            deps.update(st[1])
        for name in r:
            self.res[name][1].append(i)
        for name in w:
            self.res[name] = [i, []]
        deps.discard(i)
        self.ops.append(dict(i=i, eng=eng, fn=fn, deps=deps, dma=dma, signal=False))
        return i

    def retire(self, names):
        for n in names:
            st = self.res.pop(n, None)
            if st is None:
                continue
            if st[0] is not None:
                self.ghost.add(st[0])
            self.ghost.update(st[1])
        best = {}
        for d in self.ghost:
            p = self.ops[d]
            key = ('d', p['dma']) if p['dma'] else ('c', p['eng'])
            if key not in best or best[key] < d:
                best[key] = d
        self.ghost = set(best.values())

    def retire_prefix(self, *prefixes):
        self.retire([n for n in list(self.res) if any(n.startswith(p) for p in prefixes)])

    def emit(self, nc, es, final_eng='sp'):
        ops = self.ops
        last_dma = {}
        for op in ops:
            if op['dma']:
                last_dma[op['dma']] = op['i']
        fin = dict(i=len(ops), eng=final_eng, fn=None, deps=set(last_dma.values()), dma=None, signal=False)
        ops.append(fin)
        for op in ops:
            best = {}
            for d in op['deps']:
                p = ops[d]
                key = ('d', p['dma']) if p['dma'] else ('c', p['eng'])
                if key not in best or best[key] < d:
                    best[key] = d
            rd = []
            for key, d in best.items():
                p = ops[d]
                if p['dma'] is None and p['eng'] == 'pe' and op['eng'] == 'pe' and op['dma'] is None:
                    continue
                if NO_SAME_ENG_SYNC and p['dma'] is None and op['dma'] is None and p['eng'] == op['eng']:
                    continue
                if p['dma'] is None:
                    p['signal'] = True
                rd.append(d)
            op['rdeps'] = rd
        cnt = {}
        dcnt = {}
        for op in ops:
            if op['dma']:
                dcnt[op['dma']] = dcnt.get(op['dma'], 0) + 16
                op['sval'] = dcnt[op['dma']]
            elif op['signal']:
                cnt[op['eng']] = cnt.get(op['eng'], 0) + 1
                op['sval'] = cnt[op['eng']]
        engs = ['pe', 'act', 'dve', 'pool', 'sp']
        sems = {e: es.enter_context(nc.semaphore("s_" + e)) for e in engs}
        dsems = {k: es.enter_context(nc.semaphore("d_%d" % n)) for n, k in enumerate(sorted(dcnt))}
        self.nsem = len(sems) + len(dsems)
        self.maxcnt = dict(cnt)
        block = es.enter_context(nc.Block())
        per = {e: [op for op in ops if op['eng'] == e] for e in engs}

        def run(e, h):
            waited = {}
            for op in per[e]:
                for d in op['rdeps']:
                    p = ops[d]
                    if p['dma']:
                        s, v, k = dsems[p['dma']], p['sval'], ('d', p['dma'])
                    else:
                        s, v, k = sems[p['eng']], p['sval'], ('c', p['eng'])
                    if waited.get(k, 0) >= v:
                        continue
                    waited[k] = v
                    h.wait_ge(s, v)
                if op['fn'] is None:
                    continue
                ins = op['fn'](h)
                if op['dma']:
                    ins.then_inc(dsems[op['dma']], 16)
                elif op['signal']:
                    ins.then_inc(sems[e], 1)

        @block.tensor
        def _(h):
            run('pe', h)

        @block.scalar
        def _(h):
            run('act', h)

        @block.vector
        def _(h):
            run('dve', h)

        @block.gpsimd
        def _(h):
            run('pool', h)

        @block.sync
        def _(h):
            run('sp', h)


class Arena:
    def __init__(self, ap_f32):
        self.ap = ap_f32
        self.n = ap_f32.shape[1]
        self.free = [(0, self.n)]
        self.live = {}
        self.peak = 0

    def alloc(self, shape, dtype, name=None):
        n = int(np.prod(shape))
        words = n if dtype == F32 else (n + 1) // 2
        words = (words + 1) // 2 * 2
        for idx, (o, sz) in enumerate(self.free):
            if sz >= words:
                break
        else:
            raise AssertionError(("arena overflow", name, words, self.free))
        if sz == words:
            self.free.pop(idx)
        else:
            self.free[idx] = (o + words, sz - words)
        self.peak = max(self.peak, o + words)
        v = self.ap[:, o:o + words]
        if dtype != F32:
            v = v.bitcast(dtype)
        v = v[:, 0:n]
        if len(shape) == 2:
            v = v.rearrange("p (a b) -> p a b", a=shape[0])
        elif len(shape) == 3:
            v = v.rearrange("p (a b c) -> p a b c", a=shape[0], b=shape[1])
        self.live[id(v)] = (o, words, v)
        return v

    def release(self, *views):
        for v in views:
            o, words, _ = self.live.pop(id(v))
            self.free.append((o, words))
        self.free.sort()
        merged = []
        for o, sz in self.free:
            if merged and merged[-1][0] + merged[-1][1] == o:
                merged[-1] = (merged[-1][0], merged[-1][1] + sz)
            else:
                merged.append((o, sz))
        self.free = merged


ARENA_WORDS = 53000

C_TRI_S, C_TRIU_S, C_BD_S, C_BDU_S, C_TRI01, C_BD01 = range(6)


def make_consts():
    p = np.arange(128)
    s, t = p[:, None], p[None, :]
    same = (s // 8) == (t // 8)
    tri = (s <= t)
    triu = (s > t)
    m = np.zeros((128, 6 * 128 + 16 + 4), np.float32)
    m[:, 0:128] = tri * (-1.0 / 16)
    m[:, 128:256] = triu * (-1.0 / 16)
    m[:, 256:384] = (tri & same) * (-1.0 / 16)
    m[:, 384:512] = (triu & same) * (-1.0 / 16)
    m[:, 512:640] = tri
    m[:, 640:768] = tri & same
    m[:, 768:784] = (p[:, None] // 8) == np.arange(16)[None, :]
    m[:, 784] = -1.0 / 16
    m[:, 785] = 1.0
    m[:, 786] = -1.0 / 16
    m[:, 787] = -1.0 / 16
    return m


def build(phases=("gla", "mlp0", "sg", "mlp1"), dbg=False):
    nc = bass.Bass("TRN2", target_bir_lowering=False)

    def din(name, shape):
        return nc.dram_tensor(name, list(shape), F32, kind="ExternalInput").ap()

    def dout(name, shape):
        return nc.dram_tensor(name, list(shape), F32, kind="ExternalOutput").ap()

    xm = din("xm", [T, D])
    xp = din("xp", [NPF * 128, D])
    st_in = din("st", [16, 4, 256, 512])
    gla_w_in = din("gla_w_in", [D, 6160])
    wg_aug = din("wg_aug", [32, 1024])
    gla_ng = din("gla_ng", [1, D])
    gla_w_out = din("gla_w_out", [D, D])
    sg_w_in = din("sg_w_in", [D, 2 * D])
    sg_binT = din("sg_binT", [128, 16])
    sg_bv = din("sg_bv", [1, D])
    sg_vg = din("sg_vg", [1, D])
    sg_vb = din("sg_vb", [1, D])
    sg_wsT = din("sg_wsT", [128, 8, 128])
    sg_wsTs = din("sg_wsTs", [128, 8, 128])
    sg_bsp = din("sg_bsp", [2, 8, 128])
    sg_w_out = din("sg_w_out", [D, D])
    w1 = din("mlp_w1", [2, D, DFF])
    w2 = din("mlp_w2", [2, DFF, D])
    ln1g = din("ln1_g", [2, D])
    ln1b = din("ln1_b", [2, D])
    ln2g = din("ln2_g", [2, D])
    ln2b = din("ln2_b", [2, D])
    ident_d = din("ident", [128, 128])
    lnT_d = din("lnT", [4, 128, 2, 16])
    cm_d = din("cmask", [128, 788])
    y = dout("y", [T, D])
    gst_p = dout("gst_p", [4, 256, 512])
    gst_s = dout("gst_s", [16, 4, 256, 512])
    sgv = dout("sgv", [128, D])
    xspill = nc.dram_tensor("xspill", [T, D], F32, kind="Internal").ap()
    sscr = nc.dram_tensor("sscr", [4, 256, 512], F32, kind="Internal").ap()

    S = Sched()
    es = ExitStack()
    arena_t = es.enter_context(nc.sbuf_tensor("arena", [128, ARENA_WORDS], F32))
    A = Arena(arena_t[:])
    ps = [es.enter_context(nc.psum_tensor("ps%d" % i, [128, 512], F32)) for i in range(8)]
    psb = [p_[:].bitcast(BF16) for p_ in ps]
    bank_ctr = [0]

    pinned = set()

    def bank():
        while True:
            b = bank_ctr[0] % 8
            bank_ctr[0] += 1
            if b not in pinned:
                return b

    def PS(b):
        return 'ps%d' % b

    ident = A.alloc([128], F32)
    identb = A.alloc([128], BF16)
    NWS = 4
    wbuf = [A.alloc([16, 512], BF16) for _ in range(NWS)]
    xT = A.alloc([KD, T], BF16)
    free_w = list(range(NWS))

    wq = []
    wq_pos = [0]

    def wq_add(parts):
        wq.append(dict(parts=parts, slot=None))
        return len(wq) - 1

    def wq_issue_pending():
        while free_w and wq_pos[0] < len(wq):
            e = wq[wq_pos[0]]
            wq_pos[0] += 1
            slot = free_w.pop(0)
            e['slot'] = slot
            for n_, (dst_fn, src_ap) in enumerate(e['parts']):
                dst = dst_fn(slot)
                S.add('pool', lambda h, dst=dst, src_ap=src_ap: h.dma_start(out=dst, in_=src_ap),
                      r=(['w%d' % slot] if n_ else []), w=['w%d' % slot], dma='w%d' % slot)

    def wq_get(idx):
        if wq[idx]['slot'] is None:
            wq_issue_pending()
        assert wq[idx]['slot'] is not None, ("weight block not issuable", idx, wq_pos[0], free_w)
        return wq[idx]['slot']

    def give_w(s_):
        free_w.append(s_)
        wq_issue_pending()

    def full_dst(a, b):
        def f(slot):
            dst = wbuf[slot]
            if (a, b) != (16, 512):
                dst = dst.rearrange("p a b -> p (a b)").rearrange("p (a b) -> p a b", a=a)
            return dst
        return f

    def XT(tc, k):
        return 'xT%d_%d' % (tc, k)

    def XTG(tc, g):
        return [XT(tc, 4 * g + i_) for i_ in range(4)]

    def XTT(tt, k):
        return [XT(3 * tt + i_, k) for i_ in range(3)]

    S.add('sp', lambda h: h.dma_start(out=ident, in_=ident_d), w=['ident'], dma='c_ident')
    S.add('pool', lambda h: h.dma_start(out=identb, in_=ident_d), w=['identb'], dma='c_identb')

    def load_w(slot, src_ap, dst=None):
        if dst is None:
            a, b = src_ap.shape[1], src_ap.shape[2]
            dst = wbuf[slot]
            if (a, b) != (16, 512):
                dst = dst.rearrange("p a b -> p (a b)").rearrange("p (a b) -> p a b", a=a)
        S.add('pool', lambda h: h.dma_start(out=dst, in_=src_ap), w=['w%d' % slot], dma='w%d' % slot)
        return dst

    cp_ctr = [0]

    def evac_copy(out, in_, r, w):
        cp_ctr[0] += 1
        if cp_ctr[0] % 2:
            S.add('act', lambda h: h.copy(out=out, in_=in_), r=r, w=w)
        else:
            S.add('dve', lambda h: h.tensor_copy(out=out, in_=in_), r=r, w=w)

    def transpose_f32_chunk(src, src_res, dstT, dst_res, tc):
        for g in range(4):
            b = bank()
            for i in range(4):
                kc = 4 * g + i
                S.add('pe', lambda h, kc=kc, i=i, b=b: h.transpose(
                    out=ps[b][:, i * 128:(i + 1) * 128], in_=src[:, kc * 128:(kc + 1) * 128], identity=ident),
                    r=[src_res[g] if isinstance(src_res, list) else src_res, 'ident'], w=[PS(b)])
            evac_copy(dstT[:, 4 * g:4 * g + 4, tc * 128:(tc + 1) * 128],
                      ps[b][:].rearrange("p (a t) -> p a t", a=4), r=[PS(b)], w=dst_res(tc, g))

    def transpose_bf16_chunk(src, src_res, dstT, dst_res, tc):
        for g in range(4):
            b = bank()
            for i in range(4):
                kc = 4 * g + i
                S.add('pe', lambda h, kc=kc, i=i, b=b: h.transpose(
                    out=psb[b][:, i * 128:(i + 1) * 128], in_=src[:, kc * 128:(kc + 1) * 128], identity=identb),
                    r=[src_res, 'identb'], w=[PS(b)])
            evac_copy(dstT[:, 4 * g:4 * g + 4, tc * 128:(tc + 1) * 128],
                      psb[b][:, 0:512].rearrange("p (a t) -> p a t", a=4), r=[PS(b)], w=dst_res(tc, g))

    def load_xT(src_dram, nchunks, dstT, dst_res_fn):
        stg = [A.alloc([D], F32) for _ in range(2)]
        for tc in range(nchunks):
            i = tc % 2
            S.add('sp', lambda h, tc=tc, i=i: h.dma_start(out=stg[i], in_=src_dram[tc * 128:(tc + 1) * 128, :]),
                  w=['xstg%d' % i], dma='xstg%d' % i)
            transpose_f32_chunk(stg[i], 'xstg%d' % i, dstT, dst_res_fn, tc)
        S.retire_prefix('xstg')
        A.release(*stg)

    xres_box = [None]

    def XR(tc, c):
        return 'xres%d_%d' % (tc, c)

    def XRA(tc):
        return ['xres%d_%d' % (tc, c) for c in range(4)]

    def alloc_xres(src_dram):
        xres = A.alloc([NT, D], F32)
        xres_box[0] = xres
        for tc in range(NT):
            S.add('sp', lambda h, tc=tc: h.dma_start(out=xres[:, tc, :], in_=src_dram[tc * 128:(tc + 1) * 128, :]),
                  w=XRA(tc), dma='xres%d' % tc)

    def free_xres():
        S.retire([n for tc in range(NT) for n in XRA(tc)])
        A.release(xres_box[0])
        xres_box[0] = None

    def out_proj(plan):
        xres = xres_box[0]
        for cb in range(4):
            s_ = wq_get(plan[cb])
            for tc in range(NT):
                b = bank()
                for k in range(KD):
                    S.add('pe', lambda h, k=k, tc=tc, b=b, s_=s_: h.matmul(
                        ps[b][:, :], lhsT=xT[:, k, tc * 128:(tc + 1) * 128], rhs=wbuf[s_][:, k, :],
                        start=(k == 0), stop=(k == KD - 1)),
                        r=[XT(tc, k), 'w%d' % s_], w=[PS(b)])
                dst = xres[:, tc, cb * 512:(cb + 1) * 512]
                S.add('dve', lambda h, dst=dst, b=b: h.scalar_tensor_tensor(
                    out=dst, in0=dst, scalar=ALPHA, op0=ALU.mult, in1=ps[b][:, :], op1=ALU.add),
                    r=[PS(b), XR(tc, cb)], w=[XR(tc, cb)])
            give_w(s_)

    def mlp(layer):
        xres = xres_box[0]
        hT = [A.alloc([4, T], BF16) for _ in range(2)]
        rtmp = [A.alloc([TT], BF16) for _ in range(2)]
        NFB = DFF // 512
        w1v = w1[layer].rearrange("(k p) f -> p k f", p=128)
        w2v = w2[layer].rearrange("(fb c p) d -> fb p c d", p=128, c=4)
        slots = {}

        plan = PLAN['mlp%d' % layer]

        def issue1(fb):
            slots[fb] = [wq_get(plan[fb][0]), None, None]

        def issue2(fb):
            s2 = wq_get(plan[fb][1])
            slots[fb][1] = s2
            slots[fb][2] = full_dst(4, 2048)(s2)

        def stage_a(fb):
            s1 = slots[fb][0]
            hs = fb % 2
            for fc in range(4):
                bs = [bank() for _ in range(3)]
                for k in range(KD):
                    for tt in range(3):
                        S.add('pe', lambda h, k=k, tt=tt, fc=fc, b=bs[tt]: h.matmul(
                            ps[b][:, 0:TT], lhsT=wbuf[s1][:, k, fc * 128:(fc + 1) * 128],
                            rhs=xT[:, k, tt * TT:(tt + 1) * TT], start=(k == 0), stop=(k == KD - 1)),
                            r=['w%d' % s1] + XTT(tt, k), w=[PS(bs[tt])])
                for tt in range(3):
                    rt = (fc * 3 + tt) % 2
                    S.add('act', lambda h, tt=tt, b=bs[tt], rt=rt: h.activation(
                        out=rtmp[rt], in_=ps[b][:, 0:TT], func=AF.Relu),
                        r=[PS(bs[tt])], w=['rtmp%d' % rt])
                    eng = 'pool' if tt == 1 else 'dve'
                    S.add(eng, lambda h, tt=tt, fc=fc, rt=rt: h.tensor_tensor(
                        out=hT[hs][:, fc, tt * TT:(tt + 1) * TT], in0=rtmp[rt], in1=rtmp[rt], op=ALU.mult),
                        r=['rtmp%d' % rt], w=['hT%d' % hs])

        def stage_b(fb):
            s2, d2 = slots[fb][1], slots[fb][2]
            hs = fb % 2
            for tc in range(NT):
                for cb in range(4):
                    b = bank()
                    for fc in range(4):
                        S.add('pe', lambda h, fc=fc, tc=tc, cb=cb, b=b: h.matmul(
                            ps[b][:, :], lhsT=hT[hs][:, fc, tc * 128:(tc + 1) * 128],
                            rhs=d2[:, fc, cb * 512:(cb + 1) * 512], start=(fc == 0), stop=(fc == 3)),
                            r=['hT%d' % hs, 'w%d' % s2], w=[PS(b)])
                    dst = xres[:, tc, cb * 512:(cb + 1) * 512]
                    if fb == 0:
                        S.add('dve', lambda h, dst=dst, b=b: h.scalar_tensor_tensor(
                            out=dst, in0=dst, scalar=ALPHA, op0=ALU.mult, in1=ps[b][:, :], op1=ALU.add),
                            r=[PS(b), XR(tc, cb)], w=[XR(tc, cb)])
                    else:
                        S.add('dve', lambda h, dst=dst, b=b: h.tensor_tensor(
                            out=dst, in0=dst, in1=ps[b][:, :], op=ALU.add),
                            r=[PS(b), XR(tc, cb)], w=[XR(tc, cb)])

        issue1(0)
        issue2(0)
        for fb in range(NFB + 1):
            if fb < NFB:
                if fb > 0:
                    issue1(fb)
                stage_a(fb)
                give_w(slots[fb][0])
            if fb >= 1:
                if fb - 1 > 0:
                    issue2(fb - 1)
                stage_b(fb - 1)
                give_w(slots[fb - 1][1])
        S.retire_prefix('hT', 'rtmp')
        A.release(*hT, *rtmp)

    def ln_stats(z, zr, st, sm, tag, eps):
        for c in range(4):
            S.add('dve', lambda h, c=c: h.bn_stats(out=st[:, c, :], in_=z[:, c * 512:(c + 1) * 512]),
                  r=[zr[c]], w=[tag + 'st'])
        S.add('dve', lambda h: h.bn_aggr(out=sm[:, 0:2], in_=st), r=[tag + 'st'], w=[tag + 'sm'])
        S.add('act', lambda h: h.activation(out=sm[:, 2:3], in_=sm[:, 1:2], func=AF.Ln, bias=eps_t[:, 0:1], scale=1.0),
              r=[tag + 'sm', 'eps'], w=[tag + 'sm'])
        S.add('act', lambda h: h.activation(out=sm[:, 3:4], in_=sm[:, 2:3], func=AF.Exp, scale=-0.5),
              r=[tag + 'sm'], w=[tag + 'sm'])
        S.add('dve', lambda h: h.tensor_scalar(out=sm[:, 4:5], in0=sm[:, 0:1], scalar1=sm[:, 3:4],
                                               scalar2=-1.0, op0=ALU.mult, op1=ALU.mult),
              r=[tag + 'sm'], w=[tag + 'sm'])

    def layer_norm(idx, g_dram, b_dram, final):
        xres = xres_box[0]
        gt = A.alloc([D], F32)
        bt = A.alloc([D], F32)
        gbc = A.alloc([2, 16], F32)
        st = [A.alloc([4, 6], F32) for _ in range(3)]
        sm = [A.alloc([8], F32) for _ in range(3)]
        S.add('sp', lambda h: h.dma_start(out=gt, in_=g_dram.partition_broadcast(128)), w=['ln_g'], dma='ln_g')
        S.add('sp', lambda h: h.dma_start(out=bt, in_=b_dram.partition_broadcast(128)), w=['ln_b'], dma='ln_b')
        S.add('sp', lambda h: h.dma_start(out=gbc, in_=lnT_d[idx]), w=['ln_c'], dma='ln_c')

        def st_stage(tc):
            i = tc % 3
            ln_stats(xres[:, tc, :], XRA(tc), st[i], sm[i], 'ln%d' % i, LN_EPS)

        def nrm_stage(tc):
            i = tc % 3
            z = xres[:, tc, :]
            for c in range(4):
                zr = XR(tc, c)
                zc = z[:, c * 512:(c + 1) * 512]
                S.add('act', lambda h, i=i, zc=zc: h.activation(out=zc, in_=zc, func=AF.Identity,
                                                             scale=sm[i][:, 3:4], bias=sm[i][:, 4:5]),
                      r=[zr, 'ln%dsm' % i], w=[zr])

        def t_stage(tc):
            z = xres[:, tc, :]
            for g in range(4):
                zr = XR(tc, g)
                b = bank()
                for q_ in range(4):
                    kc = 4 * g + q_
                    S.add('pe', lambda h, kc=kc, q_=q_, b=b: h.transpose(
                        out=ps[b][:, q_ * 128:(q_ + 1) * 128], in_=z[:, kc * 128:(kc + 1) * 128], identity=ident),
                        r=[zr, 'ident'], w=[PS(b)])
                for q_ in range(4):
                    kc = 4 * g + q_
                    dst = xT[:, kc, tc * 128:(tc + 1) * 128]
                    src = ps[b][:, q_ * 128:(q_ + 1) * 128]
                    if g % 2 == 0:
                        S.add('act', lambda h, kc=kc, dst=dst, src=src: h.activation(
                            out=dst, in_=src, func=AF.Identity, scale=gbc[:, 0, kc:kc + 1], bias=gbc[:, 1, kc:kc + 1]),
                            r=[PS(b), 'ln_c'], w=[XT(tc, kc)])
                    else:
                        S.add('dve', lambda h, kc=kc, dst=dst, src=src: h.tensor_scalar(
                            out=dst, in0=src, scalar1=gbc[:, 0, kc:kc + 1], scalar2=gbc[:, 1, kc:kc + 1],
                            op0=ALU.mult, op1=ALU.add), r=[PS(b), 'ln_c'], w=[XT(tc, kc)])

        def gb_stage(tc):
            z = xres[:, tc, :]
            for c in range(4):
                zr = XR(tc, c)
                zc = z[:, c * 512:(c + 1) * 512]
                S.add('pool', lambda h, c=c, zc=zc: h.tensor_tensor(out=zc, in0=zc, in1=gt[:, c * 512:(c + 1) * 512], op=ALU.mult),
                      r=[zr, 'ln_g'], w=[zr])
                S.add('dve', lambda h, c=c, zc=zc: h.tensor_tensor(out=zc, in0=zc, in1=bt[:, c * 512:(c + 1) * 512], op=ALU.add),
                      r=[zr, 'ln_b'], w=[zr])
            if final:
                S.add('sp', lambda h, tc=tc, z=z: h.dma_start(out=y[tc * 128:(tc + 1) * 128, :], in_=z),
                      r=XRA(tc), dma='yout%d' % (tc % 3))

        st_stage(0)
        st_stage(1)
        nrm_stage(0)
        for tc in range(NT):
            if tc + 2 < NT:
                st_stage(tc + 2)
            if tc + 1 < NT:
                nrm_stage(tc + 1)
            if not final:
                t_stage(tc)
            else:
                gb_stage(tc)
        if not final:
            for tc in range(NT):
                gb_stage(tc)
        S.retire_prefix('ln')
        A.release(gt, bt, gbc, *st, *sm)

    cm = A.alloc([788], F32)
    S.add('sp', lambda h: h.dma_start(out=cm, in_=cm_d), w=['cm'], dma='c_cm')
    eps_t = A.alloc([2], F32)
    S.add('dve', lambda h: h.memset(eps_t[:, 0:1], LN_EPS), w=['eps'])
    S.add('dve', lambda h: h.memset(eps_t[:, 1:2], HN_EPS), w=['eps'])

    def CM(i):
        return cm[:, i * 128:(i + 1) * 128]

    def gla_layer():
        winv = gla_w_in.rearrange("(k p) f -> p k f", p=128)
        wg = A.alloc([1024], F32)
        S.add('sp', lambda h: h.dma_start(out=wg[0:32, :], in_=wg_aug), w=['wg'], dma='c_wg')
        wg16 = A.alloc([16, 16], BF16)
        S.add('pool', lambda h: h.dma_start(out=wg16, in_=winv[:, :, 6144:6160]), w=['wg16'], dma='c_wg16')
        identb_r = ['identb']

        def gate_T(srcT, src_res_list, ntok, name):
            gTa = A.alloc([ntok], F32)
            S.add('dve', lambda h: h.memset(gTa[0:32, :], 1.0), w=[name])
            ntt = ntok // TT if ntok % TT == 0 else None
            tiles = [(i * TT, TT) for i in range(ntok // TT)] if ntt else [(i * 512, 512) for i in range(ntok // 512)]
            for (o, n) in tiles:
                b = bank()
                for k in range(KD):
                    if src_res_list is None:
                        rr = [XT(tc_, k) for tc_ in range(o // 128, (o + n - 1) // 128 + 1)]
                    else:
                        rr = sorted(set(src_res_list[(o // 128):((o + n - 1) // 128) + 1]))
                    S.add('pe', lambda h, k=k, b=b, o=o, n=n: h.matmul(
                        ps[b][0:16, 0:n], lhsT=wg16[:, k, :], rhs=srcT[:, k, o:o + n],
                        start=(k == 0), stop=(k == KD - 1)), r=['wg16'] + rr, w=[PS(b)])
                S.add('act', lambda h, b=b, o=o, n=n: h.copy(out=gTa[0:16, o:o + n], in_=ps[b][0:16, 0:n]),
                      r=[PS(b)], w=[name])
            return gTa

        class WS:
            pass

        sgS = A.alloc([512], F32)
        t1S = A.alloc([512], BF16)
        junkS = A.alloc([512], BF16)
        v3 = [A.alloc([512], BF16) for _ in range(4)]

        def make_ws():
            w_ = WS()
            w_.qk = A.alloc([512], BF16)
            w_.sg = sgS
            w_.t1 = t1S
            w_.srg = A.alloc([512], BF16)
            w_.nl = A.alloc([256], F32)
            w_.eend = A.alloc([256], F32)
            w_.kte = A.alloc([256], BF16)
            w_.epos = A.alloc([2, 128], F32)
            w_.eneg = A.alloc([2, 128], F32)
            w_.qdT = A.alloc([2, 128], BF16)
            w_.kiT = A.alloc([2, 128], BF16)
            w_.scm = A.alloc([128], BF16)
            w_.junk = junkS
            w_.sm = A.alloc([8], F32)
            w_.all = [w_.qk, w_.srg, w_.nl, w_.eend, w_.kte, w_.epos, w_.eneg,
                      w_.qdT, w_.kiT, w_.scm, w_.sm]
            return w_

        xpT = A.alloc([KD, NPF * 128], BF16)
        load_xT(xp, NPF, xpT, lambda tc, g: ['xpT%d' % tc])
        XPT = ['xpT%d' % tc for tc in range(NPF)]
        gTp = gate_T(xpT, XPT, NPF * 128, 'gTp')
        Sf = A.alloc([2, 512], F32)
        wsets = [make_ws() for _ in range(3)]
        dec = [A.alloc([4], F32) for _ in range(2)]

        def gate_common(W, i, gsrc, gres, c, h_, tri_u):
            R = 'g%d' % i
            bg = bank()
            S.add('pe', lambda h, bg=bg: h.matmul(ps[bg][:, 0:256], lhsT=gsrc[0:32, c * 128:(c + 1) * 128],
                                                   rhs=wg[0:32, h_ * 256:(h_ + 1) * 256], start=True, stop=True),
                  r=['wg', gres], w=[PS(bg)])
            S.add('act', lambda h, bg=bg: h.activation(out=W.nl, in_=ps[bg][:, 0:256], func=AF.Exp, scale=-1.0),
                  r=[PS(bg)], w=[R + 'nl'])
            S.add('act', lambda h: h.activation(out=W.nl, in_=W.nl, func=AF.Ln, bias=cm[:, 785:786], scale=1.0),
                  r=[R + 'nl', 'cm'], w=[R + 'nl'])
            brc = bank()
            S.add('pe', lambda h, brc=brc: h.matmul(ps[brc][:, 0:256], lhsT=tri_u, rhs=W.nl, start=True, stop=True),
                  r=['cm', R + 'nl'], w=[PS(brc)])
            S.add('act', lambda h, brc=brc: h.activation(out=W.eend, in_=ps[brc][:, 0:256], func=AF.Exp),
                  r=[PS(brc)], w=[R + 'eend'])
            S.add('pool', lambda h: h.tensor_tensor(out=W.kte, in0=W.qk[:, 256:512], in1=W.eend, op=ALU.mult),
                  r=[R + 'qk', R + 'eend'], w=[R + 'kte'])

        def pf_Pk(h_, c, sk):
            W, R = wsets[c % 2], 'g%d' % (c % 2)
            bk = bank()
            for k in range(KD):
                S.add('pe', lambda h, k=k: h.matmul(
                    ps[bk][:, 0:256], lhsT=xpT[:, k, c * 128:(c + 1) * 128], rhs=wbuf[sk][:, k, 256:512],
                    start=(k == 0), stop=(k == KD - 1)), r=[XPT[c], 'w%d' % sk], w=[PS(bk)])
            S.add('dve', lambda h: h.tensor_copy(out=W.qk[:, 256:512], in_=ps[bk][:, 0:256]), r=[PS(bk)], w=[R + 'qk'])

        def pf_Pv(h_, c, sv):
            vv, VR = v3[c % 3], 'gv%d' % (c % 3)
            bv_ = bank()
            for k in range(KD):
                S.add('pe', lambda h, k=k: h.matmul(
                    ps[bv_][:, :], lhsT=xpT[:, k, c * 128:(c + 1) * 128], rhs=wbuf[sv][:, k, :],
                    start=(k == 0), stop=(k == KD - 1)), r=[XPT[c], 'w%d' % sv], w=[PS(bv_)])
            S.add('act', lambda h: h.copy(out=vv, in_=ps[bv_][:, :]), r=[PS(bv_)], w=[VR])

        def pf_G01(h_, c):
            W, R = wsets[c % 2], 'g%d' % (c % 2)
            bg = bank()
            S.add('pe', lambda h: h.matmul(ps[bg][:, 0:256], lhsT=gTp[0:32, c * 128:(c + 1) * 128],
                                           rhs=wg[0:32, h_ * 256:(h_ + 1) * 256], start=True, stop=True),
                  r=['wg', 'gTp'], w=[PS(bg)])
            S.add('act', lambda h: h.activation(out=W.nl, in_=ps[bg][:, 0:256], func=AF.Exp, scale=-1.0),
                  r=[PS(bg)], w=[R + 'nl'])
            S.add('act', lambda h: h.activation(out=W.nl, in_=W.nl, func=AF.Ln, bias=cm[:, 785:786], scale=1.0),
                  r=[R + 'nl', 'cm'], w=[R + 'nl'])

        def pf_G2(h_, c):
            W, R, i = wsets[c % 2], 'g%d' % (c % 2), c % 2
            brc = bank()
            S.add('pe', lambda h: h.matmul(ps[brc][:, 0:256], lhsT=CM(C_TRIU_S), rhs=W.nl, start=True, stop=True),
                  r=['cm', R + 'nl'], w=[PS(brc)])
            bb = bank()
            for j in range(2):
                S.add('pe', lambda h, j=j: h.matmul(
                    ps[bb][:, 2 * j:2 * j + 2], lhsT=W.nl[:, j * 128:(j + 1) * 128], rhs=cm[:, 786:788],
                    start=True, stop=True), r=[R + 'nl', 'cm'], w=[PS(bb)])
            S.add('act', lambda h: h.activation(out=W.eend, in_=ps[brc][:, 0:256], func=AF.Exp),
                  r=[PS(brc)], w=[R + 'eend'])
            S.add('act', lambda h: h.activation(out=dec[i], in_=ps[bb][:, 0:4], func=AF.Exp),
                  r=[PS(bb)], w=['dec%d' % i])
            S.add('pool', lambda h: h.tensor_tensor(out=W.kte, in0=W.qk[:, 256:512], in1=W.eend, op=ALU.mult),
                  r=[R + 'qk', R + 'eend'], w=[R + 'kte'])

        def pf_U(h_, c):
            W, R, i = wsets[c % 2], 'g%d' % (c % 2), c % 2
            vv, VR = v3[c % 3], 'gv%d' % (c % 3)
            for j in range(2):
                bu = bank()
                S.add('pe', lambda h, j=j, bu=bu: h.matmul(
                    ps[bu][:, :], lhsT=W.kte[:, j * 128:(j + 1) * 128], rhs=vv, start=True, stop=True),
                    r=[R + 'kte', VR], w=[PS(bu)])
                S.add('dve', lambda h, j=j, bu=bu: h.scalar_tensor_tensor(
                    out=Sf[:, j, :], in0=Sf[:, j, :], scalar=dec[i][:, 2 * j:2 * j + 1], op0=ALU.mult,
                    in1=ps[bu][:, :], op1=ALU.add), r=[PS(bu), 'dec%d' % i, 'Sf'], w=['Sf'])

        for h_ in range(4):
            sk = wq_get(PLAN['pf'][h_][0])
            sv = wq_get(PLAN['pf'][h_][1])
            S.add('dve', lambda h: h.memset(Sf, 0.0), w=['Sf'])
            for c in range(-2, NPF):
                if 0 <= c + 2 < NPF:
                    pf_Pk(h_, c + 2, sk)
                if 0 <= c + 1 < NPF:
                    pf_G2(h_, c + 1)
                if 0 <= c + 2 < NPF:
                    pf_Pv(h_, c + 2, sv)
                if c >= 0:
                    pf_U(h_, c)
                if 0 <= c + 2 < NPF:
                    pf_G01(h_, c + 2)
            give_w(sk)
            give_w(sv)
            S.add('sp', lambda h, h_=h_: h.dma_start(out=sscr[h_].rearrange("(j p) v -> p j v", p=128), in_=Sf),
                  r=['Sf'], w=['sscr%d' % h_], dma='sscr%d' % h_)
        S.retire(XPT + ['gTp'] + ['dec0', 'dec1'])
        A.release(xpT, gTp, *dec)

        gated = A.alloc([NT, D], BF16)
        gTm = gate_T(xT, None, T, 'gTm')
        gng = A.alloc([512], F32)
        Sb = A.alloc([2, 512], BF16)
        s0 = [A.alloc([2, 512], F32) for _ in range(3)]
        Qm = [A.alloc([2, 128], BF16) for _ in range(2)]
        s0b = A.alloc([2, 512], BF16)
        kteM = [A.alloc([256], BF16) for _ in range(2)]

        class Ck:
            pass

        def mk(h_, c, slots):
            ck = Ck()
            ck.h, ck.c, ck.slots = h_, c, slots
            ck.sample = (c == NT - 1)
            if ck.sample:
                ck.W, ck.R, ck.v, ck.VR = wsets[2], 'g2', v3[3], 'gv3'
            else:
                ck.W, ck.R, ck.v, ck.VR = wsets[c % 2], 'g%d' % (c % 2), v3[c % 3], 'gv%d' % (c % 3)
            return ck

        def proj(ck, slot):
            b = bank()
            c = ck.c
            for k in range(KD):
                S.add('pe', lambda h, k=k: h.matmul(
                    ps[b][:, :], lhsT=xT[:, k, c * 128:(c + 1) * 128], rhs=wbuf[slot][:, k, :],
                    start=(k == 0), stop=(k == KD - 1)), r=[XT(c, k), 'w%d' % slot], w=[PS(b)])
            return b

        def P_qk(ck):
            W, R = ck.W, ck.R
            bq = proj(ck, ck.slots[0])
            S.add('act', lambda h: h.mul(out=W.qk[:, 0:256], in_=ps[bq][:, 0:256], mul=1.0 / 16), r=[PS(bq)], w=[R + 'qk'])
            S.add('act', lambda h: h.copy(out=W.qk[:, 256:512], in_=ps[bq][:, 256:512]), r=[PS(bq)], w=[R + 'qk'])

        def P_v(ck):
            bv_ = proj(ck, ck.slots[1])
            S.add('act', lambda h: h.copy(out=ck.v, in_=ps[bv_][:, :]), r=[PS(bv_)], w=[ck.VR])

        def P_r(ck):
            W, R, h_ = ck.W, ck.R, ck.h
            br = proj(ck, ck.slots[2])
            S.add('act', lambda h: h.activation(out=W.sg, in_=ps[br][:, :], func=AF.Exp, scale=-1.0), r=[PS(br)], w=['gsg'])
            S.add('act', lambda h: h.activation(out=W.sg, in_=W.sg, func=AF.Ln, bias=cm[:, 785:786], scale=1.0),
                  r=['gsg', 'cm'], w=['gsg'])
            S.add('act', lambda h: h.activation(out=W.sg, in_=W.sg, func=AF.Exp, scale=-1.0), r=['gsg'], w=['gsg'])
            S.add('dve', lambda h: h.tensor_tensor(out=W.t1, in0=ps[br][:, :], in1=W.sg, op=ALU.mult),
                  r=[PS(br), 'gsg'], w=['gt1'])
            S.add('pool', lambda h: h.tensor_tensor(out=W.srg, in0=W.t1, in1=gng, op=ALU.mult),
                  r=['gt1', 'gng'], w=[R + 'srg'])

        def G01(ck):
            W, R, c, h_ = ck.W, ck.R, ck.c, ck.h
            bg = bank()
            S.add('pe', lambda h: h.matmul(ps[bg][:, 0:256], lhsT=gTm[0:32, c * 128:(c + 1) * 128],
                                           rhs=wg[0:32, h_ * 256:(h_ + 1) * 256], start=True, stop=True),
                  r=['wg', 'gTm'], w=[PS(bg)])
            S.add('act', lambda h: h.activation(out=W.nl, in_=ps[bg][:, 0:256], func=AF.Exp, scale=-1.0),
                  r=[PS(bg)], w=[R + 'nl'])
            S.add('act', lambda h: h.activation(out=W.nl, in_=W.nl, func=AF.Ln, bias=cm[:, 785:786], scale=1.0),
                  r=[R + 'nl', 'cm'], w=[R + 'nl'])

        def G23(ck):
            W, R = ck.W, ck.R
            tri_u = CM(C_BDU_S) if ck.sample else CM(C_TRIU_S)
            tri = CM(C_BD_S) if ck.sample else CM(C_TRI_S)
            brc = bank()
            S.add('pe', lambda h: h.matmul(ps[brc][:, 0:256], lhsT=tri_u, rhs=W.nl, start=True, stop=True),
                  r=['cm', R + 'nl'], w=[PS(brc)])
            bbt = bank()
            for j in range(2):
                S.add('pe', lambda h, j=j: h.matmul(ps[bbt][:, j * 128:(j + 1) * 128], lhsT=W.nl[:, j * 128:(j + 1) * 128],
                                                     rhs=tri, start=True, stop=True), r=[R + 'nl', 'cm'], w=[PS(bbt)])
            S.add('act', lambda h: h.activation(out=W.eend, in_=ps[brc][:, 0:256], func=AF.Exp),
                  r=[PS(brc)], w=[R + 'eend'])
            S.add('act', lambda h: h.activation(out=W.epos.rearrange("p a b -> p (a b)"), in_=ps[bbt][:, 0:256], func=AF.Exp),
                  r=[PS(bbt)], w=[R + 'epos'])
            S.add('act', lambda h: h.activation(out=W.eneg.rearrange("p a b -> p (a b)"), in_=ps[bbt][:, 0:256], func=AF.Exp, scale=-1.0),
                  r=[PS(bbt)], w=[R + 'eneg'])
            S.add('pool', lambda h: h.tensor_tensor(out=W.kte, in0=W.qk[:, 256:512], in1=W.eend, op=ALU.mult),
                  r=[R + 'qk', R + 'eend'], w=[R + 'kte'])

        def G45(ck):
            W, R = ck.W, ck.R
            btr = bank()
            for j in range(4):
                S.add('pe', lambda h, j=j: h.transpose(out=psb[btr][:, j * 128:(j + 1) * 128],
                                                        in_=W.qk[:, j * 128:(j + 1) * 128], identity=identb),
                      r=[R + 'qk', 'identb'], w=[PS(btr)])
            S.add('dve', lambda h: h.tensor_tensor(out=W.qdT.rearrange("p a b -> p (a b)"), in0=psb[btr][:, 0:256],
                                                   in1=W.epos.rearrange("p a b -> p (a b)"), op=ALU.mult),
                  r=[PS(btr), R + 'epos'], w=[R + 'qdT'])
            S.add('dve', lambda h: h.tensor_tensor(out=W.kiT.rearrange("p a b -> p (a b)"), in0=psb[btr][:, 256:512],
                                                   in1=W.eneg.rearrange("p a b -> p (a b)"), op=ALU.mult),
                  r=[PS(btr), R + 'eneg'], w=[R + 'kiT'])

        def B12(ck):
            W, R = ck.W, ck.R
            bs = bank()
            for j in range(2):
                S.add('pe', lambda h, j=j: h.matmul(ps[bs][:, 0:128], lhsT=W.kiT[:, j, :], rhs=W.qdT[:, j, :],
                                                     start=(j == 0), stop=(j == 1)), r=[R + 'kiT', R + 'qdT'], w=[PS(bs)])
            m01 = CM(C_BD01) if ck.sample else CM(C_TRI01)
            S.add('dve', lambda h: h.tensor_tensor(out=W.scm, in0=ps[bs][:, 0:128], in1=m01, op=ALU.mult),
                  r=[PS(bs), 'cm'], w=[R + 'scm'])

        def B3(ck):
            W, R, c, h_ = ck.W, ck.R, ck.c, ck.h
            vv, VR = ck.v, ck.VR
            bo = bank()
            pinned.add(bo)
            ck.bo = bo
            S.add('pe', lambda h: h.matmul(ps[bo][:, :], lhsT=W.scm, rhs=vv, start=True, stop=False),
                  r=[R + 'scm', VR], w=[PS(bo)])
            if not ck.sample:
                for j in range(2):
                    S.add('pe', lambda h, j=j: h.matmul(ps[bo][:, :], lhsT=W.qdT[:, j, :], rhs=Sb[:, j, :],
                                                         start=False, stop=(j == 1)), r=[R + 'qdT', 'Sb'], w=[PS(bo)])
                for j in range(2):
                    bu = bank()
                    S.add('pe', lambda h, j=j, bu=bu: h.matmul(ps[bu][:, :], lhsT=W.kte[:, j * 128:(j + 1) * 128], rhs=vv,
                                                               start=True, stop=True), r=[R + 'kte', VR], w=[PS(bu)])
                    S.add('dve', lambda h, j=j, bu=bu: h.scalar_tensor_tensor(
                        out=Sf[:, j, :], in0=Sf[:, j, :], scalar=W.epos[:, j, 127:128], op0=ALU.mult,
                        in1=ps[bu][:, :], op1=ALU.add), r=[PS(bu), R + 'epos', 'Sf'], w=['Sf'])
                    S.add('act', lambda h, j=j: h.copy(out=Sb[:, j, :], in_=Sf[:, j, :]), r=['Sf'], w=['Sb'])
                if c == NT - 2:
                    S.add('sp', lambda h: h.dma_start(out=gst_p[h_].rearrange("(j p) v -> p j v", p=128), in_=Sf),
                          r=['Sf'], dma='gstp')

        def s0_load(ck, q_):
            sb_ = q_ % 3
            h_ = ck.h
            S.add('sp', lambda h: h.dma_start(out=s0[sb_], in_=st_in[q_, h_].rearrange("(j p) v -> p j v", p=128)),
                  w=['s0_%d' % sb_], dma='s0_%d' % sb_)

        def unit_prep(ck, q_):
            if DBG_NOUNIT or q_ >= 16:
                return
            W, R = ck.W, ck.R
            qb_ = q_ % 2
            S.add('pool', lambda h: h.memset(Qm[qb_], 0.0), w=['Qm%d' % qb_])
            S.add('pool', lambda h: h.tensor_copy(out=Qm[qb_][:, :, 8 * q_:8 * q_ + 8], in_=W.qdT[:, :, 8 * q_:8 * q_ + 8]),
                  r=[R + 'qdT'], w=['Qm%d' % qb_])
            S.add('dve', lambda h: h.tensor_scalar(
                out=kteM[qb_], in0=W.kte, scalar1=cm[:, 768 + q_:769 + q_], scalar2=None, op0=ALU.mult),
                r=[R + 'kte', 'cm'], w=['kteM%d' % qb_])

        def unit_cast(ck, q_):
            if DBG_NOUNIT or q_ < 0 or q_ >= 16:
                return
            sb_ = q_ % 3
            S.add('act', lambda h: h.copy(out=s0b.rearrange("p a b -> p (a b)"), in_=s0[sb_].rearrange("p a b -> p (a b)")),
                  r=['s0_%d' % sb_], w=['s0b'])

        def unit(ck, q_):
            if DBG_NOUNIT:
                return
            W, R, h_ = ck.W, ck.R, ck.h
            vv, VR, bo = ck.v, ck.VR, ck.bo
            sb_ = q_ % 3
            qb_ = q_ % 2
            if q_ + 1 < 16:
                s0_load(ck, q_ + 1)
            for j in range(2):
                S.add('pe', lambda h, j=j: h.matmul(
                    ps[bo][:, :], lhsT=Qm[qb_][:, j, :], rhs=s0b[:, j, :], start=False,
                    stop=(q_ == 15 and j == 1)), r=['Qm%d' % qb_, 's0b'], w=[PS(bo)])
            for j in range(2):
                bu = bank()
                S.add('pe', lambda h, j=j, bu=bu: h.matmul(
                    ps[bu][:, :], lhsT=kteM[qb_][:, j * 128:(j + 1) * 128], rhs=vv, start=True, stop=True),
                    r=['kteM%d' % qb_, VR], w=[PS(bu)])
                S.add('dve', lambda h, j=j, bu=bu: h.scalar_tensor_tensor(
                    out=s0[sb_][:, j, :], in0=s0[sb_][:, j, :], scalar=W.epos[:, j, 8 * q_ + 7:8 * q_ + 8],
                    op0=ALU.mult, in1=ps[bu][:, :], op1=ALU.add),
                    r=[PS(bu), R + 'epos', 's0_%d' % sb_], w=['s0_%d' % sb_])
            S.add('sp', lambda h: h.dma_start(out=gst_s[q_, h_].rearrange("(j p) v -> p j v", p=128), in_=s0[sb_]),
                  r=['s0_%d' % sb_], dma='s0_%d' % sb_)

        def B4(ck):
            W, R, c, h_, bo = ck.W, ck.R, ck.c, ck.h, ck.bo
            S.add('act', lambda h: h.activation(out=W.junk, in_=ps[bo][:, :], func=AF.Square, accum_out=W.sm[:, 0:1]),
                  r=[PS(bo)], w=['gjunk', R + 'sm'])
            S.add('act', lambda h: h.activation(out=W.sm[:, 1:2], in_=W.sm[:, 0:1], func=AF.Ln, bias=eps_t[:, 1:2], scale=1.0 / 512),
                  r=[R + 'sm', 'eps'], w=[R + 'sm'])
            S.add('act', lambda h: h.activation(out=W.sm[:, 2:3], in_=W.sm[:, 1:2], func=AF.Exp, scale=-0.5),
                  r=[R + 'sm'], w=[R + 'sm'])
            S.add('dve', lambda h: h.scalar_tensor_tensor(out=gated[:, c, h_ * 512:(h_ + 1) * 512], in0=ps[bo][:, :],
                                                          scalar=W.sm[:, 2:3], op0=ALU.mult, in1=W.srg, op1=ALU.mult),
                  r=[PS(bo), R + 'sm', R + 'srg'], w=['gated%d' % c])
            pinned.discard(bo)

        def issue_head(h_):
            return tuple(wq_get(i_) for i_ in PLAN['main'][h_])

        for h_ in range(4):
            slots = issue_head(h_)
            S.add('sp', lambda h, h_=h_: h.dma_start(out=gng, in_=gla_ng[:, h_ * 512:(h_ + 1) * 512].partition_broadcast(128)),
                  w=['gng'], dma='c_gng')
            S.add('sp', lambda h, h_=h_: h.dma_start(out=Sf, in_=sscr[h_].rearrange("(j p) v -> p j v", p=128)),
                  r=['sscr%d' % h_], w=['Sf'], dma='sfl')
            S.add('act', lambda h: h.copy(out=Sb.rearrange("p a b -> p (a b)"), in_=Sf.rearrange("p a b -> p (a b)")),
                  r=['Sf'], w=['Sb'])
            cks = {c: mk(h_, c, slots) for c in range(NT)}
            ck8 = cks[NT - 1]
            s0_load(ck8, 0)
            L = [NT - 1] + list(range(NT - 1))
            for i in range(-2, len(L)):
                p = cks[L[i + 2]] if 0 <= i + 2 < len(L) else None
                g = cks[L[i + 1]] if 0 <= i + 1 < len(L) else None
                b_ = cks[L[i]] if 0 <= i < len(L) else None
                u0 = 2 * (i - 1)
                if b_ and not b_.sample:
                    unit_cast(ck8, u0)
                if p:
                    P_qk(p)
                if b_ and not b_.sample:
                    unit(ck8, u0)
                if b_:
                    B12(b_)
                if g:
                    G23(g)
                if b_ and not b_.sample:
                    unit_cast(ck8, u0 + 1)
                if p:
                    P_v(p)
                if b_:
                    B3(b_)
                    if not b_.sample:
                        unit(ck8, u0 + 1)
                        B4(b_)
                if g:
                    G45(g)
                if p:
                    P_r(p)
                    G01(p)
                if b_:
                    unit_prep(ck8, u0 + 2)
                    unit_prep(ck8, u0 + 3)
            B4(ck8)
            for s_ in slots:
                give_w(s_)
        for tc in range(NT):
            transpose_bf16_chunk(gated[:, tc, :], 'gated%d' % tc, xT, XTG, tc)
        S.retire_prefix('g0', 'g1', 'g2', 'gsg', 'gt1', 'gjunk', 'gv', 's0b', 'gated', 'gTm', 'gng', 'Sf', 'Sb', 's0_', 'Qm', 'kteM', 'wg')
        A.release(gated, gTm, gng, Sf, Sb, *s0, *Qm, *kteM, wg, wg16, *v3, sgS, t1S, junkS, s0b)
        for w_ in wsets:
            A.release(*w_.all)

    def sg_layer():
        winv = sg_w_in.rearrange("(k p) f -> p k f", p=128)
        uT = A.alloc([KD, T], BF16)
        binT = A.alloc([16], F32)
        S.add('sp', lambda h: h.dma_start(out=binT, in_=sg_binT), w=['binT'], dma='c_binT')
        wtmp = A.alloc([8, 128], F32)
        WT = [A.alloc([8, 128], BF16) for _ in range(2)]
        for v_ in range(2):
            src = sg_wsT if v_ == 0 else sg_wsTs
            S.add('sp', lambda h, src=src: h.dma_start(out=wtmp, in_=src), w=['wtmp'], dma='c_wtmp')
            m01 = CM(C_TRI01) if v_ == 0 else CM(C_BD01)
            for g in range(8):
                S.add('dve', lambda h, g=g, v_=v_, m01=m01: h.tensor_tensor(out=WT[v_][:, g, :], in0=wtmp[:, g, :], in1=m01, op=ALU.mult),
                      r=['wtmp', 'cm'], w=['WT%d' % v_])
        S.retire(['wtmp'])
        A.release(wtmp)
        bsp1 = A.alloc([8, 128], F32)
        bsp = [bsp1, bsp1]

        def load_bsp(v_):
            S.add('sp', lambda h: h.dma_start(out=bsp1, in_=sg_bsp[v_].partition_broadcast(128)),
                  w=['bsp'], dma='c_bsp')

        load_bsp(0)
        bv = A.alloc([D], F32)
        vg = A.alloc([D], F32)
        vb = A.alloc([D], F32)
        S.add('sp', lambda h: h.dma_start(out=bv, in_=sg_bv.partition_broadcast(128)), w=['sgbv'], dma='c_sgbv')
        S.add('sp', lambda h: h.dma_start(out=vg, in_=sg_vg.partition_broadcast(128)), w=['sgvg'], dma='c_sgvg')
        S.add('sp', lambda h: h.dma_start(out=vb, in_=sg_vb.partition_broadcast(128)), w=['sgvb'], dma='c_sgvb')

        for cb in range(4):
            s_ = wq_get(PLAN['sg_u'][cb])
            for fc in range(4):
                bs = [bank() for _ in range(3)]
                for k in range(KD):
                    for tt in range(3):
                        S.add('pe', lambda h, k=k, tt=tt, fc=fc, b=bs[tt], s_=s_: h.matmul(
                            ps[b][:, 0:TT], lhsT=wbuf[s_][:, k, fc * 128:(fc + 1) * 128],
                            rhs=xT[:, k, tt * TT:(tt + 1) * TT], start=(k == 0), stop=(k == KD - 1)),
                            r=['w%d' % s_] + XTT(tt, k), w=[PS(bs[tt])])
                f_ = cb * 4 + fc
                for tt in range(3):
                    S.add('act', lambda h, tt=tt, b=bs[tt], f_=f_: h.activation(
                        out=uT[:, f_, tt * TT:(tt + 1) * TT], in_=ps[b][:, 0:TT], func=AF.Gelu,
                        bias=binT[:, f_:f_ + 1], scale=1.0), r=[PS(bs[tt]), 'binT'], w=['uT'])
            give_w(s_)
        vs = [wq_get(PLAN['sg_v'][cb]) for cb in range(4)]
        vt = [A.alloc([D], F32) for _ in range(2)]
        vnb = [A.alloc([D], BF16) for _ in range(2)]
        tmp = [A.alloc([512], F32) for _ in range(2)]
        mt = [A.alloc([4, 128], F32) for _ in range(2)]
        st = [A.alloc([4, 6], F32) for _ in range(2)]
        sm = [A.alloc([8], F32) for _ in range(2)]

        def p_stage(tc):
            i = tc % 2
            VT = 'sv%dvt' % i
            bs = [bank() for _ in range(4)]
            for k in range(KD):
                for cb in range(4):
                    S.add('pe', lambda h, k=k, cb=cb, b=bs[cb], tc=tc: h.matmul(
                        ps[b][:, :], lhsT=xT[:, k, tc * 128:(tc + 1) * 128], rhs=wbuf[vs[cb]][:, k, :],
                        start=(k == 0), stop=(k == KD - 1)), r=[XT(tc, k), 'w%d' % vs[cb]], w=[PS(bs[cb])])
            for cb in range(4):
                sl = slice(cb * 512, (cb + 1) * 512)
                S.add('dve', lambda h, b=bs[cb], sl=sl, i=i: h.tensor_tensor(out=vt[i][:, sl], in0=ps[b][:, :], in1=bv[:, sl], op=ALU.add),
                      r=[PS(bs[cb]), 'sgbv'], w=[VT + str(cb)])
                S.add('act', lambda h, sl=sl, i=i: h.activation(out=vt[i][:, sl], in_=vt[i][:, sl], func=AF.Gelu),
                      r=[VT + str(cb)], w=[VT + str(cb)])

        def l_stage(tc):
            i = tc % 2
            sample = (tc == NT - 1)
            V = 'sv%d' % i
            VT = 'sv%dvt' % i
            ln_stats(vt[i], [VT + str(c_) for c_ in range(4)], st[i], sm[i], V, LN_EPS)
            for c in range(4):
                j = (tc * 4 + c) % 2
                sl = slice(c * 512, (c + 1) * 512)
                S.add('act', lambda h, i=i, sl=sl, j=j: h.activation(out=tmp[j], in_=vt[i][:, sl], func=AF.Identity,
                                                                  scale=sm[i][:, 3:4], bias=sm[i][:, 4:5]),
                      r=[VT + str(c), V + 'sm'], w=['svtmp%d' % j])
                S.add('pool', lambda h, j=j, sl=sl: h.tensor_tensor(out=tmp[j], in0=tmp[j], in1=vg[:, sl], op=ALU.mult),
                      r=['svtmp%d' % j, 'sgvg'], w=['svtmp%d' % j])
                if sample:
                    S.add('dve', lambda h, j=j, sl=sl, i=i: h.tensor_tensor(out=vt[i][:, sl], in0=tmp[j], in1=vb[:, sl], op=ALU.add),
                          r=['svtmp%d' % j, 'sgvb', VT + str(c)], w=[VT + str(c)])
                    S.add('act', lambda h, sl=sl, i=i: h.copy(out=vnb[i][:, sl], in_=vt[i][:, sl]), r=[VT + str(c)], w=[V + 'vnb' + str(c)])
                else:
                    S.add('dve', lambda h, j=j, sl=sl, i=i: h.tensor_tensor(out=vnb[i][:, sl], in0=tmp[j], in1=vb[:, sl], op=ALU.add),
                          r=['svtmp%d' % j, 'sgvb'], w=[V + 'vnb' + str(c)])
            if sample:
                S.add('sp', lambda h, i=i: h.dma_start(out=sgv, in_=vt[i]), r=[VT + str(c_) for c_ in range(4)], dma='sgvout')

        def m_stage(tc):
            i = tc % 2
            sample = (tc == NT - 1)
            V = 'sv%d' % i
            v_ = 1 if sample else 0
            for dg in range(4):
                b = bank()
                for q_ in range(4):
                    dc = dg * 4 + q_
                    S.add('pe', lambda h, q_=q_, dc=dc, b=b, i=i, v_=v_: h.matmul(
                        ps[b][:, q_ * 128:(q_ + 1) * 128], lhsT=vnb[i][:, dc * 128:(dc + 1) * 128],
                        rhs=WT[v_][:, dc // 2, :], start=True, stop=True), r=[V + 'vnb' + str(dg), 'WT%d' % v_], w=[PS(b)])
                mi = dg % 2
                bias_ap = bsp[v_][:, 2 * dg:2 * dg + 2, :].unsqueeze(2).broadcast_to([128, 2, 2, 128])
                S.add('dve', lambda h, b=b, mi=mi, bias_ap=bias_ap: h.tensor_tensor(
                    out=mt[mi].rearrange("p (a c) t -> p a c t", a=2), in0=ps[b][:, :].rearrange("p (a c t) -> p a c t", a=2, c=2),
                    in1=bias_ap, op=ALU.add), r=[PS(b), 'bsp'], w=['mt%d' % mi])
                S.add('pool', lambda h, mi=mi, dg=dg, tc=tc: h.tensor_tensor(
                    out=xT[:, 4 * dg:4 * dg + 4, tc * 128:(tc + 1) * 128], in0=mt[mi],
                    in1=uT[:, 4 * dg:4 * dg + 4, tc * 128:(tc + 1) * 128], op=ALU.mult),
                    r=['mt%d' % mi, 'uT'], w=XTG(tc, dg))

        p_stage(0)
        p_stage(1)
        l_stage(0)
        for tc in range(NT):
            if tc + 2 < NT:
                p_stage(tc + 2)
                if tc + 2 == NT - 1:
                    for s_ in vs:
                        give_w(s_)
            if tc + 1 < NT:
                l_stage(tc + 1)
            if tc == NT - 1:
                load_bsp(1)
            m_stage(tc)
        S.retire_prefix('uT', 'binT', 'wtmp', 'WT', 'bsp', 'sgbv', 'sgvg', 'sgvb', 'sv', 'mt')
        A.release(uT, binT, *WT, bsp1, bv, vg, vb, *vt, *vnb, *tmp, *mt, *st, *sm)

    winv_g = gla_w_in.rearrange("(k p) f -> p k f", p=128)
    winv_s = sg_w_in.rearrange("(k p) f -> p k f", p=128)
    PLAN = {}
    if "gla" in phases:
        PLAN['pf'] = []
        for h_ in range(4):
            ik = wq_add([(lambda sl: wbuf[sl][:, :, 256:512], winv_g[:, :, 1024 + h_ * 256:1024 + (h_ + 1) * 256])])
            iv = wq_add([(full_dst(16, 512), winv_g[:, :, 2048 + h_ * 512:2048 + (h_ + 1) * 512])])
            PLAN['pf'].append((ik, iv))
        PLAN['main'] = []
        for h_ in range(4):
            iqk = wq_add([(lambda sl: wbuf[sl][:, :, 0:256], winv_g[:, :, h_ * 256:(h_ + 1) * 256]),
                          (lambda sl: wbuf[sl][:, :, 256:512], winv_g[:, :, 1024 + h_ * 256:1024 + (h_ + 1) * 256])])
            iv = wq_add([(full_dst(16, 512), winv_g[:, :, 2048 + h_ * 512:2048 + (h_ + 1) * 512])])
            ir = wq_add([(full_dst(16, 512), winv_g[:, :, 4096 + h_ * 512:4096 + (h_ + 1) * 512])])
            PLAN['main'].append((iqk, iv, ir))
        wv_ = gla_w_out.rearrange("(k p) d -> p k d", p=128)
        PLAN['gla_out'] = [wq_add([(full_dst(16, 512), wv_[:, :, cb * 512:(cb + 1) * 512])]) for cb in range(4)]

    def plan_mlp(layer):
        w1v = w1[layer].rearrange("(k p) f -> p k f", p=128)
        w2v = w2[layer].rearrange("(fb c p) d -> fb p c d", p=128, c=4)
        out = []
        for fb in range(DFF // 512):
            i1 = wq_add([(full_dst(16, 512), w1v[:, :, fb * 512:(fb + 1) * 512])])
            i2 = wq_add([(full_dst(4, 2048), w2v[fb])])
            out.append((i1, i2))
        return out

    if "mlp0" in phases:
        PLAN['mlp0'] = plan_mlp(0)
    if "sg" in phases:
        PLAN['sg_u'] = [wq_add([(full_dst(16, 512), winv_s[:, :, cb * 512:(cb + 1) * 512])]) for cb in range(4)]
        PLAN['sg_v'] = [wq_add([(full_dst(16, 512), winv_s[:, :, 2048 + cb * 512:2048 + (cb + 1) * 512])]) for cb in range(4)]
        wv_ = sg_w_out.rearrange("(k p) d -> p k d", p=128)
        PLAN['sg_out'] = [wq_add([(full_dst(16, 512), wv_[:, :, cb * 512:(cb + 1) * 512])]) for cb in range(4)]
    if "mlp1" in phases:
        PLAN['mlp1'] = plan_mlp(1)
    wq_issue_pending()

    load_xT(xm, NT, xT, XTG)
    last = [p for p in ("gla", "mlp0", "sg", "mlp1") if p in phases][-1]
    if "gla" in phases:
        gla_layer()
    alloc_xres(xm)
    if "gla" in phases:
        out_proj(PLAN['gla_out'])
        layer_norm(0, ln1g[0:1, :], ln1b[0:1, :], final=(last == "gla"))
    if "mlp0" in phases:
        mlp(0)
        layer_norm(1, ln2g[0:1, :], ln2b[0:1, :], final=(last == "mlp0"))
    if "sg" in phases:
        xres = xres_box[0]
        for tc in range(NT):
            S.add('sp', lambda h, tc=tc, xres=xres: h.dma_start(out=xspill[tc * 128:(tc + 1) * 128, :], in_=xres[:, tc, :]),
                  r=XRA(tc), w=['xspill%d' % tc], dma='xsp%d' % tc)
        free_xres()
        sg_layer()
        xres = A.alloc([NT, D], F32)
        xres_box[0] = xres
        for tc in range(NT):
            S.add('sp', lambda h, tc=tc, xres=xres: h.dma_start(out=xres[:, tc, :], in_=xspill[tc * 128:(tc + 1) * 128, :]),
                  r=['xspill%d' % tc], w=XRA(tc), dma='xres%d' % tc)
        out_proj(PLAN['sg_out'])
        layer_norm(2, ln1g[1:2, :], ln1b[1:2, :], final=(last == "sg"))
    if "mlp1" in phases:
        mlp(1)
        layer_norm(3, ln2g[1:2, :], ln2b[1:2, :], final=True)

    S.emit(nc, es)
    es.close()
    return nc, S, A


def prep_shared(inp):
    f = lambda a: np.ascontiguousarray(np.asarray(a, dtype=np.float32))
    wg_aug = np.zeros((32, 1024), np.float32)
    wg_aug[0:16] = inp["gla_w_gate"][0]
    wg_aug[16] = inp["gla_b_gate"][0]
    ws = np.asarray(inp["sg_w_spatial"][0])
    wsT = ws.transpose(2, 0, 1)
    wsTs = np.tile(ws[:, :8, :8].transpose(2, 0, 1), (16, 1, 16))
    bsp = np.asarray(inp["sg_b_spatial"][0])
    bsp2 = np.stack([bsp, np.tile(bsp[:, :8], (1, 16))])
    b_in = np.asarray(inp["sg_b_in"][0])
    d = dict(
        gla_w_in=f(inp["gla_w_in"][0]), wg_aug=wg_aug, gla_ng=f(np.asarray(inp["gla_norm_g"][0]).reshape(1, D)),
        gla_w_out=f(inp["gla_w_out"][0]), sg_w_in=f(inp["sg_w_in"][0]),
        sg_binT=f(b_in[:D].reshape(16, 128).T), sg_bv=f(b_in[D:].reshape(1, D)),
        sg_vg=f(np.asarray(inp["sg_v_norm_g"][0]).reshape(1, D)), sg_vb=f(np.asarray(inp["sg_v_norm_b"][0]).reshape(1, D)),
        sg_wsT=f(wsT), sg_wsTs=f(wsTs), sg_bsp=f(bsp2), sg_w_out=f(inp["sg_w_out"][0]),
        mlp_w1=f(inp["mlp_w1"]), mlp_w2=f(inp["mlp_w2"]),
        ln1_g=f(inp["ln1_g"]), ln1_b=f(inp["ln1_b"]), ln2_g=f(inp["ln2_g"]), ln2_b=f(inp["ln2_b"]),
        ident=np.eye(128, dtype=np.float32), cmask=make_consts(),
    )
    lnT = np.zeros((4, 128, 2, 16), np.float32)
    for n_, (gk, bk, li) in enumerate([("ln1_g", "ln1_b", 0), ("ln2_g", "ln2_b", 0), ("ln1_g", "ln1_b", 1), ("ln2_g", "ln2_b", 1)]):
        lnT[n_, :, 0, :] = np.asarray(inp[gk][li]).reshape(16, 128).T
        lnT[n_, :, 1, :] = np.asarray(inp[bk][li]).reshape(16, 128).T
    d["lnT"] = lnT
    return d


def prep_core(inp, c):
    xpr = np.asarray(inp["x_prompt"])
    xs = np.asarray(inp["x_sample"])
    b, hf = c // 2, c % 2
    xm = np.concatenate([xpr[b, hf * 1024:(hf + 1) * 1024], xs[16 * c:16 * (c + 1)].reshape(128, D)], axis=0)
    if hf == 1:
        xp = xpr[b, 0:1024]
    else:
        xp = np.zeros((1024, D), np.float32)
    st = np.asarray(inp["state_gla"])[0, 16 * c:16 * (c + 1)]
    return dict(xm=np.ascontiguousarray(xm, dtype=np.float32), xp=np.ascontiguousarray(xp, dtype=np.float32),
                st=np.ascontiguousarray(st, dtype=np.float32))


_CACHE = {}


def kernel(**inputs):
    if "nc" not in _CACHE:
        _CACHE["nc"] = build()[0]
    nc = _CACHE["nc"]
    shared = prep_shared(inputs)
    in_maps = []
    for c in range(8):
        m = dict(shared)
        m.update(prep_core(inputs, c))
        in_maps.append(m)
    res = run_bass_kernel_spmd(nc, in_maps, core_ids=list(range(8)))
    R = res.results
    y_prompt = np.zeros((4, 2048, D), np.float32)
    y_sample = np.zeros((128, 8, D), np.float32)
    gp = np.zeros((1, 4, 4, 256, 512), np.float32)
    gs = np.zeros((1, 128, 4, 256, 512), np.float32)
    sgv = np.zeros((1, 128, 8, D), np.float32)
    for c in range(8):
        b, hf = c // 2, c % 2
        yc = R[c]["y"]
        y_prompt[b, hf * 1024:(hf + 1) * 1024] = yc[:1024]
        y_sample[16 * c:16 * (c + 1)] = yc[1024:].reshape(16, 8, D)
        if hf == 1:
            gp[0, b] = R[c]["gst_p"]
        gs[0, 16 * c:16 * (c + 1)] = R[c]["gst_s"]
        sgv[0, 16 * c:16 * (c + 1)] = R[c]["sgv"].reshape(16, 8, D)
    return (y_prompt, y_sample, gp, gs, sgv)
```
